# Optimizing a Trainium2 kernel written in Bass

```python
import math
import jax, jax.numpy as jnp
from jax import lax
import numpy as np

D_MODEL = 1024
BATCH = 4
SEQ = 4096
DEPTH = 1
DEC_BATCH = 32
DEC_SEQ = 8
PAST_LEN = 8192
PAGE_SIZE = 128

MIX_WIDTH = D_MODEL
A_HEADS = 4
A_DK = 128
A_DV = 128
A_WIDTH = A_HEADS * A_DV
CONV_W = 4
DELTA_CHUNK = 64
CONV_DIM = 2 * A_HEADS * A_DK + A_HEADS * A_DV
B_HEAD_DIM = 64
B_WIDTH = MIX_WIDTH - A_WIDTH
B_HEADS = B_WIDTH // B_HEAD_DIM
DILATED = ((128, 1), (512, 4), (2048, 16))
MAX_WINDOW = 2048
D_FF = ((8 * D_MODEL // 3 + 255) // 256) * 256
EPS = 1e-6
NEG = -1e30
IN_COLS = CONV_DIM + A_WIDTH + 2 * A_HEADS + 3 * B_WIDTH
SPLITS = (CONV_DIM, CONV_DIM + A_WIDTH, CONV_DIM + A_WIDTH + A_HEADS, CONV_DIM + A_WIDTH + 2 * A_HEADS)

kernel_name = "hymba_gdn_dilated_swa_decode_step"


def rmsnorm(x, w):
    xf = x.astype(jnp.float32)
    y = xf * lax.rsqrt(jnp.mean(xf * xf, axis=-1, keepdims=True) + EPS)
    return (y * w.astype(jnp.float32)).astype(x.dtype)


def l2norm(x):
    return x * lax.rsqrt(jnp.sum(x * x, axis=-1, keepdims=True) + EPS)


def alibi_slopes():
    return 2.0 ** (-8.0 * jnp.arange(1, B_HEADS + 1, dtype=jnp.float32) / B_HEADS)


def causal_conv(u, buf, w):
    ext = jnp.concatenate([buf.astype(u.dtype), u], axis=1)
    T = u.shape[1]
    out = sum(ext[:, i:i + T] * w[i] for i in range(CONV_W))
    return jax.nn.silu(out), ext[:, ext.shape[1] - (CONV_W - 1):]


def gated_delta_rule(q, k, v, g, beta, s0):
    B, T, H, DK = q.shape
    DV = v.shape[-1]
    C = min(DELTA_CHUNK, T)
    Tp = -(-T // C) * C
    pad = Tp - T
    if pad:
        pw = ((0, 0), (0, pad), (0, 0), (0, 0))
        q, k, v = jnp.pad(q, pw), jnp.pad(k, pw), jnp.pad(v, pw)
        g, beta = jnp.pad(g, pw[:3]), jnp.pad(beta, pw[:3])
    N = Tp // C

    def chunks(a):
        a = a.reshape((B, N, C, H) + a.shape[3:])
        return jnp.moveaxis(a, (1, 3), (0, 2))

    qc, kc, vc, bc = chunks(q), chunks(k), chunks(v), chunks(beta)
    gc = jnp.cumsum(chunks(g), axis=-1)
    idx = jnp.arange(C)
    incl = idx[:, None] >= idx[None, :]
    strict = idx[:, None] > idx[None, :]
    decay = jnp.exp(jnp.where(incl, gc[..., :, None] - gc[..., None, :], -jnp.inf))
    kb = kc * bc[..., None]
    lower = jnp.where(strict, jnp.einsum('nbhcd,nbhsd->nbhcs', kb, kc) * decay, 0.0)
    eye = jnp.eye(C, dtype=jnp.float32)
    t_inv = lax.linalg.triangular_solve(eye + lower, jnp.broadcast_to(eye, lower.shape),
                                        left_side=True, lower=True, unit_diagonal=True)
    u = t_inv @ (vc * bc[..., None])
    w = t_inv @ (kb * jnp.exp(gc)[..., None])
    qk = jnp.einsum('nbhcd,nbhsd->nbhcs', qc, kc) * decay
    g_last = gc[..., -1]

    def step(S, xs):
        q_i, k_i, u_i, w_i, qk_i, g_i, gl_i = xs
        e = u_i - jnp.einsum('bhcd,bhde->bhce', w_i, S)
        o = (jnp.einsum('bhcd,bhde->bhce', q_i * jnp.exp(g_i)[..., None], S)
             + jnp.einsum('bhcs,bhse->bhce', qk_i, e))
        S = (S * jnp.exp(gl_i)[..., None, None]
             + jnp.einsum('bhcd,bhce->bhde', k_i * jnp.exp(gl_i[..., None] - g_i)[..., None], e))
        return S, o

    s_final, o = lax.scan(step, s0, (qc, kc, u, w, qk, gc, g_last))
    o = jnp.moveaxis(o, (0, 2), (1, 3)).reshape(B, Tp, H, DV)[:, :T]
    return o, s_final


def dilated_branch_prompt(q, k, v, slopes, dil, steps):
    B, S, H, Dh = q.shape
    L = S // dil
    blk = steps
    nb = -(-L // blk)
    Lp = nb * blk

    def split_res(a):
        a = a.reshape(B, L, dil, H, Dh)
        return jnp.pad(a, ((0, 0), (0, Lp - L), (0, 0), (0, 0), (0, 0)))

    front = ((0, 0), (blk, 0), (0, 0), (0, 0), (0, 0))
    qr = split_res(q).reshape(B, nb, blk, dil, H, Dh)
    kr = jnp.pad(split_res(k), front).reshape(B, nb + 1, blk, dil, H, Dh)
    vr = jnp.pad(split_res(v), front).reshape(B, nb + 1, blk, dil, H, Dh)
    kw = jnp.concatenate([kr[:, :-1], kr[:, 1:]], axis=2)
    vw = jnp.concatenate([vr[:, :-1], vr[:, 1:]], axis=2)
    s = jnp.einsum('bnqrhd,bnkrhd->bnrhqk', qr, kw) * (Dh ** -0.5)
    qq = jnp.arange(blk)[:, None]
    kk = jnp.arange(2 * blk)[None, :]
    j = qq + blk - kk
    u_key = jnp.arange(nb)[:, None, None] * blk - blk + kk[None]
    valid = (j >= 0) & (j <= steps) & (u_key >= 0)
    s = s - slopes[:, None, None] * (j * dil).astype(jnp.float32)
    s = jnp.where(valid[None, :, None, None], s, NEG)
    m = jnp.max(s, axis=-1)
    p = jnp.exp(s - m[..., None])
    l = jnp.sum(p, axis=-1)
    num = jnp.einsum('bnrhqk,bnkrhd->bnqrhd', p, vw)
    m = m.transpose(0, 1, 4, 2, 3).reshape(B, Lp * dil, H)[:, :S]
    l = l.transpose(0, 1, 4, 2, 3).reshape(B, Lp * dil, H)[:, :S]
    num = num.reshape(B, Lp * dil, H, Dh)[:, :S]
    return m, l, num


def dilated_branch_sample(q, kc, vc, slopes, dil, steps, n_past):
    T = q.shape[1]
    j = jnp.arange(steps + 1)
    idx = n_past + jnp.arange(T)[:, None] - j[None, :] * dil
    valid = idx >= 0
    idxc = jnp.maximum(idx, 0)
    kg = jnp.take(kc, idxc, axis=1)
    vg = jnp.take(vc, idxc, axis=1)
    s = jnp.einsum('bthd,btjhd->bhtj', q, kg) * (q.shape[-1] ** -0.5)
    s = s - slopes[:, None, None] * (j * dil).astype(jnp.float32)[None, None, :]
    s = jnp.where(valid[None, None], s, NEG)
    m = jnp.max(s, axis=-1)
    p = jnp.exp(s - m[..., None])
    l = jnp.sum(p, axis=-1)
    num = jnp.einsum('bhtj,btjhd->bthd', p, vg)
    return m.transpose(0, 2, 1), l.transpose(0, 2, 1), num


def combine_by_denominator(parts):
    m = jnp.stack([pt[0] for pt in parts])
    l = jnp.stack([pt[1] for pt in parts])
    num = jnp.stack([pt[2] for pt in parts])
    wgt = jnp.exp(m - jnp.max(m, axis=0, keepdims=True))
    den = jnp.sum(wgt * l, axis=0)
    return jnp.sum(wgt[..., None] * num, axis=0) / den[..., None]


def token_mix(h, conv_buf, s0, win_k, win_v, w_in, w_conv, a_log, dt_bias, norm_out_a, norm_out_b, w_out):
    B, T, _ = h.shape
    f32 = jnp.float32
    proj = h @ w_in
    qkv_a, z_a, b_a, a_a, qkv_b = jnp.split(proj, SPLITS, axis=-1)
    conv_out, new_conv = causal_conv(qkv_a, conv_buf, w_conv)
    qa, ka, va = jnp.split(conv_out.astype(f32), [A_HEADS * A_DK, 2 * A_HEADS * A_DK], axis=-1)
    qa = l2norm(qa.reshape(B, T, A_HEADS, A_DK)) * (A_DK ** -0.5)
    ka = l2norm(ka.reshape(B, T, A_HEADS, A_DK))
    va = va.reshape(B, T, A_HEADS, A_DV)
    beta = jax.nn.sigmoid(b_a.astype(f32))
    g = -jnp.exp(a_log.astype(f32)) * jax.nn.softplus(a_a.astype(f32) + dt_bias.astype(f32))
    o_a, s_new = gated_delta_rule(qa, ka, va, g, beta, s0.astype(f32))
    o_a = rmsnorm(o_a, norm_out_a) * jax.nn.silu(z_a.astype(f32).reshape(B, T, A_HEADS, A_DV))
    qb, kb, vb = [t.reshape(B, T, B_HEADS, B_HEAD_DIM) for t in jnp.split(qkv_b, 3, axis=-1)]
    slopes = alibi_slopes()
    qf = qb.astype(f32)
    if win_k is None:
        kf, vf = kb.astype(f32), vb.astype(f32)
        parts = [dilated_branch_prompt(qf, kf, vf, slopes, d, w // d) for (w, d) in DILATED]
        keep = min(MAX_WINDOW, T)
        new_k, new_v = kb[:, T - keep:], vb[:, T - keep:]
    else:
        n_past = win_k.shape[1]
        kc = jnp.concatenate([win_k.astype(kb.dtype), kb], axis=1)
        vc = jnp.concatenate([win_v.astype(vb.dtype), vb], axis=1)
        kcf, vcf = kc.astype(f32), vc.astype(f32)
        parts = [dilated_branch_sample(qf, kcf, vcf, slopes, d, w // d, n_past) for (w, d) in DILATED]
        keep = min(MAX_WINDOW, n_past + T)
        new_k, new_v = kc[:, n_past + T - keep:], vc[:, n_past + T - keep:]
    o_b = rmsnorm(combine_by_denominator(parts), norm_out_b)
    mixed = jnp.concatenate([o_a.reshape(B, T, A_WIDTH), o_b.reshape(B, T, B_WIDTH)], axis=-1).astype(h.dtype)
    return mixed @ w_out, new_conv, s_new.astype(h.dtype), new_k, new_v


def hybrid_block(x, conv_buf, s0, win_k, win_v, norm_mix, w_in, w_conv, a_log, dt_bias,
                 norm_out_a, norm_out_b, w_out, norm_ffn, w_gate, w_up, w_down):
    mix, new_conv, new_rec, new_k, new_v = token_mix(rmsnorm(x, norm_mix), conv_buf, s0, win_k, win_v,
                                                     w_in, w_conv, a_log, dt_bias, norm_out_a, norm_out_b, w_out)
    x = x + mix
    hf = rmsnorm(x, norm_ffn)
    x = x + (jax.nn.silu(hf @ w_gate) * (hf @ w_up)) @ w_down
    return x, new_conv, new_rec, new_k, new_v


def setup_inputs(seed: int = 0) -> dict:
    key = jax.random.key(seed)
    ks = jax.random.split(key, 20)
    f32 = jnp.float32
    win_buf = min(MAX_WINDOW, PAST_LEN)

    def nrm(k, shape, scale):
        return jax.random.normal(k, shape, f32) * scale

    dt = jnp.exp(jax.random.uniform(ks[10], (DEPTH, A_HEADS), f32, minval=math.log(1e-3), maxval=math.log(1e-1)))
    return {
        "x_prompt": nrm(ks[0], (BATCH, SEQ, D_MODEL), 1.0),
        "x_sample": nrm(ks[1], (DEC_BATCH, DEC_SEQ, D_MODEL), 1.0),
        "state_conv": nrm(ks[2], (DEPTH, DEC_BATCH, CONV_W - 1, CONV_DIM), 1.0),
        "state_rec": nrm(ks[3], (DEPTH, DEC_BATCH, A_HEADS, A_DK, A_DV), 0.5),
        "cache_win_k": nrm(ks[4], (DEPTH, DEC_BATCH, win_buf, B_HEADS, B_HEAD_DIM), 1.0),
        "cache_win_v": nrm(ks[5], (DEPTH, DEC_BATCH, win_buf, B_HEADS, B_HEAD_DIM), 1.0),
        "norm_mix": 1.0 + nrm(ks[6], (DEPTH, D_MODEL), 0.02),
        "w_in": nrm(ks[7], (DEPTH, D_MODEL, IN_COLS), D_MODEL ** -0.5),
        "w_conv": nrm(ks[8], (DEPTH, CONV_W, CONV_DIM), CONV_W ** -0.5),
        "a_log": jnp.log(jax.random.uniform(ks[9], (DEPTH, A_HEADS), f32, minval=1.0, maxval=16.0)),
        "dt_bias": dt + jnp.log(-jnp.expm1(-dt)),
        "norm_out_a": 1.0 + nrm(ks[11], (DEPTH, A_DV), 0.02),
        "norm_out_b": 1.0 + nrm(ks[12], (DEPTH, B_HEAD_DIM), 0.02),
        "w_out": nrm(ks[13], (DEPTH, MIX_WIDTH, D_MODEL), MIX_WIDTH ** -0.5),
        "norm_ffn": 1.0 + nrm(ks[14], (DEPTH, D_MODEL), 0.02),
        "w_gate": nrm(ks[15], (DEPTH, D_MODEL, D_FF), D_MODEL ** -0.5),
        "w_up": nrm(ks[16], (DEPTH, D_MODEL, D_FF), D_MODEL ** -0.5),
        "w_down": nrm(ks[17], (DEPTH, D_FF, D_MODEL), D_FF ** -0.5),
        "norm_final": 1.0 + nrm(ks[18], (D_MODEL,), 0.02),
    }


def reference(x_prompt, x_sample, state_conv, state_rec, cache_win_k, cache_win_v, norm_mix, w_in, w_conv,
              a_log, dt_bias, norm_out_a, norm_out_b, w_out, norm_ffn, w_gate, w_up, w_down, norm_final):
    xp, xs = x_prompt, x_sample
    bp = xp.shape[0]
    pc, pr, pk, pv = [], [], [], []
    sc, sr, sk, sv = [], [], [], []
    for layer in range(DEPTH):
        weights = (norm_mix[layer], w_in[layer], w_conv[layer], a_log[layer], dt_bias[layer], norm_out_a[layer],
                   norm_out_b[layer], w_out[layer], norm_ffn[layer], w_gate[layer], w_up[layer], w_down[layer])
        zero_conv = jnp.zeros((bp, CONV_W - 1, CONV_DIM), xp.dtype)
        zero_rec = jnp.zeros((bp, A_HEADS, A_DK, A_DV), jnp.float32)
        xp, c, r, k, v = hybrid_block(xp, zero_conv, zero_rec, None, None, *weights)
        pc.append(c); pr.append(r); pk.append(k); pv.append(v)
        xs, c, r, k, v = hybrid_block(xs, state_conv[layer], state_rec[layer], cache_win_k[layer],
                                      cache_win_v[layer], *weights)
        sc.append(c); sr.append(r); sk.append(k); sv.append(v)
    y_prompt = rmsnorm(xp, norm_final)
    y_sample = rmsnorm(xs, norm_final)
    return (y_prompt, y_sample, jnp.stack(pc), jnp.stack(pr), jnp.stack(pk), jnp.stack(pv),
            jnp.stack(sc), jnp.stack(sr), jnp.stack(sk), jnp.stack(sv))
```

```python
import math
import numpy as np
import ml_dtypes
import concourse.bass as bass
import concourse.mybir as mybir
from concourse.bass_utils import run_bass_kernel_spmd

F32 = mybir.dt.float32
BF16 = mybir.dt.bfloat16
ALU = mybir.AluOpType
AF = mybir.ActivationFunctionType

ENGS = ("pe", "act", "dve", "pool", "sp")
NCORES = 8
D = 1024
NPRE = 2048
NMAIN = 2048
NTOK = NPRE + NMAIN
NS = 32
DFF = 2816
EPS = 1e-6
NEGB = -240000.0
DILS = (1, 4, 16)
import os
KSTOP = int(os.environ.get("KSTOP", "9"))
KOFF = os.environ.get("KOFF", "")


class T:
    __slots__ = ("name", "ap", "lw", "rde", "rdd", "rng", "psum")

    def __init__(self, name, ap, psum=False):
        self.name = name
        self.ap = ap
        self.psum = psum
        self.lw = None
        self.rde = {}
        self.rdd = []
        self.rng = None

    def __getitem__(self, idx):
        return self.ap[idx]


class Sched:
    def __init__(self, nc, n_dma_slots=10):
        self.nc = nc
        self.ops = {e: [] for e in ENGS}
        self.cnt = {e: 0 for e in ENGS}
        self.waited = {e: {} for e in ENGS}
        self.nslots = n_dma_slots
        self.slot_total = {}
        self.slot_next = {"sp": 0, "pool": 0, "act": 0}
        self.dma_info = []

    def _need(self, eng, dep, waits):
        if dep[0] == "e":
            _, e2, seq = dep
            if e2 == eng and eng in ("pe", "sp"):
                return
            key = ("e", e2)
            val = seq
        else:
            key, val = self.dma_info[dep[1]]
        w = self.waited[eng]
        if w.get(key, 0) >= val:
            return
        w[key] = val
        waits.append((key, val))

    def _deps(self, eng, reads, writes):
        waits = []
        for t in reads:
            if t.lw is not None:
                self._need(eng, t.lw, waits)
            if t.psum:
                for e2, seq in t.rde.items():
                    if e2 != eng:
                        self._need(eng, ("e", e2, seq), waits)
        for t in writes:
            lw = t.lw
            if lw is not None:
                self._need(eng, lw, waits)
            for e2, seq in t.rde.items():
                if e2 != eng or eng != "pe":
                    self._need(eng, ("e", e2, seq), waits)
            for did in t.rdd:
                self._need(eng, ("d", did), waits)
        return waits

    def _mark(self, me, reads, writes):
        for t in reads:
            if me[0] == "e":
                if t.rde.get(me[1], 0) < me[2]:
                    t.rde[me[1]] = me[2]
            else:
                t.rdd.append(me[1])
        for t in writes:
            t.lw = me
            t.rde = {}
            t.rdd = []

    def op(self, eng, fn, reads=(), writes=()):
        waits = self._deps(eng, reads, writes)
        self.cnt[eng] += 1
        me = ("e", eng, self.cnt[eng])
        self._mark(me, reads, writes)
        self.ops[eng].append((fn, waits, "c", None))

    def dma(self, eng, out_ap, in_ap, reads=(), writes=(), **kw):
        waits = self._deps(eng, reads, writes)
        slot = self.slot_next[eng]
        self.slot_next[eng] = (slot + 1) % self.nslots
        key = ("d", eng, slot)
        prev = self.slot_total.get(key, 0)
        if prev:
            w = self.waited[eng]
            if w.get(key, 0) < prev:
                w[key] = prev
                waits.append((key, prev))
        val = prev + 16
        self.slot_total[key] = val
        did = len(self.dma_info)
        self.dma_info.append((key, val))
        self._mark(("d", did), reads, writes)
        self.ops[eng].append(((out_ap, in_ap, kw), waits, "d", key))

    def finish(self):
        for eng in ("sp", "pool", "act"):
            waits = []
            for key, val in self.slot_total.items():
                if key[1] != eng:
                    continue
                w = self.waited[eng]
                if w.get(key, 0) < val:
                    w[key] = val
                    waits.append((key, val))
            if waits:
                self.ops[eng].append((None, waits, "w", None))

    def emit(self):
        nc = self.nc
        from contextlib import ExitStack
        with ExitStack() as es:
            sems = {}
            for e in ENGS:
                sems[("e", e)] = es.enter_context(nc.semaphore("s_" + e))
            for key in self.slot_total:
                sems[key] = es.enter_context(nc.semaphore("d_%s_%d" % (key[1], key[2])))
            block = es.enter_context(nc.Block())
            refd = {e: set() for e in ENGS}
            for e in ENGS:
                for fn, waits, kind, extra in self.ops[e]:
                    for key, val in waits:
                        if key[0] == "e":
                            refd[key[1]].add(val)
            rank = {e: {s: i + 1 for i, s in enumerate(sorted(refd[e]))} for e in ENGS}

            def run(engname):
                def body(eng):
                    mysem = sems[("e", engname)]
                    myrank = rank[engname]
                    seq = 0
                    for fn, waits, kind, extra in self.ops[engname]:
                        for key, val in waits:
                            if key[0] == "e":
                                eng.wait_ge(sems[key], rank[key[1]][val])
                            else:
                                eng.wait_ge(sems[key], val)
                        if kind == "c":
                            seq += 1
                            ins = fn(eng)
                            if seq in myrank:
                                ins.then_inc(mysem, 1)
                        elif kind == "d":
                            out_ap, in_ap, kw = fn
                            eng.dma_start(out=out_ap, in_=in_ap, **kw).then_inc(sems[extra], 16)
                return body

            block.tensor(run("pe"))
            block.scalar(run("act"))
            block.vector(run("dve"))
            block.gpsimd(run("pool"))
            block.sync(run("sp"))


class Arena:
    def __init__(self, nc, nbytes):
        self.n = nbytes
        self.base = nc.alloc_sbuf_tensor("arena", [128, nbytes // 2], BF16).ap()
        self.live = []
        self.retired = []
        self.peak = 0

    def alloc(self, name, shape, dt):
        esz = 4 if dt == F32 else 2
        n = 1
        for s in shape[1:]:
            n *= s
        nb = (n * esz + 63) // 64 * 64
        pos = 0
        for a, b, _ in sorted(self.live, key=lambda x: x[0]):
            if a - pos >= nb:
                break
            pos = max(pos, b)
        if pos + nb > self.n:
            raise RuntimeError("arena full allocating %s (%d bytes) live=%d" % (name, nb, sum(b - a for a, b, _ in self.live)))
        a, b = pos, pos + nb
        v = self.base[:, a // 2:(a + n * esz) // 2]
        if dt == F32:
            v = v.bitcast(F32)
        if len(shape) == 3:
            v = v.rearrange("p (x y) -> p x y", x=shape[1])
        elif len(shape) == 4:
            v = v.rearrange("p (x y z) -> p x y z", x=shape[1], y=shape[2])
        if shape[0] < 128:
            v = v[0:shape[0]]
        t = T(name, v)
        t.rng = (a, b)
        keep = []
        for ra, rb, rt in self.retired:
            if ra < b and a < rb:
                for e2, seq in rt.rde.items():
                    if t.rde.get(e2, 0) < seq:
                        t.rde[e2] = seq
                t.rdd.extend(rt.rdd)
                if rt.lw is not None:
                    if rt.lw[0] == "e":
                        if t.rde.get(rt.lw[1], 0) < rt.lw[2]:
                            t.rde[rt.lw[1]] = rt.lw[2]
                    else:
                        t.rdd.append(rt.lw[1])
                if ra >= a and rb <= b:
                    continue
            keep.append((ra, rb, rt))
        self.retired = keep
        self.live.append((a, b, t))
        self.peak = max(self.peak, b)
        return t

    def free(self, *ts):
        for t in ts:
            for i, (a, b, tt) in enumerate(self.live):
                if tt is t:
                    self.live.pop(i)
                    self.retired.append((a, b, t))
                    break
            else:
                raise RuntimeError("free of unknown tile " + t.name)


class Ring:
    def __init__(self, tiles):
        self.t = tiles
        self.i = 0

    def next(self):
        t = self.t[self.i]
        self.i = (self.i + 1) % len(self.t)
        return t


class KB:
    def __init__(self, nc):
        self.nc = nc
        self.S = Sched(nc)
        self.A = Arena(nc, 207 * 1024)
        self.banks = [T("ps%d" % i, nc.alloc_psum_tensor("ps%d" % i, [128, 512], F32).ap(), psum=True) for i in range(8)]
        self.ps = Ring(self.banks)
        self._rr = 0

    def set_ring(self, n):
        self.ps = Ring(self.banks[0:n])

    def act(self, out, in_, func, R, W, scale=None, bias=None, accum=None):
        kw = {}
        if scale is not None:
            kw["scale"] = scale
        if bias is not None:
            kw["bias"] = bias
        if accum is not None:
            kw["accum_out"] = accum
        self.S.op("act", lambda e: e.activation(out=out, in_=in_, func=func, **kw), R, W)

    def mm(self, out, lhsT, rhs, R, W, start=True, stop=True, skip=False):
        self.S.op("pe", lambda e: e.matmul(out, lhsT=lhsT, rhs=rhs, start=start, stop=stop, skip_group_check=skip), R, W)

    def tr(self, out, in_, ident, R, W):
        self.S.op("pe", lambda e: e.transpose(out=out, in_=in_, identity=ident), R, W)

    def tt(self, eng, out, in0, in1, op, R, W):
        self.S.op(eng, lambda e: e.tensor_tensor(out=out, in0=in0, in1=in1, op=op), R, W)

    def ts(self, eng, out, in0, s1, op0, R, W, s2=None, op1=None):
        if op1 is None:
            self.S.op(eng, lambda e: e.tensor_scalar(out=out, in0=in0, scalar1=s1, scalar2=None, op0=op0), R, W)
        else:
            self.S.op(eng, lambda e: e.tensor_scalar(out=out, in0=in0, scalar1=s1, scalar2=s2, op0=op0, op1=op1), R, W)

    def stt(self, eng, out, in0, scalar, in1, op0, op1, R, W):
        self.S.op(eng, lambda e: e.scalar_tensor_tensor(out=out, in0=in0, scalar=scalar, in1=in1, op0=op0, op1=op1), R, W)

    def cp(self, eng, out, in_, R, W):
        if eng == "act":
            self.S.op("act", lambda e: e.activation(out=out, in_=in_, func=AF.Copy), R, W)
        else:
            self.S.op(eng, lambda e: e.tensor_copy(out=out, in_=in_), R, W)

    def recip(self, out, in_, R, W):
        self.S.op("dve", lambda e: e.reciprocal(out=out, in_=in_), R, W)

    def memset(self, eng, ap, val, W):
        self.S.op(eng, lambda e: e.memset(ap, val), (), W)

    def dma(self, eng, out, in_, R=(), W=(), **kw):
        self.S.dma(eng, out, in_, R, W, **kw)

    def alt(self, engs=("dve", "pool")):
        self._rr += 1
        return engs[self._rr % len(engs)]

    def rings(self, name, n, shape, dt):
        return Ring([self.A.alloc("%s%d" % (name, i), shape, dt) for i in range(n)])

    def free_ring(self, *rings):
        for r in rings:
            self.A.free(*r.t)


def sl(start, count, step):
    return slice(start, start + step * (count - 1) + 1, step)


def bank_bf(ps):
    return ps.ap.bitcast(BF16)


def norm_stats(k, xt, p, nb_tile, C, dimscale=1.0 / D):
    sm = C["sm"].next()
    hb = C["hb"].next()
    k.act(hb[0:p, :], xt[0:p, :], AF.Square, [xt], [hb, sm], accum=sm[0:p, 0:1])
    k.act(sm[0:p, 1:2], sm[0:p, 0:1], AF.Ln, [sm], [sm], scale=dimscale, bias=EPS)
    k.act(sm[0:p, 2:3], sm[0:p, 1:2], AF.Exp, [sm], [sm], scale=-0.5)
    k.stt("dve", hb[0:p, :], xt[0:p, :], sm[0:p, 2:3], nb_tile[0:p, :], ALU.mult, ALU.mult, [xt, sm, nb_tile], [hb])
    return hb


def transpose_to(k, hb, p, hT, c0, C):
    ps = k.ps.next()
    pb = bank_bf(ps)
    for c in range(8):
        k.tr(pb[:, c * 128:c * 128 + p], hb[0:p, c * 128:(c + 1) * 128], C["ident"][0:p, 0:p], [hb, C["ident"]], [ps])
    k.cp("act", hT[:, :, c0:c0 + p], pb[:, :].rearrange("q (c t) -> q c t", c=8)[:, :, 0:p], [ps], [hT])


def norm_transpose(k, xt, p, nb_tile, hT, c0, C, dimscale=1.0 / D):
    hb = norm_stats(k, xt, p, nb_tile, C, dimscale)
    transpose_to(k, hb, p, hT, c0, C)


def gdn_tile(k, C, G, hT_ap, qT_ap, kT_ap, vT_ap, S, Sb, deps, main, nf, mix_out, vm=None):
    w_a = C["w_a"]
    ident = C["ident"]
    sc = G["sc"].next()

    def bc(c0):
        return sc[:, c0:c0 + 4].unsqueeze(2).to_broadcast([128, 4, 128])

    def ps4(ps):
        return ps[:, :].rearrange("p (h c) -> p h c", h=4)

    psBA = k.ps.next()
    for kc in range(8):
        k.mm(psBA[:, 0:8], hT_ap[:, kc, :], w_a[:, kc, 2048:2056], deps + [w_a], [psBA], start=(kc == 0), stop=(kc == 7))
    k.act(sc[:, 0:4], psBA[:, 0:4], AF.Exp, [psBA], [sc], scale=-1.0)
    k.ts("dve", sc[:, 0:4], sc[:, 0:4], 1.0, ALU.add, [sc], [sc])
    k.recip(sc[:, 0:4], sc[:, 0:4], [sc], [sc])
    if vm is not None:
        k.ts("dve", sc[:, 0:4], sc[:, 0:4], vm[:, 0:1], ALU.mult, [sc, vm], [sc])
    k.ts("dve", sc[:, 4:8], sc[:, 0:4], -1.0, ALU.mult, [sc], [sc])
    yield
    k.tt("dve", sc[:, 8:12], psBA[:, 4:8], C["dtb"][:, :], ALU.add, [psBA, C["dtb"]], [sc])
    k.act(sc[:, 8:12], sc[:, 8:12], AF.Exp, [sc], [sc])
    k.act(sc[:, 8:12], sc[:, 8:12], AF.Ln, [sc], [sc], bias=1.0)
    k.tt("dve", sc[:, 8:12], sc[:, 8:12], C["negA"][:, :], ALU.mult, [sc, C["negA"]], [sc])
    if vm is not None:
        k.ts("dve", sc[:, 8:12], sc[:, 8:12], vm[:, 0:1], ALU.mult, [sc, vm], [sc])
    yield
    psG = k.ps.next()
    k.mm(psG[:, 0:4], C["mIU"][:, 0:128], sc[:, 8:12], [C["mIU"], sc], [psG])
    k.mm(psG[:, 4:8], C["onesf"][:, :], sc[:, 8:12], [C["onesf"], sc], [psG])
    k.cp("dve", sc[:, 12:20], psG[:, 0:8], [psG], [sc])
    k.act(sc[:, 20:28], sc[:, 12:20], AF.Exp, [sc], [sc])
    k.tt("dve", sc[:, 28:32], sc[:, 16:20], sc[:, 12:16], ALU.subtract, [sc], [sc])
    k.act(sc[:, 28:32], sc[:, 28:32], AF.Exp, [sc], [sc])
    yield
    tg = G["tg"].next()
    k.tt("dve", tg[:, :, :], C["mIU4"][:, :, :], bc(8), ALU.mult, [C["mIU4"], sc], [tg])
    psR = k.ps.next()
    k.mm(psR[:, :], C["onesf"][:, :], tg[:, :, :], [C["onesf"], tg], [psR])
    dec = G["dec"].next()
    k.tt("dve", dec[:, :, :], ps4(psR), bc(12), ALU.subtract, [psR, sc], [dec])
    k.act(dec[:, :, :], dec[:, :, :], AF.Exp, [dec], [dec])
    dS = G["dS"].next()
    k.stt("dve", dS[:, :, :], dec[:, :, :], 1.0, C["mSU4"][:, :, :], ALU.min, ALU.mult, [dec, C["mSU4"]], [dS])
    k.tt("dve", dS[:, :, :], dS[:, :, :], bc(4), ALU.mult, [dS, sc], [dS])
    if main:
        egr = G["egr"].next()
        k.act(egr[:, :, :], psR[:, :].rearrange("p (h c) -> p h c", h=4), AF.Exp, [psR], [egr])
        dI = G["dI"].next()
        k.stt("dve", dI[:, :, :], dec[:, :, :], 1.0, C["mIU4"][:, :, :], ALU.min, ALU.mult, [dec, C["mIU4"]], [dI])
    yield
    psk = k.ps.next()
    pbk = bank_bf(psk)
    for h in range(4):
        k.tr(pbk[:, h * 128:(h + 1) * 128], kT_ap[:, h, :], ident[:, :], deps + [ident], [psk])
    psv = k.ps.next()
    pbv = bank_bf(psv)
    for h in range(4):
        k.tr(pbv[:, h * 128:(h + 1) * 128], vT_ap[:, h, :], ident[:, :], deps + [ident], [psv])
    kg = G["kg"].next()
    kdec = G["kdec"].next()
    vtok = G["vtok"].next()
    ktok = G["e"].next()
    k.cp("act", ktok[:, :, :], pbk[:, 0:512].rearrange("p (h c) -> p h c", h=4), [psk], [ktok])
    k.cp("dve", vtok[:, :, :], pbv[:, 0:512].rearrange("p (h c) -> p h c", h=4), [psv], [vtok])
    k.tt("dve", kg[:, :, :], ktok[:, :, :], bc(20), ALU.mult, [ktok, sc], [kg])
    k.tt("dve", kdec[:, :, :], ktok[:, :, :], bc(28), ALU.mult, [ktok, sc], [kdec])
    yield
    psGm = k.ps.next()
    for h in range(4):
        k.mm(psGm[:, h * 128:(h + 1) * 128], kT_ap[:, h, :], kT_ap[:, h, :], deps, [psGm])
    R = G["R"].next()
    Rf = dec
    k.tt("dve", Rf[:, :, :], ps4(psGm), dS[:, :, :], ALU.mult, [psGm, dS], [Rf])
    k.cp("act", R[:, :, :], Rf[:, :, :], [Rf], [R])
    if main:
        psQK = k.ps.next()
        for h in range(4):
            k.mm(psQK[:, h * 128:(h + 1) * 128], kT_ap[:, h, :], qT_ap[:, h, :], deps, [psQK])
        QKT = G["QKT"].next()
        k.tt("dve", QKT[:, :, :], psQK[:, :].rearrange("p (h c) -> p h c", h=4), dI[:, :, :], ALU.mult, [psQK, dI], [QKT])
        qg = G["qg"].next()
        k.tt("dve", qg[:, :, :], qT_ap, egr[:, :, :], ALU.mult, deps + [egr], [qg])
    P = G["P"].next()
    k.tt("dve", P[:, :, :], R[:, :, :], C["id4b"][:, :, :], ALU.add, [R, C["id4b"]], [P])
    yield
    psr = k.ps.next()
    pbr = bank_bf(psr)
    for h in range(4):
        k.tr(pbr[:, h * 128:(h + 1) * 128], R[:, h, :], ident[:, :], [R, ident], [psr])
    RT = G["RT"].next()
    k.cp("act", RT[:, :, :], pbr[:, 0:512].rearrange("p (h c) -> p h c", h=4), [psr], [RT])
    for kk in range(1, nf + 1):
        yield
        psRk = psRTk = psP = None
        if kk <= nf - 2:
            psRk = k.ps.next()
            for h in range(4):
                k.mm(psRk[:, h * 128:(h + 1) * 128], RT[:, h, :], R[:, h, :], [RT, R], [psRk])
        if kk <= nf - 1:
            psRTk = k.ps.next()
            for h in range(4):
                k.mm(psRTk[:, h * 128:(h + 1) * 128], R[:, h, :], RT[:, h, :], [RT, R], [psRTk])
        if kk >= 2:
            psP = k.ps.next()
            for h in range(4):
                k.mm(psP[:, h * 128:(h + 1) * 128], RT[:, h, :], P[:, h, :], [RT, P], [psP])
        if psRk is not None:
            Rn = G["R"].next()
            k.cp("act", Rn[:, :, :], psRk[:, :].rearrange("p (h c) -> p h c", h=4), [psRk], [Rn])
        if psRTk is not None:
            RTn = G["RT"].next()
            k.cp("dve", RTn[:, :, :], psRTk[:, :].rearrange("p (h c) -> p h c", h=4), [psRTk], [RTn])
        if psP is not None:
            Pn = G["P"].next()
            k.tt("dve", Pn[:, :, :], psP[:, :].rearrange("p (h c) -> p h c", h=4), P[:, :, :], ALU.add, [psP, P], [Pn])
            P = Pn
        if psRk is not None:
            R = Rn
        if psRTk is not None:
            RT = RTn
    yield
    pst_ = k.ps.next()
    pbt_ = bank_bf(pst_)
    for h in range(4):
        k.tr(pbt_[:, h * 128:(h + 1) * 128], P[:, h, :], ident[:, :], [P, ident], [pst_])
    PTf = tg
    k.cp("act", PTf[:, :, :], pbt_[:, 0:512].rearrange("p (h c) -> p h c", h=4), [pst_], [PTf])
    psE = k.ps.next()
    for h in range(4):
        k.mm(psE[:, h * 128:(h + 1) * 128], PTf[:, h, :], Rf[:, h, :], [PTf, Rf], [psE])
    Et = G["Ec"].next()
    Ec = G["Ec"].next()
    k.tt("dve", Et[:, :, :], C["id4b"][:, :, :], P[:, :, :], ALU.subtract, [C["id4b"], P], [Et])
    k.tt("dve", Ec[:, :, :], psE[:, :].rearrange("p (h c) -> p h c", h=4), Et[:, :, :], ALU.add, [psE, Et], [Ec])
    yield
    psC1 = k.ps.next()
    for h in range(4):
        k.mm(psC1[:, h * 128:(h + 1) * 128], Ec[:, h, :], vtok[:, h, :], [Ec, vtok], [psC1])
    psC2 = k.ps.next()
    for h in range(4):
        k.mm(psC2[:, h * 128:(h + 1) * 128], Ec[:, h, :], kg[:, h, :], [Ec, kg], [psC2])
    vtok2 = G["vtok"].next()
    kg2 = G["kg"].next()
    k.tt("dve", vtok2[:, :, :], psC1[:, :].rearrange("p (h c) -> p h c", h=4), vtok[:, :, :], ALU.add, [psC1, vtok], [vtok2])
    k.tt("dve", kg2[:, :, :], psC2[:, :].rearrange("p (h c) -> p h c", h=4), kg[:, :, :], ALU.add, [psC2, kg], [kg2])
    vtok, kg = vtok2, kg2
    yield
    psU = k.ps.next()
    for h in range(4):
        k.mm(psU[:, h * 128:(h + 1) * 128], P[:, h, :], vtok[:, h, :], [P, vtok], [psU])
    psW = k.ps.next()
    for h in range(4):
        k.mm(psW[:, h * 128:(h + 1) * 128], kg[:, h, :], P[:, h, :], [P, kg], [psW])
    ub = G["ub"].next()
    k.cp("dve", ub[:, :, :], ps4(psU), [psU], [ub])
    wT = G["wT"].next()
    k.cp("act", wT[:, :, :], psW[:, :].rearrange("p (h c) -> p h c", h=4), [psW], [wT])
    yield
    psS1 = k.ps.next()
    for h in range(4):
        k.mm(psS1[:, h * 128:(h + 1) * 128], wT[:, h, :], Sb[:, h, :], [wT, Sb], [psS1])
    e = G["e"].next()
    k.tt("dve", ub[:, :, :], ps4(psS1), ub[:, :, :], ALU.subtract, [psS1, ub], [ub])
    k.tt("dve", e[:, :, :], ub[:, :, :], bc(4), ALU.mult, [ub, sc], [e])
    if main:
        psO = k.ps.next()
        for h in range(4):
            k.mm(psO[:, h * 128:(h + 1) * 128], qg[:, h, :], Sb[:, h, :], [qg, Sb], [psO], start=True, stop=False)
            k.mm(psO[:, h * 128:(h + 1) * 128], QKT[:, h, :], e[:, h, :], [QKT, e], [psO], start=False, stop=True)
    if main:
        o32 = ub
        k.cp("act", o32[:, :, :], psO[:, :].rearrange("p (h c) -> p h c", h=4), [psO], [o32])
    psSn = k.ps.next()
    for h in range(4):
        k.mm(psSn[:, h * 128:(h + 1) * 128], kdec[:, h, :], e[:, h, :], [kdec, e], [psSn])
    k.tt("dve", S[:, :, :], S[:, :, :], bc(24), ALU.mult, [S, sc], [S])
    k.tt("dve", S[:, :, :], S[:, :, :], ps4(psSn), ALU.add, [S, psSn], [S])
    k.cp("act", Sb[:, :, :], S[:, :, :], [S], [Sb])
    if not main:
        return
    yield
    jk = G["jk"].next()
    for h in range(4):
        k.act(jk[:, :], o32[:, h, :], AF.Square, [o32], [jk, sc], accum=sc[:, 32 + h:33 + h])
    k.act(sc[:, 32:36], sc[:, 32:36], AF.Ln, [sc], [sc], scale=1.0 / 128, bias=EPS)
    k.act(sc[:, 32:36], sc[:, 32:36], AF.Exp, [sc], [sc], scale=-0.5)
    yield
    psZ = k.ps.next()
    for kc in range(8):
        k.mm(psZ[:, :], hT_ap[:, kc, :], w_a[:, kc, 1536:2048], deps + [w_a], [psZ], start=(kc == 0), stop=(kc == 7))
    ez = G["ez"].next()
    k.act(ez[:, :], psZ[:, :], AF.Exp, [psZ], [ez], scale=-1.0)
    k.act(ez[:, :], ez[:, :], AF.Ln, [ez], [ez], bias=1.0)
    k.act(ez[:, :], ez[:, :], AF.Exp, [ez], [ez], scale=-1.0)
    zn = G["zn"].next()
    k.tt("dve", zn[:, :], psZ[:, :], C["noa"][:, :], ALU.mult, [psZ, C["noa"]], [zn])
    k.tt("dve", zn[:, :], zn[:, :], ez[:, :], ALU.mult, [zn, ez], [zn])
    yield
    og = G["og"].next()
    k.tt("dve", o32[:, :, :], o32[:, :, :], bc(32), ALU.mult, [o32, sc], [o32])
    k.tt("dve", og[:, :].rearrange("p (h c) -> p h c", h=4), o32[:, :, :], zn[:, :].rearrange("p (h c) -> p h c", h=4), ALU.mult, [o32, zn], [og])
    mix_out(og)


def run_interleaved(gens, offs=3, maxact=2):
    active, pending, steps = [], list(gens), {}
    while active or pending:
        if pending and len(active) < maxact and (not active or steps[id(active[-1])] >= offs):
            gnew = pending.pop(0)
            active.append(gnew)
            steps[id(gnew)] = 0
        for gg in list(active):
            try:
                next(gg)
                steps[id(gg)] += 1
            except StopIteration:
                active.remove(gg)


EXTRA_RINGS = [("R", 2, [128, 4, 128], BF16), ("RT", 2, [128, 4, 128], BF16), ("P", 2, [128, 4, 128], BF16),
               ("kg", 2, [128, 4, 128], BF16), ("vtok", 2, [128, 4, 128], BF16), ("Ec", 2, [128, 4, 128], BF16),
               ("sc", 1, [128, 40], F32), ("tg", 1, [128, 4, 128], F32), ("dec", 1, [128, 4, 128], F32),
               ("egr", 1, [128, 4, 128], F32), ("ub", 1, [128, 4, 128], F32)] + \
              [(nm, 1, [128, 4, 128], BF16) for nm in ("dS", "dI", "kdec", "QKT", "qg", "wT", "e")]


def silu_from_psum(k, G, ps_ap, psT, n):
    e32 = G["e32"].next()
    c32 = G["c32"].next()
    k.act(e32[:, 0:n], ps_ap, AF.Exp, [psT], [e32], scale=-1.0)
    k.act(e32[:, 0:n], e32[:, 0:n], AF.Ln, [e32], [e32], bias=1.0)
    k.act(e32[:, 0:n], e32[:, 0:n], AF.Exp, [e32], [e32], scale=-1.0)
    k.tt("dve", c32[:, 0:n], ps_ap, e32[:, 0:n], ALU.mult, [psT, e32], [c32])
    return c32


def l2norm_chunk(k, C, G, c32, n, out_ap, outT, qscale):
    sq = G["sq"].next()
    k.tt("dve", sq[:, 0:n], c32[:, 0:n], c32[:, 0:n], ALU.mult, [c32], [sq])
    psC = k.ps.next()
    k.mm(psC[:, 0:n], C["onesb"][:, :], sq[:, 0:n], [C["onesb"], sq], [psC])
    l32 = G["l32"].next()
    k.act(l32[:, 0:n], psC[:, 0:n], AF.Ln, [psC], [l32], bias=EPS)
    if qscale:
        k.act(l32[:, 0:n], l32[:, 0:n], AF.Exp, [l32], [l32], scale=-0.5, bias=C["lnq"][:, 0:1])
        rd = [c32, l32, C["lnq"]]
    else:
        k.act(l32[:, 0:n], l32[:, 0:n], AF.Exp, [l32], [l32], scale=-0.5)
        rd = [c32, l32]
    k.tt("dve", out_ap, c32[:, 0:n], l32[:, 0:n], ALU.mult, rd, [outT])


def stage_G(k, C, IO):
    A = k.A
    w_a = A.alloc("w_a", [128, 8, 2056], BF16)
    C["w_a"] = w_a
    for kc in range(8):
        k.dma("pool", w_a[:, kc, :], IO["w_in"][kc * 128:(kc + 1) * 128, 0:2056], W=[w_a])
    nmb = A.alloc("nmb", [128, 1024], F32)
    k.dma("sp", nmb[:, :], IO["norm_mix"].partition_broadcast(128), W=[nmb])
    wconv = A.alloc("wconv", [128, 4, 12], F32)
    for i in range(4):
        k.dma("sp", wconv[:, i, :], IO["w_conv"][i].rearrange("(c p) -> p c", p=128), W=[wconv], allow_slow_non_contiguous=True)
    diag = A.alloc("diag", [128, 12, 4, 128], BF16)
    for ch in range(12):
        for i in range(4):
            k.ts(k.alt(), diag[:, ch, i, :], C["identf"][:, 0:128], wconv[:, i, ch:ch + 1], ALU.mult, [C["identf"], wconv], [diag])
    dtb = A.alloc("dtb", [128, 4], F32)
    negA = A.alloc("negA", [128, 4], F32)
    C["dtb"], C["negA"] = dtb, negA
    k.dma("sp", dtb[:, :], IO["dt_bias"].partition_broadcast(128), W=[dtb])
    k.dma("sp", negA[:, :], IO["a_log"].partition_broadcast(128), W=[negA])
    k.act(negA[:, :], negA[:, :], AF.Exp, [negA], [negA])
    k.ts("dve", negA[:, :], negA[:, :], -1.0, ALU.mult, [negA], [negA])
    noa = A.alloc("noa", [128, 512], F32)
    C["noa"] = noa
    for h in range(4):
        k.dma("sp", noa[:, h * 128:(h + 1) * 128], IO["noa"].partition_broadcast(128), W=[noa])
    lnq = A.alloc("lnq", [128, 1], F32)
    C["lnq"] = lnq
    k.memset("pool", lnq[:, :], math.log(128.0 ** -0.5), [lnq])

    G = {}
    C["sm"] = k.rings("sm", 4, [128, 8], F32)
    C["hb"] = k.rings("hb", 2, [128, 1024], BF16)
    xr = k.rings("xr", 2, [128, 1024], F32)
    hT = A.alloc("hT", [128, 8, 512], BF16)
    ext = A.alloc("ext", [128, 12, 515], BF16)
    qkT = A.alloc("qkT", [128, 8, 512], BF16)
    vT = A.alloc("vT", [128, 4, 512], BF16)
    S = A.alloc("S", [128, 4, 128], F32)
    Sb = A.alloc("Sb", [128, 4, 128], BF16)
    for nm, n in (("e32", 2), ("c32", 2), ("l32", 2), ("ez", 1), ("zn", 1)):
        G[nm] = k.rings(nm, n, [128, 512], F32)
    G["sq"] = k.rings("sq", 2, [128, 512], BF16)
    G["og"] = k.rings("og", 2, [128, 512], BF16)
    G["sc"] = k.rings("sc", 3, [128, 40], F32)
    G["jk"] = k.rings("jk", 1, [128, 128], F32)
    for nm, n in (("tg", 2), ("dec", 2), ("egr", 2), ("ub", 2)):
        G[nm] = k.rings(nm, n, [128, 4, 128], F32)
    for nm, n in (("dS", 2), ("dI", 2), ("kg", 4), ("kdec", 2), ("vtok", 4), ("R", 4), ("RT", 4), ("P", 4), ("Ec", 4),
                  ("QKT", 2), ("qg", 2), ("wT", 2), ("e", 2)):
        G[nm] = k.rings(nm, n, [128, 4, 128], BF16)

    mixT_a = C["mixT_a"]
    k.memset("pool", ext[:, :, :], 0.0, [ext])
    k.memset("pool", S[:, :, :], 0.0, [S])
    k.memset("dve", Sb[:, :, :], 0.0, [Sb])

    def feature_chunks(pchs, chs, hT_ap, hdeps, n, ext_dst, conv_rhs, qk_out, v_out, outTs, conv_out=None):
        for ch in pchs:
            psA = k.ps.next()
            for kc in range(8):
                k.mm(psA[:, 0:n], w_a[:, kc, ch * 128:(ch + 1) * 128], hT_ap(kc), hdeps + [w_a], [psA], start=(kc == 0), stop=(kc == 7))
            ext_dst(ch, psA)
        for ch in chs:
            psB = k.ps.next()
            for i in range(4):
                k.mm(psB[:, 0:n] if conv_out is None else conv_out(psB), diag[:, ch, i, :], conv_rhs(ch, i), [diag] + outTs["ext"], [psB], start=(i == 0), stop=(i == 3))
            c32 = silu_from_psum(k, G, psB[:, 0:n], psB, n)
            if ch < 8:
                l2norm_chunk(k, C, G, c32, n, qk_out(ch), outTs["qk"], qscale=(ch < 4))
            else:
                k.cp("dve", v_out(ch - 8), c32[:, 0:n], [c32], [outTs["v"]])

    for st in range(NTOK // 512):
        main = st >= NPRE // 512
        for tt4 in range(4):
            tok0 = st * 512 + tt4 * 128
            xt = xr.next()
            k.dma("sp", xt[:, :], IO["xp"][tok0:tok0 + 128, :], W=[xt])
            norm_transpose(k, xt, 128, nmb, hT, tt4 * 128, C)
        chs = list(range(12)) if main else list(range(4, 12))
        pchs = list(range(12)) if st >= NPRE // 512 - 1 else chs

        def ext_dst(ch, psA):
            k.cp("dve", ext[:, ch, 3:515], psA[:, :], [psA], [ext])

        feature_chunks(pchs, chs, lambda kc: hT[:, kc, :], [hT], 512, ext_dst,
                       lambda ch, i: ext[:, ch, i:i + 512],
                       lambda ch: qkT[:, ch, :], lambda j: vT[:, j, :], {"ext": [ext], "qk": qkT, "v": vT})
        if st == NTOK // 512 - 1:
            pre3 = A.alloc("pre3", [3, 1536], F32)
            for j in range(3):
                psT3 = k.ps.next()
                for kc in range(8):
                    k.mm(psT3[0:3, :], hT[:, kc, 509:512], w_a[:, kc, j * 512:(j + 1) * 512], [hT, w_a], [psT3], start=(kc == 0), stop=(kc == 7))
                k.cp("dve", pre3[0:3, j * 512:(j + 1) * 512], psT3[0:3, :], [psT3], [pre3])
            k.dma("sp", IO["conv_p"][:, :], pre3[0:3, :], R=[pre3])
            A.free(pre3)
        halo = G.setdefault("halo", A.alloc("halo", [128, 12, 3], BF16))
        k.cp("pool", halo[:, :, :], ext[:, :, 512:515], [ext], [halo])
        gens = []
        for tt4 in range(4):
            cs = slice(tt4 * 128, (tt4 + 1) * 128)
            gcol = st * 512 + tt4 * 128 - NPRE

            def mix_out(og, gcol=gcol):
                psm = k.ps.next()
                pbm = bank_bf(psm)
                for h in range(4):
                    k.tr(pbm[:, h * 128:(h + 1) * 128], og[:, h * 128:(h + 1) * 128], C["ident"][:, :], [og, C["ident"]], [psm])
                k.cp("act", mixT_a[:, :, gcol:gcol + 128], pbm[:, 0:512].rearrange("p (h c) -> p h c", h=4), [psm], [mixT_a])

            gens.append(gdn_tile(k, C, G, hT[:, :, cs], qkT[:, 0:4, cs], qkT[:, 4:8, cs], vT[:, :, cs], S, Sb, [hT, qkT, vT], main, 7, mix_out))
        k.free_ring(C["hb"], xr, G["e32"], G["c32"], G["l32"], G["sq"])
        extra = {}
        for nm, n, shp, dt in EXTRA_RINGS:
            extra[nm] = [A.alloc("x_%s%d" % (nm, i), shp, dt) for i in range(n)]
            G[nm].t.extend(extra[nm])
        run_interleaved(gens, offs=3, maxact=3)
        for nm, tl in extra.items():
            for t in tl:
                G[nm].t.remove(t)
            G[nm].i = 0
            A.free(*tl)
        C["hb"] = k.rings("hb", 2, [128, 1024], BF16)
        xr = k.rings("xr", 2, [128, 1024], F32)
        for nm in ("e32", "c32", "l32"):
            G[nm] = k.rings(nm, 2, [128, 512], F32)
        G["sq"] = k.rings("sq", 2, [128, 512], BF16)
        k.cp("pool", ext[:, :, 0:3], halo[:, :, :], [halo], [ext])
    k.dma("sp", IO["rec_p"].rearrange("h d e -> d h e"), S[:, :, :], R=[S])
    A.free(hT, ext, qkT, vT, G["halo"])

    xt = xr.next()
    k.dma("sp", xt[0:NS, :], IO["xs"][:, :], W=[xt])
    hTs = A.alloc("hTs", [128, 8, NS], BF16)
    norm_transpose(k, xt, NS, nmb, hTs, 0, C)
    exts = A.alloc("exts", [128, 12, 4, 11], BF16)
    sct = A.alloc("sct", [12, 1536], F32)
    k.dma("sp", sct[0:12, :], IO["sconv"].rearrange("s i c -> (s i) c"), W=[sct])
    psh = k.ps.next()
    for ch in range(12):
        k.tr(psh[:, ch * 12:(ch + 1) * 12], sct[0:12, ch * 128:(ch + 1) * 128], C["identf"][0:12, 0:12], [sct, C["identf"]], [psh])
    k.cp("dve", exts[:, :, :, 0:3], psh[:, 0:144].rearrange("p (c s i) -> p c s i", c=12, s=4), [psh], [exts])
    qks = A.alloc("qks", [128, 8, NS], BF16)
    vs = A.alloc("vs", [128, 4, NS], BF16)

    def ext_dst_s(ch, psA):
        k.cp("act", exts[:, ch, :, 3:11], psA[:, 0:NS].rearrange("p (s t) -> p s t", s=4), [psA], [exts])

    feature_chunks(list(range(12)), list(range(12)), lambda kc: hTs[:, kc, :], [hTs], NS, ext_dst_s,
                   lambda ch, i: exts[:, ch, :, i:i + 8],
                   lambda ch: qks[:, ch, :], lambda j: vs[:, j, :], {"ext": [exts], "qk": qks, "v": vs},
                   conv_out=lambda psB: psB[:, 0:NS].rearrange("p (s t) -> p s t", s=4))
    pres = A.alloc("pres", [NS, 1536], F32)
    for j in range(3):
        psT3 = k.ps.next()
        for kc in range(8):
            k.mm(psT3[0:NS, :], hTs[:, kc, :], w_a[:, kc, j * 512:(j + 1) * 512], [hTs, w_a], [psT3], start=(kc == 0), stop=(kc == 7))
        k.cp("dve", pres[0:NS, j * 512:(j + 1) * 512], psT3[0:NS, :], [psT3], [pres])
    for s in range(4):
        k.dma("sp", IO["conv_s"][s, :, :], pres[8 * s + 5:8 * s + 8, :], R=[pres])
    hpad = k.rings("hpad", 2, [128, 8, 128], BF16)
    qkpad = k.rings("qkpad", 2, [128, 8, 128], BF16)
    vpad = k.rings("vpad", 2, [128, 4, 128], BF16)
    Ss = k.rings("Ss", 2, [128, 4, 128], F32)
    Sbs = k.rings("Sbs", 2, [128, 4, 128], BF16)
    for r in (hpad, qkpad, vpad):
        for t in r.t:
            k.memset(k.alt(), t[:, :, :], 0.0, [t])
    sgens = []
    for s in range(4):
        hp, qp, vp, S_s, Sb_s = hpad.next(), qkpad.next(), vpad.next(), Ss.next(), Sbs.next()
        k.cp("pool", hp[:, :, 0:8], hTs[:, :, 8 * s:8 * s + 8], [hTs], [hp])
        k.cp("pool", qp[:, :, 0:8], qks[:, :, 8 * s:8 * s + 8], [qks], [qp])
        k.cp("pool", vp[:, :, 0:8], vs[:, :, 8 * s:8 * s + 8], [vs], [vp])
        k.dma("sp", S_s[:, :, :], IO["srec"][s].rearrange("h d e -> d h e"), W=[S_s])
        k.cp("act", Sb_s[:, :, :], S_s[:, :, :], [S_s], [Sb_s])

        def mix_out_s(og, s=s):
            psm = k.ps.next()
            pbm = bank_bf(psm)
            for h in range(4):
                k.tr(pbm[:, h * 8:(h + 1) * 8], og[0:8, h * 128:(h + 1) * 128], C["ident"][0:8, 0:8], [og, C["ident"]], [psm])
            k.cp("act", mixT_a[:, :, NMAIN + 8 * s:NMAIN + 8 * s + 8], pbm[:, 0:32].rearrange("p (h c) -> p h c", h=4), [psm], [mixT_a])

        def seq_gen(s=s, hp=hp, qp=qp, vp=vp, S_s=S_s, Sb_s=Sb_s, mix_out_s=mix_out_s):
            yield from gdn_tile(k, C, G, hp[:, :, :], qp[:, 0:4, :], qp[:, 4:8, :], vp[:, :, :], S_s, Sb_s, [hp, qp, vp], True, 3, mix_out_s, vm=C["vmask"])
            k.dma("sp", IO["rec_s"][s].rearrange("h d e -> d h e"), S_s[:, :, :], R=[S_s])

        sgens.append(seq_gen())
        if s % 2 == 1:
            run_interleaved(sgens)
            sgens = []

    A.free(w_a, nmb, wconv, diag, dtb, negA, noa, lnq, S, Sb, hTs, exts, sct, qks, vs, pres)
    k.free_ring(C["sm"], C["hb"], xr, hpad, qkpad, vpad, Ss, Sbs)
    for nm, r in G.items():
        if isinstance(r, Ring):
            k.free_ring(r)
    G.clear()


def merge_free(A, parent, children):
    for ch in children:
        for e2, seq in ch.rde.items():
            if parent.rde.get(e2, 0) < seq:
                parent.rde[e2] = seq
        parent.rdd.extend(ch.rdd)
        if ch.lw is not None:
            if ch.lw[0] == "e":
                if parent.rde.get(ch.lw[1], 0) < ch.lw[2]:
                    parent.rde[ch.lw[1]] = ch.lw[2]
            else:
                parent.rdd.append(ch.lw[1])
    A.free(parent)


def setup_consts(k, C, IO):
    A = k.A
    cf = IO["cf32"]
    identf = A.alloc("identf", [128, 128], F32)
    mIU = A.alloc("mIU", [128, 128], F32)
    onesf = A.alloc("onesf", [128, 128], F32)
    mSU4 = A.alloc("mSU4", [128, 4, 128], F32)
    mIU4 = A.alloc("mIU4", [128, 4, 128], F32)
    id4f = A.alloc("id4f", [128, 4, 128], F32)
    k.dma("sp", identf[:, :], cf[:, 0:128], W=[identf])
    k.dma("sp", id4f[:, :, :], cf[:, 0:512].rearrange("p (h c) -> p h c", h=4), W=[id4f])
    k.dma("sp", mSU4[:, :, :], cf[:, 512:1024].rearrange("p (h c) -> p h c", h=4), W=[mSU4])
    k.dma("sp", mIU4[:, :, :], cf[:, 1024:1536].rearrange("p (h c) -> p h c", h=4), W=[mIU4])
    k.dma("sp", mIU[:, :], cf[:, 1024:1152], W=[mIU])
    k.dma("sp", onesf[:, :], cf[:, 1536:1664], W=[onesf])
    ident = A.alloc("ident", [128, 128], BF16)
    onesb = A.alloc("onesb", [128, 128], BF16)
    id4b = A.alloc("id4b", [128, 4, 128], BF16)
    k.cp("dve", ident[:, :], identf[:, :], [identf], [ident])
    k.cp("dve", onesb[:, :], onesf[:, :], [onesf], [onesb])
    k.cp("dve", id4b[:, :, :], id4f[:, :, :], [id4f], [id4b])
    vmask = A.alloc("vmask", [128, 1], F32)
    edge = A.alloc("edge", [128, 1], F32)
    k.dma("sp", vmask[:, :], IO["vmask"][:, :], W=[vmask])
    k.dma("sp", edge[:, :], IO["edge8"][:, :], W=[edge])
    C.update(identf=identf, mIU=mIU, onesf=onesf, mSU4=mSU4, mIU4=mIU4, ident=ident, onesb=onesb, id4b=id4b,
             vmask=vmask, edge=edge)
    A.free(id4f)


def stage_K(k, C, IO):
    A = k.A
    w_b = A.alloc("w_b", [128, 8, 1536], BF16)
    for kc in range(8):
        k.dma("pool", w_b[:, kc, :], IO["w_in"][kc * 128:(kc + 1) * 128, 2056:3592], W=[w_b])
    nmb = A.alloc("nmb2", [128, 1024], F32)
    k.dma("sp", nmb[:, :], IO["norm_mix"].partition_broadcast(128), W=[nmb])
    C["sm"] = k.rings("smk", 6, [128, 8], F32)
    C["hb"] = k.rings("hbk", 5, [128, 1024], BF16)
    xr = k.rings("xrk", 4, [128, 1024], F32)
    hTr = k.rings("hTk", 2, [128, 8, 512], BF16)
    kT_b = A.alloc("kT_b", [128, 4, NTOK], BF16)
    vT_b = A.alloc("vT_b", [128, 4, NTOK], BF16)
    qT_b = A.alloc("qT_b", [128, 4, NMAIN], BF16)
    kv = [T("kv%d" % st, None) for st in range(NTOK // 512)]
    for t in kv:
        for par in (kT_b, vT_b, qT_b):
            for e2, seq in par.rde.items():
                if t.rde.get(e2, 0) < seq:
                    t.rde[e2] = seq
            t.rdd.extend(par.rdd)
    C.update(kT_b=kT_b, vT_b=vT_b, qT_b=qT_b, kv=kv)
    ost = k.rings("ost", 2, [128, 512], F32)
    def k_stats(st):
        hbs = []
        for tt4 in range(4):
            tok0 = st * 512 + tt4 * 128
            xt = xr.next()
            k.dma("sp", xt[:, :], IO["xp"][tok0:tok0 + 128, :], W=[xt])
            hbs.append(norm_stats(k, xt, 128, nmb, C))
        return hbs

    def k_tr(hbs):
        hT = hTr.next()
        for tt4 in range(4):
            transpose_to(k, hbs[tt4], 128, hT, tt4 * 128, C)
        return hT

    hT_next = k_tr(k_stats(0))
    for st in range(NTOK // 512):
        main = st >= NPRE // 512
        hT = hT_next
        hbs_next = k_stats(st + 1) if st + 1 < NTOK // 512 else None
        for ch in (range(12) if main else range(4, 12)):
            psA = k.ps.next()
            for kc in range(8):
                k.mm(psA[:, :], w_b[:, kc, ch * 128:(ch + 1) * 128], hT[:, kc, :], [hT, w_b], [psA], start=(kc == 0), stop=(kc == 7))
            if ch < 4:
                dst = qT_b[:, ch, (st * 512 - NPRE):(st * 512 - NPRE) + 512]
            elif ch < 8:
                dst = kT_b[:, ch - 4, st * 512:(st + 1) * 512]
            else:
                dst = vT_b[:, ch - 8, st * 512:(st + 1) * 512]
            k.cp(k.alt(("act", "dve")), dst, psA[:, :], [psA], [kv[st]])
        if hbs_next is not None:
            hT_next = k_tr(hbs_next)
        if main and KSTOP >= 2:
            for tt4 in range(4):
                tok0 = st * 512 + tt4 * 128
                for src, dstname in ((kT_b, "wk_p"), (vT_b, "wv_p")):
                    pst = k.ps.next()
                    pbt = bank_bf(pst)
                    for c in range(4):
                        k.tr(pbt[:, c * 128:(c + 1) * 128], src[:, c, tok0:tok0 + 128], C["ident"][:, :], [kv[st], C["ident"]], [pst])
                    o = ost.next()
                    k.cp(k.alt(("act", "dve")), o[:, :], pbt[:, 0:512], [pst], [o])
                    k.dma("sp", IO[dstname][tok0 - NPRE:tok0 - NPRE + 128, :], o[:, :], R=[o])
    if KSTOP < 3:
        return
    xt = xr.next()
    k.dma("sp", xt[0:NS, :], IO["xs"][:, :], W=[xt])
    hTs = A.alloc("hTs2", [128, 8, NS], BF16)
    norm_transpose(k, xt, NS, nmb, hTs, 0, C)
    qTs = A.alloc("qTs", [128, 4, NS], BF16)
    kTn = A.alloc("kTn", [128, 4, NS], BF16)
    for ch in range(8):
        psA = k.ps.next()
        for kc in range(8):
            k.mm(psA[:, 0:NS], w_b[:, kc, ch * 128:(ch + 1) * 128], hTs[:, kc, :], [hTs, w_b], [psA], start=(kc == 0), stop=(kc == 7))
        dstT = qTs if ch < 4 else kTn
        k.cp("act", dstT[:, ch % 4, :], psA[:, 0:NS], [psA], [dstT])
    if KSTOP < 4:
        return
    vn_aug = A.alloc("vn_aug", [NS, 8, 66], BF16)
    k.memset("pool", vn_aug[:, :, :], 1.0, [vn_aug])
    for j, dstname in ((1, "wk_s"), (2, "wv_s")):
        psA = k.ps.next()
        for kc in range(8):
            k.mm(psA[0:NS, :], hTs[:, kc, :], w_b[:, kc, j * 512:(j + 1) * 512], [hTs, w_b], [psA], start=(kc == 0), stop=(kc == 7))
        o = ost.next()
        k.cp("dve", o[0:NS, :], psA[0:NS, :], [psA], [o])
        for s in range(4):
            if "D" not in KOFF:
                k.dma("sp", IO[dstname][s, 2040:2048, :], o[8 * s:8 * s + 8, :], R=[o])
        if j == 2 and "A" not in KOFF:
            k.cp("act", vn_aug[:, :, 0:64], psA[0:NS, :].rearrange("p (h e) -> p h e", h=8), [psA], [vn_aug])
    if KSTOP < 5:
        return
    Qbd = A.alloc("Qbd", [128, 4, 4, 48], BF16)
    k.memset("pool", Qbd[:, :, :, :], 0.0, [Qbd])
    for c in range(4):
        for br in range(3):
            k.cp(k.alt(), Qbd[0:64, c, :, br * 8:br * 8 + 8], qTs[0:64, c, :].rearrange("p (s t) -> p s t", s=4), [qTs], [Qbd])
            k.cp(k.alt(), Qbd[64:128, c, :, 24 + br * 8:32 + br * 8], qTs[64:128, c, :].rearrange("p (s t) -> p s t", s=4), [qTs], [Qbd])
    C.update(kTn=kTn, vn_aug=vn_aug, Qbd=Qbd)
    A.free(w_b, nmb, hTs, qTs)
    k.free_ring(C["sm"], C["hb"], xr, hTr, ost)


def finalize_attn(k, C, F, acc_ap, accT, n, dst):
    for c0 in range(0, n, 512):
        w = min(512, n - c0)
        sq = F["sq"].next()
        k.act(sq[0:65, 0:w], acc_ap[0:65, c0:c0 + w], AF.Square, [accT], [sq])
        psF = k.ps.next()
        k.mm(psF[0:64, 0:w], C["gmb"][0:65, 0:64], sq[0:65, 0:w], [C["gmb"], sq], [psF])
        l32 = F["l32"].next()
        k.act(l32[0:64, 0:w], psF[0:64, 0:w], AF.Ln, [psF], [l32])
        k.act(l32[0:64, 0:w], l32[0:64, 0:w], AF.Exp, [l32], [l32], scale=-0.5)
        dap, dT = dst(c0, w)
        k.stt("dve", dap, acc_ap[0:64, c0:c0 + w], C["nob"][0:64, 0:1], l32[0:64, 0:w], ALU.mult, ALU.mult, [accT, C["nob"], l32], [dT])


def stage_B(k, C, IO):
    A = k.A
    kT_b, vT_b, qT_b, kv = C["kT_b"], C["vT_b"], C["qT_b"], C["kv"]
    identb = C["ident"]
    pb = A.alloc("pbias", [128, 8, 3, 256], BF16)
    k.dma("sp", pb[:, :, :, :], IO["pbias"][:, :, :, :], W=[pb])
    pe_ = A.alloc("pedge", [128, 8, 3, 128], BF16)
    for h in range(8):
        k.ts(k.alt(), pe_[:, h, :, :], pb[:, h, :, 128:256], C["edge"][:, 0:1], ALU.add, [pb, C["edge"]], [pe_])
    gmb = A.alloc("gmb", [65, 64], BF16)
    k.dma("sp", gmb[:, :], IO["gmb"][:, :], W=[gmb])
    nob = A.alloc("nob", [64, 1], F32)
    k.dma("sp", nob[:, :], IO["nob"].rearrange("(p o) -> p o", o=1), W=[nob])
    C.update(gmb=gmb, nob=nob)
    mixT_b = C["mixT_b"] = A.alloc("mixT_b", [128, 4, NMAIN + NS], BF16)
    F = {"sq": k.rings("fsq", 2, [65, 512], BF16), "l32": k.rings("fl32", 2, [64, 512], F32)}
    Vblk = A.alloc("Vblk", [128, 69, 2, 66], BF16)
    k.memset("pool", Vblk[:, :, :, :], 1.0, [Vblk])
    accr = k.rings("acc", 1, [65, NMAIN], F32)
    PTr = k.rings("PT", 6, [128, 256], BF16)
    otmp = k.rings("otmp", 2, [64, 512], BF16)
    blocks = []
    for br, d in enumerate(DILS):
        for r in range(d):
            for n in range(16 // d - 1, 32 // d):
                blocks.append((br, r, n))
    bidx = {b: i for i, b in enumerate(blocks)}
    assert len(blocks) == 69
    for c in range(4):
        for g0 in range(0, 69, 4):
            grp = blocks[g0:g0 + 4]
            psv = k.ps.next()
            pbv = bank_bf(psv)
            for j, (br, r, n) in enumerate(grp):
                d = DILS[br]
                k.tr(pbv[:, j * 128:(j + 1) * 128], vT_b[:, c, sl(r + d * 128 * n, 128, d)], identb[:, :], kv + [identb], [psv])
            ng = len(grp)
            k.cp(k.alt(("act", "dve")), Vblk[:, g0:g0 + ng, :, 0:64],
                 pbv[:, 0:ng * 128].rearrange("p (g h e) -> p g h e", g=ng, h=2), [psv], [Vblk])
        for hh in range(2):
            h = 2 * c + hh
            po = 64 * hh
            acc = accr.next()
            hb_list = []
            for br, d in enumerate(DILS):
                nq0, nq1 = 16 // d, 32 // d
                for r in range(d):
                    for n in range(nq0 - 1, nq1):
                        hb_list.append((br, d, r, n, n == nq0 - 1, n == nq1 - 1, nq0))
            PTs = {}

            def emit_S(i, h=h, c=c, po=po):
                br, d, r, n, first, last, nq0 = hb_list[i]
                ks = sl(r + d * 128 * n, 128, d)
                if first:
                    q0, N, bias, bT = r + d * 128 * nq0 - NPRE, 128, pe_[:, h, br, :], pe_
                elif last:
                    q0, N, bias, bT = r + d * 128 * n - NPRE, 128, pb[:, h, br, 0:128], pb
                else:
                    q0, N, bias, bT = r + d * 128 * n - NPRE, 256, pb[:, h, br, :], pb
                psS = k.ps.next()
                k.mm(psS[:, 0:N], kT_b[po:po + 64, c, ks], qT_b[po:po + 64, c, sl(q0, N, d)], kv, [psS], start=True, stop=False)
                k.mm(psS[:, 0:N], identb[:, :], bias, [identb, bT], [psS], start=False, stop=True)
                PT = PTr.next()
                k.act(PT[:, 0:N], psS[:, 0:N], AF.Exp, [psS], [PT], scale=0.125)
                PTs[i] = PT

            def emit_PV(i, hh=hh, acc=acc):
                br, d, r, n, first, last, nq0 = hb_list[i]
                if first:
                    return
                PT, prevPT = PTs[i], PTs[i - 1]
                prev_first = hb_list[i - 1][4]
                psO = k.ps.next()
                pp = prevPT[:, 0:128] if prev_first else prevPT[:, 128:256]
                k.mm(psO[0:65, 0:128], Vblk[:, bidx[(br, r, n - 1)], hh, 0:65], pp, [Vblk, prevPT], [psO], start=True, stop=False)
                k.mm(psO[0:65, 0:128], Vblk[:, bidx[(br, r, n)], hh, 0:65], PT[:, 0:128], [Vblk, PT], [psO], start=False, stop=True)
                qc = r + d * 128 * n - NPRE
                qcols = sl(qc, 128, d)
                if br == 0:
                    k.cp("dve", acc[:, qcols], psO[0:65, 0:128], [psO], [acc])
                else:
                    k.tt("dve", acc[:, qcols], acc[:, qcols], psO[0:65, 0:128], ALU.add, [acc, psO], [acc])
                PTs.pop(i - 1, None)

            LOOK = 3
            for i in range(len(hb_list) + LOOK):
                if i < len(hb_list):
                    emit_S(i)
                if i - LOOK >= 0:
                    emit_PV(i - LOOK)
            if hh == 0:
                finalize_attn(k, C, F, acc, acc, NMAIN, lambda c0, w, c=c: (mixT_b[0:64, c, c0:c0 + w], mixT_b))
            else:
                def dst(c0, w, c=c):
                    o = otmp.next()
                    dst.last = (o, c0, w)
                    return o[0:64, 0:w], o
                for c0 in range(0, NMAIN, 512):
                    finalize_attn(k, C, F, acc[:, c0:c0 + 512], acc, 512, dst)
                    o, _, w = dst.last
                    k.dma("sp", mixT_b[64:128, c, c0:c0 + 512], o[0:64, 0:512], R=[o], W=[mixT_b])
    merge_free(A, kT_b, kv)
    A.free(vT_b, qT_b, pb, pe_, Vblk)
    k.free_ring(accr, PTr)

    k.set_ring(7)
    psOs = k.banks[7]
    sbc = A.alloc("sbc", [128, 16, 192], BF16)
    sbn = A.alloc("sbn", [NS, 4, 192], BF16)
    k.dma("sp", sbc[:, :, :], IO["sbias_c"][:, :, :], W=[sbc])
    k.dma("sp", sbn[:, :, :], IO["sbias_n"][:, :, :], W=[sbn])
    kTn, vn_aug, Qbd = C["kTn"], C["vn_aug"], C["Qbd"]
    kc32r = k.rings("kc32", 2, [128, 512], F32)
    vc32r = k.rings("vc32", 2, [128, 512], F32)
    kcbr = k.rings("kcb", 2, [128, 512], BF16)
    vaugr = k.rings("vaug", 2, [128, 8, 66], BF16)
    kTsr = k.rings("kTs", 2, [128, 4, 128], BF16)
    PTsr = k.rings("PTs", 2, [128, 192], BF16)
    tmpr = k.rings("ptmp", 2, [128, 8, 8], BF16)
    Pqr = [k.rings("Pq%d" % s, 2, [128, 8, NS], BF16) for s in range(4)]
    for t in vaugr.t:
        k.memset("pool", t[:, :, :], 1.0, [t])
    for s in range(4):
        for t in Pqr[s].t:
            k.memset(k.alt(), t[:, :, :], 0.0, [t])
    for s in range(4):
        for kt in range(17):
            if kt < 16:
                kc32, vc32, kcb, vaug, kTs = kc32r.next(), vc32r.next(), kcbr.next(), vaugr.next(), kTsr.next()
                k.dma("sp", kc32[:, :], IO["ck"][s, 128 * kt:128 * kt + 128, :], W=[kc32])
                k.dma("sp", vc32[:, :], IO["cv"][s, 128 * kt:128 * kt + 128, :], W=[vc32])
                for src, nm in ((kc32, "wk_s"), (vc32, "wv_s")):
                    if kt == 0:
                        k.dma("sp", IO[nm][s, 0:120, :], src[8:128, :], R=[src])
                    else:
                        k.dma("sp", IO[nm][s, 128 * kt - 8:128 * kt + 120, :], src[:, :], R=[src])
                k.cp("act", kcb[:, :], kc32[:, :], [kc32], [kcb])
                k.cp("dve", vaug[:, :, 0:64], vc32[:, :].rearrange("p (h e) -> p h e", h=8), [vc32], [vaug])
                pst = k.ps.next()
                pbt = bank_bf(pst)
                for c in range(4):
                    k.tr(pbt[:, c * 128:(c + 1) * 128], kcb[:, c * 128:(c + 1) * 128], identb[:, :], [kcb, identb], [pst])
                k.cp("act", kTs[:, :, :], pbt[:, 0:512].rearrange("p (c t) -> p c t", c=4), [pst], [kTs])
                np_ = 128
                lhs_k = lambda c, kTs=kTs: kTs[:, c, :]
                kdep = kTs
                bias = sbc[:, kt, :]
                bT = sbc
                vsrc = vaug
                idb = identb[:, :]
            else:
                np_ = NS
                lhs_k = lambda c: kTn[:, c, :]
                kdep = kTn
                bias = sbn[0:NS, s, :]
                bT = sbn
                vsrc = vn_aug
                idb = identb[0:NS, 0:NS]
            psS = k.ps.next()
            k.mm(psS[0:np_, 0:192], idb, bias, [identb, bT], [psS], start=True, stop=False)
            for c in range(4):
                k.mm(psS[0:np_, c * 48:(c + 1) * 48], lhs_k(c), Qbd[:, c, s, :], [kdep, Qbd], [psS], start=False, stop=(c == 3))
            PTs = PTsr.next()
            k.act(PTs[0:np_, :], psS[0:np_, 0:192], AF.Exp, [psS], [PTs], scale=0.125)
            Pq = Pqr[s].next()
            tmp = tmpr.next()
            P4 = PTs[0:np_, :].rearrange("p (h b t) -> p h b t", h=8, b=3)
            k.tt("dve", tmp[0:np_, :, :], P4[:, :, 0, :], P4[:, :, 1, :], ALU.add, [PTs], [tmp])
            k.tt("dve", Pq[0:np_, :, 8 * s:8 * s + 8], tmp[0:np_, :, :], P4[:, :, 2, :], ALU.add, [PTs, tmp], [Pq])
            for h in range(8):
                k.mm(psOs[0:65, h * NS:(h + 1) * NS], vsrc[0:np_, h, 0:65], Pq[0:np_, h, :], [vsrc, Pq], [psOs],
                     start=(s == 0 and kt == 0 and h == 0), stop=(s == 3 and kt == 16), skip=True)
    accs = A.alloc("accs", [65, 8, NS], F32)
    k.cp("dve", accs[:, :, :], psOs[0:65, 0:8 * NS].rearrange("p (h t) -> p h t", h=8), [psOs], [accs])
    for h in range(8):
        c, hh = h // 2, h % 2
        if hh == 0:
            finalize_attn(k, C, F, accs[:, h, :], accs, NS, lambda c0, w, c=c: (mixT_b[0:64, c, NMAIN:NMAIN + NS], mixT_b))
        else:
            o = otmp.next()
            finalize_attn(k, C, F, accs[:, h, :], accs, NS, lambda c0, w, o=o: (o[0:64, 0:NS], o))
            k.dma("sp", mixT_b[64:128, c, NMAIN:NMAIN + NS], o[0:64, 0:NS], R=[o], W=[mixT_b])
    k.set_ring(8)
    A.free(sbc, sbn, kTn, vn_aug, Qbd, accs, gmb, nob)
    k.free_ring(kc32r, vc32r, kcbr, vaugr, kTsr, PTsr, tmpr, otmp, F["sq"], F["l32"], *Pqr)


def stage_C(k, C, IO):
    A = k.A
    k.set_ring(4)
    accb = k.banks[4:8]
    mixT_a, mixT_b = C["mixT_a"], C["mixT_b"]
    wg = A.alloc("wg", [128, 8, DFF], BF16)
    wu = A.alloc("wu", [128, 8, DFF], BF16)
    wo = A.alloc("wo", [128, 8, 1024], BF16)
    for kc in range(8):
        k.dma("pool", wo[:, kc, :], IO["w_out"][kc * 128:(kc + 1) * 128, :], W=[wo])
    for kc in range(8):
        k.dma("pool", wg[:, kc, :], IO["w_gate"][kc * 128:(kc + 1) * 128, :], W=[wg])
        k.dma("pool", wu[:, kc, :], IO["w_up"][kc * 128:(kc + 1) * 128, :], W=[wu])
    nfb = A.alloc("nfb", [128, 1024], F32)
    nfin = A.alloc("nfin", [128, 1024], F32)
    k.dma("sp", nfb[:, :], IO["norm_ffn"].partition_broadcast(128), W=[nfb])
    k.dma("sp", nfin[:, :], IO["norm_final"].partition_broadcast(128), W=[nfin])
    C["sm"] = k.rings("smc", 4, [128, 8], F32)
    C["hb"] = k.rings("hbc", 2, [128, 1024], BF16)
    x1r = k.rings("x1", 4, [128, 1024], F32)
    hfr = k.rings("hfT", 2, [128, 8, 256], BF16)
    wdr = k.rings("wd", 3, [128, 1024], BF16)
    e32r = k.rings("ce32", 3, [128, 256], F32)
    c32r = k.rings("cc32", 3, [128, 256], F32)
    u32r = k.rings("cu32", 3, [128, 256], F32)
    aTr = k.rings("aT", 3, [128, 256], BF16)
    units = [(IO["xp"], NPRE + u * 256, u * 256, 2, 128, IO["yp"], u * 256) for u in range(NMAIN // 256)]
    units.append((IO["xs"], 0, NMAIN, 1, NS, IO["ys"], 0))
    NJ = DFF // 128

    def pre(unit, st):
        (xsrc, xrow0, mcol0, ntile, p, ydst, yrow0) = unit
        st["hfT"] = hfr.next()
        st["x1s"] = []
        for t in range(ntile):
            x1 = x1r.next()
            st["x1s"].append(x1)
            k.dma("sp", x1[0:p, :], xsrc[xrow0 + t * 128:xrow0 + t * 128 + p, :], W=[x1])
            cols = slice(mcol0 + t * 128, mcol0 + t * 128 + p)
            for half in range(2):
                psX = k.ps.next()
                for kc in range(8):
                    lhsT = mixT_a[:, kc, cols] if kc < 4 else mixT_b[:, kc - 4, cols]
                    k.mm(psX[0:p, :], lhsT, wo[:, kc, half * 512:(half + 1) * 512], [mixT_a, mixT_b, wo], [psX], start=(kc == 0), stop=(kc == 7))
                k.tt("dve", x1[0:p, half * 512:(half + 1) * 512], x1[0:p, half * 512:(half + 1) * 512], psX[0:p, :], ALU.add, [x1, psX], [x1])
                yield
            norm_transpose(k, x1, p, nfb, st["hfT"], t * 128, C)
            yield

    def ffn(unit, st):
        (xsrc, xrow0, mcol0, ntile, p, ydst, yrow0) = unit
        ntok = (ntile - 1) * 128 + p
        hfT = st["hfT"]

        def issue_gu(j):
            psG = k.ps.next()
            for kc in range(8):
                k.mm(psG[:, 0:ntok], wg[:, kc, j * 128:(j + 1) * 128], hfT[:, kc, 0:ntok], [wg, hfT], [psG], start=(kc == 0), stop=(kc == 7))
            psU = k.ps.next()
            for kc in range(8):
                k.mm(psU[:, 0:ntok], wu[:, kc, j * 128:(j + 1) * 128], hfT[:, kc, 0:ntok], [wu, hfT], [psU], start=(kc == 0), stop=(kc == 7))
            return psG, psU

        pend = issue_gu(0)
        for j in range(NJ):
            psG, psU = pend
            wd = wdr.next()
            k.dma("pool", wd[:, :], IO["w_down"][j * 128:(j + 1) * 128, :], W=[wd])
            e32, c32, u32, aT = e32r.next(), c32r.next(), u32r.next(), aTr.next()
            k.act(e32[:, 0:ntok], psG[:, 0:ntok], AF.Exp, [psG], [e32], scale=-1.0)
            k.cp("act", u32[:, 0:ntok], psU[:, 0:ntok], [psU], [u32])
            k.act(e32[:, 0:ntok], e32[:, 0:ntok], AF.Ln, [e32], [e32], bias=1.0)
            k.act(e32[:, 0:ntok], e32[:, 0:ntok], AF.Exp, [e32], [e32], scale=-1.0)
            k.tt("dve", c32[:, 0:ntok], psG[:, 0:ntok], e32[:, 0:ntok], ALU.mult, [psG, e32], [c32])
            k.tt("dve", aT[:, 0:ntok], c32[:, 0:ntok], u32[:, 0:ntok], ALU.mult, [c32, u32], [aT])
            if j + 1 < NJ:
                pend = issue_gu(j + 1)
            for t in range(ntile):
                for half in range(2):
                    ab = accb[t * 2 + half]
                    k.mm(ab[0:p, :], aT[:, t * 128:t * 128 + p], wd[:, half * 512:(half + 1) * 512], [aT, wd], [ab],
                         start=(j == 0), stop=(j == NJ - 1))
            yield

    def post(unit, st):
        (xsrc, xrow0, mcol0, ntile, p, ydst, yrow0) = unit
        for t in range(ntile):
            x1 = st["x1s"][t]
            for half in range(2):
                ab = accb[t * 2 + half]
                k.tt("dve", x1[0:p, half * 512:(half + 1) * 512], x1[0:p, half * 512:(half + 1) * 512], ab[0:p, :], ALU.add, [x1, ab], [x1])
            sm = C["sm"].next()
            jk = C["hb"].next()
            k.act(jk[0:p, :], x1[0:p, :], AF.Square, [x1], [jk, sm], accum=sm[0:p, 0:1])
            k.act(sm[0:p, 1:2], sm[0:p, 0:1], AF.Ln, [sm], [sm], scale=1.0 / D, bias=EPS)
            k.act(sm[0:p, 2:3], sm[0:p, 1:2], AF.Exp, [sm], [sm], scale=-0.5)
            k.stt("dve", x1[0:p, :], x1[0:p, :], sm[0:p, 2:3], nfin[0:p, :], ALU.mult, ALU.mult, [x1, sm, nfin], [x1])
            k.dma("sp", ydst[yrow0 + t * 128:yrow0 + t * 128 + p, :], x1[0:p, :], R=[x1])

    states = [dict() for _ in units]
    for _ in pre(units[0], states[0]):
        pass
    for u, unit in enumerate(units):
        gens = [ffn(unit, states[u])]
        if u + 1 < len(units):
            gens.append(pre(units[u + 1], states[u + 1]))
        run_interleaved(gens, offs=1, maxact=2)
        post(unit, states[u])
    k.set_ring(8)


IN_SPECS = [
    ("xp", [NTOK, D], F32), ("xs", [NS, D], F32), ("sconv", [4, 3, 1536], F32), ("srec", [4, 4, 128, 128], F32),
    ("ck", [4, 2048, 512], F32), ("cv", [4, 2048, 512], F32),
    ("norm_mix", [D], F32), ("w_in", [D, 3592], F32), ("w_conv", [4, 1536], F32), ("a_log", [4], F32), ("dt_bias", [4], F32),
    ("noa", [128], F32), ("nob", [64], F32), ("w_out", [D, D], F32), ("norm_ffn", [D], F32),
    ("w_gate", [D, DFF], F32), ("w_up", [D, DFF], F32), ("w_down", [DFF, D], F32), ("norm_final", [D], F32),
    ("cf32", [128, 1664], F32), ("vmask", [128, 1], F32), ("edge8", [128, 1], F32),
    ("pbias", [128, 8, 3, 256], BF16), ("sbias_c", [128, 16, 192], BF16), ("sbias_n", [NS, 4, 192], BF16), ("gmb", [65, 64], BF16),
]
OUT_SPECS = [
    ("yp", [NMAIN, D]), ("ys", [NS, D]), ("conv_p", [3, 1536]), ("rec_p", [4, 128, 128]), ("wk_p", [NMAIN, 512]), ("wv_p", [NMAIN, 512]),
    ("conv_s", [4, 3, 1536]), ("rec_s", [4, 4, 128, 128]), ("wk_s", [4, 2048, 512]), ("wv_s", [4, 2048, 512]),
]


def build_program(stages="GKBC", dbg=False):
    nc = bass.Bass("TRN2", target_bir_lowering=False)
    IO = {}
    if dbg:
        IO["dbg_ma"] = nc.dram_tensor("dbg_ma", [128, 4, NMAIN + NS], F32, kind="ExternalOutput").ap()
        IO["dbg_mb"] = nc.dram_tensor("dbg_mb", [128, 4, NMAIN + NS], F32, kind="ExternalOutput").ap()
    for name, shape, dt in IN_SPECS:
        IO[name] = nc.dram_tensor(name, shape, dt, kind="ExternalInput").ap()
    for name, shape in OUT_SPECS:
        IO[name] = nc.dram_tensor(name, shape, F32, kind="ExternalOutput").ap()
    k = KB(nc)
    C = {}
    setup_consts(k, C, IO)
    C["mixT_a"] = k.A.alloc("mixT_a", [128, 4, NMAIN + NS], BF16)
    if "G" in stages:
        stage_G(k, C, IO)
    if "K" in stages:
        stage_K(k, C, IO)
    if "B" in stages:
        stage_B(k, C, IO)
    if dbg:
        k.dma("pool", IO["dbg_ma"][:, :, :], C["mixT_a"][:, :, :], R=[C["mixT_a"]])
        k.dma("pool", IO["dbg_mb"][:, :, :], C["mixT_b"][:, :, :], R=[C["mixT_b"]])
    if "C" in stages:
        stage_C(k, C, IO)
    k.S.finish()
    k.S.emit()
    return nc, k


def host_tables():
    p = np.arange(128)
    ident = np.eye(128, dtype=np.float32)
    mSU = (p[:, None] < p[None, :]).astype(np.float32)
    mIU = (p[:, None] <= p[None, :]).astype(np.float32)
    ones = np.ones((128, 128), np.float32)
    cf = np.concatenate([np.tile(ident, (1, 4)), np.tile(mSU, (1, 4)), np.tile(mIU, (1, 4)), ones], axis=1)
    slopes = 2.0 ** (-np.arange(1, 9, dtype=np.float64))
    ki = p[:, None].astype(np.float64)
    qi = p[None, :].astype(np.float64)
    pbias = np.zeros((128, 8, 3, 256), np.float64)
    for h in range(8):
        for br, d in enumerate(DILS):
            j = qi - ki
            pbias[:, h, br, 0:128] = np.where(j >= 0, -slopes[h] * d * j * 8.0, NEGB)
            j = qi + 128 - ki
            pbias[:, h, br, 128:256] = np.where(j <= 128, -slopes[h] * d * j * 8.0, NEGB)
    sbc = np.full((128, 16, 192), NEGB, np.float64)
    sbn = np.full((NS, 4, 192), NEGB, np.float64)
    for h in range(8):
        for br, d in enumerate(DILS):
            for t in range(8):
                col = h * 24 + br * 8 + t
                kp = np.arange(2048)
                dist = 2048 + t - kp
                ok = (dist % d == 0) & (dist // d <= 128)
                vals = np.where(ok, -slopes[h] * dist * 8.0, NEGB)
                sbc[:, :, col] = vals.reshape(16, 128).T
                for s in range(4):
                    for t2 in range(8):
                        dist2 = t - t2
                        if dist2 >= 0 and dist2 % d == 0:
                            sbn[8 * s + t2, s, col] = -slopes[h] * dist2 * 8.0
    gm = np.full((65, 64), 1.0 / 64, np.float64)
    gm[64, :] = EPS
    vmask = (p < 8).astype(np.float32).reshape(128, 1)
    bf = ml_dtypes.bfloat16
    return dict(cf32=cf, vmask=vmask, pbias=pbias.astype(np.float32).astype(bf), sbias_c=sbc.astype(np.float32).astype(bf),
                sbias_n=sbn.astype(np.float32).astype(bf), gmb=gm.astype(np.float32).astype(bf))


_CACHE = {}


def make_in_maps(x_prompt, x_sample, state_conv, state_rec, cache_win_k, cache_win_v, norm_mix, w_in, w_conv, a_log, dt_bias,
                 norm_out_a, norm_out_b, w_out, norm_ffn, w_gate, w_up, w_down, norm_final):
    f = lambda a: np.ascontiguousarray(np.asarray(a, dtype=np.float32))
    tabs = host_tables()
    shared = dict(norm_mix=f(norm_mix[0]), w_in=f(w_in[0]), w_conv=f(w_conv[0]), a_log=f(a_log[0]), dt_bias=f(dt_bias[0]),
                  noa=f(norm_out_a[0]), nob=f(norm_out_b[0]), w_out=f(w_out[0]), norm_ffn=f(norm_ffn[0]),
                  w_gate=f(w_gate[0]), w_up=f(w_up[0]), w_down=f(w_down[0]), norm_final=f(norm_final), **tabs)
    in_maps = []
    for c in range(NCORES):
        b, half = c // 2, c % 2
        xp = np.zeros((NTOK, D), np.float32)
        if half == 1:
            xp[:] = x_prompt[b]
        else:
            xp[NPRE:] = x_prompt[b, 0:NMAIN]
        sl = slice(4 * c, 4 * c + 4)
        m = dict(shared)
        m.update(xp=xp, xs=f(x_sample[sl]).reshape(NS, D), sconv=f(state_conv[0, sl]), srec=f(state_rec[0, sl]),
                 ck=f(cache_win_k[0, sl]).reshape(4, 2048, 512), cv=f(cache_win_v[0, sl]).reshape(4, 2048, 512),
                 edge8=np.full((128, 1), 0.0 if half == 1 else NEGB, np.float32))
        in_maps.append(m)
    return in_maps


def assemble(res):
    y_prompt = np.zeros((4, 4096, D), np.float32)
    y_sample = np.zeros((32, 8, D), np.float32)
    conv_p = np.zeros((1, 4, 3, 1536), np.float32)
    rec_p = np.zeros((1, 4, 4, 128, 128), np.float32)
    wk_p = np.zeros((1, 4, 2048, 8, 64), np.float32)
    wv_p = np.zeros((1, 4, 2048, 8, 64), np.float32)
    conv_s = np.zeros((1, 32, 3, 1536), np.float32)
    rec_s = np.zeros((1, 32, 4, 128, 128), np.float32)
    wk_s = np.zeros((1, 32, 2048, 8, 64), np.float32)
    wv_s = np.zeros((1, 32, 2048, 8, 64), np.float32)
    for c in range(NCORES):
        b, half = c // 2, c % 2
        r = res[c]
        y_prompt[b, half * NMAIN:(half + 1) * NMAIN] = r["yp"]
        sl = slice(4 * c, 4 * c + 4)
        y_sample[sl] = r["ys"].reshape(4, 8, D)
        conv_s[0, sl] = r["conv_s"]
        rec_s[0, sl] = r["rec_s"]
        wk_s[0, sl] = r["wk_s"].reshape(4, 2048, 8, 64)
        wv_s[0, sl] = r["wv_s"].reshape(4, 2048, 8, 64)
        if half == 1:
            conv_p[0, b] = r["conv_p"]
            rec_p[0, b] = r["rec_p"]
            wk_p[0, b] = r["wk_p"].reshape(2048, 8, 64)
            wv_p[0, b] = r["wv_p"].reshape(2048, 8, 64)
    return (y_prompt, y_sample, conv_p, rec_p, wk_p, wv_p, conv_s, rec_s, wk_s, wv_s)


def kernel(**inputs):
    in_maps = make_in_maps(**inputs)
    if "nc" not in _CACHE:
        _CACHE["nc"] = build_program()[0]
    res = run_bass_kernel_spmd(_CACHE["nc"], in_maps, core_ids=list(range(NCORES)))
    return assemble(res.results)
```

```python
import math
import numpy as np
import ml_dtypes
import concourse.bass as bass
import concourse.mybir as mybir
from concourse.bass_utils import run_bass_kernel_spmd

F32 = mybir.dt.float32
BF16 = mybir.dt.bfloat16
ALU = mybir.AluOpType
AF = mybir.ActivationFunctionType

ENGS = ("pe", "act", "dve", "pool", "sp")
NCORES = 8
D = 1024
NPRE = 2048
NMAIN = 2048
NTOK = NPRE + NMAIN
NS = 32
DFF = 2816
EPS = 1e-6
NEGB = -240000.0
DILS = (1, 4, 16)
import os
KSTOP = int(os.environ.get("KSTOP", "9"))
KOFF = os.environ.get("KOFF", "")


class T:
    __slots__ = ("name", "ap", "lw", "rde", "rdd", "rng", "psum")

    def __init__(self, name, ap, psum=False):
        self.name = name
        self.ap = ap
        self.psum = psum
        self.lw = None
        self.rde = {}
        self.rdd = []
        self.rng = None

    def __getitem__(self, idx):
        return self.ap[idx]


class Sched:
    def __init__(self, nc, n_dma_slots=10):
        self.nc = nc
        self.ops = {e: [] for e in ENGS}
        self.cnt = {e: 0 for e in ENGS}
        self.waited = {e: {} for e in ENGS}
        self.nslots = n_dma_slots
        self.slot_total = {}
        self.slot_next = {"sp": 0, "pool": 0, "act": 0}
        self.dma_info = []

    def _need(self, eng, dep, waits):
        if dep[0] == "e":
            _, e2, seq = dep
            if e2 == eng and eng in ("pe", "sp"):
                return
            key = ("e", e2)
            val = seq
        else:
            key, val = self.dma_info[dep[1]]
        w = self.waited[eng]
        if w.get(key, 0) >= val:
            return
        w[key] = val
        waits.append((key, val))

    def _deps(self, eng, reads, writes):
        waits = []
        for t in reads:
            if t.lw is not None:
                self._need(eng, t.lw, waits)
            if t.psum:
                for e2, seq in t.rde.items():
                    if e2 != eng:
                        self._need(eng, ("e", e2, seq), waits)
        for t in writes:
            lw = t.lw
            if lw is not None:
                self._need(eng, lw, waits)
            for e2, seq in t.rde.items():
                if e2 != eng or eng != "pe":
                    self._need(eng, ("e", e2, seq), waits)
            for did in t.rdd:
                self._need(eng, ("d", did), waits)
        return waits

    def _mark(self, me, reads, writes):
        for t in reads:
            if me[0] == "e":
                if t.rde.get(me[1], 0) < me[2]:
                    t.rde[me[1]] = me[2]
            else:
                t.rdd.append(me[1])
        for t in writes:
            t.lw = me
            t.rde = {}
            t.rdd = []

    def op(self, eng, fn, reads=(), writes=()):
        waits = self._deps(eng, reads, writes)
        self.cnt[eng] += 1
        me = ("e", eng, self.cnt[eng])
        self._mark(me, reads, writes)
        self.ops[eng].append((fn, waits, "c", None))

    def dma(self, eng, out_ap, in_ap, reads=(), writes=(), **kw):
        waits = self._deps(eng, reads, writes)
        slot = self.slot_next[eng]
        self.slot_next[eng] = (slot + 1) % self.nslots
        key = ("d", eng, slot)
        prev = self.slot_total.get(key, 0)
        if prev:
            w = self.waited[eng]
            if w.get(key, 0) < prev:
                w[key] = prev
                waits.append((key, prev))
        val = prev + 16
        self.slot_total[key] = val
        did = len(self.dma_info)
        self.dma_info.append((key, val))
        self._mark(("d", did), reads, writes)
        self.ops[eng].append(((out_ap, in_ap, kw), waits, "d", key))

    def finish(self):
        for eng in ("sp", "pool", "act"):
            waits = []
            for key, val in self.slot_total.items():
                if key[1] != eng:
                    continue
                w = self.waited[eng]
                if w.get(key, 0) < val:
                    w[key] = val
                    waits.append((key, val))
            if waits:
                self.ops[eng].append((None, waits, "w", None))

    def emit(self):
        nc = self.nc
        from contextlib import ExitStack
        with ExitStack() as es:
            sems = {}
            for e in ENGS:
                sems[("e", e)] = es.enter_context(nc.semaphore("s_" + e))
            for key in self.slot_total:
                sems[key] = es.enter_context(nc.semaphore("d_%s_%d" % (key[1], key[2])))
            block = es.enter_context(nc.Block())
            refd = {e: set() for e in ENGS}
            for e in ENGS:
                for fn, waits, kind, extra in self.ops[e]:
                    for key, val in waits:
                        if key[0] == "e":
                            refd[key[1]].add(val)
            rank = {e: {s: i + 1 for i, s in enumerate(sorted(refd[e]))} for e in ENGS}

            def run(engname):
                def body(eng):
                    mysem = sems[("e", engname)]
                    myrank = rank[engname]
                    seq = 0
                    for fn, waits, kind, extra in self.ops[engname]:
                        for key, val in waits:
                            if key[0] == "e":
                                eng.wait_ge(sems[key], rank[key[1]][val])
                            else:
                                eng.wait_ge(sems[key], val)
                        if kind == "c":
                            seq += 1
                            ins = fn(eng)
                            if seq in myrank:
                                ins.then_inc(mysem, 1)
                        elif kind == "d":
                            out_ap, in_ap, kw = fn
                            eng.dma_start(out=out_ap, in_=in_ap, **kw).then_inc(sems[extra], 16)
                return body

            block.tensor(run("pe"))
            block.scalar(run("act"))
            block.vector(run("dve"))
            block.gpsimd(run("pool"))
            block.sync(run("sp"))


class Arena:
    def __init__(self, nc, nbytes):
        self.n = nbytes
        self.base = nc.alloc_sbuf_tensor("arena", [128, nbytes // 2], BF16).ap()
        self.live = []
        self.retired = []
        self.peak = 0

    def alloc(self, name, shape, dt):
        esz = 4 if dt == F32 else 2
        n = 1
        for s in shape[1:]:
            n *= s
        nb = (n * esz + 63) // 64 * 64
        pos = 0
        for a, b, _ in sorted(self.live, key=lambda x: x[0]):
            if a - pos >= nb:
                break
            pos = max(pos, b)
        if pos + nb > self.n:
            raise RuntimeError("arena full allocating %s (%d bytes) live=%d" % (name, nb, sum(b - a for a, b, _ in self.live)))
        a, b = pos, pos + nb
        v = self.base[:, a // 2:(a + n * esz) // 2]
        if dt == F32:
            v = v.bitcast(F32)
        if len(shape) == 3:
            v = v.rearrange("p (x y) -> p x y", x=shape[1])
        elif len(shape) == 4:
            v = v.rearrange("p (x y z) -> p x y z", x=shape[1], y=shape[2])
        if shape[0] < 128:
            v = v[0:shape[0]]
        t = T(name, v)
        t.rng = (a, b)
        keep = []
        for ra, rb, rt in self.retired:
            if ra < b and a < rb:
                for e2, seq in rt.rde.items():
                    if t.rde.get(e2, 0) < seq:
                        t.rde[e2] = seq
                t.rdd.extend(rt.rdd)
                if rt.lw is not None:
                    if rt.lw[0] == "e":
                        if t.rde.get(rt.lw[1], 0) < rt.lw[2]:
                            t.rde[rt.lw[1]] = rt.lw[2]
                    else:
                        t.rdd.append(rt.lw[1])
                if ra >= a and rb <= b:
                    continue
            keep.append((ra, rb, rt))
        self.retired = keep
        self.live.append((a, b, t))
        self.peak = max(self.peak, b)
        return t

    def free(self, *ts):
        for t in ts:
            for i, (a, b, tt) in enumerate(self.live):
                if tt is t:
                    self.live.pop(i)
                    self.retired.append((a, b, t))
                    break
            else:
                raise RuntimeError("free of unknown tile " + t.name)


class Ring:
    def __init__(self, tiles):
        self.t = tiles
        self.i = 0

    def next(self):
        t = self.t[self.i]
        self.i = (self.i + 1) % len(self.t)
        return t


class KB:
    def __init__(self, nc):
        self.nc = nc
        self.S = Sched(nc)
        self.A = Arena(nc, 207 * 1024)
        self.banks = [T("ps%d" % i, nc.alloc_psum_tensor("ps%d" % i, [128, 512], F32).ap(), psum=True) for i in range(8)]
        self.ps = Ring(self.banks)
        self._rr = 0

    def set_ring(self, n):
        self.ps = Ring(self.banks[0:n])

    def act(self, out, in_, func, R, W, scale=None, bias=None, accum=None):
        kw = {}
        if scale is not None:
            kw["scale"] = scale
        if bias is not None:
            kw["bias"] = bias
        if accum is not None:
            kw["accum_out"] = accum
        self.S.op("act", lambda e: e.activation(out=out, in_=in_, func=func, **kw), R, W)

    def mm(self, out, lhsT, rhs, R, W, start=True, stop=True, skip=False):
        self.S.op("pe", lambda e: e.matmul(out, lhsT=lhsT, rhs=rhs, start=start, stop=stop, skip_group_check=skip), R, W)

    def tr(self, out, in_, ident, R, W):
        self.S.op("pe", lambda e: e.transpose(out=out, in_=in_, identity=ident), R, W)

    def tt(self, eng, out, in0, in1, op, R, W):
        self.S.op(eng, lambda e: e.tensor_tensor(out=out, in0=in0, in1=in1, op=op), R, W)

    def ts(self, eng, out, in0, s1, op0, R, W, s2=None, op1=None):
        if op1 is None:
            self.S.op(eng, lambda e: e.tensor_scalar(out=out, in0=in0, scalar1=s1, scalar2=None, op0=op0), R, W)
        else:
            self.S.op(eng, lambda e: e.tensor_scalar(out=out, in0=in0, scalar1=s1, scalar2=s2, op0=op0, op1=op1), R, W)

    def stt(self, eng, out, in0, scalar, in1, op0, op1, R, W):
        self.S.op(eng, lambda e: e.scalar_tensor_tensor(out=out, in0=in0, scalar=scalar, in1=in1, op0=op0, op1=op1), R, W)

    def cp(self, eng, out, in_, R, W):
        if eng == "act":
            self.S.op("act", lambda e: e.activation(out=out, in_=in_, func=AF.Copy), R, W)
        else:
            self.S.op(eng, lambda e: e.tensor_copy(out=out, in_=in_), R, W)

    def recip(self, out, in_, R, W):
        self.S.op("dve", lambda e: e.reciprocal(out=out, in_=in_), R, W)

    def memset(self, eng, ap, val, W):
        self.S.op(eng, lambda e: e.memset(ap, val), (), W)

    def dma(self, eng, out, in_, R=(), W=(), **kw):
        self.S.dma(eng, out, in_, R, W, **kw)

    def alt(self, engs=("dve", "pool")):
        self._rr += 1
        return engs[self._rr % len(engs)]

    def rings(self, name, n, shape, dt):
        return Ring([self.A.alloc("%s%d" % (name, i), shape, dt) for i in range(n)])

    def free_ring(self, *rings):
        for r in rings:
            self.A.free(*r.t)


def sl(start, count, step):
    return slice(start, start + step * (count - 1) + 1, step)


def bank_bf(ps):
    return ps.ap.bitcast(BF16)


def norm_stats(k, xt, p, nb_tile, C, dimscale=1.0 / D):
    sm = C["sm"].next()
    hb = C["hb"].next()
    k.act(hb[0:p, :], xt[0:p, :], AF.Square, [xt], [hb, sm], accum=sm[0:p, 0:1])
    k.act(sm[0:p, 1:2], sm[0:p, 0:1], AF.Ln, [sm], [sm], scale=dimscale, bias=EPS)
    k.act(sm[0:p, 2:3], sm[0:p, 1:2], AF.Exp, [sm], [sm], scale=-0.5)
    k.stt("dve", hb[0:p, :], xt[0:p, :], sm[0:p, 2:3], nb_tile[0:p, :], ALU.mult, ALU.mult, [xt, sm, nb_tile], [hb])
    return hb


def transpose_to(k, hb, p, hT, c0, C):
    ps = k.ps.next()
    pb = bank_bf(ps)
    for c in range(8):
        k.tr(pb[:, c * 128:c * 128 + p], hb[0:p, c * 128:(c + 1) * 128], C["ident"][0:p, 0:p], [hb, C["ident"]], [ps])
    k.cp("act", hT[:, :, c0:c0 + p], pb[:, :].rearrange("q (c t) -> q c t", c=8)[:, :, 0:p], [ps], [hT])


def norm_transpose(k, xt, p, nb_tile, hT, c0, C, dimscale=1.0 / D):
    hb = norm_stats(k, xt, p, nb_tile, C, dimscale)
    transpose_to(k, hb, p, hT, c0, C)


def gdn_tile(k, C, G, hT_ap, qT_ap, kT_ap, vT_ap, S, Sb, deps, main, nf, mix_out, vm=None):
    w_a = C["w_a"]
    ident = C["ident"]
    sc = G["sc"].next()

    def bc(c0):
        return sc[:, c0:c0 + 4].unsqueeze(2).to_broadcast([128, 4, 128])

    def ps4(ps):
        return ps[:, :].rearrange("p (h c) -> p h c", h=4)

    psBA = k.ps.next()
    for kc in range(8):
        k.mm(psBA[:, 0:8], hT_ap[:, kc, :], w_a[:, kc, 2048:2056], deps + [C["wa_ba"]], [psBA], start=(kc == 0), stop=(kc == 7))
    k.act(sc[:, 0:4], psBA[:, 0:4], AF.Exp, [psBA], [sc], scale=-1.0)
    k.ts("dve", sc[:, 0:4], sc[:, 0:4], 1.0, ALU.add, [sc], [sc])
    k.recip(sc[:, 0:4], sc[:, 0:4], [sc], [sc])
    if vm is not None:
        k.ts("dve", sc[:, 0:4], sc[:, 0:4], vm[:, 0:1], ALU.mult, [sc, vm], [sc])
    k.ts("dve", sc[:, 4:8], sc[:, 0:4], -1.0, ALU.mult, [sc], [sc])
    yield
    k.tt("dve", sc[:, 8:12], psBA[:, 4:8], C["dtb"][:, :], ALU.add, [psBA, C["dtb"]], [sc])
    k.act(sc[:, 8:12], sc[:, 8:12], AF.Exp, [sc], [sc])
    k.act(sc[:, 8:12], sc[:, 8:12], AF.Ln, [sc], [sc], bias=1.0)
    k.tt("dve", sc[:, 8:12], sc[:, 8:12], C["negA"][:, :], ALU.mult, [sc, C["negA"]], [sc])
    if vm is not None:
        k.ts("dve", sc[:, 8:12], sc[:, 8:12], vm[:, 0:1], ALU.mult, [sc, vm], [sc])
    yield
    psG = k.ps.next()
    k.mm(psG[:, 0:4], C["mIU"][:, 0:128], sc[:, 8:12], [C["mIU"], sc], [psG])
    k.mm(psG[:, 4:8], C["onesf"][:, :], sc[:, 8:12], [C["onesf"], sc], [psG])
    k.cp("dve", sc[:, 12:20], psG[:, 0:8], [psG], [sc])
    k.act(sc[:, 20:28], sc[:, 12:20], AF.Exp, [sc], [sc])
    k.tt("dve", sc[:, 28:32], sc[:, 16:20], sc[:, 12:16], ALU.subtract, [sc], [sc])
    k.act(sc[:, 28:32], sc[:, 28:32], AF.Exp, [sc], [sc])
    yield
    tg = G["tg"].next()
    k.tt("dve", tg[:, :, :], C["mIU4"][:, :, :], bc(8), ALU.mult, [C["mIU4"], sc], [tg])
    psR = k.ps.next()
    k.mm(psR[:, :], C["onesf"][:, :], tg[:, :, :], [C["onesf"], tg], [psR])
    dec = G["dec"].next()
    k.tt("dve", dec[:, :, :], ps4(psR), bc(12), ALU.subtract, [psR, sc], [dec])
    k.act(dec[:, :, :], dec[:, :, :], AF.Exp, [dec], [dec])
    dS = G["dS"].next()
    k.stt("dve", dS[:, :, :], dec[:, :, :], 1.0, C["mSU4"][:, :, :], ALU.min, ALU.mult, [dec, C["mSU4"]], [dS])
    k.tt("dve", dS[:, :, :], dS[:, :, :], bc(4), ALU.mult, [dS, sc], [dS])
    if main:
        egr = G["egr"].next()
        k.act(egr[:, :, :], psR[:, :].rearrange("p (h c) -> p h c", h=4), AF.Exp, [psR], [egr])
        dI = G["dI"].next()
        k.stt("dve", dI[:, :, :], dec[:, :, :], 1.0, C["mIU4"][:, :, :], ALU.min, ALU.mult, [dec, C["mIU4"]], [dI])
    yield
    psk = k.ps.next()
    pbk = bank_bf(psk)
    for h in range(4):
        k.tr(pbk[:, h * 128:(h + 1) * 128], kT_ap[:, h, :], ident[:, :], deps + [ident], [psk])
    psv = k.ps.next()
    pbv = bank_bf(psv)
    for h in range(4):
        k.tr(pbv[:, h * 128:(h + 1) * 128], vT_ap[:, h, :], ident[:, :], deps + [ident], [psv])
    kg = G["kg"].next()
    kdec = G["kdec"].next()
    vtok = G["vtok"].next()
    ktok = G["e"].next()
    k.cp("act", ktok[:, :, :], pbk[:, 0:512].rearrange("p (h c) -> p h c", h=4), [psk], [ktok])
    k.cp("dve", vtok[:, :, :], pbv[:, 0:512].rearrange("p (h c) -> p h c", h=4), [psv], [vtok])
    k.tt("dve", kg[:, :, :], ktok[:, :, :], bc(20), ALU.mult, [ktok, sc], [kg])
    k.tt("dve", kdec[:, :, :], ktok[:, :, :], bc(28), ALU.mult, [ktok, sc], [kdec])
    yield
    psGm = k.ps.next()
    for h in range(4):
        k.mm(psGm[:, h * 128:(h + 1) * 128], kT_ap[:, h, :], kT_ap[:, h, :], deps, [psGm])
    R = G["R"].next()
    Rf = dec
    k.tt("dve", Rf[:, :, :], ps4(psGm), dS[:, :, :], ALU.mult, [psGm, dS], [Rf])
    k.cp("act", R[:, :, :], Rf[:, :, :], [Rf], [R])
    if main:
        psQK = k.ps.next()
        for h in range(4):
            k.mm(psQK[:, h * 128:(h + 1) * 128], kT_ap[:, h, :], qT_ap[:, h, :], deps, [psQK])
        QKT = G["QKT"].next()
        k.tt("dve", QKT[:, :, :], psQK[:, :].rearrange("p (h c) -> p h c", h=4), dI[:, :, :], ALU.mult, [psQK, dI], [QKT])
        qg = G["qg"].next()
        k.tt("dve", qg[:, :, :], qT_ap, egr[:, :, :], ALU.mult, deps + [egr], [qg])
    P = G["P"].next()
    k.tt("dve", P[:, :, :], R[:, :, :], C["id4b"][:, :, :], ALU.add, [R, C["id4b"]], [P])
    yield
    psr = k.ps.next()
    pbr = bank_bf(psr)
    for h in range(4):
        k.tr(pbr[:, h * 128:(h + 1) * 128], R[:, h, :], ident[:, :], [R, ident], [psr])
    RT = G["RT"].next()
    k.cp("act", RT[:, :, :], pbr[:, 0:512].rearrange("p (h c) -> p h c", h=4), [psr], [RT])
    for kk in range(1, nf + 1):
        yield
        psRk = psRTk = psP = None
        if kk <= nf - 2:
            psRk = k.ps.next()
            for h in range(4):
                k.mm(psRk[:, h * 128:(h + 1) * 128], RT[:, h, :], R[:, h, :], [RT, R], [psRk])
        if kk <= nf - 1:
            psRTk = k.ps.next()
            for h in range(4):
                k.mm(psRTk[:, h * 128:(h + 1) * 128], R[:, h, :], RT[:, h, :], [RT, R], [psRTk])
        if kk >= 2:
            psP = k.ps.next()
            for h in range(4):
                k.mm(psP[:, h * 128:(h + 1) * 128], RT[:, h, :], P[:, h, :], [RT, P], [psP])
        if psRk is not None:
            Rn = G["R"].next()
            k.cp("act", Rn[:, :, :], psRk[:, :].rearrange("p (h c) -> p h c", h=4), [psRk], [Rn])
        if psRTk is not None:
            RTn = G["RT"].next()
            k.cp("dve", RTn[:, :, :], psRTk[:, :].rearrange("p (h c) -> p h c", h=4), [psRTk], [RTn])
        if psP is not None:
            Pn = G["P"].next()
            k.tt("dve", Pn[:, :, :], psP[:, :].rearrange("p (h c) -> p h c", h=4), P[:, :, :], ALU.add, [psP, P], [Pn])
            P = Pn
        if psRk is not None:
            R = Rn
        if psRTk is not None:
            RT = RTn
    yield
    pst_ = k.ps.next()
    pbt_ = bank_bf(pst_)
    for h in range(4):
        k.tr(pbt_[:, h * 128:(h + 1) * 128], P[:, h, :], ident[:, :], [P, ident], [pst_])
    PTf = tg
    k.cp("act", PTf[:, :, :], pbt_[:, 0:512].rearrange("p (h c) -> p h c", h=4), [pst_], [PTf])
    psE = k.ps.next()
    for h in range(4):
        k.mm(psE[:, h * 128:(h + 1) * 128], PTf[:, h, :], Rf[:, h, :], [PTf, Rf], [psE])
    Et = G["Ec"].next()
    Ec = G["Ec"].next()
    k.tt("dve", Et[:, :, :], C["id4b"][:, :, :], P[:, :, :], ALU.subtract, [C["id4b"], P], [Et])
    k.tt("dve", Ec[:, :, :], psE[:, :].rearrange("p (h c) -> p h c", h=4), Et[:, :, :], ALU.add, [psE, Et], [Ec])
    yield
    psC1 = k.ps.next()
    for h in range(4):
        k.mm(psC1[:, h * 128:(h + 1) * 128], Ec[:, h, :], vtok[:, h, :], [Ec, vtok], [psC1])
    psC2 = k.ps.next()
    for h in range(4):
        k.mm(psC2[:, h * 128:(h + 1) * 128], Ec[:, h, :], kg[:, h, :], [Ec, kg], [psC2])
    vtok2 = G["vtok"].next()
    kg2 = G["kg"].next()
    k.tt("dve", vtok2[:, :, :], psC1[:, :].rearrange("p (h c) -> p h c", h=4), vtok[:, :, :], ALU.add, [psC1, vtok], [vtok2])
    k.tt("dve", kg2[:, :, :], psC2[:, :].rearrange("p (h c) -> p h c", h=4), kg[:, :, :], ALU.add, [psC2, kg], [kg2])
    vtok, kg = vtok2, kg2
    yield
    psU = k.ps.next()
    for h in range(4):
        k.mm(psU[:, h * 128:(h + 1) * 128], P[:, h, :], vtok[:, h, :], [P, vtok], [psU])
    psW = k.ps.next()
    for h in range(4):
        k.mm(psW[:, h * 128:(h + 1) * 128], kg[:, h, :], P[:, h, :], [P, kg], [psW])
    ub = G["ub"].next()
    k.cp("dve", ub[:, :, :], ps4(psU), [psU], [ub])
    wT = G["wT"].next()
    k.cp("act", wT[:, :, :], psW[:, :].rearrange("p (h c) -> p h c", h=4), [psW], [wT])
    yield
    psS1 = k.ps.next()
    for h in range(4):
        k.mm(psS1[:, h * 128:(h + 1) * 128], wT[:, h, :], Sb[:, h, :], [wT, Sb], [psS1])
    e = G["e"].next()
    k.tt("dve", ub[:, :, :], ps4(psS1), ub[:, :, :], ALU.subtract, [psS1, ub], [ub])
    k.tt("dve", e[:, :, :], ub[:, :, :], bc(4), ALU.mult, [ub, sc], [e])
    if main:
        psO = k.ps.next()
        for h in range(4):
            k.mm(psO[:, h * 128:(h + 1) * 128], qg[:, h, :], Sb[:, h, :], [qg, Sb], [psO], start=True, stop=False)
            k.mm(psO[:, h * 128:(h + 1) * 128], QKT[:, h, :], e[:, h, :], [QKT, e], [psO], start=False, stop=True)
    if main:
        o32 = ub
        k.cp("act", o32[:, :, :], psO[:, :].rearrange("p (h c) -> p h c", h=4), [psO], [o32])
    psSn = k.ps.next()
    for h in range(4):
        k.mm(psSn[:, h * 128:(h + 1) * 128], kdec[:, h, :], e[:, h, :], [kdec, e], [psSn])
    k.tt("dve", S[:, :, :], S[:, :, :], bc(24), ALU.mult, [S, sc], [S])
    k.tt("dve", S[:, :, :], S[:, :, :], ps4(psSn), ALU.add, [S, psSn], [S])
    k.cp("act", Sb[:, :, :], S[:, :, :], [S], [Sb])
    if not main:
        return
    yield
    jk = G["jk"].next()
    for h in range(4):
        k.act(jk[:, :], o32[:, h, :], AF.Square, [o32], [jk, sc], accum=sc[:, 32 + h:33 + h])
    k.act(sc[:, 32:36], sc[:, 32:36], AF.Ln, [sc], [sc], scale=1.0 / 128, bias=EPS)
    k.act(sc[:, 32:36], sc[:, 32:36], AF.Exp, [sc], [sc], scale=-0.5)
    yield
    psZ = k.ps.next()
    for kc in range(8):
        k.mm(psZ[:, :], hT_ap[:, kc, :], w_a[:, kc, 1536:2048], deps + [C["wa_z"]], [psZ], start=(kc == 0), stop=(kc == 7))
    ez = G["ez"].next()
    k.act(ez[:, :], psZ[:, :], AF.Exp, [psZ], [ez], scale=-1.0)
    k.act(ez[:, :], ez[:, :], AF.Ln, [ez], [ez], bias=1.0)
    k.act(ez[:, :], ez[:, :], AF.Exp, [ez], [ez], scale=-1.0)
    zn = G["zn"].next()
    k.tt("dve", zn[:, :], psZ[:, :], C["noa"][:, :], ALU.mult, [psZ, C["noa"]], [zn])
    k.tt("dve", zn[:, :], zn[:, :], ez[:, :], ALU.mult, [zn, ez], [zn])
    yield
    og = G["og"].next()
    k.tt("dve", o32[:, :, :], o32[:, :, :], bc(32), ALU.mult, [o32, sc], [o32])
    k.tt("dve", og[:, :].rearrange("p (h c) -> p h c", h=4), o32[:, :, :], zn[:, :].rearrange("p (h c) -> p h c", h=4), ALU.mult, [o32, zn], [og])
    mix_out(og)


def run_interleaved(gens, offs=3, maxact=2):
    active, pending, steps = [], list(gens), {}
    while active or pending:
        if pending and len(active) < maxact and (not active or steps[id(active[-1])] >= offs):
            gnew = pending.pop(0)
            active.append(gnew)
            steps[id(gnew)] = 0
        for gg in list(active):
            try:
                next(gg)
                steps[id(gg)] += 1
            except StopIteration:
                active.remove(gg)


EXTRA_RINGS = [("R", 2, [128, 4, 128], BF16), ("RT", 2, [128, 4, 128], BF16), ("P", 2, [128, 4, 128], BF16),
               ("kg", 2, [128, 4, 128], BF16), ("vtok", 2, [128, 4, 128], BF16), ("Ec", 2, [128, 4, 128], BF16),
               ("sc", 1, [128, 40], F32), ("tg", 1, [128, 4, 128], F32), ("dec", 1, [128, 4, 128], F32),
               ("egr", 1, [128, 4, 128], F32), ("ub", 1, [128, 4, 128], F32)] + \
              [(nm, 1, [128, 4, 128], BF16) for nm in ("dS", "dI", "kdec", "QKT", "qg", "wT", "e")]


def silu_from_psum(k, G, ps_ap, psT, n):
    e32 = G["e32"].next()
    c32 = G["c32"].next()
    k.act(e32[:, 0:n], ps_ap, AF.Exp, [psT], [e32], scale=-1.0)
    k.act(e32[:, 0:n], e32[:, 0:n], AF.Ln, [e32], [e32], bias=1.0)
    k.act(e32[:, 0:n], e32[:, 0:n], AF.Exp, [e32], [e32], scale=-1.0)
    k.tt("dve", c32[:, 0:n], ps_ap, e32[:, 0:n], ALU.mult, [psT, e32], [c32])
    return c32


def l2norm_chunk(k, C, G, c32, n, out_ap, outT, qscale):
    sq = G["sq"].next()
    k.tt("dve", sq[:, 0:n], c32[:, 0:n], c32[:, 0:n], ALU.mult, [c32], [sq])
    psC = k.ps.next()
    k.mm(psC[:, 0:n], C["onesb"][:, :], sq[:, 0:n], [C["onesb"], sq], [psC])
    l32 = G["l32"].next()
    k.act(l32[:, 0:n], psC[:, 0:n], AF.Ln, [psC], [l32], bias=EPS)
    if qscale:
        k.act(l32[:, 0:n], l32[:, 0:n], AF.Exp, [l32], [l32], scale=-0.5, bias=C["lnq"][:, 0:1])
        rd = [c32, l32, C["lnq"]]
    else:
        k.act(l32[:, 0:n], l32[:, 0:n], AF.Exp, [l32], [l32], scale=-0.5)
        rd = [c32, l32]
    k.tt("dve", out_ap, c32[:, 0:n], l32[:, 0:n], ALU.mult, rd, [outT])


def stage_G(k, C, IO):
    A = k.A
    w_a = A.alloc("w_a", [128, 8, 2056], BF16)
    C["w_a"] = w_a
    wa_units = {nm: T("wa_" + nm, None) for nm in ("q", "k", "v", "z", "ba")}
    for t in wa_units.values():
        for e2, seq in w_a.rde.items():
            t.rde[e2] = seq
        t.rdd.extend(w_a.rdd)
    C["wa_g"] = [wa_units["q"], wa_units["k"], wa_units["v"]]
    C["wa_z"], C["wa_ba"] = wa_units["z"], wa_units["ba"]
    for nm, c0, c1 in (("k", 512, 1024), ("v", 1024, 1536), ("ba", 2048, 2056), ("q", 0, 512), ("z", 1536, 2048)):
        k.dma("pool", w_a[:, :, c0:c1], IO["w_in"][:, c0:c1].rearrange("(c p) n -> p c n", p=128), W=[wa_units[nm]],
              allow_slow_non_contiguous=(nm == "ba"))
    nmb = A.alloc("nmb", [128, 1024], F32)
    k.dma("sp", nmb[:, :], IO["norm_mix"].partition_broadcast(128), W=[nmb])
    wconv = A.alloc("wconv", [128, 4, 12], F32)
    for i in range(4):
        k.dma("sp", wconv[:, i, :], IO["w_conv"][i].rearrange("(c p) -> p c", p=128), W=[wconv], allow_slow_non_contiguous=True)
    diag = A.alloc("diag", [128, 12, 4, 128], BF16)
    for ch in range(12):
        for i in range(4):
            k.ts(k.alt(), diag[:, ch, i, :], C["identf"][:, 0:128], wconv[:, i, ch:ch + 1], ALU.mult, [C["identf"], wconv], [diag])
    dtb = A.alloc("dtb", [128, 4], F32)
    negA = A.alloc("negA", [128, 4], F32)
    C["dtb"], C["negA"] = dtb, negA
    k.dma("sp", dtb[:, :], IO["dt_bias"].partition_broadcast(128), W=[dtb])
    k.dma("sp", negA[:, :], IO["a_log"].partition_broadcast(128), W=[negA])
    k.act(negA[:, :], negA[:, :], AF.Exp, [negA], [negA])
    k.ts("dve", negA[:, :], negA[:, :], -1.0, ALU.mult, [negA], [negA])
    noa = A.alloc("noa", [128, 512], F32)
    C["noa"] = noa
    for h in range(4):
        k.dma("sp", noa[:, h * 128:(h + 1) * 128], IO["noa"].partition_broadcast(128), W=[noa])
    lnq = A.alloc("lnq", [128, 1], F32)
    C["lnq"] = lnq
    k.memset("pool", lnq[:, :], math.log(128.0 ** -0.5), [lnq])

    G = {}
    C["sm"] = k.rings("sm", 4, [128, 8], F32)
    C["hb"] = k.rings("hb", 2, [128, 1024], BF16)
    xr = k.rings("xr", 2, [128, 1024], F32)
    hT = A.alloc("hT", [128, 8, 512], BF16)
    ext = A.alloc("ext", [128, 12, 515], BF16)
    qkT = A.alloc("qkT", [128, 8, 512], BF16)
    vT = A.alloc("vT", [128, 4, 512], BF16)
    S = A.alloc("S", [128, 4, 128], F32)
    Sb = A.alloc("Sb", [128, 4, 128], BF16)
    for nm, n in (("e32", 2), ("c32", 2), ("l32", 2), ("ez", 1), ("zn", 1)):
        G[nm] = k.rings(nm, n, [128, 512], F32)
    G["sq"] = k.rings("sq", 2, [128, 512], BF16)
    G["og"] = k.rings("og", 2, [128, 512], BF16)
    G["sc"] = k.rings("sc", 3, [128, 40], F32)
    G["jk"] = k.rings("jk", 1, [128, 128], F32)
    for nm, n in (("tg", 2), ("dec", 2), ("egr", 2), ("ub", 2)):
        G[nm] = k.rings(nm, n, [128, 4, 128], F32)
    for nm, n in (("dS", 2), ("dI", 2), ("kg", 4), ("kdec", 2), ("vtok", 4), ("R", 4), ("RT", 4), ("P", 4), ("Ec", 4),
                  ("QKT", 2), ("qg", 2), ("wT", 2), ("e", 2)):
        G[nm] = k.rings(nm, n, [128, 4, 128], BF16)

    mixT_a = C["mixT_a"]
    k.memset("pool", ext[:, :, :], 0.0, [ext])
    k.memset("pool", S[:, :, :], 0.0, [S])
    k.memset("dve", Sb[:, :, :], 0.0, [Sb])

    def feature_chunks(pchs, chs, hT_ap, hdeps, n, ext_dst, conv_rhs, qk_out, v_out, outTs, conv_out=None):
        for ch in pchs:
            psA = k.ps.next()
            for kc in range(8):
                k.mm(psA[:, 0:n], w_a[:, kc, ch * 128:(ch + 1) * 128], hT_ap(kc), hdeps + [C["wa_g"][ch // 4]], [psA], start=(kc == 0), stop=(kc == 7))
            ext_dst(ch, psA)
        for ch in chs:
            psB = k.ps.next()
            for i in range(4):
                k.mm(psB[:, 0:n] if conv_out is None else conv_out(psB), diag[:, ch, i, :], conv_rhs(ch, i), [diag] + outTs["ext"], [psB], start=(i == 0), stop=(i == 3))
            c32 = silu_from_psum(k, G, psB[:, 0:n], psB, n)
            if ch < 8:
                l2norm_chunk(k, C, G, c32, n, qk_out(ch), outTs["qk"], qscale=(ch < 4))
            else:
                k.cp("dve", v_out(ch - 8), c32[:, 0:n], [c32], [outTs["v"]])

    for st in range(NTOK // 512):
        main = st >= NPRE // 512
        for tt4 in range(4):
            tok0 = st * 512 + tt4 * 128
            xt = xr.next()
            k.dma("sp", xt[:, :], IO["xp"][tok0:tok0 + 128, :], W=[xt])
            norm_transpose(k, xt, 128, nmb, hT, tt4 * 128, C)
        chs = list(range(12)) if main else list(range(4, 12))
        pchs = list(range(12)) if st >= NPRE // 512 - 1 else chs

        def ext_dst(ch, psA):
            k.cp("dve", ext[:, ch, 3:515], psA[:, :], [psA], [ext])

        feature_chunks(pchs, chs, lambda kc: hT[:, kc, :], [hT], 512, ext_dst,
                       lambda ch, i: ext[:, ch, i:i + 512],
                       lambda ch: qkT[:, ch, :], lambda j: vT[:, j, :], {"ext": [ext], "qk": qkT, "v": vT})
        if st == NTOK // 512 - 1:
            pre3 = A.alloc("pre3", [3, 1536], F32)
            for j in range(3):
                psT3 = k.ps.next()
                for kc in range(8):
                    k.mm(psT3[0:3, :], hT[:, kc, 509:512], w_a[:, kc, j * 512:(j + 1) * 512], [hT, C["wa_g"][j]], [psT3], start=(kc == 0), stop=(kc == 7))
                k.cp("dve", pre3[0:3, j * 512:(j + 1) * 512], psT3[0:3, :], [psT3], [pre3])
            k.dma("sp", IO["conv_p"][:, :], pre3[0:3, :], R=[pre3])
            A.free(pre3)
        halo = G.setdefault("halo", A.alloc("halo", [128, 12, 3], BF16))
        k.cp("pool", halo[:, :, :], ext[:, :, 512:515], [ext], [halo])
        gens = []
        for tt4 in range(4):
            cs = slice(tt4 * 128, (tt4 + 1) * 128)
            gcol = st * 512 + tt4 * 128 - NPRE

            def mix_out(og, gcol=gcol):
                psm = k.ps.next()
                pbm = bank_bf(psm)
                for h in range(4):
                    k.tr(pbm[:, h * 128:(h + 1) * 128], og[:, h * 128:(h + 1) * 128], C["ident"][:, :], [og, C["ident"]], [psm])
                k.cp("act", mixT_a[:, :, gcol:gcol + 128], pbm[:, 0:512].rearrange("p (h c) -> p h c", h=4), [psm], [mixT_a])

            gens.append(gdn_tile(k, C, G, hT[:, :, cs], qkT[:, 0:4, cs], qkT[:, 4:8, cs], vT[:, :, cs], S, Sb, [hT, qkT, vT], main, 7, mix_out))
        k.free_ring(C["hb"], xr, G["e32"], G["c32"], G["l32"], G["sq"])
        extra = {}
        for nm, n, shp, dt in EXTRA_RINGS:
            extra[nm] = [A.alloc("x_%s%d" % (nm, i), shp, dt) for i in range(n)]
            G[nm].t.extend(extra[nm])
        run_interleaved(gens, offs=3, maxact=3)
        for nm, tl in extra.items():
            for t in tl:
                G[nm].t.remove(t)
            G[nm].i = 0
            A.free(*tl)
        C["hb"] = k.rings("hb", 2, [128, 1024], BF16)
        xr = k.rings("xr", 2, [128, 1024], F32)
        for nm in ("e32", "c32", "l32"):
            G[nm] = k.rings(nm, 2, [128, 512], F32)
        G["sq"] = k.rings("sq", 2, [128, 512], BF16)
        k.cp("pool", ext[:, :, 0:3], halo[:, :, :], [halo], [ext])
    k.dma("sp", IO["rec_p"].rearrange("h d e -> d h e"), S[:, :, :], R=[S])
    A.free(hT, ext, qkT, vT, G["halo"])

    xt = xr.next()
    k.dma("sp", xt[0:NS, :], IO["xs"][:, :], W=[xt])
    hTs = A.alloc("hTs", [128, 8, NS], BF16)
    norm_transpose(k, xt, NS, nmb, hTs, 0, C)
    exts = A.alloc("exts", [128, 12, 4, 11], BF16)
    sct = A.alloc("sct", [12, 1536], F32)
    k.dma("sp", sct[0:12, :], IO["sconv"].rearrange("s i c -> (s i) c"), W=[sct])
    psh = k.ps.next()
    for ch in range(12):
        k.tr(psh[:, ch * 12:(ch + 1) * 12], sct[0:12, ch * 128:(ch + 1) * 128], C["identf"][0:12, 0:12], [sct, C["identf"]], [psh])
    k.cp("dve", exts[:, :, :, 0:3], psh[:, 0:144].rearrange("p (c s i) -> p c s i", c=12, s=4), [psh], [exts])
    qks = A.alloc("qks", [128, 8, NS], BF16)
    vs = A.alloc("vs", [128, 4, NS], BF16)

    def ext_dst_s(ch, psA):
        k.cp("act", exts[:, ch, :, 3:11], psA[:, 0:NS].rearrange("p (s t) -> p s t", s=4), [psA], [exts])

    feature_chunks(list(range(12)), list(range(12)), lambda kc: hTs[:, kc, :], [hTs], NS, ext_dst_s,
                   lambda ch, i: exts[:, ch, :, i:i + 8],
                   lambda ch: qks[:, ch, :], lambda j: vs[:, j, :], {"ext": [exts], "qk": qks, "v": vs},
                   conv_out=lambda psB: psB[:, 0:NS].rearrange("p (s t) -> p s t", s=4))
    pres = A.alloc("pres", [NS, 1536], F32)
    for j in range(3):
        psT3 = k.ps.next()
        for kc in range(8):
            k.mm(psT3[0:NS, :], hTs[:, kc, :], w_a[:, kc, j * 512:(j + 1) * 512], [hTs, C["wa_g"][j]], [psT3], start=(kc == 0), stop=(kc == 7))
        k.cp("dve", pres[0:NS, j * 512:(j + 1) * 512], psT3[0:NS, :], [psT3], [pres])
    for s in range(4):
        k.dma("sp", IO["conv_s"][s, :, :], pres[8 * s + 5:8 * s + 8, :], R=[pres])
    hpad = k.rings("hpad", 2, [128, 8, 128], BF16)
    qkpad = k.rings("qkpad", 2, [128, 8, 128], BF16)
    vpad = k.rings("vpad", 2, [128, 4, 128], BF16)
    Ss = k.rings("Ss", 2, [128, 4, 128], F32)
    Sbs = k.rings("Sbs", 2, [128, 4, 128], BF16)
    for r in (hpad, qkpad, vpad):
        for t in r.t:
            k.memset(k.alt(), t[:, :, :], 0.0, [t])
    sgens = []
    for s in range(4):
        hp, qp, vp, S_s, Sb_s = hpad.next(), qkpad.next(), vpad.next(), Ss.next(), Sbs.next()
        k.cp("pool", hp[:, :, 0:8], hTs[:, :, 8 * s:8 * s + 8], [hTs], [hp])
        k.cp("pool", qp[:, :, 0:8], qks[:, :, 8 * s:8 * s + 8], [qks], [qp])
        k.cp("pool", vp[:, :, 0:8], vs[:, :, 8 * s:8 * s + 8], [vs], [vp])
        k.dma("sp", S_s[:, :, :], IO["srec"][s].rearrange("h d e -> d h e"), W=[S_s])
        k.cp("act", Sb_s[:, :, :], S_s[:, :, :], [S_s], [Sb_s])

        def mix_out_s(og, s=s):
            psm = k.ps.next()
            pbm = bank_bf(psm)
            for h in range(4):
                k.tr(pbm[:, h * 8:(h + 1) * 8], og[0:8, h * 128:(h + 1) * 128], C["ident"][0:8, 0:8], [og, C["ident"]], [psm])
            k.cp("act", mixT_a[:, :, NMAIN + 8 * s:NMAIN + 8 * s + 8], pbm[:, 0:32].rearrange("p (h c) -> p h c", h=4), [psm], [mixT_a])

        def seq_gen(s=s, hp=hp, qp=qp, vp=vp, S_s=S_s, Sb_s=Sb_s, mix_out_s=mix_out_s):
            yield from gdn_tile(k, C, G, hp[:, :, :], qp[:, 0:4, :], qp[:, 4:8, :], vp[:, :, :], S_s, Sb_s, [hp, qp, vp], True, 3, mix_out_s, vm=C["vmask"])
            k.dma("sp", IO["rec_s"][s].rearrange("h d e -> d h e"), S_s[:, :, :], R=[S_s])

        sgens.append(seq_gen())
        if s % 2 == 1:
            run_interleaved(sgens)
            sgens = []

    merge_free(A, w_a, list(wa_units.values()))
    A.free(nmb, wconv, diag, dtb, negA, noa, lnq, S, Sb, hTs, exts, sct, qks, vs, pres)
    k.free_ring(C["sm"], C["hb"], xr, hpad, qkpad, vpad, Ss, Sbs)
    for nm, r in G.items():
        if isinstance(r, Ring):
            k.free_ring(r)
    G.clear()


def merge_free(A, parent, children):
    for ch in children:
        for e2, seq in ch.rde.items():
            if parent.rde.get(e2, 0) < seq:
                parent.rde[e2] = seq
        parent.rdd.extend(ch.rdd)
        if ch.lw is not None:
            if ch.lw[0] == "e":
                if parent.rde.get(ch.lw[1], 0) < ch.lw[2]:
                    parent.rde[ch.lw[1]] = ch.lw[2]
            else:
                parent.rdd.append(ch.lw[1])
    A.free(parent)


def setup_consts(k, C, IO):
    A = k.A
    cf = IO["cf32"]
    identf = A.alloc("identf", [128, 128], F32)
    mIU = A.alloc("mIU", [128, 128], F32)
    onesf = A.alloc("onesf", [128, 128], F32)
    mSU4 = A.alloc("mSU4", [128, 4, 128], F32)
    mIU4 = A.alloc("mIU4", [128, 4, 128], F32)
    id4f = A.alloc("id4f", [128, 4, 128], F32)
    k.dma("sp", identf[:, :], cf[:, 0:128], W=[identf])
    k.dma("sp", id4f[:, :, :], cf[:, 0:512].rearrange("p (h c) -> p h c", h=4), W=[id4f])
    k.dma("sp", mSU4[:, :, :], cf[:, 512:1024].rearrange("p (h c) -> p h c", h=4), W=[mSU4])
    k.dma("sp", mIU4[:, :, :], cf[:, 1024:1536].rearrange("p (h c) -> p h c", h=4), W=[mIU4])
    k.dma("sp", mIU[:, :], cf[:, 1024:1152], W=[mIU])
    k.dma("sp", onesf[:, :], cf[:, 1536:1664], W=[onesf])
    ident = A.alloc("ident", [128, 128], BF16)
    onesb = A.alloc("onesb", [128, 128], BF16)
    id4b = A.alloc("id4b", [128, 4, 128], BF16)
    k.cp("dve", ident[:, :], identf[:, :], [identf], [ident])
    k.cp("dve", onesb[:, :], onesf[:, :], [onesf], [onesb])
    k.cp("dve", id4b[:, :, :], id4f[:, :, :], [id4f], [id4b])
    vmask = A.alloc("vmask", [128, 1], F32)
    edge = A.alloc("edge", [128, 1], F32)
    k.dma("sp", vmask[:, :], IO["vmask"][:, :], W=[vmask])
    k.dma("sp", edge[:, :], IO["edge8"][:, :], W=[edge])
    C.update(identf=identf, mIU=mIU, onesf=onesf, mSU4=mSU4, mIU4=mIU4, ident=ident, onesb=onesb, id4b=id4b,
             vmask=vmask, edge=edge)
    A.free(id4f)


def stage_K(k, C, IO):
    A = k.A
    w_b = A.alloc("w_b", [128, 8, 1536], BF16)
    for kc in range(8):
        k.dma("pool", w_b[:, kc, :], IO["w_in"][kc * 128:(kc + 1) * 128, 2056:3592], W=[w_b])
    nmb = A.alloc("nmb2", [128, 1024], F32)
    k.dma("sp", nmb[:, :], IO["norm_mix"].partition_broadcast(128), W=[nmb])
    C["sm"] = k.rings("smk", 6, [128, 8], F32)
    C["hb"] = k.rings("hbk", 5, [128, 1024], BF16)
    xr = k.rings("xrk", 4, [128, 1024], F32)
    hTr = k.rings("hTk", 2, [128, 8, 512], BF16)
    kT_b = A.alloc("kT_b", [128, 4, NTOK], BF16)
    vT_b = A.alloc("vT_b", [128, 4, NTOK], BF16)
    qT_b = A.alloc("qT_b", [128, 4, NMAIN], BF16)
    kv = [T("kv%d" % st, None) for st in range(NTOK // 512)]
    for t in kv:
        for par in (kT_b, vT_b, qT_b):
            for e2, seq in par.rde.items():
                if t.rde.get(e2, 0) < seq:
                    t.rde[e2] = seq
            t.rdd.extend(par.rdd)
    C.update(kT_b=kT_b, vT_b=vT_b, qT_b=qT_b, kv=kv)
    ost = k.rings("ost", 2, [128, 512], F32)
    def k_stats(st):
        hbs = []
        for tt4 in range(4):
            tok0 = st * 512 + tt4 * 128
            xt = xr.next()
            k.dma("sp", xt[:, :], IO["xp"][tok0:tok0 + 128, :], W=[xt])
            hbs.append(norm_stats(k, xt, 128, nmb, C))
        return hbs

    def k_tr(hbs):
        hT = hTr.next()
        for tt4 in range(4):
            transpose_to(k, hbs[tt4], 128, hT, tt4 * 128, C)
        return hT

    hT_next = k_tr(k_stats(0))
    for st in range(NTOK // 512):
        main = st >= NPRE // 512
        hT = hT_next
        hbs_next = k_stats(st + 1) if st + 1 < NTOK // 512 else None
        for ch in (range(12) if main else range(4, 12)):
            psA = k.ps.next()
            for kc in range(8):
                k.mm(psA[:, :], w_b[:, kc, ch * 128:(ch + 1) * 128], hT[:, kc, :], [hT, w_b], [psA], start=(kc == 0), stop=(kc == 7))
            if ch < 4:
                dst = qT_b[:, ch, (st * 512 - NPRE):(st * 512 - NPRE) + 512]
            elif ch < 8:
                dst = kT_b[:, ch - 4, st * 512:(st + 1) * 512]
            else:
                dst = vT_b[:, ch - 8, st * 512:(st + 1) * 512]
            k.cp(k.alt(("act", "dve")), dst, psA[:, :], [psA], [kv[st]])
        if hbs_next is not None:
            hT_next = k_tr(hbs_next)
        if main and KSTOP >= 2:
            for tt4 in range(4):
                tok0 = st * 512 + tt4 * 128
                for src, dstname in ((kT_b, "wk_p"), (vT_b, "wv_p")):
                    pst = k.ps.next()
                    pbt = bank_bf(pst)
                    for c in range(4):
                        k.tr(pbt[:, c * 128:(c + 1) * 128], src[:, c, tok0:tok0 + 128], C["ident"][:, :], [kv[st], C["ident"]], [pst])
                    o = ost.next()
                    k.cp(k.alt(("act", "dve")), o[:, :], pbt[:, 0:512], [pst], [o])
                    k.dma("sp", IO[dstname][tok0 - NPRE:tok0 - NPRE + 128, :], o[:, :], R=[o])
    if KSTOP < 3:
        return
    xt = xr.next()
    k.dma("sp", xt[0:NS, :], IO["xs"][:, :], W=[xt])
    hTs = A.alloc("hTs2", [128, 8, NS], BF16)
    norm_transpose(k, xt, NS, nmb, hTs, 0, C)
    qTs = A.alloc("qTs", [128, 4, NS], BF16)
    kTn = A.alloc("kTn", [128, 4, NS], BF16)
    for ch in range(8):
        psA = k.ps.next()
        for kc in range(8):
            k.mm(psA[:, 0:NS], w_b[:, kc, ch * 128:(ch + 1) * 128], hTs[:, kc, :], [hTs, w_b], [psA], start=(kc == 0), stop=(kc == 7))
        dstT = qTs if ch < 4 else kTn
        k.cp("act", dstT[:, ch % 4, :], psA[:, 0:NS], [psA], [dstT])
    if KSTOP < 4:
        return
    vn_aug = A.alloc("vn_aug", [NS, 8, 66], BF16)
    k.memset("pool", vn_aug[:, :, :], 1.0, [vn_aug])
    for j, dstname in ((1, "wk_s"), (2, "wv_s")):
        psA = k.ps.next()
        for kc in range(8):
            k.mm(psA[0:NS, :], hTs[:, kc, :], w_b[:, kc, j * 512:(j + 1) * 512], [hTs, w_b], [psA], start=(kc == 0), stop=(kc == 7))
        o = ost.next()
        k.cp("dve", o[0:NS, :], psA[0:NS, :], [psA], [o])
        for s in range(4):
            if "D" not in KOFF:
                k.dma("sp", IO[dstname][s, 2040:2048, :], o[8 * s:8 * s + 8, :], R=[o])
        if j == 2 and "A" not in KOFF:
            k.cp("act", vn_aug[:, :, 0:64], psA[0:NS, :].rearrange("p (h e) -> p h e", h=8), [psA], [vn_aug])
    if KSTOP < 5:
        return
    Qbd = A.alloc("Qbd", [128, 4, 4, 48], BF16)
    k.memset("pool", Qbd[:, :, :, :], 0.0, [Qbd])
    for c in range(4):
        for br in range(3):
            k.cp(k.alt(), Qbd[0:64, c, :, br * 8:br * 8 + 8], qTs[0:64, c, :].rearrange("p (s t) -> p s t", s=4), [qTs], [Qbd])
            k.cp(k.alt(), Qbd[64:128, c, :, 24 + br * 8:32 + br * 8], qTs[64:128, c, :].rearrange("p (s t) -> p s t", s=4), [qTs], [Qbd])
    C.update(kTn=kTn, vn_aug=vn_aug, Qbd=Qbd)
    A.free(w_b, nmb, hTs, qTs)
    k.free_ring(C["sm"], C["hb"], xr, hTr, ost)


def finalize_attn(k, C, F, acc_ap, accT, n, dst):
    for c0 in range(0, n, 512):
        w = min(512, n - c0)
        sq = F["sq"].next()
        k.act(sq[0:65, 0:w], acc_ap[0:65, c0:c0 + w], AF.Square, [accT], [sq])
        psF = k.ps.next()
        k.mm(psF[0:64, 0:w], C["gmb"][0:65, 0:64], sq[0:65, 0:w], [C["gmb"], sq], [psF])
        l32 = F["l32"].next()
        k.act(l32[0:64, 0:w], psF[0:64, 0:w], AF.Ln, [psF], [l32])
        k.act(l32[0:64, 0:w], l32[0:64, 0:w], AF.Exp, [l32], [l32], scale=-0.5)
        dap, dT = dst(c0, w)
        k.stt("dve", dap, acc_ap[0:64, c0:c0 + w], C["nob"][0:64, 0:1], l32[0:64, 0:w], ALU.mult, ALU.mult, [accT, C["nob"], l32], [dT])


def stage_B(k, C, IO):
    A = k.A
    kT_b, vT_b, qT_b, kv = C["kT_b"], C["vT_b"], C["qT_b"], C["kv"]
    identb = C["ident"]
    pb = A.alloc("pbias", [128, 8, 3, 256], BF16)
    k.dma("sp", pb[:, :, :, :], IO["pbias"][:, :, :, :], W=[pb])
    pe_ = A.alloc("pedge", [128, 8, 3, 128], BF16)
    for h in range(8):
        k.ts(k.alt(), pe_[:, h, :, :], pb[:, h, :, 128:256], C["edge"][:, 0:1], ALU.add, [pb, C["edge"]], [pe_])
    gmb = A.alloc("gmb", [65, 64], BF16)
    k.dma("sp", gmb[:, :], IO["gmb"][:, :], W=[gmb])
    nob = A.alloc("nob", [64, 1], F32)
    k.dma("sp", nob[:, :], IO["nob"].rearrange("(p o) -> p o", o=1), W=[nob])
    C.update(gmb=gmb, nob=nob)
    mixT_b = C["mixT_b"] = A.alloc("mixT_b", [128, 4, NMAIN + NS], BF16)
    F = {"sq": k.rings("fsq", 2, [65, 512], BF16), "l32": k.rings("fl32", 2, [64, 512], F32)}
    Vblk = A.alloc("Vblk", [128, 69, 2, 66], BF16)
    k.memset("pool", Vblk[:, :, :, :], 1.0, [Vblk])
    accr = k.rings("acc", 1, [65, NMAIN], F32)
    PTr = k.rings("PT", 6, [128, 256], BF16)
    otmp = k.rings("otmp", 2, [64, 512], BF16)
    blocks = []
    for br, d in enumerate(DILS):
        for r in range(d):
            for n in range(16 // d - 1, 32 // d):
                blocks.append((br, r, n))
    bidx = {b: i for i, b in enumerate(blocks)}
    assert len(blocks) == 69
    for c in range(4):
        for g0 in range(0, 69, 4):
            grp = blocks[g0:g0 + 4]
            psv = k.ps.next()
            pbv = bank_bf(psv)
            for j, (br, r, n) in enumerate(grp):
                d = DILS[br]
                k.tr(pbv[:, j * 128:(j + 1) * 128], vT_b[:, c, sl(r + d * 128 * n, 128, d)], identb[:, :], kv + [identb], [psv])
            ng = len(grp)
            k.cp(k.alt(("act", "dve")), Vblk[:, g0:g0 + ng, :, 0:64],
                 pbv[:, 0:ng * 128].rearrange("p (g h e) -> p g h e", g=ng, h=2), [psv], [Vblk])
        for hh in range(2):
            h = 2 * c + hh
            po = 64 * hh
            acc = accr.next()
            hb_list = []
            for br, d in enumerate(DILS):
                nq0, nq1 = 16 // d, 32 // d
                for r in range(d):
                    for n in range(nq0 - 1, nq1):
                        hb_list.append((br, d, r, n, n == nq0 - 1, n == nq1 - 1, nq0))
            PTs = {}

            def emit_S(i, h=h, c=c, po=po):
                br, d, r, n, first, last, nq0 = hb_list[i]
                ks = sl(r + d * 128 * n, 128, d)
                if first:
                    q0, N, bias, bT = r + d * 128 * nq0 - NPRE, 128, pe_[:, h, br, :], pe_
                elif last:
                    q0, N, bias, bT = r + d * 128 * n - NPRE, 128, pb[:, h, br, 0:128], pb
                else:
                    q0, N, bias, bT = r + d * 128 * n - NPRE, 256, pb[:, h, br, :], pb
                psS = k.ps.next()
                k.mm(psS[:, 0:N], kT_b[po:po + 64, c, ks], qT_b[po:po + 64, c, sl(q0, N, d)], kv, [psS], start=True, stop=False)
                k.mm(psS[:, 0:N], identb[:, :], bias, [identb, bT], [psS], start=False, stop=True)
                PT = PTr.next()
                k.act(PT[:, 0:N], psS[:, 0:N], AF.Exp, [psS], [PT], scale=0.125)
                PTs[i] = PT

            def emit_PV(i, hh=hh, acc=acc):
                br, d, r, n, first, last, nq0 = hb_list[i]
                if first:
                    return
                PT, prevPT = PTs[i], PTs[i - 1]
                prev_first = hb_list[i - 1][4]
                psO = k.ps.next()
                pp = prevPT[:, 0:128] if prev_first else prevPT[:, 128:256]
                k.mm(psO[0:65, 0:128], Vblk[:, bidx[(br, r, n - 1)], hh, 0:65], pp, [Vblk, prevPT], [psO], start=True, stop=False)
                k.mm(psO[0:65, 0:128], Vblk[:, bidx[(br, r, n)], hh, 0:65], PT[:, 0:128], [Vblk, PT], [psO], start=False, stop=True)
                qc = r + d * 128 * n - NPRE
                qcols = sl(qc, 128, d)
                if br == 0:
                    k.cp("dve", acc[:, qcols], psO[0:65, 0:128], [psO], [acc])
                else:
                    k.tt("dve", acc[:, qcols], acc[:, qcols], psO[0:65, 0:128], ALU.add, [acc, psO], [acc])
                PTs.pop(i - 1, None)

            LOOK = 3
            for i in range(len(hb_list) + LOOK):
                if i < len(hb_list):
                    emit_S(i)
                if i - LOOK >= 0:
                    emit_PV(i - LOOK)
            if hh == 0:
                finalize_attn(k, C, F, acc, acc, NMAIN, lambda c0, w, c=c: (mixT_b[0:64, c, c0:c0 + w], mixT_b))
            else:
                def dst(c0, w, c=c):
                    o = otmp.next()
                    dst.last = (o, c0, w)
                    return o[0:64, 0:w], o
                for c0 in range(0, NMAIN, 512):
                    finalize_attn(k, C, F, acc[:, c0:c0 + 512], acc, 512, dst)
                    o, _, w = dst.last
                    k.dma("sp", mixT_b[64:128, c, c0:c0 + 512], o[0:64, 0:512], R=[o], W=[mixT_b])
    merge_free(A, kT_b, kv)
    A.free(vT_b, qT_b, pb, pe_, Vblk)
    k.free_ring(accr, PTr)

    k.set_ring(7)
    psOs = k.banks[7]
    sbc = A.alloc("sbc", [128, 16, 192], BF16)
    sbn = A.alloc("sbn", [NS, 4, 192], BF16)
    k.dma("sp", sbc[:, :, :], IO["sbias_c"][:, :, :], W=[sbc])
    k.dma("sp", sbn[:, :, :], IO["sbias_n"][:, :, :], W=[sbn])
    kTn, vn_aug, Qbd = C["kTn"], C["vn_aug"], C["Qbd"]
    kc32r = k.rings("kc32", 2, [128, 512], F32)
    vc32r = k.rings("vc32", 2, [128, 512], F32)
    kcbr = k.rings("kcb", 2, [128, 512], BF16)
    vaugr = k.rings("vaug", 2, [128, 8, 66], BF16)
    kTsr = k.rings("kTs", 2, [128, 4, 128], BF16)
    PTsr = k.rings("PTs", 2, [128, 192], BF16)
    tmpr = k.rings("ptmp", 2, [128, 8, 8], BF16)
    Pqr = [k.rings("Pq%d" % s, 2, [128, 8, NS], BF16) for s in range(4)]
    for t in vaugr.t:
        k.memset("pool", t[:, :, :], 1.0, [t])
    for s in range(4):
        for t in Pqr[s].t:
            k.memset(k.alt(), t[:, :, :], 0.0, [t])
    for s in range(4):
        for kt in range(17):
            if kt < 16:
                kc32, vc32, kcb, vaug, kTs = kc32r.next(), vc32r.next(), kcbr.next(), vaugr.next(), kTsr.next()
                k.dma("sp", kc32[:, :], IO["ck"][s, 128 * kt:128 * kt + 128, :], W=[kc32])
                k.dma("sp", vc32[:, :], IO["cv"][s, 128 * kt:128 * kt + 128, :], W=[vc32])
                for src, nm in ((kc32, "wk_s"), (vc32, "wv_s")):
                    if kt == 0:
                        k.dma("sp", IO[nm][s, 0:120, :], src[8:128, :], R=[src])
                    else:
                        k.dma("sp", IO[nm][s, 128 * kt - 8:128 * kt + 120, :], src[:, :], R=[src])
                k.cp("act", kcb[:, :], kc32[:, :], [kc32], [kcb])
                k.cp("dve", vaug[:, :, 0:64], vc32[:, :].rearrange("p (h e) -> p h e", h=8), [vc32], [vaug])
                pst = k.ps.next()
                pbt = bank_bf(pst)
                for c in range(4):
                    k.tr(pbt[:, c * 128:(c + 1) * 128], kcb[:, c * 128:(c + 1) * 128], identb[:, :], [kcb, identb], [pst])
                k.cp("act", kTs[:, :, :], pbt[:, 0:512].rearrange("p (c t) -> p c t", c=4), [pst], [kTs])
                np_ = 128
                lhs_k = lambda c, kTs=kTs: kTs[:, c, :]
                kdep = kTs
                bias = sbc[:, kt, :]
                bT = sbc
                vsrc = vaug
                idb = identb[:, :]
            else:
                np_ = NS
                lhs_k = lambda c: kTn[:, c, :]
                kdep = kTn
                bias = sbn[0:NS, s, :]
                bT = sbn
                vsrc = vn_aug
                idb = identb[0:NS, 0:NS]
            psS = k.ps.next()
            k.mm(psS[0:np_, 0:192], idb, bias, [identb, bT], [psS], start=True, stop=False)
            for c in range(4):
                k.mm(psS[0:np_, c * 48:(c + 1) * 48], lhs_k(c), Qbd[:, c, s, :], [kdep, Qbd], [psS], start=False, stop=(c == 3))
            PTs = PTsr.next()
            k.act(PTs[0:np_, :], psS[0:np_, 0:192], AF.Exp, [psS], [PTs], scale=0.125)
            Pq = Pqr[s].next()
            tmp = tmpr.next()
            P4 = PTs[0:np_, :].rearrange("p (h b t) -> p h b t", h=8, b=3)
            k.tt("dve", tmp[0:np_, :, :], P4[:, :, 0, :], P4[:, :, 1, :], ALU.add, [PTs], [tmp])
            k.tt("dve", Pq[0:np_, :, 8 * s:8 * s + 8], tmp[0:np_, :, :], P4[:, :, 2, :], ALU.add, [PTs, tmp], [Pq])
            for h in range(8):
                k.mm(psOs[0:65, h * NS:(h + 1) * NS], vsrc[0:np_, h, 0:65], Pq[0:np_, h, :], [vsrc, Pq], [psOs],
                     start=(s == 0 and kt == 0 and h == 0), stop=(s == 3 and kt == 16), skip=True)
    accs = A.alloc("accs", [65, 8, NS], F32)
    k.cp("dve", accs[:, :, :], psOs[0:65, 0:8 * NS].rearrange("p (h t) -> p h t", h=8), [psOs], [accs])
    for h in range(8):
        c, hh = h // 2, h % 2
        if hh == 0:
            finalize_attn(k, C, F, accs[:, h, :], accs, NS, lambda c0, w, c=c: (mixT_b[0:64, c, NMAIN:NMAIN + NS], mixT_b))
        else:
            o = otmp.next()
            finalize_attn(k, C, F, accs[:, h, :], accs, NS, lambda c0, w, o=o: (o[0:64, 0:NS], o))
            k.dma("sp", mixT_b[64:128, c, NMAIN:NMAIN + NS], o[0:64, 0:NS], R=[o], W=[mixT_b])
    k.set_ring(8)
    A.free(sbc, sbn, kTn, vn_aug, Qbd, accs, gmb, nob)
    k.free_ring(kc32r, vc32r, kcbr, vaugr, kTsr, PTsr, tmpr, otmp, F["sq"], F["l32"], *Pqr)


def stage_C(k, C, IO):
    A = k.A
    k.set_ring(4)
    accb = k.banks[4:8]
    mixT_a, mixT_b = C["mixT_a"], C["mixT_b"]
    wo = A.alloc("wo", [128, 8, 1024], BF16)
    for kc in range(8):
        k.dma("pool", wo[:, kc, :], IO["w_out"][kc * 128:(kc + 1) * 128, :], W=[wo])
    WB = 256
    wgb = [A.alloc("wg%d" % i, [128, 8, WB], BF16) for i in range(DFF // WB)]
    wub = [A.alloc("wu%d" % i, [128, 8, WB], BF16) for i in range(DFF // WB)]
    for i in range(DFF // WB):
        k.dma("pool", wgb[i][:, :, :], IO["w_gate"][:, i * WB:(i + 1) * WB].rearrange("(c p) n -> p c n", p=128), W=[wgb[i]])
        k.dma("pool", wub[i][:, :, :], IO["w_up"][:, i * WB:(i + 1) * WB].rearrange("(c p) n -> p c n", p=128), W=[wub[i]])
    nfb = A.alloc("nfb", [128, 1024], F32)
    nfin = A.alloc("nfin", [128, 1024], F32)
    k.dma("sp", nfb[:, :], IO["norm_ffn"].partition_broadcast(128), W=[nfb])
    k.dma("sp", nfin[:, :], IO["norm_final"].partition_broadcast(128), W=[nfin])
    C["sm"] = k.rings("smc", 4, [128, 8], F32)
    C["hb"] = k.rings("hbc", 2, [128, 1024], BF16)
    x1r = k.rings("x1", 4, [128, 1024], F32)
    hfr = k.rings("hfT", 2, [128, 8, 256], BF16)
    wdr = k.rings("wd", 3, [128, 1024], BF16)
    e32r = k.rings("ce32", 3, [128, 256], F32)
    c32r = k.rings("cc32", 3, [128, 256], F32)
    u32r = k.rings("cu32", 3, [128, 256], F32)
    aTr = k.rings("aT", 3, [128, 256], BF16)
    units = [(IO["xp"], NPRE + u * 256, u * 256, 2, 128, IO["yp"], u * 256) for u in range(NMAIN // 256)]
    units.append((IO["xs"], 0, NMAIN, 1, NS, IO["ys"], 0))
    NJ = DFF // 128

    def pre(unit, st):
        (xsrc, xrow0, mcol0, ntile, p, ydst, yrow0) = unit
        st["hfT"] = hfr.next()
        st["x1s"] = []
        for t in range(ntile):
            x1 = x1r.next()
            st["x1s"].append(x1)
            k.dma("sp", x1[0:p, :], xsrc[xrow0 + t * 128:xrow0 + t * 128 + p, :], W=[x1])
            cols = slice(mcol0 + t * 128, mcol0 + t * 128 + p)
            for half in range(2):
                psX = k.ps.next()
                for kc in range(8):
                    lhsT = mixT_a[:, kc, cols] if kc < 4 else mixT_b[:, kc - 4, cols]
                    k.mm(psX[0:p, :], lhsT, wo[:, kc, half * 512:(half + 1) * 512], [mixT_a, mixT_b, wo], [psX], start=(kc == 0), stop=(kc == 7))
                k.tt("dve", x1[0:p, half * 512:(half + 1) * 512], x1[0:p, half * 512:(half + 1) * 512], psX[0:p, :], ALU.add, [x1, psX], [x1])
                yield
            norm_transpose(k, x1, p, nfb, st["hfT"], t * 128, C)
            yield

    def ffn(unit, st):
        (xsrc, xrow0, mcol0, ntile, p, ydst, yrow0) = unit
        ntok = (ntile - 1) * 128 + p
        hfT = st["hfT"]

        def issue_gu(j):
            psG = k.ps.next()
            for kc in range(8):
                k.mm(psG[:, 0:ntok], wgb[j // 2][:, kc, (j % 2) * 128:(j % 2) * 128 + 128], hfT[:, kc, 0:ntok], [wgb[j // 2], hfT], [psG], start=(kc == 0), stop=(kc == 7))
            psU = k.ps.next()
            for kc in range(8):
                k.mm(psU[:, 0:ntok], wub[j // 2][:, kc, (j % 2) * 128:(j % 2) * 128 + 128], hfT[:, kc, 0:ntok], [wub[j // 2], hfT], [psU], start=(kc == 0), stop=(kc == 7))
            return psG, psU

        pend = issue_gu(0)
        for j in range(NJ):
            psG, psU = pend
            wd = wdr.next()
            k.dma("pool", wd[:, :], IO["w_down"][j * 128:(j + 1) * 128, :], W=[wd])
            e32, c32, u32, aT = e32r.next(), c32r.next(), u32r.next(), aTr.next()
            k.act(e32[:, 0:ntok], psG[:, 0:ntok], AF.Exp, [psG], [e32], scale=-1.0)
            k.cp("act", u32[:, 0:ntok], psU[:, 0:ntok], [psU], [u32])
            k.act(e32[:, 0:ntok], e32[:, 0:ntok], AF.Ln, [e32], [e32], bias=1.0)
            k.act(e32[:, 0:ntok], e32[:, 0:ntok], AF.Exp, [e32], [e32], scale=-1.0)
            k.tt("dve", c32[:, 0:ntok], psG[:, 0:ntok], e32[:, 0:ntok], ALU.mult, [psG, e32], [c32])
            k.tt("dve", aT[:, 0:ntok], c32[:, 0:ntok], u32[:, 0:ntok], ALU.mult, [c32, u32], [aT])
            if j + 1 < NJ:
                pend = issue_gu(j + 1)
            for t in range(ntile):
                for half in range(2):
                    ab = accb[t * 2 + half]
                    k.mm(ab[0:p, :], aT[:, t * 128:t * 128 + p], wd[:, half * 512:(half + 1) * 512], [aT, wd], [ab],
                         start=(j == 0), stop=(j == NJ - 1))
            yield

    def post(unit, st):
        (xsrc, xrow0, mcol0, ntile, p, ydst, yrow0) = unit
        for t in range(ntile):
            x1 = st["x1s"][t]
            for half in range(2):
                ab = accb[t * 2 + half]
                k.tt("dve", x1[0:p, half * 512:(half + 1) * 512], x1[0:p, half * 512:(half + 1) * 512], ab[0:p, :], ALU.add, [x1, ab], [x1])
            sm = C["sm"].next()
            jk = C["hb"].next()
            k.act(jk[0:p, :], x1[0:p, :], AF.Square, [x1], [jk, sm], accum=sm[0:p, 0:1])
            k.act(sm[0:p, 1:2], sm[0:p, 0:1], AF.Ln, [sm], [sm], scale=1.0 / D, bias=EPS)
            k.act(sm[0:p, 2:3], sm[0:p, 1:2], AF.Exp, [sm], [sm], scale=-0.5)
            k.stt("dve", x1[0:p, :], x1[0:p, :], sm[0:p, 2:3], nfin[0:p, :], ALU.mult, ALU.mult, [x1, sm, nfin], [x1])
            k.dma("sp", ydst[yrow0 + t * 128:yrow0 + t * 128 + p, :], x1[0:p, :], R=[x1])

    states = [dict() for _ in units]
    for _ in pre(units[0], states[0]):
        pass
    for u, unit in enumerate(units):
        gens = [ffn(unit, states[u])]
        if u + 1 < len(units):
            gens.append(pre(units[u + 1], states[u + 1]))
        run_interleaved(gens, offs=1, maxact=2)
        post(unit, states[u])
    k.set_ring(8)


IN_SPECS = [
    ("xp", [NTOK, D], F32), ("xs", [NS, D], F32), ("sconv", [4, 3, 1536], F32), ("srec", [4, 4, 128, 128], F32),
    ("ck", [4, 2048, 512], F32), ("cv", [4, 2048, 512], F32),
    ("norm_mix", [D], F32), ("w_in", [D, 3592], F32), ("w_conv", [4, 1536], F32), ("a_log", [4], F32), ("dt_bias", [4], F32),
    ("noa", [128], F32), ("nob", [64], F32), ("w_out", [D, D], F32), ("norm_ffn", [D], F32),
    ("w_gate", [D, DFF], F32), ("w_up", [D, DFF], F32), ("w_down", [DFF, D], F32), ("norm_final", [D], F32),
    ("cf32", [128, 1664], F32), ("vmask", [128, 1], F32), ("edge8", [128, 1], F32),
    ("pbias", [128, 8, 3, 256], BF16), ("sbias_c", [128, 16, 192], BF16), ("sbias_n", [NS, 4, 192], BF16), ("gmb", [65, 64], BF16),
]
OUT_SPECS = [
    ("yp", [NMAIN, D]), ("ys", [NS, D]), ("conv_p", [3, 1536]), ("rec_p", [4, 128, 128]), ("wk_p", [NMAIN, 512]), ("wv_p", [NMAIN, 512]),
    ("conv_s", [4, 3, 1536]), ("rec_s", [4, 4, 128, 128]), ("wk_s", [4, 2048, 512]), ("wv_s", [4, 2048, 512]),
]


def build_program(stages="GKBC", dbg=False):
    nc = bass.Bass("TRN2", target_bir_lowering=False)
    IO = {}
    if dbg:
        IO["dbg_ma"] = nc.dram_tensor("dbg_ma", [128, 4, NMAIN + NS], F32, kind="ExternalOutput").ap()
        IO["dbg_mb"] = nc.dram_tensor("dbg_mb", [128, 4, NMAIN + NS], F32, kind="ExternalOutput").ap()
    for name, shape, dt in IN_SPECS:
        IO[name] = nc.dram_tensor(name, shape, dt, kind="ExternalInput").ap()
    for name, shape in OUT_SPECS:
        IO[name] = nc.dram_tensor(name, shape, F32, kind="ExternalOutput").ap()
    k = KB(nc)
    C = {}
    setup_consts(k, C, IO)
    C["mixT_a"] = k.A.alloc("mixT_a", [128, 4, NMAIN + NS], BF16)
    if "G" in stages:
        stage_G(k, C, IO)
    if "K" in stages:
        stage_K(k, C, IO)
    if "B" in stages:
        stage_B(k, C, IO)
    if dbg:
        k.dma("pool", IO["dbg_ma"][:, :, :], C["mixT_a"][:, :, :], R=[C["mixT_a"]])
        k.dma("pool", IO["dbg_mb"][:, :, :], C["mixT_b"][:, :, :], R=[C["mixT_b"]])
    if "C" in stages:
        stage_C(k, C, IO)
    k.S.finish()
    k.S.emit()
    return nc, k


def host_tables():
    p = np.arange(128)
    ident = np.eye(128, dtype=np.float32)
    mSU = (p[:, None] < p[None, :]).astype(np.float32)
    mIU = (p[:, None] <= p[None, :]).astype(np.float32)
    ones = np.ones((128, 128), np.float32)
    cf = np.concatenate([np.tile(ident, (1, 4)), np.tile(mSU, (1, 4)), np.tile(mIU, (1, 4)), ones], axis=1)
    slopes = 2.0 ** (-np.arange(1, 9, dtype=np.float64))
    ki = p[:, None].astype(np.float64)
    qi = p[None, :].astype(np.float64)
    pbias = np.zeros((128, 8, 3, 256), np.float64)
    for h in range(8):
        for br, d in enumerate(DILS):
            j = qi - ki
            pbias[:, h, br, 0:128] = np.where(j >= 0, -slopes[h] * d * j * 8.0, NEGB)
            j = qi + 128 - ki
            pbias[:, h, br, 128:256] = np.where(j <= 128, -slopes[h] * d * j * 8.0, NEGB)
    sbc = np.full((128, 16, 192), NEGB, np.float64)
    sbn = np.full((NS, 4, 192), NEGB, np.float64)
    for h in range(8):
        for br, d in enumerate(DILS):
            for t in range(8):
                col = h * 24 + br * 8 + t
                kp = np.arange(2048)
                dist = 2048 + t - kp
                ok = (dist % d == 0) & (dist // d <= 128)
                vals = np.where(ok, -slopes[h] * dist * 8.0, NEGB)
                sbc[:, :, col] = vals.reshape(16, 128).T
                for s in range(4):
                    for t2 in range(8):
                        dist2 = t - t2
                        if dist2 >= 0 and dist2 % d == 0:
                            sbn[8 * s + t2, s, col] = -slopes[h] * dist2 * 8.0
    gm = np.full((65, 64), 1.0 / 64, np.float64)
    gm[64, :] = EPS
    vmask = (p < 8).astype(np.float32).reshape(128, 1)
    bf = ml_dtypes.bfloat16
    return dict(cf32=cf, vmask=vmask, pbias=pbias.astype(np.float32).astype(bf), sbias_c=sbc.astype(np.float32).astype(bf),
                sbias_n=sbn.astype(np.float32).astype(bf), gmb=gm.astype(np.float32).astype(bf))


_CACHE = {}


def make_in_maps(x_prompt, x_sample, state_conv, state_rec, cache_win_k, cache_win_v, norm_mix, w_in, w_conv, a_log, dt_bias,
                 norm_out_a, norm_out_b, w_out, norm_ffn, w_gate, w_up, w_down, norm_final):
    f = lambda a: np.ascontiguousarray(np.asarray(a, dtype=np.float32))
    tabs = host_tables()
    shared = dict(norm_mix=f(norm_mix[0]), w_in=f(w_in[0]), w_conv=f(w_conv[0]), a_log=f(a_log[0]), dt_bias=f(dt_bias[0]),
                  noa=f(norm_out_a[0]), nob=f(norm_out_b[0]), w_out=f(w_out[0]), norm_ffn=f(norm_ffn[0]),
                  w_gate=f(w_gate[0]), w_up=f(w_up[0]), w_down=f(w_down[0]), norm_final=f(norm_final), **tabs)
    in_maps = []
    for c in range(NCORES):
        b, half = c // 2, c % 2
        xp = np.zeros((NTOK, D), np.float32)
        if half == 1:
            xp[:] = x_prompt[b]
        else:
            xp[NPRE:] = x_prompt[b, 0:NMAIN]
        sl = slice(4 * c, 4 * c + 4)
        m = dict(shared)
        m.update(xp=xp, xs=f(x_sample[sl]).reshape(NS, D), sconv=f(state_conv[0, sl]), srec=f(state_rec[0, sl]),
                 ck=f(cache_win_k[0, sl]).reshape(4, 2048, 512), cv=f(cache_win_v[0, sl]).reshape(4, 2048, 512),
                 edge8=np.full((128, 1), 0.0 if half == 1 else NEGB, np.float32))
        in_maps.append(m)
    return in_maps


def assemble(res):
    y_prompt = np.zeros((4, 4096, D), np.float32)
    y_sample = np.zeros((32, 8, D), np.float32)
    conv_p = np.zeros((1, 4, 3, 1536), np.float32)
    rec_p = np.zeros((1, 4, 4, 128, 128), np.float32)
    wk_p = np.zeros((1, 4, 2048, 8, 64), np.float32)
    wv_p = np.zeros((1, 4, 2048, 8, 64), np.float32)
    conv_s = np.zeros((1, 32, 3, 1536), np.float32)
    rec_s = np.zeros((1, 32, 4, 128, 128), np.float32)
    wk_s = np.zeros((1, 32, 2048, 8, 64), np.float32)
    wv_s = np.zeros((1, 32, 2048, 8, 64), np.float32)
    for c in range(NCORES):
        b, half = c // 2, c % 2
        r = res[c]
        y_prompt[b, half * NMAIN:(half + 1) * NMAIN] = r["yp"]
        sl = slice(4 * c, 4 * c + 4)
        y_sample[sl] = r["ys"].reshape(4, 8, D)
        conv_s[0, sl] = r["conv_s"]
        rec_s[0, sl] = r["rec_s"]
        wk_s[0, sl] = r["wk_s"].reshape(4, 2048, 8, 64)
        wv_s[0, sl] = r["wv_s"].reshape(4, 2048, 8, 64)
        if half == 1:
            conv_p[0, b] = r["conv_p"]
            rec_p[0, b] = r["rec_p"]
            wk_p[0, b] = r["wk_p"].reshape(2048, 8, 64)
            wv_p[0, b] = r["wv_p"].reshape(2048, 8, 64)
    return (y_prompt, y_sample, conv_p, rec_p, wk_p, wv_p, conv_s, rec_s, wk_s, wv_s)


def kernel(**inputs):
    in_maps = make_in_maps(**inputs)
    if "nc" not in _CACHE:
        _CACHE["nc"] = build_program()[0]
    res = run_bass_kernel_spmd(_CACHE["nc"], in_maps, core_ids=list(range(NCORES)))
    return assemble(res.results)
```

```python
import math
import numpy as np
import ml_dtypes
import concourse.bass as bass
import concourse.mybir as mybir
from concourse.bass_utils import run_bass_kernel_spmd

F32 = mybir.dt.float32
BF16 = mybir.dt.bfloat16
ALU = mybir.AluOpType
AF = mybir.ActivationFunctionType

ENGS = ("pe", "act", "dve", "pool", "sp")
NCORES = 8
D = 1024
NPRE = 2048
NMAIN = 2048
NTOK = NPRE + NMAIN
NS = 32
DFF = 2816
EPS = 1e-6
NEGB = -240000.0
DILS = (1, 4, 16)
import os
KSTOP = int(os.environ.get("KSTOP", "9"))
KOFF = os.environ.get("KOFF", "")


class T:
    __slots__ = ("name", "ap", "lw", "rde", "rdd", "rng", "psum")

    def __init__(self, name, ap, psum=False):
        self.name = name
        self.ap = ap
        self.psum = psum
        self.lw = None
        self.rde = {}
        self.rdd = []
        self.rng = None

    def __getitem__(self, idx):
        return self.ap[idx]


class Sched:
    def __init__(self, nc, n_dma_slots=10):
        self.nc = nc
        self.ops = {e: [] for e in ENGS}
        self.cnt = {e: 0 for e in ENGS}
        self.waited = {e: {} for e in ENGS}
        self.nslots = n_dma_slots
        self.slot_total = {}
        self.slot_next = {"sp": 0, "pool": 0, "act": 0}
        self.dma_info = []

    def _need(self, eng, dep, waits):
        if dep[0] == "e":
            _, e2, seq = dep
            if e2 == eng and eng in ("pe", "sp"):
                return
            key = ("e", e2)
            val = seq
        else:
            key, val = self.dma_info[dep[1]]
        w = self.waited[eng]
        if w.get(key, 0) >= val:
            return
        w[key] = val
        waits.append((key, val))

    def _deps(self, eng, reads, writes):
        waits = []
        for t in reads:
            if t.lw is not None:
                self._need(eng, t.lw, waits)
            if t.psum:
                for e2, seq in t.rde.items():
                    if e2 != eng:
                        self._need(eng, ("e", e2, seq), waits)
        for t in writes:
            lw = t.lw
            if lw is not None:
                self._need(eng, lw, waits)
            for e2, seq in t.rde.items():
                if e2 != eng or eng != "pe":
                    self._need(eng, ("e", e2, seq), waits)
            for did in t.rdd:
                self._need(eng, ("d", did), waits)
        return waits

    def _mark(self, me, reads, writes):
        for t in reads:
            if me[0] == "e":
                if t.rde.get(me[1], 0) < me[2]:
                    t.rde[me[1]] = me[2]
            else:
                t.rdd.append(me[1])
        for t in writes:
            t.lw = me
            t.rde = {}
            t.rdd = []

    def op(self, eng, fn, reads=(), writes=()):
        waits = self._deps(eng, reads, writes)
        self.cnt[eng] += 1
        me = ("e", eng, self.cnt[eng])
        self._mark(me, reads, writes)
        self.ops[eng].append((fn, waits, "c", None))

    def dma(self, eng, out_ap, in_ap, reads=(), writes=(), **kw):
        waits = self._deps(eng, reads, writes)
        slot = self.slot_next[eng]
        self.slot_next[eng] = (slot + 1) % self.nslots
        key = ("d", eng, slot)
        prev = self.slot_total.get(key, 0)
        if prev:
            w = self.waited[eng]
            if w.get(key, 0) < prev:
                w[key] = prev
                waits.append((key, prev))
        val = prev + 16
        self.slot_total[key] = val
        did = len(self.dma_info)
        self.dma_info.append((key, val))
        self._mark(("d", did), reads, writes)
        self.ops[eng].append(((out_ap, in_ap, kw), waits, "d", key))

    def finish(self):
        for eng in ("sp", "pool", "act"):
            waits = []
            for key, val in self.slot_total.items():
                if key[1] != eng:
                    continue
                w = self.waited[eng]
                if w.get(key, 0) < val:
                    w[key] = val
                    waits.append((key, val))
            if waits:
                self.ops[eng].append((None, waits, "w", None))

    def emit(self):
        nc = self.nc
        from contextlib import ExitStack
        with ExitStack() as es:
            sems = {}
            for e in ENGS:
                sems[("e", e)] = es.enter_context(nc.semaphore("s_" + e))
            for key in self.slot_total:
                sems[key] = es.enter_context(nc.semaphore("d_%s_%d" % (key[1], key[2])))
            block = es.enter_context(nc.Block())
            refd = {e: set() for e in ENGS}
            for e in ENGS:
                for fn, waits, kind, extra in self.ops[e]:
                    for key, val in waits:
                        if key[0] == "e":
                            refd[key[1]].add(val)
            rank = {e: {s: i + 1 for i, s in enumerate(sorted(refd[e]))} for e in ENGS}

            def run(engname):
                def body(eng):
                    mysem = sems[("e", engname)]
                    myrank = rank[engname]
                    seq = 0
                    for fn, waits, kind, extra in self.ops[engname]:
                        for key, val in waits:
                            if key[0] == "e":
                                eng.wait_ge(sems[key], rank[key[1]][val])
                            else:
                                eng.wait_ge(sems[key], val)
                        if kind == "c":
                            seq += 1
                            ins = fn(eng)
                            if seq in myrank:
                                ins.then_inc(mysem, 1)
                        elif kind == "d":
                            out_ap, in_ap, kw = fn
                            eng.dma_start(out=out_ap, in_=in_ap, **kw).then_inc(sems[extra], 16)
                return body

            block.tensor(run("pe"))
            block.scalar(run("act"))
            block.vector(run("dve"))
            block.gpsimd(run("pool"))
            block.sync(run("sp"))


class Arena:
    def __init__(self, nc, nbytes):
        self.n = nbytes
        self.base = nc.alloc_sbuf_tensor("arena", [128, nbytes // 2], BF16).ap()
        self.live = []
        self.retired = []
        self.peak = 0

    def alloc(self, name, shape, dt):
        esz = 4 if dt == F32 else 2
        n = 1
        for s in shape[1:]:
            n *= s
        nb = (n * esz + 63) // 64 * 64
        pos = 0
        for a, b, _ in sorted(self.live, key=lambda x: x[0]):
            if a - pos >= nb:
                break
            pos = max(pos, b)
        if pos + nb > self.n:
            raise RuntimeError("arena full allocating %s (%d bytes) live=%d" % (name, nb, sum(b - a for a, b, _ in self.live)))
        a, b = pos, pos + nb
        v = self.base[:, a // 2:(a + n * esz) // 2]
        if dt == F32:
            v = v.bitcast(F32)
        if len(shape) == 3:
            v = v.rearrange("p (x y) -> p x y", x=shape[1])
        elif len(shape) == 4:
            v = v.rearrange("p (x y z) -> p x y z", x=shape[1], y=shape[2])
        if shape[0] < 128:
            v = v[0:shape[0]]
        t = T(name, v)
        t.rng = (a, b)
        keep = []
        for ra, rb, rt in self.retired:
            if ra < b and a < rb:
                for e2, seq in rt.rde.items():
                    if t.rde.get(e2, 0) < seq:
                        t.rde[e2] = seq
                t.rdd.extend(rt.rdd)
                if rt.lw is not None:
                    if rt.lw[0] == "e":
                        if t.rde.get(rt.lw[1], 0) < rt.lw[2]:
                            t.rde[rt.lw[1]] = rt.lw[2]
                    else:
                        t.rdd.append(rt.lw[1])
                if ra >= a and rb <= b:
                    continue
            keep.append((ra, rb, rt))
        self.retired = keep
        self.live.append((a, b, t))
        self.peak = max(self.peak, b)
        return t

    def free(self, *ts):
        for t in ts:
            for i, (a, b, tt) in enumerate(self.live):
                if tt is t:
                    self.live.pop(i)
                    self.retired.append((a, b, t))
                    break
            else:
                raise RuntimeError("free of unknown tile " + t.name)


class Ring:
    def __init__(self, tiles):
        self.t = tiles
        self.i = 0

    def next(self):
        t = self.t[self.i]
        self.i = (self.i + 1) % len(self.t)
        return t


class KB:
    def __init__(self, nc):
        self.nc = nc
        self.S = Sched(nc)
        self.A = Arena(nc, 207 * 1024)
        self.banks = [T("ps%d" % i, nc.alloc_psum_tensor("ps%d" % i, [128, 512], F32).ap(), psum=True) for i in range(8)]
        self.ps = Ring(self.banks)
        self._rr = 0

    def set_ring(self, n):
        self.ps = Ring(self.banks[0:n])

    def act(self, out, in_, func, R, W, scale=None, bias=None, accum=None):
        kw = {}
        if scale is not None:
            kw["scale"] = scale
        if bias is not None:
            kw["bias"] = bias
        if accum is not None:
            kw["accum_out"] = accum
        self.S.op("act", lambda e: e.activation(out=out, in_=in_, func=func, **kw), R, W)

    def mm(self, out, lhsT, rhs, R, W, start=True, stop=True, skip=False):
        self.S.op("pe", lambda e: e.matmul(out, lhsT=lhsT, rhs=rhs, start=start, stop=stop, skip_group_check=skip), R, W)

    def tr(self, out, in_, ident, R, W):
        self.S.op("pe", lambda e: e.transpose(out=out, in_=in_, identity=ident), R, W)

    def tt(self, eng, out, in0, in1, op, R, W):
        self.S.op(eng, lambda e: e.tensor_tensor(out=out, in0=in0, in1=in1, op=op), R, W)

    def ts(self, eng, out, in0, s1, op0, R, W, s2=None, op1=None):
        if op1 is None:
            self.S.op(eng, lambda e: e.tensor_scalar(out=out, in0=in0, scalar1=s1, scalar2=None, op0=op0), R, W)
        else:
            self.S.op(eng, lambda e: e.tensor_scalar(out=out, in0=in0, scalar1=s1, scalar2=s2, op0=op0, op1=op1), R, W)

    def stt(self, eng, out, in0, scalar, in1, op0, op1, R, W):
        self.S.op(eng, lambda e: e.scalar_tensor_tensor(out=out, in0=in0, scalar=scalar, in1=in1, op0=op0, op1=op1), R, W)

    def cp(self, eng, out, in_, R, W):
        if eng == "act":
            self.S.op("act", lambda e: e.activation(out=out, in_=in_, func=AF.Copy), R, W)
        else:
            self.S.op(eng, lambda e: e.tensor_copy(out=out, in_=in_), R, W)

    def recip(self, out, in_, R, W):
        self.S.op("dve", lambda e: e.reciprocal(out=out, in_=in_), R, W)

    def memset(self, eng, ap, val, W):
        self.S.op(eng, lambda e: e.memset(ap, val), (), W)

    def dma(self, eng, out, in_, R=(), W=(), **kw):
        self.S.dma(eng, out, in_, R, W, **kw)

    def alt(self, engs=("dve", "pool")):
        self._rr += 1
        return engs[self._rr % len(engs)]

    def rings(self, name, n, shape, dt):
        return Ring([self.A.alloc("%s%d" % (name, i), shape, dt) for i in range(n)])

    def free_ring(self, *rings):
        for r in rings:
            self.A.free(*r.t)


def sl(start, count, step):
    return slice(start, start + step * (count - 1) + 1, step)


def bank_bf(ps):
    return ps.ap.bitcast(BF16)


def norm_stats(k, xt, p, nb_tile, C, dimscale=1.0 / D):
    sm = C["sm"].next()
    hb = C["hb"].next()
    k.act(hb[0:p, :], xt[0:p, :], AF.Square, [xt], [hb, sm], accum=sm[0:p, 0:1])
    k.act(sm[0:p, 1:2], sm[0:p, 0:1], AF.Ln, [sm], [sm], scale=dimscale, bias=EPS)
    k.act(sm[0:p, 2:3], sm[0:p, 1:2], AF.Exp, [sm], [sm], scale=-0.5)
    k.stt("dve", hb[0:p, :], xt[0:p, :], sm[0:p, 2:3], nb_tile[0:p, :], ALU.mult, ALU.mult, [xt, sm, nb_tile], [hb])
    return hb


def transpose_to(k, hb, p, hT, c0, C):
    ps = k.ps.next()
    pb = bank_bf(ps)
    for c in range(8):
        k.tr(pb[:, c * 128:c * 128 + p], hb[0:p, c * 128:(c + 1) * 128], C["ident"][0:p, 0:p], [hb, C["ident"]], [ps])
    k.cp("act", hT[:, :, c0:c0 + p], pb[:, :].rearrange("q (c t) -> q c t", c=8)[:, :, 0:p], [ps], [hT])


def norm_transpose(k, xt, p, nb_tile, hT, c0, C, dimscale=1.0 / D):
    hb = norm_stats(k, xt, p, nb_tile, C, dimscale)
    transpose_to(k, hb, p, hT, c0, C)


def gdn_tile(k, C, G, hT_ap, qT_ap, kT_ap, vT_ap, S, Sb, deps, main, nf, mix_out, vm=None):
    w_a = C["w_a"]
    ident = C["ident"]
    sc = G["sc"].next()

    def bc(c0):
        return sc[:, c0:c0 + 4].unsqueeze(2).to_broadcast([128, 4, 128])

    def ps4(ps):
        return ps[:, :].rearrange("p (h c) -> p h c", h=4)

    psBA = k.ps.next()
    for kc in range(8):
        k.mm(psBA[:, 0:8], hT_ap[:, kc, :], w_a[:, kc, 2048:2056], deps + [C["wa_ba"]], [psBA], start=(kc == 0), stop=(kc == 7))
    k.act(sc[:, 0:4], psBA[:, 0:4], AF.Exp, [psBA], [sc], scale=-1.0)
    k.ts("dve", sc[:, 0:4], sc[:, 0:4], 1.0, ALU.add, [sc], [sc])
    k.recip(sc[:, 0:4], sc[:, 0:4], [sc], [sc])
    if vm is not None:
        k.ts("dve", sc[:, 0:4], sc[:, 0:4], vm[:, 0:1], ALU.mult, [sc, vm], [sc])
    k.ts("dve", sc[:, 4:8], sc[:, 0:4], -1.0, ALU.mult, [sc], [sc])
    yield
    k.tt("dve", sc[:, 8:12], psBA[:, 4:8], C["dtb"][:, :], ALU.add, [psBA, C["dtb"]], [sc])
    k.act(sc[:, 8:12], sc[:, 8:12], AF.Exp, [sc], [sc])
    k.act(sc[:, 8:12], sc[:, 8:12], AF.Ln, [sc], [sc], bias=1.0)
    k.tt("dve", sc[:, 8:12], sc[:, 8:12], C["negA"][:, :], ALU.mult, [sc, C["negA"]], [sc])
    if vm is not None:
        k.ts("dve", sc[:, 8:12], sc[:, 8:12], vm[:, 0:1], ALU.mult, [sc, vm], [sc])
    yield
    psG = k.ps.next()
    k.mm(psG[:, 0:4], C["mIU"][:, 0:128], sc[:, 8:12], [C["mIU"], sc], [psG])
    k.mm(psG[:, 4:8], C["onesf"][:, :], sc[:, 8:12], [C["onesf"], sc], [psG])
    k.cp("dve", sc[:, 12:20], psG[:, 0:8], [psG], [sc])
    k.act(sc[:, 20:28], sc[:, 12:20], AF.Exp, [sc], [sc])
    k.tt("dve", sc[:, 28:32], sc[:, 16:20], sc[:, 12:16], ALU.subtract, [sc], [sc])
    k.act(sc[:, 28:32], sc[:, 28:32], AF.Exp, [sc], [sc])
    yield
    tg = G["tg"].next()
    k.tt("dve", tg[:, :, :], C["mIU4"][:, :, :], bc(8), ALU.mult, [C["mIU4"], sc], [tg])
    psR = k.ps.next()
    k.mm(psR[:, :], C["onesf"][:, :], tg[:, :, :], [C["onesf"], tg], [psR])
    dec = G["dec"].next()
    k.tt("dve", dec[:, :, :], ps4(psR), bc(12), ALU.subtract, [psR, sc], [dec])
    k.act(dec[:, :, :], dec[:, :, :], AF.Exp, [dec], [dec])
    dS = G["dS"].next()
    k.stt("dve", dS[:, :, :], dec[:, :, :], 1.0, C["mSU4"][:, :, :], ALU.min, ALU.mult, [dec, C["mSU4"]], [dS])
    k.tt("dve", dS[:, :, :], dS[:, :, :], bc(4), ALU.mult, [dS, sc], [dS])
    if main:
        egr = G["egr"].next()
        k.act(egr[:, :, :], psR[:, :].rearrange("p (h c) -> p h c", h=4), AF.Exp, [psR], [egr])
        dI = G["dI"].next()
        k.stt("dve", dI[:, :, :], dec[:, :, :], 1.0, C["mIU4"][:, :, :], ALU.min, ALU.mult, [dec, C["mIU4"]], [dI])
    yield
    psk = k.ps.next()
    pbk = bank_bf(psk)
    for h in range(4):
        k.tr(pbk[:, h * 128:(h + 1) * 128], kT_ap[:, h, :], ident[:, :], deps + [ident], [psk])
    psv = k.ps.next()
    pbv = bank_bf(psv)
    for h in range(4):
        k.tr(pbv[:, h * 128:(h + 1) * 128], vT_ap[:, h, :], ident[:, :], deps + [ident], [psv])
    kg = G["kg"].next()
    kdec = G["kdec"].next()
    vtok = G["vtok"].next()
    ktok = G["e"].next()
    k.cp("act", ktok[:, :, :], pbk[:, 0:512].rearrange("p (h c) -> p h c", h=4), [psk], [ktok])
    k.cp("dve", vtok[:, :, :], pbv[:, 0:512].rearrange("p (h c) -> p h c", h=4), [psv], [vtok])
    k.tt("dve", kg[:, :, :], ktok[:, :, :], bc(20), ALU.mult, [ktok, sc], [kg])
    k.tt("dve", kdec[:, :, :], ktok[:, :, :], bc(28), ALU.mult, [ktok, sc], [kdec])
    yield
    psGm = k.ps.next()
    for h in range(4):
        k.mm(psGm[:, h * 128:(h + 1) * 128], kT_ap[:, h, :], kT_ap[:, h, :], deps, [psGm])
    R = G["R"].next()
    Rf = dec
    k.tt("dve", Rf[:, :, :], ps4(psGm), dS[:, :, :], ALU.mult, [psGm, dS], [Rf])
    k.cp("act", R[:, :, :], Rf[:, :, :], [Rf], [R])
    if main:
        psQK = k.ps.next()
        for h in range(4):
            k.mm(psQK[:, h * 128:(h + 1) * 128], kT_ap[:, h, :], qT_ap[:, h, :], deps, [psQK])
        QKT = G["QKT"].next()
        k.tt("dve", QKT[:, :, :], psQK[:, :].rearrange("p (h c) -> p h c", h=4), dI[:, :, :], ALU.mult, [psQK, dI], [QKT])
        qg = G["qg"].next()
        k.tt("dve", qg[:, :, :], qT_ap, egr[:, :, :], ALU.mult, deps + [egr], [qg])
    P = G["P"].next()
    k.tt("dve", P[:, :, :], R[:, :, :], C["id4b"][:, :, :], ALU.add, [R, C["id4b"]], [P])
    yield
    psr = k.ps.next()
    pbr = bank_bf(psr)
    for h in range(4):
        k.tr(pbr[:, h * 128:(h + 1) * 128], R[:, h, :], ident[:, :], [R, ident], [psr])
    RT = G["RT"].next()
    k.cp("act", RT[:, :, :], pbr[:, 0:512].rearrange("p (h c) -> p h c", h=4), [psr], [RT])
    for kk in range(1, nf + 1):
        yield
        psRk = psRTk = psP = None
        if kk <= nf - 2:
            psRk = k.ps.next()
            for h in range(4):
                k.mm(psRk[:, h * 128:(h + 1) * 128], RT[:, h, :], R[:, h, :], [RT, R], [psRk])
        if kk <= nf - 1:
            psRTk = k.ps.next()
            for h in range(4):
                k.mm(psRTk[:, h * 128:(h + 1) * 128], R[:, h, :], RT[:, h, :], [RT, R], [psRTk])
        if kk >= 2:
            psP = k.ps.next()
            for h in range(4):
                k.mm(psP[:, h * 128:(h + 1) * 128], RT[:, h, :], P[:, h, :], [RT, P], [psP])
        if psRk is not None:
            Rn = G["R"].next()
            k.cp("act", Rn[:, :, :], psRk[:, :].rearrange("p (h c) -> p h c", h=4), [psRk], [Rn])
        if psRTk is not None:
            RTn = G["RT"].next()
            k.cp("dve", RTn[:, :, :], psRTk[:, :].rearrange("p (h c) -> p h c", h=4), [psRTk], [RTn])
        if psP is not None:
            Pn = G["P"].next()
            k.tt("dve", Pn[:, :, :], psP[:, :].rearrange("p (h c) -> p h c", h=4), P[:, :, :], ALU.add, [psP, P], [Pn])
            P = Pn
        if psRk is not None:
            R = Rn
        if psRTk is not None:
            RT = RTn
    yield
    pst_ = k.ps.next()
    pbt_ = bank_bf(pst_)
    for h in range(4):
        k.tr(pbt_[:, h * 128:(h + 1) * 128], P[:, h, :], ident[:, :], [P, ident], [pst_])
    PTf = tg
    k.cp("act", PTf[:, :, :], pbt_[:, 0:512].rearrange("p (h c) -> p h c", h=4), [pst_], [PTf])
    psE = k.ps.next()
    for h in range(4):
        k.mm(psE[:, h * 128:(h + 1) * 128], PTf[:, h, :], Rf[:, h, :], [PTf, Rf], [psE])
    Et = G["Ec"].next()
    Ec = G["Ec"].next()
    k.tt("dve", Et[:, :, :], C["id4b"][:, :, :], P[:, :, :], ALU.subtract, [C["id4b"], P], [Et])
    k.tt("dve", Ec[:, :, :], psE[:, :].rearrange("p (h c) -> p h c", h=4), Et[:, :, :], ALU.add, [psE, Et], [Ec])
    yield
    psC1 = k.ps.next()
    for h in range(4):
        k.mm(psC1[:, h * 128:(h + 1) * 128], Ec[:, h, :], vtok[:, h, :], [Ec, vtok], [psC1])
    psC2 = k.ps.next()
    for h in range(4):
        k.mm(psC2[:, h * 128:(h + 1) * 128], Ec[:, h, :], kg[:, h, :], [Ec, kg], [psC2])
    vtok2 = G["vtok"].next()
    kg2 = G["kg"].next()
    k.tt("dve", vtok2[:, :, :], psC1[:, :].rearrange("p (h c) -> p h c", h=4), vtok[:, :, :], ALU.add, [psC1, vtok], [vtok2])
    k.tt("dve", kg2[:, :, :], psC2[:, :].rearrange("p (h c) -> p h c", h=4), kg[:, :, :], ALU.add, [psC2, kg], [kg2])
    vtok, kg = vtok2, kg2
    yield
    psU = k.ps.next()
    for h in range(4):
        k.mm(psU[:, h * 128:(h + 1) * 128], P[:, h, :], vtok[:, h, :], [P, vtok], [psU])
    psW = k.ps.next()
    for h in range(4):
        k.mm(psW[:, h * 128:(h + 1) * 128], kg[:, h, :], P[:, h, :], [P, kg], [psW])
    ub = G["ub"].next()
    k.cp("dve", ub[:, :, :], ps4(psU), [psU], [ub])
    wT = G["wT"].next()
    k.cp("act", wT[:, :, :], psW[:, :].rearrange("p (h c) -> p h c", h=4), [psW], [wT])
    yield
    psS1 = k.ps.next()
    for h in range(4):
        k.mm(psS1[:, h * 128:(h + 1) * 128], wT[:, h, :], Sb[:, h, :], [wT, Sb], [psS1])
    e = G["e"].next()
    k.tt("dve", ub[:, :, :], ps4(psS1), ub[:, :, :], ALU.subtract, [psS1, ub], [ub])
    k.tt("dve", e[:, :, :], ub[:, :, :], bc(4), ALU.mult, [ub, sc], [e])
    if main:
        psO = k.ps.next()
        for h in range(4):
            k.mm(psO[:, h * 128:(h + 1) * 128], qg[:, h, :], Sb[:, h, :], [qg, Sb], [psO], start=True, stop=False)
            k.mm(psO[:, h * 128:(h + 1) * 128], QKT[:, h, :], e[:, h, :], [QKT, e], [psO], start=False, stop=True)
    if main:
        o32 = ub
        k.cp("act", o32[:, :, :], psO[:, :].rearrange("p (h c) -> p h c", h=4), [psO], [o32])
    psSn = k.ps.next()
    for h in range(4):
        k.mm(psSn[:, h * 128:(h + 1) * 128], kdec[:, h, :], e[:, h, :], [kdec, e], [psSn])
    k.tt("dve", S[:, :, :], S[:, :, :], bc(24), ALU.mult, [S, sc], [S])
    k.tt("dve", S[:, :, :], S[:, :, :], ps4(psSn), ALU.add, [S, psSn], [S])
    k.cp("act", Sb[:, :, :], S[:, :, :], [S], [Sb])
    if not main:
        return
    yield
    jk = G["jk"].next()
    for h in range(4):
        k.act(jk[:, :], o32[:, h, :], AF.Square, [o32], [jk, sc], accum=sc[:, 32 + h:33 + h])
    k.act(sc[:, 32:36], sc[:, 32:36], AF.Ln, [sc], [sc], scale=1.0 / 128, bias=EPS)
    k.act(sc[:, 32:36], sc[:, 32:36], AF.Exp, [sc], [sc], scale=-0.5)
    yield
    psZ = k.ps.next()
    for kc in range(8):
        k.mm(psZ[:, :], hT_ap[:, kc, :], w_a[:, kc, 1536:2048], deps + [C["wa_z"]], [psZ], start=(kc == 0), stop=(kc == 7))
    ez = G["ez"].next()
    k.act(ez[:, :], psZ[:, :], AF.Exp, [psZ], [ez], scale=-1.0)
    k.act(ez[:, :], ez[:, :], AF.Ln, [ez], [ez], bias=1.0)
    k.act(ez[:, :], ez[:, :], AF.Exp, [ez], [ez], scale=-1.0)
    zn = G["zn"].next()
    k.tt("dve", zn[:, :], psZ[:, :], C["noa"][:, :], ALU.mult, [psZ, C["noa"]], [zn])
    k.tt("dve", zn[:, :], zn[:, :], ez[:, :], ALU.mult, [zn, ez], [zn])
    yield
    og = G["og"].next()
    k.tt("dve", o32[:, :, :], o32[:, :, :], bc(32), ALU.mult, [o32, sc], [o32])
    k.tt("dve", og[:, :].rearrange("p (h c) -> p h c", h=4), o32[:, :, :], zn[:, :].rearrange("p (h c) -> p h c", h=4), ALU.mult, [o32, zn], [og])
    mix_out(og)


def run_interleaved(gens, offs=3, maxact=2):
    active, pending, steps = [], list(gens), {}
    while active or pending:
        if pending and len(active) < maxact and (not active or steps[id(active[-1])] >= offs):
            gnew = pending.pop(0)
            active.append(gnew)
            steps[id(gnew)] = 0
        for gg in list(active):
            try:
                next(gg)
                steps[id(gg)] += 1
            except StopIteration:
                active.remove(gg)


EXTRA_RINGS = [("R", 2, [128, 4, 128], BF16), ("RT", 2, [128, 4, 128], BF16), ("P", 2, [128, 4, 128], BF16),
               ("kg", 2, [128, 4, 128], BF16), ("vtok", 2, [128, 4, 128], BF16), ("Ec", 2, [128, 4, 128], BF16),
               ("sc", 1, [128, 40], F32), ("tg", 1, [128, 4, 128], F32), ("dec", 1, [128, 4, 128], F32),
               ("egr", 1, [128, 4, 128], F32), ("ub", 1, [128, 4, 128], F32)] + \
              [(nm, 1, [128, 4, 128], BF16) for nm in ("dS", "dI", "kdec", "QKT", "qg", "wT", "e")]


def silu_from_psum(k, G, ps_ap, psT, n):
    e32 = G["e32"].next()
    c32 = G["c32"].next()
    k.act(e32[:, 0:n], ps_ap, AF.Exp, [psT], [e32], scale=-1.0)
    k.act(e32[:, 0:n], e32[:, 0:n], AF.Ln, [e32], [e32], bias=1.0)
    k.act(e32[:, 0:n], e32[:, 0:n], AF.Exp, [e32], [e32], scale=-1.0)
    k.tt("dve", c32[:, 0:n], ps_ap, e32[:, 0:n], ALU.mult, [psT, e32], [c32])
    return c32


def l2norm_chunk(k, C, G, c32, n, out_ap, outT, qscale):
    sq = G["sq"].next()
    k.tt("dve", sq[:, 0:n], c32[:, 0:n], c32[:, 0:n], ALU.mult, [c32], [sq])
    psC = k.ps.next()
    k.mm(psC[:, 0:n], C["onesb"][:, :], sq[:, 0:n], [C["onesb"], sq], [psC])
    l32 = G["l32"].next()
    k.act(l32[:, 0:n], psC[:, 0:n], AF.Ln, [psC], [l32], bias=EPS)
    if qscale:
        k.act(l32[:, 0:n], l32[:, 0:n], AF.Exp, [l32], [l32], scale=-0.5, bias=C["lnq"][:, 0:1])
        rd = [c32, l32, C["lnq"]]
    else:
        k.act(l32[:, 0:n], l32[:, 0:n], AF.Exp, [l32], [l32], scale=-0.5)
        rd = [c32, l32]
    k.tt("dve", out_ap, c32[:, 0:n], l32[:, 0:n], ALU.mult, rd, [outT])


def stage_G(k, C, IO):
    A = k.A
    w_a = A.alloc("w_a", [128, 8, 2056], BF16)
    C["w_a"] = w_a
    wa_units = {nm: T("wa_" + nm, None) for nm in ("q", "k", "v", "z", "ba")}
    for t in wa_units.values():
        for e2, seq in w_a.rde.items():
            t.rde[e2] = seq
        t.rdd.extend(w_a.rdd)
    C["wa_g"] = [wa_units["q"], wa_units["k"], wa_units["v"]]
    C["wa_z"], C["wa_ba"] = wa_units["z"], wa_units["ba"]
    for nm, c0, c1 in (("k", 512, 1024), ("v", 1024, 1536), ("ba", 2048, 2056), ("q", 0, 512), ("z", 1536, 2048)):
        k.dma("pool", w_a[:, :, c0:c1], IO["w_in"][:, c0:c1].rearrange("(c p) n -> p c n", p=128), W=[wa_units[nm]],
              allow_slow_non_contiguous=(nm == "ba"))
    nmb = A.alloc("nmb", [128, 1024], F32)
    k.dma("sp", nmb[:, :], IO["norm_mix"].partition_broadcast(128), W=[nmb])
    wconv = A.alloc("wconv", [128, 4, 12], F32)
    for i in range(4):
        k.dma("sp", wconv[:, i, :], IO["w_conv"][i].rearrange("(c p) -> p c", p=128), W=[wconv], allow_slow_non_contiguous=True)
    diag = A.alloc("diag", [128, 12, 4, 128], BF16)
    for ch in range(12):
        for i in range(4):
            k.ts(k.alt(), diag[:, ch, i, :], C["identf"][:, 0:128], wconv[:, i, ch:ch + 1], ALU.mult, [C["identf"], wconv], [diag])
    dtb = A.alloc("dtb", [128, 4], F32)
    negA = A.alloc("negA", [128, 4], F32)
    C["dtb"], C["negA"] = dtb, negA
    k.dma("sp", dtb[:, :], IO["dt_bias"].partition_broadcast(128), W=[dtb])
    k.dma("sp", negA[:, :], IO["a_log"].partition_broadcast(128), W=[negA])
    k.act(negA[:, :], negA[:, :], AF.Exp, [negA], [negA])
    k.ts("dve", negA[:, :], negA[:, :], -1.0, ALU.mult, [negA], [negA])
    noa = A.alloc("noa", [128, 512], F32)
    C["noa"] = noa
    for h in range(4):
        k.dma("sp", noa[:, h * 128:(h + 1) * 128], IO["noa"].partition_broadcast(128), W=[noa])
    lnq = A.alloc("lnq", [128, 1], F32)
    C["lnq"] = lnq
    k.memset("pool", lnq[:, :], math.log(128.0 ** -0.5), [lnq])

    G = {}
    C["sm"] = k.rings("sm", 4, [128, 8], F32)
    C["hb"] = k.rings("hb", 4, [128, 1024], BF16)
    xr = k.rings("xr", 2, [128, 1024], F32)
    hT = A.alloc("hT", [128, 8, 512], BF16)
    ext = A.alloc("ext", [128, 12, 515], BF16)
    qkT = A.alloc("qkT", [128, 8, 512], BF16)
    vT = A.alloc("vT", [128, 4, 512], BF16)
    S = A.alloc("S", [128, 4, 128], F32)
    Sb = A.alloc("Sb", [128, 4, 128], BF16)
    for nm, n in (("e32", 2), ("c32", 2), ("l32", 2), ("ez", 1), ("zn", 1)):
        G[nm] = k.rings(nm, n, [128, 512], F32)
    G["sq"] = k.rings("sq", 2, [128, 512], BF16)
    G["og"] = k.rings("og", 2, [128, 512], BF16)
    G["sc"] = k.rings("sc", 3, [128, 40], F32)
    G["jk"] = k.rings("jk", 1, [128, 128], F32)
    for nm, n in (("tg", 2), ("dec", 2), ("egr", 2), ("ub", 2)):
        G[nm] = k.rings(nm, n, [128, 4, 128], F32)
    for nm, n in (("dS", 2), ("dI", 2), ("kg", 4), ("kdec", 2), ("vtok", 4), ("R", 4), ("RT", 4), ("P", 4), ("Ec", 4),
                  ("QKT", 2), ("qg", 2), ("wT", 2), ("e", 2)):
        G[nm] = k.rings(nm, n, [128, 4, 128], BF16)

    mixT_a = C["mixT_a"]
    k.memset("pool", ext[:, :, :], 0.0, [ext])
    k.memset("pool", S[:, :, :], 0.0, [S])
    k.memset("dve", Sb[:, :, :], 0.0, [Sb])

    def feature_chunks(pchs, chs, hT_ap, hdeps, n, ext_dst, conv_rhs, qk_out, v_out, outTs, conv_out=None):
        for ch in pchs:
            psA = k.ps.next()
            for kc in range(8):
                k.mm(psA[:, 0:n], w_a[:, kc, ch * 128:(ch + 1) * 128], hT_ap(kc), hdeps + [C["wa_g"][ch // 4]], [psA], start=(kc == 0), stop=(kc == 7))
            ext_dst(ch, psA)
        def stage1(ch):
            psB = k.ps.next()
            for i in range(4):
                k.mm(psB[:, 0:n] if conv_out is None else conv_out(psB), diag[:, ch, i, :], conv_rhs(ch, i), [diag] + outTs["ext"], [psB], start=(i == 0), stop=(i == 3))
            return silu_from_psum(k, G, psB[:, 0:n], psB, n)

        def stage2(ch, c32):
            if ch < 8:
                l2norm_chunk(k, C, G, c32, n, qk_out(ch), outTs["qk"], qscale=(ch < 4))
            else:
                k.cp("dve", v_out(ch - 8), c32[:, 0:n], [c32], [outTs["v"]])

        pend = None
        for ch in chs:
            c32 = stage1(ch)
            if pend is not None:
                stage2(*pend)
            pend = (ch, c32)
        if pend is not None:
            stage2(*pend)

    for st in range(NTOK // 512):
        main = st >= NPRE // 512
        hbs = []
        for tt4 in range(4):
            tok0 = st * 512 + tt4 * 128
            xt = xr.next()
            k.dma("sp", xt[:, :], IO["xp"][tok0:tok0 + 128, :], W=[xt])
            hbs.append(norm_stats(k, xt, 128, nmb, C))
        for tt4 in range(4):
            transpose_to(k, hbs[tt4], 128, hT, tt4 * 128, C)
        chs = list(range(12)) if main else list(range(4, 12))
        pchs = list(range(12)) if st >= NPRE // 512 - 1 else chs

        def ext_dst(ch, psA):
            k.cp("dve", ext[:, ch, 3:515], psA[:, :], [psA], [ext])

        feature_chunks(pchs, chs, lambda kc: hT[:, kc, :], [hT], 512, ext_dst,
                       lambda ch, i: ext[:, ch, i:i + 512],
                       lambda ch: qkT[:, ch, :], lambda j: vT[:, j, :], {"ext": [ext], "qk": qkT, "v": vT})
        if st == NTOK // 512 - 1:
            for j in range(3):
                psT3 = k.ps.next()
                for kc in range(8):
                    k.mm(psT3[0:3, :], hT[:, kc, 509:512], w_a[:, kc, j * 512:(j + 1) * 512], [hT, C["wa_g"][j]], [psT3], start=(kc == 0), stop=(kc == 7))
                pre3 = G["l32"].next()
                k.cp("dve", pre3[0:3, :], psT3[0:3, :], [psT3], [pre3])
                k.dma("sp", IO["conv_p"][:, j * 512:(j + 1) * 512], pre3[0:3, :], R=[pre3])
        halo = G.setdefault("halo", A.alloc("halo", [128, 12, 3], BF16))
        k.cp("pool", halo[:, :, :], ext[:, :, 512:515], [ext], [halo])
        gens = []
        for tt4 in range(4):
            cs = slice(tt4 * 128, (tt4 + 1) * 128)
            gcol = st * 512 + tt4 * 128 - NPRE

            def mix_out(og, gcol=gcol):
                psm = k.ps.next()
                pbm = bank_bf(psm)
                for h in range(4):
                    k.tr(pbm[:, h * 128:(h + 1) * 128], og[:, h * 128:(h + 1) * 128], C["ident"][:, :], [og, C["ident"]], [psm])
                k.cp("act", mixT_a[:, :, gcol:gcol + 128], pbm[:, 0:512].rearrange("p (h c) -> p h c", h=4), [psm], [mixT_a])

            gens.append(gdn_tile(k, C, G, hT[:, :, cs], qkT[:, 0:4, cs], qkT[:, 4:8, cs], vT[:, :, cs], S, Sb, [hT, qkT, vT], main, 7, mix_out))
        k.free_ring(C["hb"], xr, G["e32"], G["c32"], G["l32"], G["sq"])
        extra = {}
        for nm, n, shp, dt in EXTRA_RINGS:
            extra[nm] = [A.alloc("x_%s%d" % (nm, i), shp, dt) for i in range(n)]
            G[nm].t.extend(extra[nm])
        run_interleaved(gens, offs=3, maxact=3)
        for nm, tl in extra.items():
            for t in tl:
                G[nm].t.remove(t)
            G[nm].i = 0
            A.free(*tl)
        C["hb"] = k.rings("hb", 4, [128, 1024], BF16)
        xr = k.rings("xr", 2, [128, 1024], F32)
        for nm, n_ in (("e32", 2), ("c32", 2), ("l32", 2)):
            G[nm] = k.rings(nm, n_, [128, 512], F32)
        G["sq"] = k.rings("sq", 2, [128, 512], BF16)
        k.cp("pool", ext[:, :, 0:3], halo[:, :, :], [halo], [ext])
    k.dma("sp", IO["rec_p"].rearrange("h d e -> d h e"), S[:, :, :], R=[S])
    A.free(hT, ext, qkT, vT, G["halo"])

    xt = xr.next()
    k.dma("sp", xt[0:NS, :], IO["xs"][:, :], W=[xt])
    hTs = A.alloc("hTs", [128, 8, NS], BF16)
    norm_transpose(k, xt, NS, nmb, hTs, 0, C)
    exts = A.alloc("exts", [128, 12, 4, 11], BF16)
    sct = A.alloc("sct", [12, 1536], F32)
    k.dma("sp", sct[0:12, :], IO["sconv"].rearrange("s i c -> (s i) c"), W=[sct])
    psh = k.ps.next()
    for ch in range(12):
        k.tr(psh[:, ch * 12:(ch + 1) * 12], sct[0:12, ch * 128:(ch + 1) * 128], C["identf"][0:12, 0:12], [sct, C["identf"]], [psh])
    k.cp("dve", exts[:, :, :, 0:3], psh[:, 0:144].rearrange("p (c s i) -> p c s i", c=12, s=4), [psh], [exts])
    qks = A.alloc("qks", [128, 8, NS], BF16)
    vs = A.alloc("vs", [128, 4, NS], BF16)

    def ext_dst_s(ch, psA):
        k.cp("act", exts[:, ch, :, 3:11], psA[:, 0:NS].rearrange("p (s t) -> p s t", s=4), [psA], [exts])

    feature_chunks(list(range(12)), list(range(12)), lambda kc: hTs[:, kc, :], [hTs], NS, ext_dst_s,
                   lambda ch, i: exts[:, ch, :, i:i + 8],
                   lambda ch: qks[:, ch, :], lambda j: vs[:, j, :], {"ext": [exts], "qk": qks, "v": vs},
                   conv_out=lambda psB: psB[:, 0:NS].rearrange("p (s t) -> p s t", s=4))
    pres = A.alloc("pres", [NS, 1536], F32)
    for j in range(3):
        psT3 = k.ps.next()
        for kc in range(8):
            k.mm(psT3[0:NS, :], hTs[:, kc, :], w_a[:, kc, j * 512:(j + 1) * 512], [hTs, C["wa_g"][j]], [psT3], start=(kc == 0), stop=(kc == 7))
        k.cp("dve", pres[0:NS, j * 512:(j + 1) * 512], psT3[0:NS, :], [psT3], [pres])
    for s in range(4):
        k.dma("sp", IO["conv_s"][s, :, :], pres[8 * s + 5:8 * s + 8, :], R=[pres])
    hpad = k.rings("hpad", 2, [128, 8, 128], BF16)
    qkpad = k.rings("qkpad", 2, [128, 8, 128], BF16)
    vpad = k.rings("vpad", 2, [128, 4, 128], BF16)
    Ss = k.rings("Ss", 2, [128, 4, 128], F32)
    Sbs = k.rings("Sbs", 2, [128, 4, 128], BF16)
    for r in (hpad, qkpad, vpad):
        for t in r.t:
            k.memset(k.alt(), t[:, :, :], 0.0, [t])
    sgens = []
    for s in range(4):
        hp, qp, vp, S_s, Sb_s = hpad.next(), qkpad.next(), vpad.next(), Ss.next(), Sbs.next()
        k.cp("pool", hp[:, :, 0:8], hTs[:, :, 8 * s:8 * s + 8], [hTs], [hp])
        k.cp("pool", qp[:, :, 0:8], qks[:, :, 8 * s:8 * s + 8], [qks], [qp])
        k.cp("pool", vp[:, :, 0:8], vs[:, :, 8 * s:8 * s + 8], [vs], [vp])
        k.dma("sp", S_s[:, :, :], IO["srec"][s].rearrange("h d e -> d h e"), W=[S_s])
        k.cp("act", Sb_s[:, :, :], S_s[:, :, :], [S_s], [Sb_s])

        def mix_out_s(og, s=s):
            psm = k.ps.next()
            pbm = bank_bf(psm)
            for h in range(4):
                k.tr(pbm[:, h * 8:(h + 1) * 8], og[0:8, h * 128:(h + 1) * 128], C["ident"][0:8, 0:8], [og, C["ident"]], [psm])
            k.cp("act", mixT_a[:, :, NMAIN + 8 * s:NMAIN + 8 * s + 8], pbm[:, 0:32].rearrange("p (h c) -> p h c", h=4), [psm], [mixT_a])

        def seq_gen(s=s, hp=hp, qp=qp, vp=vp, S_s=S_s, Sb_s=Sb_s, mix_out_s=mix_out_s):
            yield from gdn_tile(k, C, G, hp[:, :, :], qp[:, 0:4, :], qp[:, 4:8, :], vp[:, :, :], S_s, Sb_s, [hp, qp, vp], True, 3, mix_out_s, vm=C["vmask"])
            k.dma("sp", IO["rec_s"][s].rearrange("h d e -> d h e"), S_s[:, :, :], R=[S_s])

        sgens.append(seq_gen())
        if s % 2 == 1:
            run_interleaved(sgens)
            sgens = []

    merge_free(A, w_a, list(wa_units.values()))
    A.free(nmb, wconv, diag, dtb, negA, noa, lnq, S, Sb, hTs, exts, sct, qks, vs, pres)
    k.free_ring(C["sm"], C["hb"], xr, hpad, qkpad, vpad, Ss, Sbs)
    for nm, r in G.items():
        if isinstance(r, Ring):
            k.free_ring(r)
    G.clear()


def merge_free(A, parent, children):
    for ch in children:
        for e2, seq in ch.rde.items():
            if parent.rde.get(e2, 0) < seq:
                parent.rde[e2] = seq
        parent.rdd.extend(ch.rdd)
        if ch.lw is not None:
            if ch.lw[0] == "e":
                if parent.rde.get(ch.lw[1], 0) < ch.lw[2]:
                    parent.rde[ch.lw[1]] = ch.lw[2]
            else:
                parent.rdd.append(ch.lw[1])
    A.free(parent)


def setup_consts(k, C, IO):
    A = k.A
    cf = IO["cf32"]
    identf = A.alloc("identf", [128, 128], F32)
    mIU = A.alloc("mIU", [128, 128], F32)
    onesf = A.alloc("onesf", [128, 128], F32)
    mSU4 = A.alloc("mSU4", [128, 4, 128], F32)
    mIU4 = A.alloc("mIU4", [128, 4, 128], F32)
    id4f = A.alloc("id4f", [128, 4, 128], F32)
    k.dma("sp", identf[:, :], cf[:, 0:128], W=[identf])
    k.dma("sp", id4f[:, :, :], cf[:, 0:512].rearrange("p (h c) -> p h c", h=4), W=[id4f])
    k.dma("sp", mSU4[:, :, :], cf[:, 512:1024].rearrange("p (h c) -> p h c", h=4), W=[mSU4])
    k.dma("sp", mIU4[:, :, :], cf[:, 1024:1536].rearrange("p (h c) -> p h c", h=4), W=[mIU4])
    k.dma("sp", mIU[:, :], cf[:, 1024:1152], W=[mIU])
    k.dma("sp", onesf[:, :], cf[:, 1536:1664], W=[onesf])
    ident = A.alloc("ident", [128, 128], BF16)
    onesb = A.alloc("onesb", [128, 128], BF16)
    id4b = A.alloc("id4b", [128, 4, 128], BF16)
    k.cp("dve", ident[:, :], identf[:, :], [identf], [ident])
    k.cp("dve", onesb[:, :], onesf[:, :], [onesf], [onesb])
    k.cp("dve", id4b[:, :, :], id4f[:, :, :], [id4f], [id4b])
    vmask = A.alloc("vmask", [128, 1], F32)
    edge = A.alloc("edge", [128, 1], F32)
    k.dma("sp", vmask[:, :], IO["vmask"][:, :], W=[vmask])
    k.dma("sp", edge[:, :], IO["edge8"][:, :], W=[edge])
    C.update(identf=identf, mIU=mIU, onesf=onesf, mSU4=mSU4, mIU4=mIU4, ident=ident, onesb=onesb, id4b=id4b,
             vmask=vmask, edge=edge)
    A.free(id4f)


def stage_K(k, C, IO):
    A = k.A
    w_b = A.alloc("w_b", [128, 8, 1536], BF16)
    for kc in range(8):
        k.dma("pool", w_b[:, kc, :], IO["w_in"][kc * 128:(kc + 1) * 128, 2056:3592], W=[w_b])
    nmb = A.alloc("nmb2", [128, 1024], F32)
    k.dma("sp", nmb[:, :], IO["norm_mix"].partition_broadcast(128), W=[nmb])
    C["sm"] = k.rings("smk", 6, [128, 8], F32)
    C["hb"] = k.rings("hbk", 5, [128, 1024], BF16)
    xr = k.rings("xrk", 4, [128, 1024], F32)
    hTr = k.rings("hTk", 2, [128, 8, 512], BF16)
    kT_b = A.alloc("kT_b", [128, 4, NTOK], BF16)
    vT_b = A.alloc("vT_b", [128, 4, NTOK], BF16)
    qT_b = A.alloc("qT_b", [128, 4, NMAIN], BF16)
    kv = [T("kv%d" % st, None) for st in range(NTOK // 512)]
    for t in kv:
        for par in (kT_b, vT_b, qT_b):
            for e2, seq in par.rde.items():
                if t.rde.get(e2, 0) < seq:
                    t.rde[e2] = seq
            t.rdd.extend(par.rdd)
    C.update(kT_b=kT_b, vT_b=vT_b, qT_b=qT_b, kv=kv)
    ost = k.rings("ost", 2, [128, 512], F32)
    def k_stats(st):
        hbs = []
        for tt4 in range(4):
            tok0 = st * 512 + tt4 * 128
            xt = xr.next()
            k.dma("sp", xt[:, :], IO["xp"][tok0:tok0 + 128, :], W=[xt])
            hbs.append(norm_stats(k, xt, 128, nmb, C))
        return hbs

    def k_tr(hbs):
        hT = hTr.next()
        for tt4 in range(4):
            transpose_to(k, hbs[tt4], 128, hT, tt4 * 128, C)
        return hT

    hT_next = k_tr(k_stats(0))
    for st in range(NTOK // 512):
        main = st >= NPRE // 512
        hT = hT_next
        hbs_next = k_stats(st + 1) if st + 1 < NTOK // 512 else None
        for ch in (range(12) if main else range(4, 12)):
            psA = k.ps.next()
            for kc in range(8):
                k.mm(psA[:, :], w_b[:, kc, ch * 128:(ch + 1) * 128], hT[:, kc, :], [hT, w_b], [psA], start=(kc == 0), stop=(kc == 7))
            if ch < 4:
                dst = qT_b[:, ch, (st * 512 - NPRE):(st * 512 - NPRE) + 512]
            elif ch < 8:
                dst = kT_b[:, ch - 4, st * 512:(st + 1) * 512]
            else:
                dst = vT_b[:, ch - 8, st * 512:(st + 1) * 512]
            k.cp(k.alt(("act", "dve")), dst, psA[:, :], [psA], [kv[st]])
        if hbs_next is not None:
            hT_next = k_tr(hbs_next)
        if main and KSTOP >= 2:
            for tt4 in range(4):
                tok0 = st * 512 + tt4 * 128
                for src, dstname in ((kT_b, "wk_p"), (vT_b, "wv_p")):
                    pst = k.ps.next()
                    pbt = bank_bf(pst)
                    for c in range(4):
                        k.tr(pbt[:, c * 128:(c + 1) * 128], src[:, c, tok0:tok0 + 128], C["ident"][:, :], [kv[st], C["ident"]], [pst])
                    o = ost.next()
                    k.cp(k.alt(("act", "dve")), o[:, :], pbt[:, 0:512], [pst], [o])
                    k.dma("sp", IO[dstname][tok0 - NPRE:tok0 - NPRE + 128, :], o[:, :], R=[o])
    if KSTOP < 3:
        return
    xt = xr.next()
    k.dma("sp", xt[0:NS, :], IO["xs"][:, :], W=[xt])
    hTs = A.alloc("hTs2", [128, 8, NS], BF16)
    norm_transpose(k, xt, NS, nmb, hTs, 0, C)
    qTs = A.alloc("qTs", [128, 4, NS], BF16)
    kTn = A.alloc("kTn", [128, 4, NS], BF16)
    for ch in range(8):
        psA = k.ps.next()
        for kc in range(8):
            k.mm(psA[:, 0:NS], w_b[:, kc, ch * 128:(ch + 1) * 128], hTs[:, kc, :], [hTs, w_b], [psA], start=(kc == 0), stop=(kc == 7))
        dstT = qTs if ch < 4 else kTn
        k.cp("act", dstT[:, ch % 4, :], psA[:, 0:NS], [psA], [dstT])
    if KSTOP < 4:
        return
    vn_aug = A.alloc("vn_aug", [NS, 8, 66], BF16)
    k.memset("pool", vn_aug[:, :, :], 1.0, [vn_aug])
    for j, dstname in ((1, "wk_s"), (2, "wv_s")):
        psA = k.ps.next()
        for kc in range(8):
            k.mm(psA[0:NS, :], hTs[:, kc, :], w_b[:, kc, j * 512:(j + 1) * 512], [hTs, w_b], [psA], start=(kc == 0), stop=(kc == 7))
        o = ost.next()
        k.cp("dve", o[0:NS, :], psA[0:NS, :], [psA], [o])
        for s in range(4):
            if "D" not in KOFF:
                k.dma("sp", IO[dstname][s, 2040:2048, :], o[8 * s:8 * s + 8, :], R=[o])
        if j == 2 and "A" not in KOFF:
            k.cp("act", vn_aug[:, :, 0:64], psA[0:NS, :].rearrange("p (h e) -> p h e", h=8), [psA], [vn_aug])
    if KSTOP < 5:
        return
    Qbd = A.alloc("Qbd", [128, 4, 4, 48], BF16)
    k.memset("pool", Qbd[:, :, :, :], 0.0, [Qbd])
    for c in range(4):
        for br in range(3):
            k.cp(k.alt(), Qbd[0:64, c, :, br * 8:br * 8 + 8], qTs[0:64, c, :].rearrange("p (s t) -> p s t", s=4), [qTs], [Qbd])
            k.cp(k.alt(), Qbd[64:128, c, :, 24 + br * 8:32 + br * 8], qTs[64:128, c, :].rearrange("p (s t) -> p s t", s=4), [qTs], [Qbd])
    C.update(kTn=kTn, vn_aug=vn_aug, Qbd=Qbd)
    A.free(w_b, nmb, hTs, qTs)
    k.free_ring(C["sm"], C["hb"], xr, hTr, ost)


def finalize_attn(k, C, F, acc_ap, accT, n, dst):
    for c0 in range(0, n, 512):
        w = min(512, n - c0)
        sq = F["sq"].next()
        k.act(sq[0:65, 0:w], acc_ap[0:65, c0:c0 + w], AF.Square, [accT], [sq])
        psF = k.ps.next()
        k.mm(psF[0:64, 0:w], C["gmb"][0:65, 0:64], sq[0:65, 0:w], [C["gmb"], sq], [psF])
        l32 = F["l32"].next()
        k.act(l32[0:64, 0:w], psF[0:64, 0:w], AF.Ln, [psF], [l32])
        k.act(l32[0:64, 0:w], l32[0:64, 0:w], AF.Exp, [l32], [l32], scale=-0.5)
        dap, dT = dst(c0, w)
        k.stt("dve", dap, acc_ap[0:64, c0:c0 + w], C["nob"][0:64, 0:1], l32[0:64, 0:w], ALU.mult, ALU.mult, [accT, C["nob"], l32], [dT])


def stage_B(k, C, IO):
    A = k.A
    kT_b, vT_b, qT_b, kv = C["kT_b"], C["vT_b"], C["qT_b"], C["kv"]
    identb = C["ident"]
    pb = A.alloc("pbias", [128, 8, 3, 256], BF16)
    k.dma("sp", pb[:, :, :, :], IO["pbias"][:, :, :, :], W=[pb])
    pe_ = A.alloc("pedge", [128, 8, 3, 128], BF16)
    for h in range(8):
        k.ts(k.alt(), pe_[:, h, :, :], pb[:, h, :, 128:256], C["edge"][:, 0:1], ALU.add, [pb, C["edge"]], [pe_])
    gmb = A.alloc("gmb", [65, 64], BF16)
    k.dma("sp", gmb[:, :], IO["gmb"][:, :], W=[gmb])
    nob = A.alloc("nob", [64, 1], F32)
    k.dma("sp", nob[:, :], IO["nob"].rearrange("(p o) -> p o", o=1), W=[nob])
    C.update(gmb=gmb, nob=nob)
    mixT_b = C["mixT_b"] = A.alloc("mixT_b", [128, 4, NMAIN + NS], BF16)
    F = {"sq": k.rings("fsq", 2, [65, 512], BF16), "l32": k.rings("fl32", 2, [64, 512], F32)}
    Vblk = A.alloc("Vblk", [128, 69, 2, 66], BF16)
    k.memset("pool", Vblk[:, :, :, :], 1.0, [Vblk])
    accr = k.rings("acc", 1, [65, NMAIN], F32)
    PTr = k.rings("PT", 6, [128, 256], BF16)
    otmp = k.rings("otmp", 2, [64, 512], BF16)
    blocks = []
    for br, d in enumerate(DILS):
        for r in range(d):
            for n in range(16 // d - 1, 32 // d):
                blocks.append((br, r, n))
    bidx = {b: i for i, b in enumerate(blocks)}
    assert len(blocks) == 69
    for c in range(4):
        for g0 in range(0, 69, 4):
            grp = blocks[g0:g0 + 4]
            psv = k.ps.next()
            pbv = bank_bf(psv)
            for j, (br, r, n) in enumerate(grp):
                d = DILS[br]
                k.tr(pbv[:, j * 128:(j + 1) * 128], vT_b[:, c, sl(r + d * 128 * n, 128, d)], identb[:, :], kv + [identb], [psv])
            ng = len(grp)
            k.cp(k.alt(("act", "dve")), Vblk[:, g0:g0 + ng, :, 0:64],
                 pbv[:, 0:ng * 128].rearrange("p (g h e) -> p g h e", g=ng, h=2), [psv], [Vblk])
        for hh in range(2):
            h = 2 * c + hh
            po = 64 * hh
            acc = accr.next()
            hb_list = []
            for br, d in enumerate(DILS):
                nq0, nq1 = 16 // d, 32 // d
                for r in range(d):
                    for n in range(nq0 - 1, nq1):
                        hb_list.append((br, d, r, n, n == nq0 - 1, n == nq1 - 1, nq0))
            PTs = {}

            def emit_S(i, h=h, c=c, po=po):
                br, d, r, n, first, last, nq0 = hb_list[i]
                ks = sl(r + d * 128 * n, 128, d)
                if first:
                    q0, N, bias, bT = r + d * 128 * nq0 - NPRE, 128, pe_[:, h, br, :], pe_
                elif last:
                    q0, N, bias, bT = r + d * 128 * n - NPRE, 128, pb[:, h, br, 0:128], pb
                else:
                    q0, N, bias, bT = r + d * 128 * n - NPRE, 256, pb[:, h, br, :], pb
                psS = k.ps.next()
                k.mm(psS[:, 0:N], kT_b[po:po + 64, c, ks], qT_b[po:po + 64, c, sl(q0, N, d)], kv, [psS], start=True, stop=False)
                k.mm(psS[:, 0:N], identb[:, :], bias, [identb, bT], [psS], start=False, stop=True)
                PT = PTr.next()
                k.act(PT[:, 0:N], psS[:, 0:N], AF.Exp, [psS], [PT], scale=0.125)
                PTs[i] = PT

            def emit_PV(i, hh=hh, acc=acc):
                br, d, r, n, first, last, nq0 = hb_list[i]
                if first:
                    return
                PT, prevPT = PTs[i], PTs[i - 1]
                prev_first = hb_list[i - 1][4]
                psO = k.ps.next()
                pp = prevPT[:, 0:128] if prev_first else prevPT[:, 128:256]
                k.mm(psO[0:65, 0:128], Vblk[:, bidx[(br, r, n - 1)], hh, 0:65], pp, [Vblk, prevPT], [psO], start=True, stop=False)
                k.mm(psO[0:65, 0:128], Vblk[:, bidx[(br, r, n)], hh, 0:65], PT[:, 0:128], [Vblk, PT], [psO], start=False, stop=True)
                qc = r + d * 128 * n - NPRE
                qcols = sl(qc, 128, d)
                if br == 0:
                    k.cp("dve", acc[:, qcols], psO[0:65, 0:128], [psO], [acc])
                else:
                    k.tt("dve", acc[:, qcols], acc[:, qcols], psO[0:65, 0:128], ALU.add, [acc, psO], [acc])
                PTs.pop(i - 1, None)

            LOOK = 3
            for i in range(len(hb_list) + LOOK):
                if i < len(hb_list):
                    emit_S(i)
                if i - LOOK >= 0:
                    emit_PV(i - LOOK)
            if hh == 0:
                finalize_attn(k, C, F, acc, acc, NMAIN, lambda c0, w, c=c: (mixT_b[0:64, c, c0:c0 + w], mixT_b))
            else:
                def dst(c0, w, c=c):
                    o = otmp.next()
                    dst.last = (o, c0, w)
                    return o[0:64, 0:w], o
                for c0 in range(0, NMAIN, 512):
                    finalize_attn(k, C, F, acc[:, c0:c0 + 512], acc, 512, dst)
                    o, _, w = dst.last
                    k.dma("sp", mixT_b[64:128, c, c0:c0 + 512], o[0:64, 0:512], R=[o], W=[mixT_b])
    merge_free(A, kT_b, kv)
    A.free(vT_b, qT_b, pb, pe_, Vblk)
    k.free_ring(accr, PTr)

    k.set_ring(7)
    psOs = k.banks[7]
    sbc = A.alloc("sbc", [128, 16, 192], BF16)
    sbn = A.alloc("sbn", [NS, 4, 192], BF16)
    k.dma("sp", sbc[:, :, :], IO["sbias_c"][:, :, :], W=[sbc])
    k.dma("sp", sbn[:, :, :], IO["sbias_n"][:, :, :], W=[sbn])
    kTn, vn_aug, Qbd = C["kTn"], C["vn_aug"], C["Qbd"]
    kc32r = k.rings("kc32", 2, [128, 512], F32)
    vc32r = k.rings("vc32", 2, [128, 512], F32)
    kcbr = k.rings("kcb", 2, [128, 512], BF16)
    vaugr = k.rings("vaug", 2, [128, 8, 66], BF16)
    kTsr = k.rings("kTs", 2, [128, 4, 128], BF16)
    PTsr = k.rings("PTs", 2, [128, 192], BF16)
    tmpr = k.rings("ptmp", 2, [128, 8, 8], BF16)
    Pqr = [k.rings("Pq%d" % s, 2, [128, 8, NS], BF16) for s in range(4)]
    for t in vaugr.t:
        k.memset("pool", t[:, :, :], 1.0, [t])
    for s in range(4):
        for t in Pqr[s].t:
            k.memset(k.alt(), t[:, :, :], 0.0, [t])
    for s in range(4):
        for kt in range(17):
            if kt < 16:
                kc32, vc32, kcb, vaug, kTs = kc32r.next(), vc32r.next(), kcbr.next(), vaugr.next(), kTsr.next()
                k.dma("sp", kc32[:, :], IO["ck"][s, 128 * kt:128 * kt + 128, :], W=[kc32])
                k.dma("sp", vc32[:, :], IO["cv"][s, 128 * kt:128 * kt + 128, :], W=[vc32])
                for src, nm in ((kc32, "wk_s"), (vc32, "wv_s")):
                    if kt == 0:
                        k.dma("sp", IO[nm][s, 0:120, :], src[8:128, :], R=[src])
                    else:
                        k.dma("sp", IO[nm][s, 128 * kt - 8:128 * kt + 120, :], src[:, :], R=[src])
                k.cp("act", kcb[:, :], kc32[:, :], [kc32], [kcb])
                k.cp("dve", vaug[:, :, 0:64], vc32[:, :].rearrange("p (h e) -> p h e", h=8), [vc32], [vaug])
                pst = k.ps.next()
                pbt = bank_bf(pst)
                for c in range(4):
                    k.tr(pbt[:, c * 128:(c + 1) * 128], kcb[:, c * 128:(c + 1) * 128], identb[:, :], [kcb, identb], [pst])
                k.cp("act", kTs[:, :, :], pbt[:, 0:512].rearrange("p (c t) -> p c t", c=4), [pst], [kTs])
                np_ = 128
                lhs_k = lambda c, kTs=kTs: kTs[:, c, :]
                kdep = kTs
                bias = sbc[:, kt, :]
                bT = sbc
                vsrc = vaug
                idb = identb[:, :]
            else:
                np_ = NS
                lhs_k = lambda c: kTn[:, c, :]
                kdep = kTn
                bias = sbn[0:NS, s, :]
                bT = sbn
                vsrc = vn_aug
                idb = identb[0:NS, 0:NS]
            psS = k.ps.next()
            k.mm(psS[0:np_, 0:192], idb, bias, [identb, bT], [psS], start=True, stop=False)
            for c in range(4):
                k.mm(psS[0:np_, c * 48:(c + 1) * 48], lhs_k(c), Qbd[:, c, s, :], [kdep, Qbd], [psS], start=False, stop=(c == 3))
            PTs = PTsr.next()
            k.act(PTs[0:np_, :], psS[0:np_, 0:192], AF.Exp, [psS], [PTs], scale=0.125)
            Pq = Pqr[s].next()
            tmp = tmpr.next()
            P4 = PTs[0:np_, :].rearrange("p (h b t) -> p h b t", h=8, b=3)
            k.tt("dve", tmp[0:np_, :, :], P4[:, :, 0, :], P4[:, :, 1, :], ALU.add, [PTs], [tmp])
            k.tt("dve", Pq[0:np_, :, 8 * s:8 * s + 8], tmp[0:np_, :, :], P4[:, :, 2, :], ALU.add, [PTs, tmp], [Pq])
            for h in range(8):
                k.mm(psOs[0:65, h * NS:(h + 1) * NS], vsrc[0:np_, h, 0:65], Pq[0:np_, h, :], [vsrc, Pq], [psOs],
                     start=(s == 0 and kt == 0 and h == 0), stop=(s == 3 and kt == 16), skip=True)
    accs = A.alloc("accs", [65, 8, NS], F32)
    k.cp("dve", accs[:, :, :], psOs[0:65, 0:8 * NS].rearrange("p (h t) -> p h t", h=8), [psOs], [accs])
    for h in range(8):
        c, hh = h // 2, h % 2
        if hh == 0:
            finalize_attn(k, C, F, accs[:, h, :], accs, NS, lambda c0, w, c=c: (mixT_b[0:64, c, NMAIN:NMAIN + NS], mixT_b))
        else:
            o = otmp.next()
            finalize_attn(k, C, F, accs[:, h, :], accs, NS, lambda c0, w, o=o: (o[0:64, 0:NS], o))
            k.dma("sp", mixT_b[64:128, c, NMAIN:NMAIN + NS], o[0:64, 0:NS], R=[o], W=[mixT_b])
    k.set_ring(8)
    A.free(sbc, sbn, kTn, vn_aug, Qbd, accs, gmb, nob)
    k.free_ring(kc32r, vc32r, kcbr, vaugr, kTsr, PTsr, tmpr, otmp, F["sq"], F["l32"], *Pqr)


def stage_C(k, C, IO):
    A = k.A
    k.set_ring(4)
    accb = k.banks[4:8]
    mixT_a, mixT_b = C["mixT_a"], C["mixT_b"]
    wo = A.alloc("wo", [128, 8, 1024], BF16)
    for kc in range(8):
        k.dma("pool", wo[:, kc, :], IO["w_out"][kc * 128:(kc + 1) * 128, :], W=[wo])
    WB = 256
    wgb = [A.alloc("wg%d" % i, [128, 8, WB], BF16) for i in range(DFF // WB)]
    wub = [A.alloc("wu%d" % i, [128, 8, WB], BF16) for i in range(DFF // WB)]
    for i in range(DFF // WB):
        k.dma("pool", wgb[i][:, :, :], IO["w_gate"][:, i * WB:(i + 1) * WB].rearrange("(c p) n -> p c n", p=128), W=[wgb[i]])
        k.dma("pool", wub[i][:, :, :], IO["w_up"][:, i * WB:(i + 1) * WB].rearrange("(c p) n -> p c n", p=128), W=[wub[i]])
    nfb = A.alloc("nfb", [128, 1024], F32)
    nfin = A.alloc("nfin", [128, 1024], F32)
    k.dma("sp", nfb[:, :], IO["norm_ffn"].partition_broadcast(128), W=[nfb])
    k.dma("sp", nfin[:, :], IO["norm_final"].partition_broadcast(128), W=[nfin])
    C["sm"] = k.rings("smc", 4, [128, 8], F32)
    C["hb"] = k.rings("hbc", 2, [128, 1024], BF16)
    x1r = k.rings("x1", 4, [128, 1024], F32)
    hfr = k.rings("hfT", 2, [128, 8, 256], BF16)
    wdr = k.rings("wd", 3, [128, 1024], BF16)
    e32r = k.rings("ce32", 3, [128, 256], F32)
    c32r = k.rings("cc32", 3, [128, 256], F32)
    u32r = k.rings("cu32", 3, [128, 256], F32)
    aTr = k.rings("aT", 3, [128, 256], BF16)
    units = [(IO["xp"], NPRE + u * 256, u * 256, 2, 128, IO["yp"], u * 256) for u in range(NMAIN // 256)]
    units.append((IO["xs"], 0, NMAIN, 1, NS, IO["ys"], 0))
    NJ = DFF // 128

    def pre(unit, st):
        (xsrc, xrow0, mcol0, ntile, p, ydst, yrow0) = unit
        st["hfT"] = hfr.next()
        st["x1s"] = []
        for t in range(ntile):
            x1 = x1r.next()
            st["x1s"].append(x1)
            k.dma("sp", x1[0:p, :], xsrc[xrow0 + t * 128:xrow0 + t * 128 + p, :], W=[x1])
            cols = slice(mcol0 + t * 128, mcol0 + t * 128 + p)
            for half in range(2):
                psX = k.ps.next()
                for kc in range(8):
                    lhsT = mixT_a[:, kc, cols] if kc < 4 else mixT_b[:, kc - 4, cols]
                    k.mm(psX[0:p, :], lhsT, wo[:, kc, half * 512:(half + 1) * 512], [mixT_a, mixT_b, wo], [psX], start=(kc == 0), stop=(kc == 7))
                k.tt("dve", x1[0:p, half * 512:(half + 1) * 512], x1[0:p, half * 512:(half + 1) * 512], psX[0:p, :], ALU.add, [x1, psX], [x1])
                yield
            norm_transpose(k, x1, p, nfb, st["hfT"], t * 128, C)
            yield

    def ffn(unit, st):
        (xsrc, xrow0, mcol0, ntile, p, ydst, yrow0) = unit
        ntok = (ntile - 1) * 128 + p
        hfT = st["hfT"]

        def issue_gu(j):
            psG = k.ps.next()
            for kc in range(8):
                k.mm(psG[:, 0:ntok], wgb[j // 2][:, kc, (j % 2) * 128:(j % 2) * 128 + 128], hfT[:, kc, 0:ntok], [wgb[j // 2], hfT], [psG], start=(kc == 0), stop=(kc == 7))
            psU = k.ps.next()
            for kc in range(8):
                k.mm(psU[:, 0:ntok], wub[j // 2][:, kc, (j % 2) * 128:(j % 2) * 128 + 128], hfT[:, kc, 0:ntok], [wub[j // 2], hfT], [psU], start=(kc == 0), stop=(kc == 7))
            return psG, psU

        pend = issue_gu(0)
        for j in range(NJ):
            psG, psU = pend
            wd = wdr.next()
            k.dma("pool", wd[:, :], IO["w_down"][j * 128:(j + 1) * 128, :], W=[wd])
            e32, c32, u32, aT = e32r.next(), c32r.next(), u32r.next(), aTr.next()
            k.act(e32[:, 0:ntok], psG[:, 0:ntok], AF.Exp, [psG], [e32], scale=-1.0)
            k.cp("act", u32[:, 0:ntok], psU[:, 0:ntok], [psU], [u32])
            k.act(e32[:, 0:ntok], e32[:, 0:ntok], AF.Ln, [e32], [e32], bias=1.0)
            k.act(e32[:, 0:ntok], e32[:, 0:ntok], AF.Exp, [e32], [e32], scale=-1.0)
            k.tt("dve", c32[:, 0:ntok], psG[:, 0:ntok], e32[:, 0:ntok], ALU.mult, [psG, e32], [c32])
            k.tt("dve", aT[:, 0:ntok], c32[:, 0:ntok], u32[:, 0:ntok], ALU.mult, [c32, u32], [aT])
            if j + 1 < NJ:
                pend = issue_gu(j + 1)
            for t in range(ntile):
                for half in range(2):
                    ab = accb[t * 2 + half]
                    k.mm(ab[0:p, :], aT[:, t * 128:t * 128 + p], wd[:, half * 512:(half + 1) * 512], [aT, wd], [ab],
                         start=(j == 0), stop=(j == NJ - 1))
            yield

    def post(unit, st):
        (xsrc, xrow0, mcol0, ntile, p, ydst, yrow0) = unit
        for t in range(ntile):
            x1 = st["x1s"][t]
            for half in range(2):
                ab = accb[t * 2 + half]
                k.tt("dve", x1[0:p, half * 512:(half + 1) * 512], x1[0:p, half * 512:(half + 1) * 512], ab[0:p, :], ALU.add, [x1, ab], [x1])
            sm = C["sm"].next()
            jk = C["hb"].next()
            k.act(jk[0:p, :], x1[0:p, :], AF.Square, [x1], [jk, sm], accum=sm[0:p, 0:1])
            k.act(sm[0:p, 1:2], sm[0:p, 0:1], AF.Ln, [sm], [sm], scale=1.0 / D, bias=EPS)
            k.act(sm[0:p, 2:3], sm[0:p, 1:2], AF.Exp, [sm], [sm], scale=-0.5)
            k.stt("dve", x1[0:p, :], x1[0:p, :], sm[0:p, 2:3], nfin[0:p, :], ALU.mult, ALU.mult, [x1, sm, nfin], [x1])
            k.dma("sp", ydst[yrow0 + t * 128:yrow0 + t * 128 + p, :], x1[0:p, :], R=[x1])

    states = [dict() for _ in units]
    for _ in pre(units[0], states[0]):
        pass
    for u, unit in enumerate(units):
        gens = [ffn(unit, states[u])]
        if u + 1 < len(units):
            gens.append(pre(units[u + 1], states[u + 1]))
        run_interleaved(gens, offs=1, maxact=2)
        post(unit, states[u])
    k.set_ring(8)


IN_SPECS = [
    ("xp", [NTOK, D], F32), ("xs", [NS, D], F32), ("sconv", [4, 3, 1536], F32), ("srec", [4, 4, 128, 128], F32),
    ("ck", [4, 2048, 512], F32), ("cv", [4, 2048, 512], F32),
    ("norm_mix", [D], F32), ("w_in", [D, 3592], F32), ("w_conv", [4, 1536], F32), ("a_log", [4], F32), ("dt_bias", [4], F32),
    ("noa", [128], F32), ("nob", [64], F32), ("w_out", [D, D], F32), ("norm_ffn", [D], F32),
    ("w_gate", [D, DFF], F32), ("w_up", [D, DFF], F32), ("w_down", [DFF, D], F32), ("norm_final", [D], F32),
    ("cf32", [128, 1664], F32), ("vmask", [128, 1], F32), ("edge8", [128, 1], F32),
    ("pbias", [128, 8, 3, 256], BF16), ("sbias_c", [128, 16, 192], BF16), ("sbias_n", [NS, 4, 192], BF16), ("gmb", [65, 64], BF16),
]
OUT_SPECS = [
    ("yp", [NMAIN, D]), ("ys", [NS, D]), ("conv_p", [3, 1536]), ("rec_p", [4, 128, 128]), ("wk_p", [NMAIN, 512]), ("wv_p", [NMAIN, 512]),
    ("conv_s", [4, 3, 1536]), ("rec_s", [4, 4, 128, 128]), ("wk_s", [4, 2048, 512]), ("wv_s", [4, 2048, 512]),
]


def build_program(stages="GKBC", dbg=False):
    nc = bass.Bass("TRN2", target_bir_lowering=False)
    IO = {}
    if dbg:
        IO["dbg_ma"] = nc.dram_tensor("dbg_ma", [128, 4, NMAIN + NS], F32, kind="ExternalOutput").ap()
        IO["dbg_mb"] = nc.dram_tensor("dbg_mb", [128, 4, NMAIN + NS], F32, kind="ExternalOutput").ap()
    for name, shape, dt in IN_SPECS:
        IO[name] = nc.dram_tensor(name, shape, dt, kind="ExternalInput").ap()
    for name, shape in OUT_SPECS:
        IO[name] = nc.dram_tensor(name, shape, F32, kind="ExternalOutput").ap()
    k = KB(nc)
    C = {}
    setup_consts(k, C, IO)
    C["mixT_a"] = k.A.alloc("mixT_a", [128, 4, NMAIN + NS], BF16)
    if "G" in stages:
        stage_G(k, C, IO)
    if "K" in stages:
        stage_K(k, C, IO)
    if "B" in stages:
        stage_B(k, C, IO)
    if dbg:
        k.dma("pool", IO["dbg_ma"][:, :, :], C["mixT_a"][:, :, :], R=[C["mixT_a"]])
        k.dma("pool", IO["dbg_mb"][:, :, :], C["mixT_b"][:, :, :], R=[C["mixT_b"]])
    if "C" in stages:
        stage_C(k, C, IO)
    k.S.finish()
    k.S.emit()
    return nc, k


def host_tables():
    p = np.arange(128)
    ident = np.eye(128, dtype=np.float32)
    mSU = (p[:, None] < p[None, :]).astype(np.float32)
    mIU = (p[:, None] <= p[None, :]).astype(np.float32)
    ones = np.ones((128, 128), np.float32)
    cf = np.concatenate([np.tile(ident, (1, 4)), np.tile(mSU, (1, 4)), np.tile(mIU, (1, 4)), ones], axis=1)
    slopes = 2.0 ** (-np.arange(1, 9, dtype=np.float64))
    ki = p[:, None].astype(np.float64)
    qi = p[None, :].astype(np.float64)
    pbias = np.zeros((128, 8, 3, 256), np.float64)
    for h in range(8):
        for br, d in enumerate(DILS):
            j = qi - ki
            pbias[:, h, br, 0:128] = np.where(j >= 0, -slopes[h] * d * j * 8.0, NEGB)
            j = qi + 128 - ki
            pbias[:, h, br, 128:256] = np.where(j <= 128, -slopes[h] * d * j * 8.0, NEGB)
    sbc = np.full((128, 16, 192), NEGB, np.float64)
    sbn = np.full((NS, 4, 192), NEGB, np.float64)
    for h in range(8):
        for br, d in enumerate(DILS):
            for t in range(8):
                col = h * 24 + br * 8 + t
                kp = np.arange(2048)
                dist = 2048 + t - kp
                ok = (dist % d == 0) & (dist // d <= 128)
                vals = np.where(ok, -slopes[h] * dist * 8.0, NEGB)
                sbc[:, :, col] = vals.reshape(16, 128).T
                for s in range(4):
                    for t2 in range(8):
                        dist2 = t - t2
                        if dist2 >= 0 and dist2 % d == 0:
                            sbn[8 * s + t2, s, col] = -slopes[h] * dist2 * 8.0
    gm = np.full((65, 64), 1.0 / 64, np.float64)
    gm[64, :] = EPS
    vmask = (p < 8).astype(np.float32).reshape(128, 1)
    bf = ml_dtypes.bfloat16
    return dict(cf32=cf, vmask=vmask, pbias=pbias.astype(np.float32).astype(bf), sbias_c=sbc.astype(np.float32).astype(bf),
                sbias_n=sbn.astype(np.float32).astype(bf), gmb=gm.astype(np.float32).astype(bf))


_CACHE = {}


def make_in_maps(x_prompt, x_sample, state_conv, state_rec, cache_win_k, cache_win_v, norm_mix, w_in, w_conv, a_log, dt_bias,
                 norm_out_a, norm_out_b, w_out, norm_ffn, w_gate, w_up, w_down, norm_final):
    f = lambda a: np.ascontiguousarray(np.asarray(a, dtype=np.float32))
    tabs = host_tables()
    shared = dict(norm_mix=f(norm_mix[0]), w_in=f(w_in[0]), w_conv=f(w_conv[0]), a_log=f(a_log[0]), dt_bias=f(dt_bias[0]),
                  noa=f(norm_out_a[0]), nob=f(norm_out_b[0]), w_out=f(w_out[0]), norm_ffn=f(norm_ffn[0]),
                  w_gate=f(w_gate[0]), w_up=f(w_up[0]), w_down=f(w_down[0]), norm_final=f(norm_final), **tabs)
    in_maps = []
    for c in range(NCORES):
        b, half = c // 2, c % 2
        xp = np.zeros((NTOK, D), np.float32)
        if half == 1:
            xp[:] = x_prompt[b]
        else:
            xp[NPRE:] = x_prompt[b, 0:NMAIN]
        sl = slice(4 * c, 4 * c + 4)
        m = dict(shared)
        m.update(xp=xp, xs=f(x_sample[sl]).reshape(NS, D), sconv=f(state_conv[0, sl]), srec=f(state_rec[0, sl]),
                 ck=f(cache_win_k[0, sl]).reshape(4, 2048, 512), cv=f(cache_win_v[0, sl]).reshape(4, 2048, 512),
                 edge8=np.full((128, 1), 0.0 if half == 1 else NEGB, np.float32))
        in_maps.append(m)
    return in_maps


def assemble(res):
    y_prompt = np.zeros((4, 4096, D), np.float32)
    y_sample = np.zeros((32, 8, D), np.float32)
    conv_p = np.zeros((1, 4, 3, 1536), np.float32)
    rec_p = np.zeros((1, 4, 4, 128, 128), np.float32)
    wk_p = np.zeros((1, 4, 2048, 8, 64), np.float32)
    wv_p = np.zeros((1, 4, 2048, 8, 64), np.float32)
    conv_s = np.zeros((1, 32, 3, 1536), np.float32)
    rec_s = np.zeros((1, 32, 4, 128, 128), np.float32)
    wk_s = np.zeros((1, 32, 2048, 8, 64), np.float32)
    wv_s = np.zeros((1, 32, 2048, 8, 64), np.float32)
    for c in range(NCORES):
        b, half = c // 2, c % 2
        r = res[c]
        y_prompt[b, half * NMAIN:(half + 1) * NMAIN] = r["yp"]
        sl = slice(4 * c, 4 * c + 4)
        y_sample[sl] = r["ys"].reshape(4, 8, D)
        conv_s[0, sl] = r["conv_s"]
        rec_s[0, sl] = r["rec_s"]
        wk_s[0, sl] = r["wk_s"].reshape(4, 2048, 8, 64)
        wv_s[0, sl] = r["wv_s"].reshape(4, 2048, 8, 64)
        if half == 1:
            conv_p[0, b] = r["conv_p"]
            rec_p[0, b] = r["rec_p"]
            wk_p[0, b] = r["wk_p"].reshape(2048, 8, 64)
            wv_p[0, b] = r["wv_p"].reshape(2048, 8, 64)
    return (y_prompt, y_sample, conv_p, rec_p, wk_p, wv_p, conv_s, rec_s, wk_s, wv_s)


def kernel(**inputs):
    in_maps = make_in_maps(**inputs)
    if "nc" not in _CACHE:
        _CACHE["nc"] = build_program()[0]
    res = run_bass_kernel_spmd(_CACHE["nc"], in_maps, core_ids=list(range(NCORES)))
    return assemble(res.results)
```

```python
import math
import numpy as np
import ml_dtypes
import concourse.bass as bass
import concourse.mybir as mybir
from concourse.bass_utils import run_bass_kernel_spmd

F32 = mybir.dt.float32
BF16 = mybir.dt.bfloat16
ALU = mybir.AluOpType
AF = mybir.ActivationFunctionType

ENGS = ("pe", "act", "dve", "pool", "sp")
NCORES = 8
D = 1024
NPRE = 2048
NMAIN = 2048
NTOK = NPRE + NMAIN
NS = 32
DFF = 2816
EPS = 1e-6
NEGB = -240000.0
DILS = (1, 4, 16)
import os
KSTOP = int(os.environ.get("KSTOP", "9"))
KOFF = os.environ.get("KOFF", "")


class T:
    __slots__ = ("name", "ap", "lw", "rde", "rdd", "rng", "psum")

    def __init__(self, name, ap, psum=False):
        self.name = name
        self.ap = ap
        self.psum = psum
        self.lw = None
        self.rde = {}
        self.rdd = []
        self.rng = None

    def __getitem__(self, idx):
        return self.ap[idx]


class Sched:
    def __init__(self, nc, n_dma_slots=10):
        self.nc = nc
        self.ops = {e: [] for e in ENGS}
        self.cnt = {e: 0 for e in ENGS}
        self.waited = {e: {} for e in ENGS}
        self.nslots = n_dma_slots
        self.slot_total = {}
        self.slot_next = {"sp": 0, "pool": 0, "act": 0}
        self.dma_info = []

    def _need(self, eng, dep, waits):
        if dep[0] == "e":
            _, e2, seq = dep
            if e2 == eng and eng in ("pe", "sp"):
                return
            key = ("e", e2)
            val = seq
        else:
            key, val = self.dma_info[dep[1]]
        w = self.waited[eng]
        if w.get(key, 0) >= val:
            return
        w[key] = val
        waits.append((key, val))

    def _deps(self, eng, reads, writes):
        waits = []
        for t in reads:
            if t.lw is not None:
                self._need(eng, t.lw, waits)
            if t.psum:
                for e2, seq in t.rde.items():
                    if e2 != eng:
                        self._need(eng, ("e", e2, seq), waits)
        for t in writes:
            lw = t.lw
            if lw is not None:
                self._need(eng, lw, waits)
            for e2, seq in t.rde.items():
                if e2 != eng or eng != "pe":
                    self._need(eng, ("e", e2, seq), waits)
            for did in t.rdd:
                self._need(eng, ("d", did), waits)
        return waits

    def _mark(self, me, reads, writes):
        for t in reads:
            if me[0] == "e":
                if t.rde.get(me[1], 0) < me[2]:
                    t.rde[me[1]] = me[2]
            else:
                t.rdd.append(me[1])
        for t in writes:
            t.lw = me
            t.rde = {}
            t.rdd = []

    def op(self, eng, fn, reads=(), writes=()):
        waits = self._deps(eng, reads, writes)
        self.cnt[eng] += 1
        me = ("e", eng, self.cnt[eng])
        self._mark(me, reads, writes)
        self.ops[eng].append((fn, waits, "c", None))

    def dma(self, eng, out_ap, in_ap, reads=(), writes=(), **kw):
        waits = self._deps(eng, reads, writes)
        slot = self.slot_next[eng]
        self.slot_next[eng] = (slot + 1) % self.nslots
        key = ("d", eng, slot)
        prev = self.slot_total.get(key, 0)
        if prev:
            w = self.waited[eng]
            if w.get(key, 0) < prev:
                w[key] = prev
                waits.append((key, prev))
        val = prev + 16
        self.slot_total[key] = val
        did = len(self.dma_info)
        self.dma_info.append((key, val))
        self._mark(("d", did), reads, writes)
        self.ops[eng].append(((out_ap, in_ap, kw), waits, "d", key))

    def finish(self):
        for eng in ("sp", "pool", "act"):
            waits = []
            for key, val in self.slot_total.items():
                if key[1] != eng:
                    continue
                w = self.waited[eng]
                if w.get(key, 0) < val:
                    w[key] = val
                    waits.append((key, val))
            if waits:
                self.ops[eng].append((None, waits, "w", None))

    def emit(self):
        nc = self.nc
        from contextlib import ExitStack
        with ExitStack() as es:
            sems = {}
            for e in ENGS:
                sems[("e", e)] = es.enter_context(nc.semaphore("s_" + e))
            for key in self.slot_total:
                sems[key] = es.enter_context(nc.semaphore("d_%s_%d" % (key[1], key[2])))
            block = es.enter_context(nc.Block())
            refd = {e: set() for e in ENGS}
            for e in ENGS:
                for fn, waits, kind, extra in self.ops[e]:
                    for key, val in waits:
                        if key[0] == "e":
                            refd[key[1]].add(val)
            rank = {e: {s: i + 1 for i, s in enumerate(sorted(refd[e]))} for e in ENGS}

            def run(engname):
                def body(eng):
                    mysem = sems[("e", engname)]
                    myrank = rank[engname]
                    seq = 0
                    for fn, waits, kind, extra in self.ops[engname]:
                        for key, val in waits:
                            if key[0] == "e":
                                eng.wait_ge(sems[key], rank[key[1]][val])
                            else:
                                eng.wait_ge(sems[key], val)
                        if kind == "c":
                            seq += 1
                            ins = fn(eng)
                            if seq in myrank:
                                ins.then_inc(mysem, 1)
                        elif kind == "d":
                            out_ap, in_ap, kw = fn
                            eng.dma_start(out=out_ap, in_=in_ap, **kw).then_inc(sems[extra], 16)
                return body

            block.tensor(run("pe"))
            block.scalar(run("act"))
            block.vector(run("dve"))
            block.gpsimd(run("pool"))
            block.sync(run("sp"))


class Arena:
    def __init__(self, nc, nbytes):
        self.n = nbytes
        self.base = nc.alloc_sbuf_tensor("arena", [128, nbytes // 2], BF16).ap()
        self.live = []
        self.retired = []
        self.peak = 0

    def alloc(self, name, shape, dt):
        esz = 4 if dt == F32 else 2
        n = 1
        for s in shape[1:]:
            n *= s
        nb = (n * esz + 63) // 64 * 64
        pos = 0
        for a, b, _ in sorted(self.live, key=lambda x: x[0]):
            if a - pos >= nb:
                break
            pos = max(pos, b)
        if pos + nb > self.n:
            raise RuntimeError("arena full allocating %s (%d bytes) live=%d" % (name, nb, sum(b - a for a, b, _ in self.live)))
        a, b = pos, pos + nb
        v = self.base[:, a // 2:(a + n * esz) // 2]
        if dt == F32:
            v = v.bitcast(F32)
        if len(shape) == 3:
            v = v.rearrange("p (x y) -> p x y", x=shape[1])
        elif len(shape) == 4:
            v = v.rearrange("p (x y z) -> p x y z", x=shape[1], y=shape[2])
        if shape[0] < 128:
            v = v[0:shape[0]]
        t = T(name, v)
        t.rng = (a, b)
        keep = []
        for ra, rb, rt in self.retired:
            if ra < b and a < rb:
                for e2, seq in rt.rde.items():
                    if t.rde.get(e2, 0) < seq:
                        t.rde[e2] = seq
                t.rdd.extend(rt.rdd)
                if rt.lw is not None:
                    if rt.lw[0] == "e":
                        if t.rde.get(rt.lw[1], 0) < rt.lw[2]:
                            t.rde[rt.lw[1]] = rt.lw[2]
                    else:
                        t.rdd.append(rt.lw[1])
                if ra >= a and rb <= b:
                    continue
            keep.append((ra, rb, rt))
        self.retired = keep
        self.live.append((a, b, t))
        self.peak = max(self.peak, b)
        return t

    def free(self, *ts):
        for t in ts:
            for i, (a, b, tt) in enumerate(self.live):
                if tt is t:
                    self.live.pop(i)
                    self.retired.append((a, b, t))
                    break
            else:
                raise RuntimeError("free of unknown tile " + t.name)


class Ring:
    def __init__(self, tiles):
        self.t = tiles
        self.i = 0

    def next(self):
        t = self.t[self.i]
        self.i = (self.i + 1) % len(self.t)
        return t


class KB:
    def __init__(self, nc):
        self.nc = nc
        self.S = Sched(nc)
        self.A = Arena(nc, 207 * 1024)
        self.banks = [T("ps%d" % i, nc.alloc_psum_tensor("ps%d" % i, [128, 512], F32).ap(), psum=True) for i in range(8)]
        self.ps = Ring(self.banks)
        self._rr = 0

    def set_ring(self, n):
        self.ps = Ring(self.banks[0:n])

    def act(self, out, in_, func, R, W, scale=None, bias=None, accum=None):
        kw = {}
        if scale is not None:
            kw["scale"] = scale
        if bias is not None:
            kw["bias"] = bias
        if accum is not None:
            kw["accum_out"] = accum
        self.S.op("act", lambda e: e.activation(out=out, in_=in_, func=func, **kw), R, W)

    def mm(self, out, lhsT, rhs, R, W, start=True, stop=True, skip=False):
        self.S.op("pe", lambda e: e.matmul(out, lhsT=lhsT, rhs=rhs, start=start, stop=stop, skip_group_check=skip), R, W)

    def tr(self, out, in_, ident, R, W):
        self.S.op("pe", lambda e: e.transpose(out=out, in_=in_, identity=ident), R, W)

    def tt(self, eng, out, in0, in1, op, R, W):
        self.S.op(eng, lambda e: e.tensor_tensor(out=out, in0=in0, in1=in1, op=op), R, W)

    def ts(self, eng, out, in0, s1, op0, R, W, s2=None, op1=None):
        if op1 is None:
            self.S.op(eng, lambda e: e.tensor_scalar(out=out, in0=in0, scalar1=s1, scalar2=None, op0=op0), R, W)
        else:
            self.S.op(eng, lambda e: e.tensor_scalar(out=out, in0=in0, scalar1=s1, scalar2=s2, op0=op0, op1=op1), R, W)

    def stt(self, eng, out, in0, scalar, in1, op0, op1, R, W):
        self.S.op(eng, lambda e: e.scalar_tensor_tensor(out=out, in0=in0, scalar=scalar, in1=in1, op0=op0, op1=op1), R, W)

    def cp(self, eng, out, in_, R, W):
        if eng == "act":
            self.S.op("act", lambda e: e.activation(out=out, in_=in_, func=AF.Copy), R, W)
        else:
            self.S.op(eng, lambda e: e.tensor_copy(out=out, in_=in_), R, W)

    def recip(self, out, in_, R, W):
        self.S.op("dve", lambda e: e.reciprocal(out=out, in_=in_), R, W)

    def memset(self, eng, ap, val, W):
        self.S.op(eng, lambda e: e.memset(ap, val), (), W)

    def dma(self, eng, out, in_, R=(), W=(), **kw):
        self.S.dma(eng, out, in_, R, W, **kw)

    def alt(self, engs=("dve", "pool")):
        self._rr += 1
        return engs[self._rr % len(engs)]

    def rings(self, name, n, shape, dt):
        return Ring([self.A.alloc("%s%d" % (name, i), shape, dt) for i in range(n)])

    def free_ring(self, *rings):
        for r in rings:
            self.A.free(*r.t)


def sl(start, count, step):
    return slice(start, start + step * (count - 1) + 1, step)


def bank_bf(ps):
    return ps.ap.bitcast(BF16)


def norm_stats(k, xt, p, nb_tile, C, dimscale=1.0 / D):
    sm = C["sm"].next()
    hb = C["hb"].next()
    k.act(hb[0:p, :], xt[0:p, :], AF.Square, [xt], [hb, sm], accum=sm[0:p, 0:1])
    k.act(sm[0:p, 1:2], sm[0:p, 0:1], AF.Ln, [sm], [sm], scale=dimscale, bias=EPS)
    k.act(sm[0:p, 2:3], sm[0:p, 1:2], AF.Exp, [sm], [sm], scale=-0.5)
    k.stt("dve", hb[0:p, :], xt[0:p, :], sm[0:p, 2:3], nb_tile[0:p, :], ALU.mult, ALU.mult, [xt, sm, nb_tile], [hb])
    return hb


def transpose_to(k, hb, p, hT, c0, C):
    ps = k.ps.next()
    pb = bank_bf(ps)
    for c in range(8):
        k.tr(pb[:, c * 128:c * 128 + p], hb[0:p, c * 128:(c + 1) * 128], C["ident"][0:p, 0:p], [hb, C["ident"]], [ps])
    k.cp("act", hT[:, :, c0:c0 + p], pb[:, :].rearrange("q (c t) -> q c t", c=8)[:, :, 0:p], [ps], [hT])


def norm_transpose(k, xt, p, nb_tile, hT, c0, C, dimscale=1.0 / D):
    hb = norm_stats(k, xt, p, nb_tile, C, dimscale)
    transpose_to(k, hb, p, hT, c0, C)


def gdn_tile(k, C, G, hT_ap, qT_ap, kT_ap, vT_ap, S, Sb, deps, main, nf, mix_out, vm=None):
    w_a = C["w_a"]
    ident = C["ident"]
    sc = G["sc"].next()

    def bc(c0):
        return sc[:, c0:c0 + 4].unsqueeze(2).to_broadcast([128, 4, 128])

    def ps4(ps):
        return ps[:, :].rearrange("p (h c) -> p h c", h=4)

    psBA = k.ps.next()
    for kc in range(8):
        k.mm(psBA[:, 0:8], hT_ap[:, kc, :], w_a[:, kc, 2048:2056], deps + [C["wa_ba"]], [psBA], start=(kc == 0), stop=(kc == 7))
    k.act(sc[:, 0:4], psBA[:, 0:4], AF.Exp, [psBA], [sc], scale=-1.0)
    k.ts("dve", sc[:, 0:4], sc[:, 0:4], 1.0, ALU.add, [sc], [sc])
    k.recip(sc[:, 0:4], sc[:, 0:4], [sc], [sc])
    if vm is not None:
        k.ts("dve", sc[:, 0:4], sc[:, 0:4], vm[:, 0:1], ALU.mult, [sc, vm], [sc])
    k.ts("dve", sc[:, 4:8], sc[:, 0:4], -1.0, ALU.mult, [sc], [sc])
    yield
    k.tt("dve", sc[:, 8:12], psBA[:, 4:8], C["dtb"][:, :], ALU.add, [psBA, C["dtb"]], [sc])
    k.act(sc[:, 8:12], sc[:, 8:12], AF.Exp, [sc], [sc])
    k.act(sc[:, 8:12], sc[:, 8:12], AF.Ln, [sc], [sc], bias=1.0)
    k.tt("dve", sc[:, 8:12], sc[:, 8:12], C["negA"][:, :], ALU.mult, [sc, C["negA"]], [sc])
    if vm is not None:
        k.ts("dve", sc[:, 8:12], sc[:, 8:12], vm[:, 0:1], ALU.mult, [sc, vm], [sc])
    yield
    psG = k.ps.next()
    k.mm(psG[:, 0:4], C["mIU"][:, 0:128], sc[:, 8:12], [C["mIU"], sc], [psG])
    k.mm(psG[:, 4:8], C["onesf"][:, :], sc[:, 8:12], [C["onesf"], sc], [psG])
    k.cp("dve", sc[:, 12:20], psG[:, 0:8], [psG], [sc])
    k.act(sc[:, 20:28], sc[:, 12:20], AF.Exp, [sc], [sc])
    k.tt("dve", sc[:, 28:32], sc[:, 16:20], sc[:, 12:16], ALU.subtract, [sc], [sc])
    k.act(sc[:, 28:32], sc[:, 28:32], AF.Exp, [sc], [sc])
    yield
    tg = G["tg"].next()
    k.tt("dve", tg[:, :, :], C["mIU4"][:, :, :], bc(8), ALU.mult, [C["mIU4"], sc], [tg])
    psR = k.ps.next()
    k.mm(psR[:, :], C["onesf"][:, :], tg[:, :, :], [C["onesf"], tg], [psR])
    dec = G["dec"].next()
    k.tt("dve", dec[:, :, :], ps4(psR), bc(12), ALU.subtract, [psR, sc], [dec])
    k.act(dec[:, :, :], dec[:, :, :], AF.Exp, [dec], [dec])
    dS = G["dS"].next()
    k.stt("dve", dS[:, :, :], dec[:, :, :], 1.0, C["mSU4"][:, :, :], ALU.min, ALU.mult, [dec, C["mSU4"]], [dS])
    k.tt("dve", dS[:, :, :], dS[:, :, :], bc(4), ALU.mult, [dS, sc], [dS])
    if main:
        egr = G["egr"].next()
        k.act(egr[:, :, :], psR[:, :].rearrange("p (h c) -> p h c", h=4), AF.Exp, [psR], [egr])
        dI = G["dI"].next()
        k.stt("dve", dI[:, :, :], dec[:, :, :], 1.0, C["mIU4"][:, :, :], ALU.min, ALU.mult, [dec, C["mIU4"]], [dI])
    yield
    psk = k.ps.next()
    pbk = bank_bf(psk)
    for h in range(4):
        k.tr(pbk[:, h * 128:(h + 1) * 128], kT_ap[:, h, :], ident[:, :], deps + [ident], [psk])
    psv = k.ps.next()
    pbv = bank_bf(psv)
    for h in range(4):
        k.tr(pbv[:, h * 128:(h + 1) * 128], vT_ap[:, h, :], ident[:, :], deps + [ident], [psv])
    kg = G["kg"].next()
    kdec = G["kdec"].next()
    vtok = G["vtok"].next()
    ktok = G["e"].next()
    k.cp("act", ktok[:, :, :], pbk[:, 0:512].rearrange("p (h c) -> p h c", h=4), [psk], [ktok])
    k.cp("dve", vtok[:, :, :], pbv[:, 0:512].rearrange("p (h c) -> p h c", h=4), [psv], [vtok])
    k.tt("dve", kg[:, :, :], ktok[:, :, :], bc(20), ALU.mult, [ktok, sc], [kg])
    k.tt("dve", kdec[:, :, :], ktok[:, :, :], bc(28), ALU.mult, [ktok, sc], [kdec])
    yield
    psGm = k.ps.next()
    for h in range(4):
        k.mm(psGm[:, h * 128:(h + 1) * 128], kT_ap[:, h, :], kT_ap[:, h, :], deps, [psGm])
    R = G["R"].next()
    Rf = dec
    k.tt("dve", Rf[:, :, :], ps4(psGm), dS[:, :, :], ALU.mult, [psGm, dS], [Rf])
    k.cp("act", R[:, :, :], Rf[:, :, :], [Rf], [R])
    if main:
        psQK = k.ps.next()
        for h in range(4):
            k.mm(psQK[:, h * 128:(h + 1) * 128], kT_ap[:, h, :], qT_ap[:, h, :], deps, [psQK])
        QKT = G["QKT"].next()
        k.tt("dve", QKT[:, :, :], psQK[:, :].rearrange("p (h c) -> p h c", h=4), dI[:, :, :], ALU.mult, [psQK, dI], [QKT])
        qg = G["qg"].next()
        k.tt("dve", qg[:, :, :], qT_ap, egr[:, :, :], ALU.mult, deps + [egr], [qg])
    P = G["P"].next()
    k.tt("dve", P[:, :, :], R[:, :, :], C["id4b"][:, :, :], ALU.add, [R, C["id4b"]], [P])
    yield
    psr = k.ps.next()
    pbr = bank_bf(psr)
    for h in range(4):
        k.tr(pbr[:, h * 128:(h + 1) * 128], R[:, h, :], ident[:, :], [R, ident], [psr])
    RT = G["RT"].next()
    k.cp("act", RT[:, :, :], pbr[:, 0:512].rearrange("p (h c) -> p h c", h=4), [psr], [RT])
    for kk in range(1, nf + 1):
        yield
        psRk = psRTk = psP = None
        if kk <= nf - 2:
            psRk = k.ps.next()
            for h in range(4):
                k.mm(psRk[:, h * 128:(h + 1) * 128], RT[:, h, :], R[:, h, :], [RT, R], [psRk])
        if kk <= nf - 1:
            psRTk = k.ps.next()
            for h in range(4):
                k.mm(psRTk[:, h * 128:(h + 1) * 128], R[:, h, :], RT[:, h, :], [RT, R], [psRTk])
        if kk >= 2:
            psP = k.ps.next()
            for h in range(4):
                k.mm(psP[:, h * 128:(h + 1) * 128], RT[:, h, :], P[:, h, :], [RT, P], [psP])
        if psRk is not None:
            Rn = G["R"].next()
            k.cp("act", Rn[:, :, :], psRk[:, :].rearrange("p (h c) -> p h c", h=4), [psRk], [Rn])
        if psRTk is not None:
            RTn = G["RT"].next()
            k.cp("dve", RTn[:, :, :], psRTk[:, :].rearrange("p (h c) -> p h c", h=4), [psRTk], [RTn])
        if psP is not None:
            Pn = G["P"].next()
            k.tt("dve", Pn[:, :, :], psP[:, :].rearrange("p (h c) -> p h c", h=4), P[:, :, :], ALU.add, [psP, P], [Pn])
            P = Pn
        if psRk is not None:
            R = Rn
        if psRTk is not None:
            RT = RTn
    yield
    pst_ = k.ps.next()
    pbt_ = bank_bf(pst_)
    for h in range(4):
        k.tr(pbt_[:, h * 128:(h + 1) * 128], P[:, h, :], ident[:, :], [P, ident], [pst_])
    PTf = tg
    k.cp("act", PTf[:, :, :], pbt_[:, 0:512].rearrange("p (h c) -> p h c", h=4), [pst_], [PTf])
    psE = k.ps.next()
    for h in range(4):
        k.mm(psE[:, h * 128:(h + 1) * 128], PTf[:, h, :], Rf[:, h, :], [PTf, Rf], [psE])
    Et = G["Ec"].next()
    Ec = G["Ec"].next()
    k.tt("dve", Et[:, :, :], C["id4b"][:, :, :], P[:, :, :], ALU.subtract, [C["id4b"], P], [Et])
    k.tt("dve", Ec[:, :, :], psE[:, :].rearrange("p (h c) -> p h c", h=4), Et[:, :, :], ALU.add, [psE, Et], [Ec])
    yield
    psC1 = k.ps.next()
    for h in range(4):
        k.mm(psC1[:, h * 128:(h + 1) * 128], Ec[:, h, :], vtok[:, h, :], [Ec, vtok], [psC1])
    psC2 = k.ps.next()
    for h in range(4):
        k.mm(psC2[:, h * 128:(h + 1) * 128], Ec[:, h, :], kg[:, h, :], [Ec, kg], [psC2])
    vtok2 = G["vtok"].next()
    kg2 = G["kg"].next()
    k.tt("dve", vtok2[:, :, :], psC1[:, :].rearrange("p (h c) -> p h c", h=4), vtok[:, :, :], ALU.add, [psC1, vtok], [vtok2])
    k.tt("dve", kg2[:, :, :], psC2[:, :].rearrange("p (h c) -> p h c", h=4), kg[:, :, :], ALU.add, [psC2, kg], [kg2])
    vtok, kg = vtok2, kg2
    yield
    psU = k.ps.next()
    for h in range(4):
        k.mm(psU[:, h * 128:(h + 1) * 128], P[:, h, :], vtok[:, h, :], [P, vtok], [psU])
    psW = k.ps.next()
    for h in range(4):
        k.mm(psW[:, h * 128:(h + 1) * 128], kg[:, h, :], P[:, h, :], [P, kg], [psW])
    ub = G["ub"].next()
    k.cp("dve", ub[:, :, :], ps4(psU), [psU], [ub])
    wT = G["wT"].next()
    k.cp("act", wT[:, :, :], psW[:, :].rearrange("p (h c) -> p h c", h=4), [psW], [wT])
    yield
    psS1 = k.ps.next()
    for h in range(4):
        k.mm(psS1[:, h * 128:(h + 1) * 128], wT[:, h, :], Sb[:, h, :], [wT, Sb], [psS1])
    e = G["e"].next()
    k.tt("dve", ub[:, :, :], ps4(psS1), ub[:, :, :], ALU.subtract, [psS1, ub], [ub])
    k.tt("dve", e[:, :, :], ub[:, :, :], bc(4), ALU.mult, [ub, sc], [e])
    if main:
        psO = k.ps.next()
        for h in range(4):
            k.mm(psO[:, h * 128:(h + 1) * 128], qg[:, h, :], Sb[:, h, :], [qg, Sb], [psO], start=True, stop=False)
            k.mm(psO[:, h * 128:(h + 1) * 128], QKT[:, h, :], e[:, h, :], [QKT, e], [psO], start=False, stop=True)
    if main:
        o32 = ub
        k.cp("act", o32[:, :, :], psO[:, :].rearrange("p (h c) -> p h c", h=4), [psO], [o32])
    psSn = k.ps.next()
    for h in range(4):
        k.mm(psSn[:, h * 128:(h + 1) * 128], kdec[:, h, :], e[:, h, :], [kdec, e], [psSn])
    k.tt("dve", S[:, :, :], S[:, :, :], bc(24), ALU.mult, [S, sc], [S])
    k.tt("dve", S[:, :, :], S[:, :, :], ps4(psSn), ALU.add, [S, psSn], [S])
    k.cp("act", Sb[:, :, :], S[:, :, :], [S], [Sb])
    if not main:
        return
    yield
    jk = G["jk"].next()
    for h in range(4):
        k.act(jk[:, :], o32[:, h, :], AF.Square, [o32], [jk, sc], accum=sc[:, 32 + h:33 + h])
    k.act(sc[:, 32:36], sc[:, 32:36], AF.Ln, [sc], [sc], scale=1.0 / 128, bias=EPS)
    k.act(sc[:, 32:36], sc[:, 32:36], AF.Exp, [sc], [sc], scale=-0.5)
    yield
    psZ = k.ps.next()
    for kc in range(8):
        k.mm(psZ[:, :], hT_ap[:, kc, :], w_a[:, kc, 1536:2048], deps + [C["wa_z"]], [psZ], start=(kc == 0), stop=(kc == 7))
    ez = G["ez"].next()
    k.act(ez[:, :], psZ[:, :], AF.Exp, [psZ], [ez], scale=-1.0)
    k.act(ez[:, :], ez[:, :], AF.Ln, [ez], [ez], bias=1.0)
    k.act(ez[:, :], ez[:, :], AF.Exp, [ez], [ez], scale=-1.0)
    zn = G["zn"].next()
    k.tt("dve", zn[:, :], psZ[:, :], C["noa"][:, :], ALU.mult, [psZ, C["noa"]], [zn])
    k.tt("dve", zn[:, :], zn[:, :], ez[:, :], ALU.mult, [zn, ez], [zn])
    yield
    og = G["og"].next()
    k.tt("dve", o32[:, :, :], o32[:, :, :], bc(32), ALU.mult, [o32, sc], [o32])
    k.tt("dve", og[:, :].rearrange("p (h c) -> p h c", h=4), o32[:, :, :], zn[:, :].rearrange("p (h c) -> p h c", h=4), ALU.mult, [o32, zn], [og])
    mix_out(og)


def run_interleaved(gens, offs=3, maxact=2):
    active, pending, steps = [], list(gens), {}
    while active or pending:
        if pending and len(active) < maxact and (not active or steps[id(active[-1])] >= offs):
            gnew = pending.pop(0)
            active.append(gnew)
            steps[id(gnew)] = 0
        for gg in list(active):
            try:
                next(gg)
                steps[id(gg)] += 1
            except StopIteration:
                active.remove(gg)


EXTRA_RINGS = [("R", 2, [128, 4, 128], BF16), ("RT", 2, [128, 4, 128], BF16), ("P", 2, [128, 4, 128], BF16),
               ("kg", 2, [128, 4, 128], BF16), ("vtok", 2, [128, 4, 128], BF16), ("Ec", 2, [128, 4, 128], BF16),
               ("sc", 1, [128, 40], F32), ("tg", 1, [128, 4, 128], F32), ("dec", 1, [128, 4, 128], F32),
               ("egr", 1, [128, 4, 128], F32), ("ub", 1, [128, 4, 128], F32)] + \
              [(nm, 1, [128, 4, 128], BF16) for nm in ("dS", "dI", "kdec", "QKT", "qg", "wT", "e")]


def silu_from_psum(k, G, ps_ap, psT, n):
    e32 = G["e32"].next()
    c32 = G["c32"].next()
    k.act(e32[:, 0:n], ps_ap, AF.Exp, [psT], [e32], scale=-1.0)
    k.act(e32[:, 0:n], e32[:, 0:n], AF.Ln, [e32], [e32], bias=1.0)
    k.act(e32[:, 0:n], e32[:, 0:n], AF.Exp, [e32], [e32], scale=-1.0)
    k.tt("dve", c32[:, 0:n], ps_ap, e32[:, 0:n], ALU.mult, [psT, e32], [c32])
    return c32


def l2norm_chunk(k, C, G, c32, n, out_ap, outT, qscale):
    sq = G["sq"].next()
    k.tt("dve", sq[:, 0:n], c32[:, 0:n], c32[:, 0:n], ALU.mult, [c32], [sq])
    psC = k.ps.next()
    k.mm(psC[:, 0:n], C["onesb"][:, :], sq[:, 0:n], [C["onesb"], sq], [psC])
    l32 = G["l32"].next()
    k.act(l32[:, 0:n], psC[:, 0:n], AF.Ln, [psC], [l32], bias=EPS)
    if qscale:
        k.act(l32[:, 0:n], l32[:, 0:n], AF.Exp, [l32], [l32], scale=-0.5, bias=C["lnq"][:, 0:1])
        rd = [c32, l32, C["lnq"]]
    else:
        k.act(l32[:, 0:n], l32[:, 0:n], AF.Exp, [l32], [l32], scale=-0.5)
        rd = [c32, l32]
    k.tt("dve", out_ap, c32[:, 0:n], l32[:, 0:n], ALU.mult, rd, [outT])


def stage_G(k, C, IO):
    A = k.A
    w_a = A.alloc("w_a", [128, 8, 2056], BF16)
    C["w_a"] = w_a
    wa_units = {nm: T("wa_" + nm, None) for nm in ("q", "k", "v", "z", "ba")}
    for t in wa_units.values():
        for e2, seq in w_a.rde.items():
            t.rde[e2] = seq
        t.rdd.extend(w_a.rdd)
    C["wa_g"] = [wa_units["q"], wa_units["k"], wa_units["v"]]
    C["wa_z"], C["wa_ba"] = wa_units["z"], wa_units["ba"]
    for nm, c0, c1 in (("k", 512, 1024), ("v", 1024, 1536), ("ba", 2048, 2056), ("q", 0, 512), ("z", 1536, 2048)):
        k.dma("pool", w_a[:, :, c0:c1], IO["w_in"][:, c0:c1].rearrange("(c p) n -> p c n", p=128), W=[wa_units[nm]],
              allow_slow_non_contiguous=(nm == "ba"))
    nmb = A.alloc("nmb", [128, 1024], F32)
    k.dma("sp", nmb[:, :], IO["norm_mix"].partition_broadcast(128), W=[nmb])
    wconv = A.alloc("wconv", [128, 4, 12], F32)
    for i in range(4):
        k.dma("sp", wconv[:, i, :], IO["w_conv"][i].rearrange("(c p) -> p c", p=128), W=[wconv], allow_slow_non_contiguous=True)
    diag = A.alloc("diag", [128, 12, 4, 128], BF16)
    for ch in range(12):
        for i in range(4):
            k.ts(k.alt(), diag[:, ch, i, :], C["identf"][:, 0:128], wconv[:, i, ch:ch + 1], ALU.mult, [C["identf"], wconv], [diag])
    dtb = A.alloc("dtb", [128, 4], F32)
    negA = A.alloc("negA", [128, 4], F32)
    C["dtb"], C["negA"] = dtb, negA
    k.dma("sp", dtb[:, :], IO["dt_bias"].partition_broadcast(128), W=[dtb])
    k.dma("sp", negA[:, :], IO["a_log"].partition_broadcast(128), W=[negA])
    k.act(negA[:, :], negA[:, :], AF.Exp, [negA], [negA])
    k.ts("dve", negA[:, :], negA[:, :], -1.0, ALU.mult, [negA], [negA])
    noa = A.alloc("noa", [128, 512], F32)
    C["noa"] = noa
    for h in range(4):
        k.dma("sp", noa[:, h * 128:(h + 1) * 128], IO["noa"].partition_broadcast(128), W=[noa])
    lnq = A.alloc("lnq", [128, 1], F32)
    C["lnq"] = lnq
    k.memset("pool", lnq[:, :], math.log(128.0 ** -0.5), [lnq])

    G = {}
    C["sm"] = k.rings("sm", 4, [128, 8], F32)
    C["hb"] = k.rings("hb", 4, [128, 1024], BF16)
    xr = k.rings("xr", 2, [128, 1024], F32)
    hT = A.alloc("hT", [128, 8, 512], BF16)
    ext = A.alloc("ext", [128, 12, 515], BF16)
    qkT = A.alloc("qkT", [128, 8, 512], BF16)
    vT = A.alloc("vT", [128, 4, 512], BF16)
    S = A.alloc("S", [128, 4, 128], F32)
    Sb = A.alloc("Sb", [128, 4, 128], BF16)
    for nm, n in (("e32", 2), ("c32", 4), ("l32", 2), ("ez", 1), ("zn", 1)):
        G[nm] = k.rings(nm, n, [128, 512], F32)
    G["sq"] = k.rings("sq", 2, [128, 512], BF16)
    G["og"] = k.rings("og", 2, [128, 512], BF16)
    G["sc"] = k.rings("sc", 3, [128, 40], F32)
    G["jk"] = k.rings("jk", 1, [128, 128], F32)
    for nm, n in (("tg", 2), ("dec", 2), ("egr", 2), ("ub", 2)):
        G[nm] = k.rings(nm, n, [128, 4, 128], F32)
    for nm, n in (("dS", 2), ("dI", 2), ("kg", 4), ("kdec", 2), ("vtok", 4), ("R", 4), ("RT", 4), ("P", 4), ("Ec", 4),
                  ("QKT", 2), ("qg", 2), ("wT", 2), ("e", 2)):
        G[nm] = k.rings(nm, n, [128, 4, 128], BF16)

    mixT_a = C["mixT_a"]
    k.memset("pool", ext[:, :, :], 0.0, [ext])
    k.memset("pool", S[:, :, :], 0.0, [S])
    k.memset("dve", Sb[:, :, :], 0.0, [Sb])

    def feature_chunks(pchs, chs, hT_ap, hdeps, n, ext_dst, conv_rhs, qk_out, v_out, outTs, conv_out=None):
        for ch in pchs:
            psA = k.ps.next()
            for kc in range(8):
                k.mm(psA[:, 0:n], w_a[:, kc, ch * 128:(ch + 1) * 128], hT_ap(kc), hdeps + [C["wa_g"][ch // 4]], [psA], start=(kc == 0), stop=(kc == 7))
            ext_dst(ch, psA)
        def stage1(pair):
            st1 = []
            for ch in pair:
                psB = k.ps.next()
                for i in range(4):
                    k.mm(psB[:, 0:n] if conv_out is None else conv_out(psB), diag[:, ch, i, :], conv_rhs(ch, i), [diag] + outTs["ext"], [psB], start=(i == 0), stop=(i == 3))
                st1.append((ch, psB, G["e32"].next(), G["c32"].next()))
            for ch, psB, e32, c32 in st1:
                k.act(e32[:, 0:n], psB[:, 0:n], AF.Exp, [psB], [e32], scale=-1.0)
            for ch, psB, e32, c32 in st1:
                k.act(e32[:, 0:n], e32[:, 0:n], AF.Ln, [e32], [e32], bias=1.0)
            for ch, psB, e32, c32 in st1:
                k.act(e32[:, 0:n], e32[:, 0:n], AF.Exp, [e32], [e32], scale=-1.0)
            for ch, psB, e32, c32 in st1:
                k.tt("dve", c32[:, 0:n], psB[:, 0:n], e32[:, 0:n], ALU.mult, [psB, e32], [c32])
            return [(ch, c32) for ch, psB, e32, c32 in st1]

        def stage2(items):
            qk = [(ch, c32) for ch, c32 in items if ch < 8]
            for ch, c32 in items:
                if ch >= 8:
                    k.cp("dve", v_out(ch - 8), c32[:, 0:n], [c32], [outTs["v"]])
            st2 = []
            for ch, c32 in qk:
                sq = G["sq"].next()
                k.tt("dve", sq[:, 0:n], c32[:, 0:n], c32[:, 0:n], ALU.mult, [c32], [sq])
                psC = k.ps.next()
                k.mm(psC[:, 0:n], C["onesb"][:, :], sq[:, 0:n], [C["onesb"], sq], [psC])
                st2.append((ch, c32, psC, G["l32"].next()))
            for ch, c32, psC, l32 in st2:
                k.act(l32[:, 0:n], psC[:, 0:n], AF.Ln, [psC], [l32], bias=EPS)
            for ch, c32, psC, l32 in st2:
                if ch < 4:
                    k.act(l32[:, 0:n], l32[:, 0:n], AF.Exp, [l32, C["lnq"]], [l32], scale=-0.5, bias=C["lnq"][:, 0:1])
                else:
                    k.act(l32[:, 0:n], l32[:, 0:n], AF.Exp, [l32], [l32], scale=-0.5)
            for ch, c32, psC, l32 in st2:
                k.tt("dve", qk_out(ch), c32[:, 0:n], l32[:, 0:n], ALU.mult, [c32, l32], [outTs["qk"]])

        pairs = [chs[i:i + 2] for i in range(0, len(chs), 2)]
        pend = None
        for pair in pairs:
            cur = stage1(pair)
            if pend is not None:
                stage2(pend)
            pend = cur
        if pend is not None:
            stage2(pend)

    for st in range(NTOK // 512):
        main = st >= NPRE // 512
        hbs = []
        for tt4 in range(4):
            tok0 = st * 512 + tt4 * 128
            xt = xr.next()
            k.dma("sp", xt[:, :], IO["xp"][tok0:tok0 + 128, :], W=[xt])
            hbs.append(norm_stats(k, xt, 128, nmb, C))
        for tt4 in range(4):
            transpose_to(k, hbs[tt4], 128, hT, tt4 * 128, C)
        chs = list(range(12)) if main else list(range(4, 12))
        pchs = list(range(12)) if st >= NPRE // 512 - 1 else chs

        def ext_dst(ch, psA):
            k.cp("dve", ext[:, ch, 3:515], psA[:, :], [psA], [ext])

        feature_chunks(pchs, chs, lambda kc: hT[:, kc, :], [hT], 512, ext_dst,
                       lambda ch, i: ext[:, ch, i:i + 512],
                       lambda ch: qkT[:, ch, :], lambda j: vT[:, j, :], {"ext": [ext], "qk": qkT, "v": vT})
        if st == NTOK // 512 - 1:
            for j in range(3):
                psT3 = k.ps.next()
                for kc in range(8):
                    k.mm(psT3[0:3, :], hT[:, kc, 509:512], w_a[:, kc, j * 512:(j + 1) * 512], [hT, C["wa_g"][j]], [psT3], start=(kc == 0), stop=(kc == 7))
                pre3 = G["l32"].next()
                k.cp("dve", pre3[0:3, :], psT3[0:3, :], [psT3], [pre3])
                k.dma("sp", IO["conv_p"][:, j * 512:(j + 1) * 512], pre3[0:3, :], R=[pre3])
        halo = G.setdefault("halo", A.alloc("halo", [128, 12, 3], BF16))
        k.cp("pool", halo[:, :, :], ext[:, :, 512:515], [ext], [halo])
        gens = []
        for tt4 in range(4):
            cs = slice(tt4 * 128, (tt4 + 1) * 128)
            gcol = st * 512 + tt4 * 128 - NPRE

            def mix_out(og, gcol=gcol):
                psm = k.ps.next()
                pbm = bank_bf(psm)
                for h in range(4):
                    k.tr(pbm[:, h * 128:(h + 1) * 128], og[:, h * 128:(h + 1) * 128], C["ident"][:, :], [og, C["ident"]], [psm])
                k.cp("act", mixT_a[:, :, gcol:gcol + 128], pbm[:, 0:512].rearrange("p (h c) -> p h c", h=4), [psm], [mixT_a])

            gens.append(gdn_tile(k, C, G, hT[:, :, cs], qkT[:, 0:4, cs], qkT[:, 4:8, cs], vT[:, :, cs], S, Sb, [hT, qkT, vT], main, 7, mix_out))
        k.free_ring(C["hb"], xr, G["e32"], G["c32"], G["l32"], G["sq"])
        extra = {}
        for nm, n, shp, dt in EXTRA_RINGS:
            extra[nm] = [A.alloc("x_%s%d" % (nm, i), shp, dt) for i in range(n)]
            G[nm].t.extend(extra[nm])
        run_interleaved(gens, offs=3, maxact=3)
        for nm, tl in extra.items():
            for t in tl:
                G[nm].t.remove(t)
            G[nm].i = 0
            A.free(*tl)
        C["hb"] = k.rings("hb", 4, [128, 1024], BF16)
        xr = k.rings("xr", 2, [128, 1024], F32)
        for nm, n_ in (("e32", 2), ("c32", 4), ("l32", 2)):
            G[nm] = k.rings(nm, n_, [128, 512], F32)
        G["sq"] = k.rings("sq", 2, [128, 512], BF16)
        k.cp("pool", ext[:, :, 0:3], halo[:, :, :], [halo], [ext])
    k.dma("sp", IO["rec_p"].rearrange("h d e -> d h e"), S[:, :, :], R=[S])
    A.free(hT, ext, qkT, vT, G["halo"])

    xt = xr.next()
    k.dma("sp", xt[0:NS, :], IO["xs"][:, :], W=[xt])
    hTs = A.alloc("hTs", [128, 8, NS], BF16)
    norm_transpose(k, xt, NS, nmb, hTs, 0, C)
    exts = A.alloc("exts", [128, 12, 4, 11], BF16)
    sct = A.alloc("sct", [12, 1536], F32)
    k.dma("sp", sct[0:12, :], IO["sconv"].rearrange("s i c -> (s i) c"), W=[sct])
    psh = k.ps.next()
    for ch in range(12):
        k.tr(psh[:, ch * 12:(ch + 1) * 12], sct[0:12, ch * 128:(ch + 1) * 128], C["identf"][0:12, 0:12], [sct, C["identf"]], [psh])
    k.cp("dve", exts[:, :, :, 0:3], psh[:, 0:144].rearrange("p (c s i) -> p c s i", c=12, s=4), [psh], [exts])
    qks = A.alloc("qks", [128, 8, NS], BF16)
    vs = A.alloc("vs", [128, 4, NS], BF16)

    def ext_dst_s(ch, psA):
        k.cp("act", exts[:, ch, :, 3:11], psA[:, 0:NS].rearrange("p (s t) -> p s t", s=4), [psA], [exts])

    feature_chunks(list(range(12)), list(range(12)), lambda kc: hTs[:, kc, :], [hTs], NS, ext_dst_s,
                   lambda ch, i: exts[:, ch, :, i:i + 8],
                   lambda ch: qks[:, ch, :], lambda j: vs[:, j, :], {"ext": [exts], "qk": qks, "v": vs},
                   conv_out=lambda psB: psB[:, 0:NS].rearrange("p (s t) -> p s t", s=4))
    pres = A.alloc("pres", [NS, 1536], F32)
    for j in range(3):
        psT3 = k.ps.next()
        for kc in range(8):
            k.mm(psT3[0:NS, :], hTs[:, kc, :], w_a[:, kc, j * 512:(j + 1) * 512], [hTs, C["wa_g"][j]], [psT3], start=(kc == 0), stop=(kc == 7))
        k.cp("dve", pres[0:NS, j * 512:(j + 1) * 512], psT3[0:NS, :], [psT3], [pres])
    for s in range(4):
        k.dma("sp", IO["conv_s"][s, :, :], pres[8 * s + 5:8 * s + 8, :], R=[pres])
    hpad = k.rings("hpad", 2, [128, 8, 128], BF16)
    qkpad = k.rings("qkpad", 2, [128, 8, 128], BF16)
    vpad = k.rings("vpad", 2, [128, 4, 128], BF16)
    Ss = k.rings("Ss", 2, [128, 4, 128], F32)
    Sbs = k.rings("Sbs", 2, [128, 4, 128], BF16)
    for r in (hpad, qkpad, vpad):
        for t in r.t:
            k.memset(k.alt(), t[:, :, :], 0.0, [t])
    sgens = []
    for s in range(4):
        hp, qp, vp, S_s, Sb_s = hpad.next(), qkpad.next(), vpad.next(), Ss.next(), Sbs.next()
        k.cp("pool", hp[:, :, 0:8], hTs[:, :, 8 * s:8 * s + 8], [hTs], [hp])
        k.cp("pool", qp[:, :, 0:8], qks[:, :, 8 * s:8 * s + 8], [qks], [qp])
        k.cp("pool", vp[:, :, 0:8], vs[:, :, 8 * s:8 * s + 8], [vs], [vp])
        k.dma("sp", S_s[:, :, :], IO["srec"][s].rearrange("h d e -> d h e"), W=[S_s])
        k.cp("act", Sb_s[:, :, :], S_s[:, :, :], [S_s], [Sb_s])

        def mix_out_s(og, s=s):
            psm = k.ps.next()
            pbm = bank_bf(psm)
            for h in range(4):
                k.tr(pbm[:, h * 8:(h + 1) * 8], og[0:8, h * 128:(h + 1) * 128], C["ident"][0:8, 0:8], [og, C["ident"]], [psm])
            k.cp("act", mixT_a[:, :, NMAIN + 8 * s:NMAIN + 8 * s + 8], pbm[:, 0:32].rearrange("p (h c) -> p h c", h=4), [psm], [mixT_a])

        def seq_gen(s=s, hp=hp, qp=qp, vp=vp, S_s=S_s, Sb_s=Sb_s, mix_out_s=mix_out_s):
            yield from gdn_tile(k, C, G, hp[:, :, :], qp[:, 0:4, :], qp[:, 4:8, :], vp[:, :, :], S_s, Sb_s, [hp, qp, vp], True, 3, mix_out_s, vm=C["vmask"])
            k.dma("sp", IO["rec_s"][s].rearrange("h d e -> d h e"), S_s[:, :, :], R=[S_s])

        sgens.append(seq_gen())
        if s % 2 == 1:
            run_interleaved(sgens)
            sgens = []

    merge_free(A, w_a, list(wa_units.values()))
    A.free(nmb, wconv, diag, dtb, negA, noa, lnq, S, Sb, hTs, exts, sct, qks, vs, pres)
    k.free_ring(C["sm"], C["hb"], xr, hpad, qkpad, vpad, Ss, Sbs)
    for nm, r in G.items():
        if isinstance(r, Ring):
            k.free_ring(r)
    G.clear()


def merge_free(A, parent, children):
    for ch in children:
        for e2, seq in ch.rde.items():
            if parent.rde.get(e2, 0) < seq:
                parent.rde[e2] = seq
        parent.rdd.extend(ch.rdd)
        if ch.lw is not None:
            if ch.lw[0] == "e":
                if parent.rde.get(ch.lw[1], 0) < ch.lw[2]:
                    parent.rde[ch.lw[1]] = ch.lw[2]
            else:
                parent.rdd.append(ch.lw[1])
    A.free(parent)


def setup_consts(k, C, IO):
    A = k.A
    cf = IO["cf32"]
    identf = A.alloc("identf", [128, 128], F32)
    mIU = A.alloc("mIU", [128, 128], F32)
    onesf = A.alloc("onesf", [128, 128], F32)
    mSU4 = A.alloc("mSU4", [128, 4, 128], F32)
    mIU4 = A.alloc("mIU4", [128, 4, 128], F32)
    id4f = A.alloc("id4f", [128, 4, 128], F32)
    k.dma("sp", identf[:, :], cf[:, 0:128], W=[identf])
    k.dma("sp", id4f[:, :, :], cf[:, 0:512].rearrange("p (h c) -> p h c", h=4), W=[id4f])
    k.dma("sp", mSU4[:, :, :], cf[:, 512:1024].rearrange("p (h c) -> p h c", h=4), W=[mSU4])
    k.dma("sp", mIU4[:, :, :], cf[:, 1024:1536].rearrange("p (h c) -> p h c", h=4), W=[mIU4])
    k.dma("sp", mIU[:, :], cf[:, 1024:1152], W=[mIU])
    k.dma("sp", onesf[:, :], cf[:, 1536:1664], W=[onesf])
    ident = A.alloc("ident", [128, 128], BF16)
    onesb = A.alloc("onesb", [128, 128], BF16)
    id4b = A.alloc("id4b", [128, 4, 128], BF16)
    k.cp("dve", ident[:, :], identf[:, :], [identf], [ident])
    k.cp("dve", onesb[:, :], onesf[:, :], [onesf], [onesb])
    k.cp("dve", id4b[:, :, :], id4f[:, :, :], [id4f], [id4b])
    vmask = A.alloc("vmask", [128, 1], F32)
    edge = A.alloc("edge", [128, 1], F32)
    k.dma("sp", vmask[:, :], IO["vmask"][:, :], W=[vmask])
    k.dma("sp", edge[:, :], IO["edge8"][:, :], W=[edge])
    C.update(identf=identf, mIU=mIU, onesf=onesf, mSU4=mSU4, mIU4=mIU4, ident=ident, onesb=onesb, id4b=id4b,
             vmask=vmask, edge=edge)
    A.free(id4f)


def stage_K(k, C, IO):
    A = k.A
    w_b = A.alloc("w_b", [128, 8, 1536], BF16)
    for kc in range(8):
        k.dma("pool", w_b[:, kc, :], IO["w_in"][kc * 128:(kc + 1) * 128, 2056:3592], W=[w_b])
    nmb = A.alloc("nmb2", [128, 1024], F32)
    k.dma("sp", nmb[:, :], IO["norm_mix"].partition_broadcast(128), W=[nmb])
    C["sm"] = k.rings("smk", 6, [128, 8], F32)
    C["hb"] = k.rings("hbk", 5, [128, 1024], BF16)
    xr = k.rings("xrk", 4, [128, 1024], F32)
    hTr = k.rings("hTk", 2, [128, 8, 512], BF16)
    kT_b = A.alloc("kT_b", [128, 4, NTOK], BF16)
    vT_b = A.alloc("vT_b", [128, 4, NTOK], BF16)
    qT_b = A.alloc("qT_b", [128, 4, NMAIN], BF16)
    kv = [T("kv%d" % st, None) for st in range(NTOK // 512)]
    for t in kv:
        for par in (kT_b, vT_b, qT_b):
            for e2, seq in par.rde.items():
                if t.rde.get(e2, 0) < seq:
                    t.rde[e2] = seq
            t.rdd.extend(par.rdd)
    C.update(kT_b=kT_b, vT_b=vT_b, qT_b=qT_b, kv=kv)
    ost = k.rings("ost", 2, [128, 512], F32)
    def k_stats(st):
        hbs = []
        for tt4 in range(4):
            tok0 = st * 512 + tt4 * 128
            xt = xr.next()
            k.dma("sp", xt[:, :], IO["xp"][tok0:tok0 + 128, :], W=[xt])
            hbs.append(norm_stats(k, xt, 128, nmb, C))
        return hbs

    def k_tr(hbs):
        hT = hTr.next()
        for tt4 in range(4):
            transpose_to(k, hbs[tt4], 128, hT, tt4 * 128, C)
        return hT

    hT_next = k_tr(k_stats(0))
    for st in range(NTOK // 512):
        main = st >= NPRE // 512
        hT = hT_next
        hbs_next = k_stats(st + 1) if st + 1 < NTOK // 512 else None
        for ch in (range(12) if main else range(4, 12)):
            psA = k.ps.next()
            for kc in range(8):
                k.mm(psA[:, :], w_b[:, kc, ch * 128:(ch + 1) * 128], hT[:, kc, :], [hT, w_b], [psA], start=(kc == 0), stop=(kc == 7))
            if ch < 4:
                dst = qT_b[:, ch, (st * 512 - NPRE):(st * 512 - NPRE) + 512]
            elif ch < 8:
                dst = kT_b[:, ch - 4, st * 512:(st + 1) * 512]
            else:
                dst = vT_b[:, ch - 8, st * 512:(st + 1) * 512]
            k.cp(k.alt(("act", "dve")), dst, psA[:, :], [psA], [kv[st]])
        if hbs_next is not None:
            hT_next = k_tr(hbs_next)
        if main and KSTOP >= 2:
            for tt4 in range(4):
                tok0 = st * 512 + tt4 * 128
                for src, dstname in ((kT_b, "wk_p"), (vT_b, "wv_p")):
                    pst = k.ps.next()
                    pbt = bank_bf(pst)
                    for c in range(4):
                        k.tr(pbt[:, c * 128:(c + 1) * 128], src[:, c, tok0:tok0 + 128], C["ident"][:, :], [kv[st], C["ident"]], [pst])
                    o = ost.next()
                    k.cp(k.alt(("act", "dve")), o[:, :], pbt[:, 0:512], [pst], [o])
                    k.dma("sp", IO[dstname][tok0 - NPRE:tok0 - NPRE + 128, :], o[:, :], R=[o])
    if KSTOP < 3:
        return
    xt = xr.next()
    k.dma("sp", xt[0:NS, :], IO["xs"][:, :], W=[xt])
    hTs = A.alloc("hTs2", [128, 8, NS], BF16)
    norm_transpose(k, xt, NS, nmb, hTs, 0, C)
    qTs = A.alloc("qTs", [128, 4, NS], BF16)
    kTn = A.alloc("kTn", [128, 4, NS], BF16)
    for ch in range(8):
        psA = k.ps.next()
        for kc in range(8):
            k.mm(psA[:, 0:NS], w_b[:, kc, ch * 128:(ch + 1) * 128], hTs[:, kc, :], [hTs, w_b], [psA], start=(kc == 0), stop=(kc == 7))
        dstT = qTs if ch < 4 else kTn
        k.cp("act", dstT[:, ch % 4, :], psA[:, 0:NS], [psA], [dstT])
    if KSTOP < 4:
        return
    vn_aug = A.alloc("vn_aug", [NS, 8, 66], BF16)
    k.memset("pool", vn_aug[:, :, :], 1.0, [vn_aug])
    for j, dstname in ((1, "wk_s"), (2, "wv_s")):
        psA = k.ps.next()
        for kc in range(8):
            k.mm(psA[0:NS, :], hTs[:, kc, :], w_b[:, kc, j * 512:(j + 1) * 512], [hTs, w_b], [psA], start=(kc == 0), stop=(kc == 7))
        o = ost.next()
        k.cp("dve", o[0:NS, :], psA[0:NS, :], [psA], [o])
        for s in range(4):
            if "D" not in KOFF:
                k.dma("sp", IO[dstname][s, 2040:2048, :], o[8 * s:8 * s + 8, :], R=[o])
        if j == 2 and "A" not in KOFF:
            k.cp("act", vn_aug[:, :, 0:64], psA[0:NS, :].rearrange("p (h e) -> p h e", h=8), [psA], [vn_aug])
    if KSTOP < 5:
        return
    Qbd = A.alloc("Qbd", [128, 4, 4, 48], BF16)
    k.memset("pool", Qbd[:, :, :, :], 0.0, [Qbd])
    for c in range(4):
        for br in range(3):
            k.cp(k.alt(), Qbd[0:64, c, :, br * 8:br * 8 + 8], qTs[0:64, c, :].rearrange("p (s t) -> p s t", s=4), [qTs], [Qbd])
            k.cp(k.alt(), Qbd[64:128, c, :, 24 + br * 8:32 + br * 8], qTs[64:128, c, :].rearrange("p (s t) -> p s t", s=4), [qTs], [Qbd])
    C.update(kTn=kTn, vn_aug=vn_aug, Qbd=Qbd)
    A.free(w_b, nmb, hTs, qTs)
    k.free_ring(C["sm"], C["hb"], xr, hTr, ost)


def finalize_attn(k, C, F, acc_ap, accT, n, dst):
    for c0 in range(0, n, 512):
        w = min(512, n - c0)
        sq = F["sq"].next()
        k.act(sq[0:65, 0:w], acc_ap[0:65, c0:c0 + w], AF.Square, [accT], [sq])
        psF = k.ps.next()
        k.mm(psF[0:64, 0:w], C["gmb"][0:65, 0:64], sq[0:65, 0:w], [C["gmb"], sq], [psF])
        l32 = F["l32"].next()
        k.act(l32[0:64, 0:w], psF[0:64, 0:w], AF.Ln, [psF], [l32])
        k.act(l32[0:64, 0:w], l32[0:64, 0:w], AF.Exp, [l32], [l32], scale=-0.5)
        dap, dT = dst(c0, w)
        k.stt("dve", dap, acc_ap[0:64, c0:c0 + w], C["nob"][0:64, 0:1], l32[0:64, 0:w], ALU.mult, ALU.mult, [accT, C["nob"], l32], [dT])


def stage_B(k, C, IO):
    A = k.A
    kT_b, vT_b, qT_b, kv = C["kT_b"], C["vT_b"], C["qT_b"], C["kv"]
    identb = C["ident"]
    pb = A.alloc("pbias", [128, 8, 3, 256], BF16)
    k.dma("sp", pb[:, :, :, :], IO["pbias"][:, :, :, :], W=[pb])
    pe_ = A.alloc("pedge", [128, 8, 3, 128], BF16)
    for h in range(8):
        k.ts(k.alt(), pe_[:, h, :, :], pb[:, h, :, 128:256], C["edge"][:, 0:1], ALU.add, [pb, C["edge"]], [pe_])
    gmb = A.alloc("gmb", [65, 64], BF16)
    k.dma("sp", gmb[:, :], IO["gmb"][:, :], W=[gmb])
    nob = A.alloc("nob", [64, 1], F32)
    k.dma("sp", nob[:, :], IO["nob"].rearrange("(p o) -> p o", o=1), W=[nob])
    C.update(gmb=gmb, nob=nob)
    mixT_b = C["mixT_b"] = A.alloc("mixT_b", [128, 4, NMAIN + NS], BF16)
    F = {"sq": k.rings("fsq", 2, [65, 512], BF16), "l32": k.rings("fl32", 2, [64, 512], F32)}
    Vblk = A.alloc("Vblk", [128, 69, 2, 66], BF16)
    k.memset("pool", Vblk[:, :, :, :], 1.0, [Vblk])
    accr = k.rings("acc", 1, [65, NMAIN], F32)
    PTr = k.rings("PT", 6, [128, 256], BF16)
    otmp = k.rings("otmp", 2, [64, 512], BF16)
    blocks = []
    for br, d in enumerate(DILS):
        for r in range(d):
            for n in range(16 // d - 1, 32 // d):
                blocks.append((br, r, n))
    bidx = {b: i for i, b in enumerate(blocks)}
    assert len(blocks) == 69
    for c in range(4):
        for g0 in range(0, 69, 4):
            grp = blocks[g0:g0 + 4]
            psv = k.ps.next()
            pbv = bank_bf(psv)
            for j, (br, r, n) in enumerate(grp):
                d = DILS[br]
                k.tr(pbv[:, j * 128:(j + 1) * 128], vT_b[:, c, sl(r + d * 128 * n, 128, d)], identb[:, :], kv + [identb], [psv])
            ng = len(grp)
            k.cp(k.alt(("act", "dve")), Vblk[:, g0:g0 + ng, :, 0:64],
                 pbv[:, 0:ng * 128].rearrange("p (g h e) -> p g h e", g=ng, h=2), [psv], [Vblk])
        for hh in range(2):
            h = 2 * c + hh
            po = 64 * hh
            acc = accr.next()
            hb_list = []
            for br, d in enumerate(DILS):
                nq0, nq1 = 16 // d, 32 // d
                for r in range(d):
                    for n in range(nq0 - 1, nq1):
                        hb_list.append((br, d, r, n, n == nq0 - 1, n == nq1 - 1, nq0))
            PTs = {}

            def emit_S(i, h=h, c=c, po=po):
                br, d, r, n, first, last, nq0 = hb_list[i]
                ks = sl(r + d * 128 * n, 128, d)
                if first:
                    q0, N, bias, bT = r + d * 128 * nq0 - NPRE, 128, pe_[:, h, br, :], pe_
                elif last:
                    q0, N, bias, bT = r + d * 128 * n - NPRE, 128, pb[:, h, br, 0:128], pb
                else:
                    q0, N, bias, bT = r + d * 128 * n - NPRE, 256, pb[:, h, br, :], pb
                psS = k.ps.next()
                k.mm(psS[:, 0:N], kT_b[po:po + 64, c, ks], qT_b[po:po + 64, c, sl(q0, N, d)], kv, [psS], start=True, stop=False)
                k.mm(psS[:, 0:N], identb[:, :], bias, [identb, bT], [psS], start=False, stop=True)
                PT = PTr.next()
                k.act(PT[:, 0:N], psS[:, 0:N], AF.Exp, [psS], [PT], scale=0.125)
                PTs[i] = PT

            def emit_PV(i, hh=hh, acc=acc):
                br, d, r, n, first, last, nq0 = hb_list[i]
                if first:
                    return
                PT, prevPT = PTs[i], PTs[i - 1]
                prev_first = hb_list[i - 1][4]
                psO = k.ps.next()
                pp = prevPT[:, 0:128] if prev_first else prevPT[:, 128:256]
                k.mm(psO[0:65, 0:128], Vblk[:, bidx[(br, r, n - 1)], hh, 0:65], pp, [Vblk, prevPT], [psO], start=True, stop=False)
                k.mm(psO[0:65, 0:128], Vblk[:, bidx[(br, r, n)], hh, 0:65], PT[:, 0:128], [Vblk, PT], [psO], start=False, stop=True)
                qc = r + d * 128 * n - NPRE
                qcols = sl(qc, 128, d)
                if br == 0:
                    k.cp("dve", acc[:, qcols], psO[0:65, 0:128], [psO], [acc])
                else:
                    k.tt("dve", acc[:, qcols], acc[:, qcols], psO[0:65, 0:128], ALU.add, [acc, psO], [acc])
                PTs.pop(i - 1, None)

            LOOK = 3
            for i in range(len(hb_list) + LOOK):
                if i < len(hb_list):
                    emit_S(i)
                if i - LOOK >= 0:
                    emit_PV(i - LOOK)
            if hh == 0:
                finalize_attn(k, C, F, acc, acc, NMAIN, lambda c0, w, c=c: (mixT_b[0:64, c, c0:c0 + w], mixT_b))
            else:
                def dst(c0, w, c=c):
                    o = otmp.next()
                    dst.last = (o, c0, w)
                    return o[0:64, 0:w], o
                for c0 in range(0, NMAIN, 512):
                    finalize_attn(k, C, F, acc[:, c0:c0 + 512], acc, 512, dst)
                    o, _, w = dst.last
                    k.dma("sp", mixT_b[64:128, c, c0:c0 + 512], o[0:64, 0:512], R=[o], W=[mixT_b])
    merge_free(A, kT_b, kv)
    A.free(vT_b, qT_b, pb, pe_, Vblk)
    k.free_ring(accr, PTr)

    k.set_ring(7)
    psOs = k.banks[7]
    sbc = A.alloc("sbc", [128, 16, 192], BF16)
    sbn = A.alloc("sbn", [NS, 4, 192], BF16)
    k.dma("sp", sbc[:, :, :], IO["sbias_c"][:, :, :], W=[sbc])
    k.dma("sp", sbn[:, :, :], IO["sbias_n"][:, :, :], W=[sbn])
    kTn, vn_aug, Qbd = C["kTn"], C["vn_aug"], C["Qbd"]
    kc32r = k.rings("kc32", 2, [128, 512], F32)
    vc32r = k.rings("vc32", 2, [128, 512], F32)
    kcbr = k.rings("kcb", 2, [128, 512], BF16)
    vaugr = k.rings("vaug", 2, [128, 8, 66], BF16)
    kTsr = k.rings("kTs", 2, [128, 4, 128], BF16)
    PTsr = k.rings("PTs", 2, [128, 192], BF16)
    tmpr = k.rings("ptmp", 2, [128, 8, 8], BF16)
    Pqr = [k.rings("Pq%d" % s, 2, [128, 8, NS], BF16) for s in range(4)]
    for t in vaugr.t:
        k.memset("pool", t[:, :, :], 1.0, [t])
    for s in range(4):
        for t in Pqr[s].t:
            k.memset(k.alt(), t[:, :, :], 0.0, [t])
    for s in range(4):
        for kt in range(17):
            if kt < 16:
                kc32, vc32, kcb, vaug, kTs = kc32r.next(), vc32r.next(), kcbr.next(), vaugr.next(), kTsr.next()
                k.dma("sp", kc32[:, :], IO["ck"][s, 128 * kt:128 * kt + 128, :], W=[kc32])
                k.dma("sp", vc32[:, :], IO["cv"][s, 128 * kt:128 * kt + 128, :], W=[vc32])
                for src, nm in ((kc32, "wk_s"), (vc32, "wv_s")):
                    if kt == 0:
                        k.dma("sp", IO[nm][s, 0:120, :], src[8:128, :], R=[src])
                    else:
                        k.dma("sp", IO[nm][s, 128 * kt - 8:128 * kt + 120, :], src[:, :], R=[src])
                k.cp("act", kcb[:, :], kc32[:, :], [kc32], [kcb])
                k.cp("dve", vaug[:, :, 0:64], vc32[:, :].rearrange("p (h e) -> p h e", h=8), [vc32], [vaug])
                pst = k.ps.next()
                pbt = bank_bf(pst)
                for c in range(4):
                    k.tr(pbt[:, c * 128:(c + 1) * 128], kcb[:, c * 128:(c + 1) * 128], identb[:, :], [kcb, identb], [pst])
                k.cp("act", kTs[:, :, :], pbt[:, 0:512].rearrange("p (c t) -> p c t", c=4), [pst], [kTs])
                np_ = 128
                lhs_k = lambda c, kTs=kTs: kTs[:, c, :]
                kdep = kTs
                bias = sbc[:, kt, :]
                bT = sbc
                vsrc = vaug
                idb = identb[:, :]
            else:
                np_ = NS
                lhs_k = lambda c: kTn[:, c, :]
                kdep = kTn
                bias = sbn[0:NS, s, :]
                bT = sbn
                vsrc = vn_aug
                idb = identb[0:NS, 0:NS]
            psS = k.ps.next()
            k.mm(psS[0:np_, 0:192], idb, bias, [identb, bT], [psS], start=True, stop=False)
            for c in range(4):
                k.mm(psS[0:np_, c * 48:(c + 1) * 48], lhs_k(c), Qbd[:, c, s, :], [kdep, Qbd], [psS], start=False, stop=(c == 3))
            PTs = PTsr.next()
            k.act(PTs[0:np_, :], psS[0:np_, 0:192], AF.Exp, [psS], [PTs], scale=0.125)
            Pq = Pqr[s].next()
            tmp = tmpr.next()
            P4 = PTs[0:np_, :].rearrange("p (h b t) -> p h b t", h=8, b=3)
            k.tt("dve", tmp[0:np_, :, :], P4[:, :, 0, :], P4[:, :, 1, :], ALU.add, [PTs], [tmp])
            k.tt("dve", Pq[0:np_, :, 8 * s:8 * s + 8], tmp[0:np_, :, :], P4[:, :, 2, :], ALU.add, [PTs, tmp], [Pq])
            for h in range(8):
                k.mm(psOs[0:65, h * NS:(h + 1) * NS], vsrc[0:np_, h, 0:65], Pq[0:np_, h, :], [vsrc, Pq], [psOs],
                     start=(s == 0 and kt == 0 and h == 0), stop=(s == 3 and kt == 16), skip=True)
    accs = A.alloc("accs", [65, 8, NS], F32)
    k.cp("dve", accs[:, :, :], psOs[0:65, 0:8 * NS].rearrange("p (h t) -> p h t", h=8), [psOs], [accs])
    for h in range(8):
        c, hh = h // 2, h % 2
        if hh == 0:
            finalize_attn(k, C, F, accs[:, h, :], accs, NS, lambda c0, w, c=c: (mixT_b[0:64, c, NMAIN:NMAIN + NS], mixT_b))
        else:
            o = otmp.next()
            finalize_attn(k, C, F, accs[:, h, :], accs, NS, lambda c0, w, o=o: (o[0:64, 0:NS], o))
            k.dma("sp", mixT_b[64:128, c, NMAIN:NMAIN + NS], o[0:64, 0:NS], R=[o], W=[mixT_b])
    k.set_ring(8)
    A.free(sbc, sbn, kTn, vn_aug, Qbd, accs, gmb, nob)
    k.free_ring(kc32r, vc32r, kcbr, vaugr, kTsr, PTsr, tmpr, otmp, F["sq"], F["l32"], *Pqr)


def stage_C(k, C, IO):
    A = k.A
    k.set_ring(4)
    accb = k.banks[4:8]
    mixT_a, mixT_b = C["mixT_a"], C["mixT_b"]
    wo = A.alloc("wo", [128, 8, 1024], BF16)
    for kc in range(8):
        k.dma("pool", wo[:, kc, :], IO["w_out"][kc * 128:(kc + 1) * 128, :], W=[wo])
    WB = 256
    wgb = [A.alloc("wg%d" % i, [128, 8, WB], BF16) for i in range(DFF // WB)]
    wub = [A.alloc("wu%d" % i, [128, 8, WB], BF16) for i in range(DFF // WB)]
    for i in range(DFF // WB):
        k.dma("pool", wgb[i][:, :, :], IO["w_gate"][:, i * WB:(i + 1) * WB].rearrange("(c p) n -> p c n", p=128), W=[wgb[i]])
        k.dma("pool", wub[i][:, :, :], IO["w_up"][:, i * WB:(i + 1) * WB].rearrange("(c p) n -> p c n", p=128), W=[wub[i]])
    nfb = A.alloc("nfb", [128, 1024], F32)
    nfin = A.alloc("nfin", [128, 1024], F32)
    k.dma("sp", nfb[:, :], IO["norm_ffn"].partition_broadcast(128), W=[nfb])
    k.dma("sp", nfin[:, :], IO["norm_final"].partition_broadcast(128), W=[nfin])
    C["sm"] = k.rings("smc", 4, [128, 8], F32)
    C["hb"] = k.rings("hbc", 2, [128, 1024], BF16)
    x1r = k.rings("x1", 4, [128, 1024], F32)
    hfr = k.rings("hfT", 2, [128, 8, 256], BF16)
    wdr = k.rings("wd", 3, [128, 1024], BF16)
    e32r = k.rings("ce32", 3, [128, 256], F32)
    c32r = k.rings("cc32", 3, [128, 256], F32)
    u32r = k.rings("cu32", 3, [128, 256], F32)
    aTr = k.rings("aT", 3, [128, 256], BF16)
    units = [(IO["xp"], NPRE + u * 256, u * 256, 2, 128, IO["yp"], u * 256) for u in range(NMAIN // 256)]
    units.append((IO["xs"], 0, NMAIN, 1, NS, IO["ys"], 0))
    NJ = DFF // 128

    def pre(unit, st):
        (xsrc, xrow0, mcol0, ntile, p, ydst, yrow0) = unit
        st["hfT"] = hfr.next()
        st["x1s"] = []
        for t in range(ntile):
            x1 = x1r.next()
            st["x1s"].append(x1)
            k.dma("sp", x1[0:p, :], xsrc[xrow0 + t * 128:xrow0 + t * 128 + p, :], W=[x1])
            cols = slice(mcol0 + t * 128, mcol0 + t * 128 + p)
            for half in range(2):
                psX = k.ps.next()
                for kc in range(8):
                    lhsT = mixT_a[:, kc, cols] if kc < 4 else mixT_b[:, kc - 4, cols]
                    k.mm(psX[0:p, :], lhsT, wo[:, kc, half * 512:(half + 1) * 512], [mixT_a, mixT_b, wo], [psX], start=(kc == 0), stop=(kc == 7))
                k.tt("dve", x1[0:p, half * 512:(half + 1) * 512], x1[0:p, half * 512:(half + 1) * 512], psX[0:p, :], ALU.add, [x1, psX], [x1])
                yield
            norm_transpose(k, x1, p, nfb, st["hfT"], t * 128, C)
            yield

    def ffn(unit, st):
        (xsrc, xrow0, mcol0, ntile, p, ydst, yrow0) = unit
        ntok = (ntile - 1) * 128 + p
        hfT = st["hfT"]

        def issue_gu(j):
            psG = k.ps.next()
            for kc in range(8):
                k.mm(psG[:, 0:ntok], wgb[j // 2][:, kc, (j % 2) * 128:(j % 2) * 128 + 128], hfT[:, kc, 0:ntok], [wgb[j // 2], hfT], [psG], start=(kc == 0), stop=(kc == 7))
            psU = k.ps.next()
            for kc in range(8):
                k.mm(psU[:, 0:ntok], wub[j // 2][:, kc, (j % 2) * 128:(j % 2) * 128 + 128], hfT[:, kc, 0:ntok], [wub[j // 2], hfT], [psU], start=(kc == 0), stop=(kc == 7))
            return psG, psU

        pend = issue_gu(0)
        for j in range(NJ):
            psG, psU = pend
            wd = wdr.next()
            k.dma("pool", wd[:, :], IO["w_down"][j * 128:(j + 1) * 128, :], W=[wd])
            e32, c32, u32, aT = e32r.next(), c32r.next(), u32r.next(), aTr.next()
            k.act(e32[:, 0:ntok], psG[:, 0:ntok], AF.Exp, [psG], [e32], scale=-1.0)
            k.cp("act", u32[:, 0:ntok], psU[:, 0:ntok], [psU], [u32])
            k.act(e32[:, 0:ntok], e32[:, 0:ntok], AF.Ln, [e32], [e32], bias=1.0)
            k.act(e32[:, 0:ntok], e32[:, 0:ntok], AF.Exp, [e32], [e32], scale=-1.0)
            k.tt("dve", c32[:, 0:ntok], psG[:, 0:ntok], e32[:, 0:ntok], ALU.mult, [psG, e32], [c32])
            k.tt("dve", aT[:, 0:ntok], c32[:, 0:ntok], u32[:, 0:ntok], ALU.mult, [c32, u32], [aT])
            if j + 1 < NJ:
                pend = issue_gu(j + 1)
            for t in range(ntile):
                for half in range(2):
                    ab = accb[t * 2 + half]
                    k.mm(ab[0:p, :], aT[:, t * 128:t * 128 + p], wd[:, half * 512:(half + 1) * 512], [aT, wd], [ab],
                         start=(j == 0), stop=(j == NJ - 1))
            yield

    def post(unit, st):
        (xsrc, xrow0, mcol0, ntile, p, ydst, yrow0) = unit
        for t in range(ntile):
            x1 = st["x1s"][t]
            for half in range(2):
                ab = accb[t * 2 + half]
                k.tt("dve", x1[0:p, half * 512:(half + 1) * 512], x1[0:p, half * 512:(half + 1) * 512], ab[0:p, :], ALU.add, [x1, ab], [x1])
            sm = C["sm"].next()
            jk = C["hb"].next()
            k.act(jk[0:p, :], x1[0:p, :], AF.Square, [x1], [jk, sm], accum=sm[0:p, 0:1])
            k.act(sm[0:p, 1:2], sm[0:p, 0:1], AF.Ln, [sm], [sm], scale=1.0 / D, bias=EPS)
            k.act(sm[0:p, 2:3], sm[0:p, 1:2], AF.Exp, [sm], [sm], scale=-0.5)
            k.stt("dve", x1[0:p, :], x1[0:p, :], sm[0:p, 2:3], nfin[0:p, :], ALU.mult, ALU.mult, [x1, sm, nfin], [x1])
            k.dma("sp", ydst[yrow0 + t * 128:yrow0 + t * 128 + p, :], x1[0:p, :], R=[x1])

    states = [dict() for _ in units]
    for _ in pre(units[0], states[0]):
        pass
    for u, unit in enumerate(units):
        gens = [ffn(unit, states[u])]
        if u + 1 < len(units):
            gens.append(pre(units[u + 1], states[u + 1]))
        run_interleaved(gens, offs=1, maxact=2)
        post(unit, states[u])
    k.set_ring(8)


IN_SPECS = [
    ("xp", [NTOK, D], F32), ("xs", [NS, D], F32), ("sconv", [4, 3, 1536], F32), ("srec", [4, 4, 128, 128], F32),
    ("ck", [4, 2048, 512], F32), ("cv", [4, 2048, 512], F32),
    ("norm_mix", [D], F32), ("w_in", [D, 3592], F32), ("w_conv", [4, 1536], F32), ("a_log", [4], F32), ("dt_bias", [4], F32),
    ("noa", [128], F32), ("nob", [64], F32), ("w_out", [D, D], F32), ("norm_ffn", [D], F32),
    ("w_gate", [D, DFF], F32), ("w_up", [D, DFF], F32), ("w_down", [DFF, D], F32), ("norm_final", [D], F32),
    ("cf32", [128, 1664], F32), ("vmask", [128, 1], F32), ("edge8", [128, 1], F32),
    ("pbias", [128, 8, 3, 256], BF16), ("sbias_c", [128, 16, 192], BF16), ("sbias_n", [NS, 4, 192], BF16), ("gmb", [65, 64], BF16),
]
OUT_SPECS = [
    ("yp", [NMAIN, D]), ("ys", [NS, D]), ("conv_p", [3, 1536]), ("rec_p", [4, 128, 128]), ("wk_p", [NMAIN, 512]), ("wv_p", [NMAIN, 512]),
    ("conv_s", [4, 3, 1536]), ("rec_s", [4, 4, 128, 128]), ("wk_s", [4, 2048, 512]), ("wv_s", [4, 2048, 512]),
]


def build_program(stages="GKBC", dbg=False):
    nc = bass.Bass("TRN2", target_bir_lowering=False)
    IO = {}
    if dbg:
        IO["dbg_ma"] = nc.dram_tensor("dbg_ma", [128, 4, NMAIN + NS], F32, kind="ExternalOutput").ap()
        IO["dbg_mb"] = nc.dram_tensor("dbg_mb", [128, 4, NMAIN + NS], F32, kind="ExternalOutput").ap()
    for name, shape, dt in IN_SPECS:
        IO[name] = nc.dram_tensor(name, shape, dt, kind="ExternalInput").ap()
    for name, shape in OUT_SPECS:
        IO[name] = nc.dram_tensor(name, shape, F32, kind="ExternalOutput").ap()
    k = KB(nc)
    C = {}
    setup_consts(k, C, IO)
    C["mixT_a"] = k.A.alloc("mixT_a", [128, 4, NMAIN + NS], BF16)
    if "G" in stages:
        stage_G(k, C, IO)
    if "K" in stages:
        stage_K(k, C, IO)
    if "B" in stages:
        stage_B(k, C, IO)
    if dbg:
        k.dma("pool", IO["dbg_ma"][:, :, :], C["mixT_a"][:, :, :], R=[C["mixT_a"]])
        k.dma("pool", IO["dbg_mb"][:, :, :], C["mixT_b"][:, :, :], R=[C["mixT_b"]])
    if "C" in stages:
        stage_C(k, C, IO)
    k.S.finish()
    k.S.emit()
    return nc, k


def host_tables():
    p = np.arange(128)
    ident = np.eye(128, dtype=np.float32)
    mSU = (p[:, None] < p[None, :]).astype(np.float32)
    mIU = (p[:, None] <= p[None, :]).astype(np.float32)
    ones = np.ones((128, 128), np.float32)
    cf = np.concatenate([np.tile(ident, (1, 4)), np.tile(mSU, (1, 4)), np.tile(mIU, (1, 4)), ones], axis=1)
    slopes = 2.0 ** (-np.arange(1, 9, dtype=np.float64))
    ki = p[:, None].astype(np.float64)
    qi = p[None, :].astype(np.float64)
    pbias = np.zeros((128, 8, 3, 256), np.float64)
    for h in range(8):
        for br, d in enumerate(DILS):
            j = qi - ki
            pbias[:, h, br, 0:128] = np.where(j >= 0, -slopes[h] * d * j * 8.0, NEGB)
            j = qi + 128 - ki
            pbias[:, h, br, 128:256] = np.where(j <= 128, -slopes[h] * d * j * 8.0, NEGB)
    sbc = np.full((128, 16, 192), NEGB, np.float64)
    sbn = np.full((NS, 4, 192), NEGB, np.float64)
    for h in range(8):
        for br, d in enumerate(DILS):
            for t in range(8):
                col = h * 24 + br * 8 + t
                kp = np.arange(2048)
                dist = 2048 + t - kp
                ok = (dist % d == 0) & (dist // d <= 128)
                vals = np.where(ok, -slopes[h] * dist * 8.0, NEGB)
                sbc[:, :, col] = vals.reshape(16, 128).T
                for s in range(4):
                    for t2 in range(8):
                        dist2 = t - t2
                        if dist2 >= 0 and dist2 % d == 0:
                            sbn[8 * s + t2, s, col] = -slopes[h] * dist2 * 8.0
    gm = np.full((65, 64), 1.0 / 64, np.float64)
    gm[64, :] = EPS
    vmask = (p < 8).astype(np.float32).reshape(128, 1)
    bf = ml_dtypes.bfloat16
    return dict(cf32=cf, vmask=vmask, pbias=pbias.astype(np.float32).astype(bf), sbias_c=sbc.astype(np.float32).astype(bf),
                sbias_n=sbn.astype(np.float32).astype(bf), gmb=gm.astype(np.float32).astype(bf))


_CACHE = {}


def make_in_maps(x_prompt, x_sample, state_conv, state_rec, cache_win_k, cache_win_v, norm_mix, w_in, w_conv, a_log, dt_bias,
                 norm_out_a, norm_out_b, w_out, norm_ffn, w_gate, w_up, w_down, norm_final):
    f = lambda a: np.ascontiguousarray(np.asarray(a, dtype=np.float32))
    tabs = host_tables()
    shared = dict(norm_mix=f(norm_mix[0]), w_in=f(w_in[0]), w_conv=f(w_conv[0]), a_log=f(a_log[0]), dt_bias=f(dt_bias[0]),
                  noa=f(norm_out_a[0]), nob=f(norm_out_b[0]), w_out=f(w_out[0]), norm_ffn=f(norm_ffn[0]),
                  w_gate=f(w_gate[0]), w_up=f(w_up[0]), w_down=f(w_down[0]), norm_final=f(norm_final), **tabs)
    in_maps = []
    for c in range(NCORES):
        b, half = c // 2, c % 2
        xp = np.zeros((NTOK, D), np.float32)
        if half == 1:
            xp[:] = x_prompt[b]
        else:
            xp[NPRE:] = x_prompt[b, 0:NMAIN]
        sl = slice(4 * c, 4 * c + 4)
        m = dict(shared)
        m.update(xp=xp, xs=f(x_sample[sl]).reshape(NS, D), sconv=f(state_conv[0, sl]), srec=f(state_rec[0, sl]),
                 ck=f(cache_win_k[0, sl]).reshape(4, 2048, 512), cv=f(cache_win_v[0, sl]).reshape(4, 2048, 512),
                 edge8=np.full((128, 1), 0.0 if half == 1 else NEGB, np.float32))
        in_maps.append(m)
    return in_maps


def assemble(res):
    y_prompt = np.zeros((4, 4096, D), np.float32)
    y_sample = np.zeros((32, 8, D), np.float32)
    conv_p = np.zeros((1, 4, 3, 1536), np.float32)
    rec_p = np.zeros((1, 4, 4, 128, 128), np.float32)
    wk_p = np.zeros((1, 4, 2048, 8, 64), np.float32)
    wv_p = np.zeros((1, 4, 2048, 8, 64), np.float32)
    conv_s = np.zeros((1, 32, 3, 1536), np.float32)
    rec_s = np.zeros((1, 32, 4, 128, 128), np.float32)
    wk_s = np.zeros((1, 32, 2048, 8, 64), np.float32)
    wv_s = np.zeros((1, 32, 2048, 8, 64), np.float32)
    for c in range(NCORES):
        b, half = c // 2, c % 2
        r = res[c]
        y_prompt[b, half * NMAIN:(half + 1) * NMAIN] = r["yp"]
        sl = slice(4 * c, 4 * c + 4)
        y_sample[sl] = r["ys"].reshape(4, 8, D)
        conv_s[0, sl] = r["conv_s"]
        rec_s[0, sl] = r["rec_s"]
        wk_s[0, sl] = r["wk_s"].reshape(4, 2048, 8, 64)
        wv_s[0, sl] = r["wv_s"].reshape(4, 2048, 8, 64)
        if half == 1:
            conv_p[0, b] = r["conv_p"]
            rec_p[0, b] = r["rec_p"]
            wk_p[0, b] = r["wk_p"].reshape(2048, 8, 64)
            wv_p[0, b] = r["wv_p"].reshape(2048, 8, 64)
    return (y_prompt, y_sample, conv_p, rec_p, wk_p, wv_p, conv_s, rec_s, wk_s, wv_s)


def kernel(**inputs):
    in_maps = make_in_maps(**inputs)
    if "nc" not in _CACHE:
        _CACHE["nc"] = build_program()[0]
    res = run_bass_kernel_spmd(_CACHE["nc"], in_maps, core_ids=list(range(NCORES)))
    return assemble(res.results)
```

```python
import math
import numpy as np
import ml_dtypes
import concourse.bass as bass
import concourse.mybir as mybir
from concourse.bass_utils import run_bass_kernel_spmd

F32 = mybir.dt.float32
BF16 = mybir.dt.bfloat16
ALU = mybir.AluOpType
AF = mybir.ActivationFunctionType

ENGS = ("pe", "act", "dve", "pool", "sp")
NCORES = 8
D = 1024
NPRE = 2048
NMAIN = 2048
NTOK = NPRE + NMAIN
NS = 32
DFF = 2816
EPS = 1e-6
NEGB = -240000.0
DILS = (1, 4, 16)
import os
KSTOP = int(os.environ.get("KSTOP", "9"))
KOFF = os.environ.get("KOFF", "")


class T:
    __slots__ = ("name", "ap", "lw", "rde", "rdd", "rng", "psum")

    def __init__(self, name, ap, psum=False):
        self.name = name
        self.ap = ap
        self.psum = psum
        self.lw = None
        self.rde = {}
        self.rdd = []
        self.rng = None

    def __getitem__(self, idx):
        return self.ap[idx]


class Sched:
    def __init__(self, nc, n_dma_slots=10):
        self.nc = nc
        self.ops = {e: [] for e in ENGS}
        self.cnt = {e: 0 for e in ENGS}
        self.waited = {e: {} for e in ENGS}
        self.nslots = n_dma_slots
        self.slot_total = {}
        self.slot_next = {"sp": 0, "pool": 0, "act": 0}
        self.dma_info = []

    def _need(self, eng, dep, waits):
        if dep[0] == "e":
            _, e2, seq = dep
            if e2 == eng and eng in ("pe", "sp"):
                return
            key = ("e", e2)
            val = seq
        else:
            key, val = self.dma_info[dep[1]]
        w = self.waited[eng]
        if w.get(key, 0) >= val:
            return
        w[key] = val
        waits.append((key, val))

    def _deps(self, eng, reads, writes):
        waits = []
        for t in reads:
            if t.lw is not None:
                self._need(eng, t.lw, waits)
            if t.psum:
                for e2, seq in t.rde.items():
                    if e2 != eng:
                        self._need(eng, ("e", e2, seq), waits)
        for t in writes:
            lw = t.lw
            if lw is not None:
                self._need(eng, lw, waits)
            for e2, seq in t.rde.items():
                if e2 != eng or eng != "pe":
                    self._need(eng, ("e", e2, seq), waits)
            for did in t.rdd:
                self._need(eng, ("d", did), waits)
        return waits

    def _mark(self, me, reads, writes):
        for t in reads:
            if me[0] == "e":
                if t.rde.get(me[1], 0) < me[2]:
                    t.rde[me[1]] = me[2]
            else:
                t.rdd.append(me[1])
        for t in writes:
            t.lw = me
            t.rde = {}
            t.rdd = []

    def op(self, eng, fn, reads=(), writes=()):
        waits = self._deps(eng, reads, writes)
        self.cnt[eng] += 1
        me = ("e", eng, self.cnt[eng])
        self._mark(me, reads, writes)
        self.ops[eng].append((fn, waits, "c", None))

    def dma(self, eng, out_ap, in_ap, reads=(), writes=(), **kw):
        waits = self._deps(eng, reads, writes)
        slot = self.slot_next[eng]
        self.slot_next[eng] = (slot + 1) % self.nslots
        key = ("d", eng, slot)
        prev = self.slot_total.get(key, 0)
        if prev:
            w = self.waited[eng]
            if w.get(key, 0) < prev:
                w[key] = prev
                waits.append((key, prev))
        val = prev + 16
        self.slot_total[key] = val
        did = len(self.dma_info)
        self.dma_info.append((key, val))
        self._mark(("d", did), reads, writes)
        self.ops[eng].append(((out_ap, in_ap, kw), waits, "d", key))

    def finish(self):
        for eng in ("sp", "pool", "act"):
            waits = []
            for key, val in self.slot_total.items():
                if key[1] != eng:
                    continue
                w = self.waited[eng]
                if w.get(key, 0) < val:
                    w[key] = val
                    waits.append((key, val))
            if waits:
                self.ops[eng].append((None, waits, "w", None))

    def emit(self):
        nc = self.nc
        from contextlib import ExitStack
        with ExitStack() as es:
            sems = {}
            for e in ENGS:
                sems[("e", e)] = es.enter_context(nc.semaphore("s_" + e))
            for key in self.slot_total:
                sems[key] = es.enter_context(nc.semaphore("d_%s_%d" % (key[1], key[2])))
            block = es.enter_context(nc.Block())
            refd = {e: set() for e in ENGS}
            for e in ENGS:
                for fn, waits, kind, extra in self.ops[e]:
                    for key, val in waits:
                        if key[0] == "e":
                            refd[key[1]].add(val)
            rank = {e: {s: i + 1 for i, s in enumerate(sorted(refd[e]))} for e in ENGS}

            def run(engname):
                def body(eng):
                    mysem = sems[("e", engname)]
                    myrank = rank[engname]
                    seq = 0
                    for fn, waits, kind, extra in self.ops[engname]:
                        for key, val in waits:
                            if key[0] == "e":
                                eng.wait_ge(sems[key], rank[key[1]][val])
                            else:
                                eng.wait_ge(sems[key], val)
                        if kind == "c":
                            seq += 1
                            ins = fn(eng)
                            if seq in myrank:
                                ins.then_inc(mysem, 1)
                        elif kind == "d":
                            out_ap, in_ap, kw = fn
                            eng.dma_start(out=out_ap, in_=in_ap, **kw).then_inc(sems[extra], 16)
                return body

            block.tensor(run("pe"))
            block.scalar(run("act"))
            block.vector(run("dve"))
            block.gpsimd(run("pool"))
            block.sync(run("sp"))


class Arena:
    def __init__(self, nc, nbytes):
        self.n = nbytes
        self.base = nc.alloc_sbuf_tensor("arena", [128, nbytes // 2], BF16).ap()
        self.live = []
        self.retired = []
        self.peak = 0

    def alloc(self, name, shape, dt):
        esz = 4 if dt == F32 else 2
        n = 1
        for s in shape[1:]:
            n *= s
        nb = (n * esz + 63) // 64 * 64
        pos = 0
        for a, b, _ in sorted(self.live, key=lambda x: x[0]):
            if a - pos >= nb:
                break
            pos = max(pos, b)
        if pos + nb > self.n:
            raise RuntimeError("arena full allocating %s (%d bytes) live=%d" % (name, nb, sum(b - a for a, b, _ in self.live)))
        a, b = pos, pos + nb
        v = self.base[:, a // 2:(a + n * esz) // 2]
        if dt == F32:
            v = v.bitcast(F32)
        if len(shape) == 3:
            v = v.rearrange("p (x y) -> p x y", x=shape[1])
        elif len(shape) == 4:
            v = v.rearrange("p (x y z) -> p x y z", x=shape[1], y=shape[2])
        if shape[0] < 128:
            v = v[0:shape[0]]
        t = T(name, v)
        t.rng = (a, b)
        keep = []
        for ra, rb, rt in self.retired:
            if ra < b and a < rb:
                for e2, seq in rt.rde.items():
                    if t.rde.get(e2, 0) < seq:
                        t.rde[e2] = seq
                t.rdd.extend(rt.rdd)
                if rt.lw is not None:
                    if rt.lw[0] == "e":
                        if t.rde.get(rt.lw[1], 0) < rt.lw[2]:
                            t.rde[rt.lw[1]] = rt.lw[2]
                    else:
                        t.rdd.append(rt.lw[1])
                if ra >= a and rb <= b:
                    continue
            keep.append((ra, rb, rt))
        self.retired = keep
        self.live.append((a, b, t))
        self.peak = max(self.peak, b)
        return t

    def free(self, *ts):
        for t in ts:
            for i, (a, b, tt) in enumerate(self.live):
                if tt is t:
                    self.live.pop(i)
                    self.retired.append((a, b, t))
                    break
            else:
                raise RuntimeError("free of unknown tile " + t.name)


class Ring:
    def __init__(self, tiles):
        self.t = tiles
        self.i = 0

    def next(self):
        t = self.t[self.i]
        self.i = (self.i + 1) % len(self.t)
        return t


class KB:
    def __init__(self, nc):
        self.nc = nc
        self.S = Sched(nc)
        self.A = Arena(nc, 207 * 1024)
        self.banks = [T("ps%d" % i, nc.alloc_psum_tensor("ps%d" % i, [128, 512], F32).ap(), psum=True) for i in range(8)]
        self.ps = Ring(self.banks)
        self._rr = 0

    def set_ring(self, n):
        self.ps = Ring(self.banks[0:n])

    def act(self, out, in_, func, R, W, scale=None, bias=None, accum=None):
        kw = {}
        if scale is not None:
            kw["scale"] = scale
        if bias is not None:
            kw["bias"] = bias
        if accum is not None:
            kw["accum_out"] = accum
        self.S.op("act", lambda e: e.activation(out=out, in_=in_, func=func, **kw), R, W)

    def mm(self, out, lhsT, rhs, R, W, start=True, stop=True, skip=False):
        self.S.op("pe", lambda e: e.matmul(out, lhsT=lhsT, rhs=rhs, start=start, stop=stop, skip_group_check=skip), R, W)

    def tr(self, out, in_, ident, R, W):
        self.S.op("pe", lambda e: e.transpose(out=out, in_=in_, identity=ident), R, W)

    def tt(self, eng, out, in0, in1, op, R, W):
        self.S.op(eng, lambda e: e.tensor_tensor(out=out, in0=in0, in1=in1, op=op), R, W)

    def ts(self, eng, out, in0, s1, op0, R, W, s2=None, op1=None):
        if op1 is None:
            self.S.op(eng, lambda e: e.tensor_scalar(out=out, in0=in0, scalar1=s1, scalar2=None, op0=op0), R, W)
        else:
            self.S.op(eng, lambda e: e.tensor_scalar(out=out, in0=in0, scalar1=s1, scalar2=s2, op0=op0, op1=op1), R, W)

    def stt(self, eng, out, in0, scalar, in1, op0, op1, R, W):
        self.S.op(eng, lambda e: e.scalar_tensor_tensor(out=out, in0=in0, scalar=scalar, in1=in1, op0=op0, op1=op1), R, W)

    def cp(self, eng, out, in_, R, W):
        if eng == "act":
            self.S.op("act", lambda e: e.activation(out=out, in_=in_, func=AF.Copy), R, W)
        else:
            self.S.op(eng, lambda e: e.tensor_copy(out=out, in_=in_), R, W)

    def recip(self, out, in_, R, W):
        self.S.op("dve", lambda e: e.reciprocal(out=out, in_=in_), R, W)

    def memset(self, eng, ap, val, W):
        self.S.op(eng, lambda e: e.memset(ap, val), (), W)

    def dma(self, eng, out, in_, R=(), W=(), **kw):
        self.S.dma(eng, out, in_, R, W, **kw)

    def alt(self, engs=("dve", "pool")):
        self._rr += 1
        return engs[self._rr % len(engs)]

    def rings(self, name, n, shape, dt):
        return Ring([self.A.alloc("%s%d" % (name, i), shape, dt) for i in range(n)])

    def free_ring(self, *rings):
        for r in rings:
            self.A.free(*r.t)


def sl(start, count, step):
    return slice(start, start + step * (count - 1) + 1, step)


def bank_bf(ps):
    return ps.ap.bitcast(BF16)


def norm_stats(k, xt, p, nb_tile, C, dimscale=1.0 / D):
    sm = C["sm"].next()
    hb = C["hb"].next()
    k.act(hb[0:p, :], xt[0:p, :], AF.Square, [xt], [hb, sm], accum=sm[0:p, 0:1])
    k.act(sm[0:p, 1:2], sm[0:p, 0:1], AF.Ln, [sm], [sm], scale=dimscale, bias=EPS)
    k.act(sm[0:p, 2:3], sm[0:p, 1:2], AF.Exp, [sm], [sm], scale=-0.5)
    k.stt("dve", hb[0:p, :], xt[0:p, :], sm[0:p, 2:3], nb_tile[0:p, :], ALU.mult, ALU.mult, [xt, sm, nb_tile], [hb])
    return hb


def transpose_to(k, hb, p, hT, c0, C):
    ps = k.ps.next()
    pb = bank_bf(ps)
    for c in range(8):
        k.tr(pb[:, c * 128:c * 128 + p], hb[0:p, c * 128:(c + 1) * 128], C["ident"][0:p, 0:p], [hb, C["ident"]], [ps])
    k.cp("act", hT[:, :, c0:c0 + p], pb[:, :].rearrange("q (c t) -> q c t", c=8)[:, :, 0:p], [ps], [hT])


def norm_transpose(k, xt, p, nb_tile, hT, c0, C, dimscale=1.0 / D):
    hb = norm_stats(k, xt, p, nb_tile, C, dimscale)
    transpose_to(k, hb, p, hT, c0, C)


def gdn_tile(k, C, G, hT_ap, qT_ap, kT_ap, vT_ap, S, Sb, deps, main, nf, mix_out, vm=None):
    w_a = C["w_a"]
    ident = C["ident"]
    sc = G["sc"].next()

    def bc(c0):
        return sc[:, c0:c0 + 4].unsqueeze(2).to_broadcast([128, 4, 128])

    def ps4(ps):
        return ps[:, :].rearrange("p (h c) -> p h c", h=4)

    psBA = k.ps.next()
    for kc in range(8):
        k.mm(psBA[:, 0:8], hT_ap[:, kc, :], w_a[:, kc, 2048:2056], deps + [C["wa_ba"]], [psBA], start=(kc == 0), stop=(kc == 7))
    k.act(sc[:, 0:4], psBA[:, 0:4], AF.Exp, [psBA], [sc], scale=-1.0)
    k.ts("dve", sc[:, 0:4], sc[:, 0:4], 1.0, ALU.add, [sc], [sc])
    k.recip(sc[:, 0:4], sc[:, 0:4], [sc], [sc])
    if vm is not None:
        k.ts("dve", sc[:, 0:4], sc[:, 0:4], vm[:, 0:1], ALU.mult, [sc, vm], [sc])
    k.ts("dve", sc[:, 4:8], sc[:, 0:4], -1.0, ALU.mult, [sc], [sc])
    yield
    k.tt("dve", sc[:, 8:12], psBA[:, 4:8], C["dtb"][:, :], ALU.add, [psBA, C["dtb"]], [sc])
    k.act(sc[:, 8:12], sc[:, 8:12], AF.Exp, [sc], [sc])
    k.act(sc[:, 8:12], sc[:, 8:12], AF.Ln, [sc], [sc], bias=1.0)
    k.tt("dve", sc[:, 8:12], sc[:, 8:12], C["negA"][:, :], ALU.mult, [sc, C["negA"]], [sc])
    if vm is not None:
        k.ts("dve", sc[:, 8:12], sc[:, 8:12], vm[:, 0:1], ALU.mult, [sc, vm], [sc])
    yield
    psG = k.ps.next()
    k.mm(psG[:, 0:4], C["mIU"][:, 0:128], sc[:, 8:12], [C["mIU"], sc], [psG])
    k.mm(psG[:, 4:8], C["onesf"][:, :], sc[:, 8:12], [C["onesf"], sc], [psG])
    k.cp("dve", sc[:, 12:20], psG[:, 0:8], [psG], [sc])
    k.act(sc[:, 20:28], sc[:, 12:20], AF.Exp, [sc], [sc])
    k.tt("dve", sc[:, 28:32], sc[:, 16:20], sc[:, 12:16], ALU.subtract, [sc], [sc])
    k.act(sc[:, 28:32], sc[:, 28:32], AF.Exp, [sc], [sc])
    yield
    tg = G["tg"].next()
    k.tt("dve", tg[:, :, :], C["mIU4"][:, :, :], bc(8), ALU.mult, [C["mIU4"], sc], [tg])
    psR = k.ps.next()
    k.mm(psR[:, :], C["onesf"][:, :], tg[:, :, :], [C["onesf"], tg], [psR])
    dec = G["dec"].next()
    k.tt("dve", dec[:, :, :], ps4(psR), bc(12), ALU.subtract, [psR, sc], [dec])
    k.act(dec[:, :, :], dec[:, :, :], AF.Exp, [dec], [dec])
    dS = G["dS"].next()
    k.stt("dve", dS[:, :, :], dec[:, :, :], 1.0, C["mSU4"][:, :, :], ALU.min, ALU.mult, [dec, C["mSU4"]], [dS])
    k.tt("dve", dS[:, :, :], dS[:, :, :], bc(4), ALU.mult, [dS, sc], [dS])
    if main:
        egr = G["egr"].next()
        k.act(egr[:, :, :], psR[:, :].rearrange("p (h c) -> p h c", h=4), AF.Exp, [psR], [egr])
        dI = G["dI"].next()
        k.stt("dve", dI[:, :, :], dec[:, :, :], 1.0, C["mIU4"][:, :, :], ALU.min, ALU.mult, [dec, C["mIU4"]], [dI])
    yield
    psk = k.ps.next()
    pbk = bank_bf(psk)
    for h in range(4):
        k.tr(pbk[:, h * 128:(h + 1) * 128], kT_ap[:, h, :], ident[:, :], deps + [ident], [psk])
    psv = k.ps.next()
    pbv = bank_bf(psv)
    for h in range(4):
        k.tr(pbv[:, h * 128:(h + 1) * 128], vT_ap[:, h, :], ident[:, :], deps + [ident], [psv])
    kg = G["kg"].next()
    kdec = G["kdec"].next()
    vtok = G["vtok"].next()
    ktok = G["e"].next()
    k.cp("act", ktok[:, :, :], pbk[:, 0:512].rearrange("p (h c) -> p h c", h=4), [psk], [ktok])
    k.cp("dve", vtok[:, :, :], pbv[:, 0:512].rearrange("p (h c) -> p h c", h=4), [psv], [vtok])
    k.tt("dve", kg[:, :, :], ktok[:, :, :], bc(20), ALU.mult, [ktok, sc], [kg])
    k.tt("dve", kdec[:, :, :], ktok[:, :, :], bc(28), ALU.mult, [ktok, sc], [kdec])
    yield
    psGm = k.ps.next()
    for h in range(4):
        k.mm(psGm[:, h * 128:(h + 1) * 128], kT_ap[:, h, :], kT_ap[:, h, :], deps, [psGm])
    R = G["R"].next()
    Rf = dec
    k.tt("dve", Rf[:, :, :], ps4(psGm), dS[:, :, :], ALU.mult, [psGm, dS], [Rf])
    k.cp("act", R[:, :, :], Rf[:, :, :], [Rf], [R])
    if main:
        psQK = k.ps.next()
        for h in range(4):
            k.mm(psQK[:, h * 128:(h + 1) * 128], kT_ap[:, h, :], qT_ap[:, h, :], deps, [psQK])
        QKT = G["QKT"].next()
        k.tt("dve", QKT[:, :, :], psQK[:, :].rearrange("p (h c) -> p h c", h=4), dI[:, :, :], ALU.mult, [psQK, dI], [QKT])
        qg = G["qg"].next()
        k.tt("dve", qg[:, :, :], qT_ap, egr[:, :, :], ALU.mult, deps + [egr], [qg])
    P = G["P"].next()
    k.tt("dve", P[:, :, :], R[:, :, :], C["id4b"][:, :, :], ALU.add, [R, C["id4b"]], [P])
    yield
    psr = k.ps.next()
    pbr = bank_bf(psr)
    for h in range(4):
        k.tr(pbr[:, h * 128:(h + 1) * 128], R[:, h, :], ident[:, :], [R, ident], [psr])
    RT = G["RT"].next()
    k.cp("act", RT[:, :, :], pbr[:, 0:512].rearrange("p (h c) -> p h c", h=4), [psr], [RT])
    for kk in range(1, nf + 1):
        yield
        psRk = psRTk = psP = None
        if kk <= nf - 2:
            psRk = k.ps.next()
            for h in range(4):
                k.mm(psRk[:, h * 128:(h + 1) * 128], RT[:, h, :], R[:, h, :], [RT, R], [psRk])
        if kk <= nf - 1:
            psRTk = k.ps.next()
            for h in range(4):
                k.mm(psRTk[:, h * 128:(h + 1) * 128], R[:, h, :], RT[:, h, :], [RT, R], [psRTk])
        if kk >= 2:
            psP = k.ps.next()
            for h in range(4):
                k.mm(psP[:, h * 128:(h + 1) * 128], RT[:, h, :], P[:, h, :], [RT, P], [psP])
        if psRk is not None:
            Rn = G["R"].next()
            k.cp("act", Rn[:, :, :], psRk[:, :].rearrange("p (h c) -> p h c", h=4), [psRk], [Rn])
        if psRTk is not None:
            RTn = G["RT"].next()
            k.cp("dve", RTn[:, :, :], psRTk[:, :].rearrange("p (h c) -> p h c", h=4), [psRTk], [RTn])
        if psP is not None:
            Pn = G["P"].next()
            k.tt("dve", Pn[:, :, :], psP[:, :].rearrange("p (h c) -> p h c", h=4), P[:, :, :], ALU.add, [psP, P], [Pn])
            P = Pn
        if psRk is not None:
            R = Rn
        if psRTk is not None:
            RT = RTn
    yield
    pst_ = k.ps.next()
    pbt_ = bank_bf(pst_)
    for h in range(4):
        k.tr(pbt_[:, h * 128:(h + 1) * 128], P[:, h, :], ident[:, :], [P, ident], [pst_])
    PTf = tg
    k.cp("act", PTf[:, :, :], pbt_[:, 0:512].rearrange("p (h c) -> p h c", h=4), [pst_], [PTf])
    psE = k.ps.next()
    for h in range(4):
        k.mm(psE[:, h * 128:(h + 1) * 128], PTf[:, h, :], Rf[:, h, :], [PTf, Rf], [psE])
    Et = G["Ec"].next()
    Ec = G["Ec"].next()
    k.tt("dve", Et[:, :, :], C["id4b"][:, :, :], P[:, :, :], ALU.subtract, [C["id4b"], P], [Et])
    k.tt("dve", Ec[:, :, :], psE[:, :].rearrange("p (h c) -> p h c", h=4), Et[:, :, :], ALU.add, [psE, Et], [Ec])
    yield
    psC1 = k.ps.next()
    for h in range(4):
        k.mm(psC1[:, h * 128:(h + 1) * 128], Ec[:, h, :], vtok[:, h, :], [Ec, vtok], [psC1])
    psC2 = k.ps.next()
    for h in range(4):
        k.mm(psC2[:, h * 128:(h + 1) * 128], Ec[:, h, :], kg[:, h, :], [Ec, kg], [psC2])
    vtok2 = G["vtok"].next()
    kg2 = G["kg"].next()
    k.tt("dve", vtok2[:, :, :], psC1[:, :].rearrange("p (h c) -> p h c", h=4), vtok[:, :, :], ALU.add, [psC1, vtok], [vtok2])
    k.tt("dve", kg2[:, :, :], psC2[:, :].rearrange("p (h c) -> p h c", h=4), kg[:, :, :], ALU.add, [psC2, kg], [kg2])
    vtok, kg = vtok2, kg2
    yield
    psU = k.ps.next()
    for h in range(4):
        k.mm(psU[:, h * 128:(h + 1) * 128], P[:, h, :], vtok[:, h, :], [P, vtok], [psU])
    psW = k.ps.next()
    for h in range(4):
        k.mm(psW[:, h * 128:(h + 1) * 128], kg[:, h, :], P[:, h, :], [P, kg], [psW])
    ub = G["ub"].next()
    k.cp("dve", ub[:, :, :], ps4(psU), [psU], [ub])
    wT = G["wT"].next()
    k.cp("act", wT[:, :, :], psW[:, :].rearrange("p (h c) -> p h c", h=4), [psW], [wT])
    yield
    psS1 = k.ps.next()
    for h in range(4):
        k.mm(psS1[:, h * 128:(h + 1) * 128], wT[:, h, :], Sb[:, h, :], [wT, Sb], [psS1])
    e = G["e"].next()
    k.tt("dve", ub[:, :, :], ps4(psS1), ub[:, :, :], ALU.subtract, [psS1, ub], [ub])
    k.tt("dve", e[:, :, :], ub[:, :, :], bc(4), ALU.mult, [ub, sc], [e])
    if main:
        psO = k.ps.next()
        for h in range(4):
            k.mm(psO[:, h * 128:(h + 1) * 128], qg[:, h, :], Sb[:, h, :], [qg, Sb], [psO], start=True, stop=False)
            k.mm(psO[:, h * 128:(h + 1) * 128], QKT[:, h, :], e[:, h, :], [QKT, e], [psO], start=False, stop=True)
    if main:
        o32 = ub
        k.cp("act", o32[:, :, :], psO[:, :].rearrange("p (h c) -> p h c", h=4), [psO], [o32])
    psSn = k.ps.next()
    for h in range(4):
        k.mm(psSn[:, h * 128:(h + 1) * 128], kdec[:, h, :], e[:, h, :], [kdec, e], [psSn])
    k.tt("dve", S[:, :, :], S[:, :, :], bc(24), ALU.mult, [S, sc], [S])
    k.tt("dve", S[:, :, :], S[:, :, :], ps4(psSn), ALU.add, [S, psSn], [S])
    k.cp("act", Sb[:, :, :], S[:, :, :], [S], [Sb])
    if not main:
        return
    yield
    jk = G["jk"].next()
    for h in range(4):
        k.act(jk[:, :], o32[:, h, :], AF.Square, [o32], [jk, sc], accum=sc[:, 32 + h:33 + h])
    k.act(sc[:, 32:36], sc[:, 32:36], AF.Ln, [sc], [sc], scale=1.0 / 128, bias=EPS)
    k.act(sc[:, 32:36], sc[:, 32:36], AF.Exp, [sc], [sc], scale=-0.5)
    yield
    psZ = k.ps.next()
    for kc in range(8):
        k.mm(psZ[:, :], hT_ap[:, kc, :], w_a[:, kc, 1536:2048], deps + [C["wa_z"]], [psZ], start=(kc == 0), stop=(kc == 7))
    ez = G["ez"].next()
    k.act(ez[:, :], psZ[:, :], AF.Exp, [psZ], [ez], scale=-1.0)
    k.act(ez[:, :], ez[:, :], AF.Ln, [ez], [ez], bias=1.0)
    k.act(ez[:, :], ez[:, :], AF.Exp, [ez], [ez], scale=-1.0)
    zn = G["zn"].next()
    k.tt("dve", zn[:, :], psZ[:, :], C["noa"][:, :], ALU.mult, [psZ, C["noa"]], [zn])
    k.tt("dve", zn[:, :], zn[:, :], ez[:, :], ALU.mult, [zn, ez], [zn])
    yield
    og = G["og"].next()
    k.tt("dve", o32[:, :, :], o32[:, :, :], bc(32), ALU.mult, [o32, sc], [o32])
    k.tt("dve", og[:, :].rearrange("p (h c) -> p h c", h=4), o32[:, :, :], zn[:, :].rearrange("p (h c) -> p h c", h=4), ALU.mult, [o32, zn], [og])
    mix_out(og)


def run_interleaved(gens, offs=3, maxact=2):
    active, pending, steps = [], list(gens), {}
    while active or pending:
        if pending and len(active) < maxact and (not active or steps[id(active[-1])] >= offs):
            gnew = pending.pop(0)
            active.append(gnew)
            steps[id(gnew)] = 0
        for gg in list(active):
            try:
                next(gg)
                steps[id(gg)] += 1
            except StopIteration:
                active.remove(gg)


EXTRA_RINGS = [("R", 2, [128, 4, 128], BF16), ("RT", 2, [128, 4, 128], BF16), ("P", 2, [128, 4, 128], BF16),
               ("kg", 2, [128, 4, 128], BF16), ("vtok", 2, [128, 4, 128], BF16), ("Ec", 2, [128, 4, 128], BF16),
               ("sc", 1, [128, 40], F32), ("tg", 1, [128, 4, 128], F32), ("dec", 1, [128, 4, 128], F32),
               ("egr", 1, [128, 4, 128], F32), ("ub", 1, [128, 4, 128], F32)] + \
              [(nm, 1, [128, 4, 128], BF16) for nm in ("dS", "dI", "kdec", "QKT", "qg", "wT", "e")]


def silu_from_psum(k, G, ps_ap, psT, n):
    e32 = G["e32"].next()
    c32 = G["c32"].next()
    k.act(e32[:, 0:n], ps_ap, AF.Exp, [psT], [e32], scale=-1.0)
    k.act(e32[:, 0:n], e32[:, 0:n], AF.Ln, [e32], [e32], bias=1.0)
    k.act(e32[:, 0:n], e32[:, 0:n], AF.Exp, [e32], [e32], scale=-1.0)
    k.tt("dve", c32[:, 0:n], ps_ap, e32[:, 0:n], ALU.mult, [psT, e32], [c32])
    return c32


def l2norm_chunk(k, C, G, c32, n, out_ap, outT, qscale):
    sq = G["sq"].next()
    k.tt("dve", sq[:, 0:n], c32[:, 0:n], c32[:, 0:n], ALU.mult, [c32], [sq])
    psC = k.ps.next()
    k.mm(psC[:, 0:n], C["onesb"][:, :], sq[:, 0:n], [C["onesb"], sq], [psC])
    l32 = G["l32"].next()
    k.act(l32[:, 0:n], psC[:, 0:n], AF.Ln, [psC], [l32], bias=EPS)
    if qscale:
        k.act(l32[:, 0:n], l32[:, 0:n], AF.Exp, [l32], [l32], scale=-0.5, bias=C["lnq"][:, 0:1])
        rd = [c32, l32, C["lnq"]]
    else:
        k.act(l32[:, 0:n], l32[:, 0:n], AF.Exp, [l32], [l32], scale=-0.5)
        rd = [c32, l32]
    k.tt("dve", out_ap, c32[:, 0:n], l32[:, 0:n], ALU.mult, rd, [outT])


def stage_G(k, C, IO):
    A = k.A
    w_a = A.alloc("w_a", [128, 8, 2056], BF16)
    C["w_a"] = w_a
    wa_units = {nm: T("wa_" + nm, None) for nm in ("q", "k", "v", "z", "ba")}
    for t in wa_units.values():
        for e2, seq in w_a.rde.items():
            t.rde[e2] = seq
        t.rdd.extend(w_a.rdd)
    C["wa_g"] = [wa_units["q"], wa_units["k"], wa_units["v"]]
    C["wa_z"], C["wa_ba"] = wa_units["z"], wa_units["ba"]
    for nm, c0, c1 in (("k", 512, 1024), ("v", 1024, 1536), ("ba", 2048, 2056), ("q", 0, 512), ("z", 1536, 2048)):
        k.dma("pool", w_a[:, :, c0:c1], IO["w_in"][:, c0:c1].rearrange("(c p) n -> p c n", p=128), W=[wa_units[nm]],
              allow_slow_non_contiguous=(nm == "ba"))
    nmb = A.alloc("nmb", [128, 1024], F32)
    k.dma("sp", nmb[:, :], IO["norm_mix"].partition_broadcast(128), W=[nmb])
    wconv = A.alloc("wconv", [128, 4, 12], F32)
    for i in range(4):
        k.dma("sp", wconv[:, i, :], IO["w_conv"][i].rearrange("(c p) -> p c", p=128), W=[wconv], allow_slow_non_contiguous=True)
    diag = A.alloc("diag", [128, 12, 4, 128], BF16)
    for ch in range(12):
        for i in range(4):
            k.ts(k.alt(), diag[:, ch, i, :], C["identf"][:, 0:128], wconv[:, i, ch:ch + 1], ALU.mult, [C["identf"], wconv], [diag])
    dtb = A.alloc("dtb", [128, 4], F32)
    negA = A.alloc("negA", [128, 4], F32)
    C["dtb"], C["negA"] = dtb, negA
    k.dma("sp", dtb[:, :], IO["dt_bias"].partition_broadcast(128), W=[dtb])
    k.dma("sp", negA[:, :], IO["a_log"].partition_broadcast(128), W=[negA])
    k.act(negA[:, :], negA[:, :], AF.Exp, [negA], [negA])
    k.ts("dve", negA[:, :], negA[:, :], -1.0, ALU.mult, [negA], [negA])
    noa = A.alloc("noa", [128, 512], F32)
    C["noa"] = noa
    for h in range(4):
        k.dma("sp", noa[:, h * 128:(h + 1) * 128], IO["noa"].partition_broadcast(128), W=[noa])
    lnq = A.alloc("lnq", [128, 1], F32)
    C["lnq"] = lnq
    k.memset("pool", lnq[:, :], math.log(128.0 ** -0.5), [lnq])

    G = {}
    C["sm"] = k.rings("sm", 4, [128, 8], F32)
    C["hb"] = k.rings("hb", 4, [128, 1024], BF16)
    xr = k.rings("xr", 2, [128, 1024], F32)
    hT = A.alloc("hT", [128, 8, 512], BF16)
    ext = A.alloc("ext", [128, 12, 515], BF16)
    qkT = A.alloc("qkT", [128, 8, 512], BF16)
    vT = A.alloc("vT", [128, 4, 512], BF16)
    S = A.alloc("S", [128, 4, 128], F32)
    Sb = A.alloc("Sb", [128, 4, 128], BF16)
    for nm, n in (("e32", 2), ("c32", 4), ("l32", 2), ("ez", 1), ("zn", 1)):
        G[nm] = k.rings(nm, n, [128, 512], F32)
    G["sq"] = k.rings("sq", 2, [128, 512], BF16)
    G["og"] = k.rings("og", 2, [128, 512], BF16)
    G["sc"] = k.rings("sc", 3, [128, 40], F32)
    G["jk"] = k.rings("jk", 1, [128, 128], F32)
    for nm, n in (("tg", 2), ("dec", 2), ("egr", 2), ("ub", 2)):
        G[nm] = k.rings(nm, n, [128, 4, 128], F32)
    for nm, n in (("dS", 2), ("dI", 2), ("kg", 4), ("kdec", 2), ("vtok", 4), ("R", 4), ("RT", 4), ("P", 4), ("Ec", 4),
                  ("QKT", 2), ("qg", 2), ("wT", 2), ("e", 2)):
        G[nm] = k.rings(nm, n, [128, 4, 128], BF16)

    mixT_a = C["mixT_a"]
    k.memset("pool", ext[:, :, :], 0.0, [ext])
    k.memset("pool", S[:, :, :], 0.0, [S])
    k.memset("dve", Sb[:, :, :], 0.0, [Sb])

    def feature_chunks(pchs, chs, hT_ap, hdeps, n, ext_dst, conv_rhs, qk_out, v_out, outTs, conv_out=None):
        for ch in pchs:
            psA = k.ps.next()
            for kc in range(8):
                k.mm(psA[:, 0:n], w_a[:, kc, ch * 128:(ch + 1) * 128], hT_ap(kc), hdeps + [C["wa_g"][ch // 4]], [psA], start=(kc == 0), stop=(kc == 7))
            ext_dst(ch, psA)
        def stage1(pair):
            st1 = []
            for ch in pair:
                psB = k.ps.next()
                for i in range(4):
                    k.mm(psB[:, 0:n] if conv_out is None else conv_out(psB), diag[:, ch, i, :], conv_rhs(ch, i), [diag] + outTs["ext"], [psB], start=(i == 0), stop=(i == 3))
                st1.append((ch, psB, G["e32"].next(), G["c32"].next()))
            for ch, psB, e32, c32 in st1:
                k.act(e32[:, 0:n], psB[:, 0:n], AF.Exp, [psB], [e32], scale=-1.0)
            for ch, psB, e32, c32 in st1:
                k.act(e32[:, 0:n], e32[:, 0:n], AF.Ln, [e32], [e32], bias=1.0)
            for ch, psB, e32, c32 in st1:
                k.act(e32[:, 0:n], e32[:, 0:n], AF.Exp, [e32], [e32], scale=-1.0)
            for ch, psB, e32, c32 in st1:
                k.tt("dve", c32[:, 0:n], psB[:, 0:n], e32[:, 0:n], ALU.mult, [psB, e32], [c32])
            return [(ch, c32) for ch, psB, e32, c32 in st1]

        def stage2(items):
            qk = [(ch, c32) for ch, c32 in items if ch < 8]
            for ch, c32 in items:
                if ch >= 8:
                    k.cp("dve", v_out(ch - 8), c32[:, 0:n], [c32], [outTs["v"]])
            st2 = []
            for ch, c32 in qk:
                sq = G["sq"].next()
                k.tt("dve", sq[:, 0:n], c32[:, 0:n], c32[:, 0:n], ALU.mult, [c32], [sq])
                psC = k.ps.next()
                k.mm(psC[:, 0:n], C["onesb"][:, :], sq[:, 0:n], [C["onesb"], sq], [psC])
                st2.append((ch, c32, psC, G["l32"].next()))
            for ch, c32, psC, l32 in st2:
                k.act(l32[:, 0:n], psC[:, 0:n], AF.Ln, [psC], [l32], bias=EPS)
            for ch, c32, psC, l32 in st2:
                if ch < 4:
                    k.act(l32[:, 0:n], l32[:, 0:n], AF.Exp, [l32, C["lnq"]], [l32], scale=-0.5, bias=C["lnq"][:, 0:1])
                else:
                    k.act(l32[:, 0:n], l32[:, 0:n], AF.Exp, [l32], [l32], scale=-0.5)
            for ch, c32, psC, l32 in st2:
                k.tt("dve", qk_out(ch), c32[:, 0:n], l32[:, 0:n], ALU.mult, [c32, l32], [outTs["qk"]])

        pairs = [chs[i:i + 2] for i in range(0, len(chs), 2)]
        pend = None
        for pair in pairs:
            cur = stage1(pair)
            if pend is not None:
                stage2(pend)
            pend = cur
        if pend is not None:
            stage2(pend)

    for st in range(NTOK // 512):
        main = st >= NPRE // 512
        hbs = []
        for tt4 in range(4):
            tok0 = st * 512 + tt4 * 128
            xt = xr.next()
            k.dma("sp", xt[:, :], IO["xp"][tok0:tok0 + 128, :], W=[xt])
            hbs.append(norm_stats(k, xt, 128, nmb, C))
        for tt4 in range(4):
            transpose_to(k, hbs[tt4], 128, hT, tt4 * 128, C)
        chs = list(range(12)) if main else list(range(4, 12))
        pchs = list(range(12)) if st >= NPRE // 512 - 1 else chs

        def ext_dst(ch, psA):
            k.cp("dve", ext[:, ch, 3:515], psA[:, :], [psA], [ext])

        feature_chunks(pchs, chs, lambda kc: hT[:, kc, :], [hT], 512, ext_dst,
                       lambda ch, i: ext[:, ch, i:i + 512],
                       lambda ch: qkT[:, ch, :], lambda j: vT[:, j, :], {"ext": [ext], "qk": qkT, "v": vT})
        if st == NTOK // 512 - 1:
            for j in range(3):
                psT3 = k.ps.next()
                for kc in range(8):
                    k.mm(psT3[0:3, :], hT[:, kc, 509:512], w_a[:, kc, j * 512:(j + 1) * 512], [hT, C["wa_g"][j]], [psT3], start=(kc == 0), stop=(kc == 7))
                pre3 = G["l32"].next()
                k.cp("dve", pre3[0:3, :], psT3[0:3, :], [psT3], [pre3])
                k.dma("sp", IO["conv_p"][:, j * 512:(j + 1) * 512], pre3[0:3, :], R=[pre3])
        halo = G.setdefault("halo", A.alloc("halo", [128, 12, 3], BF16))
        k.cp("pool", halo[:, :, :], ext[:, :, 512:515], [ext], [halo])
        gens = []
        for tt4 in range(4):
            cs = slice(tt4 * 128, (tt4 + 1) * 128)
            gcol = st * 512 + tt4 * 128 - NPRE

            def mix_out(og, gcol=gcol):
                psm = k.ps.next()
                pbm = bank_bf(psm)
                for h in range(4):
                    k.tr(pbm[:, h * 128:(h + 1) * 128], og[:, h * 128:(h + 1) * 128], C["ident"][:, :], [og, C["ident"]], [psm])
                k.cp("act", mixT_a[:, :, gcol:gcol + 128], pbm[:, 0:512].rearrange("p (h c) -> p h c", h=4), [psm], [mixT_a])

            gens.append(gdn_tile(k, C, G, hT[:, :, cs], qkT[:, 0:4, cs], qkT[:, 4:8, cs], vT[:, :, cs], S, Sb, [hT, qkT, vT], main, 7, mix_out))
        k.free_ring(C["hb"], xr, G["e32"], G["c32"], G["l32"], G["sq"])
        extra = {}
        for nm, n, shp, dt in EXTRA_RINGS:
            extra[nm] = [A.alloc("x_%s%d" % (nm, i), shp, dt) for i in range(n)]
            G[nm].t.extend(extra[nm])
        run_interleaved(gens, offs=3, maxact=3)
        for nm, tl in extra.items():
            for t in tl:
                G[nm].t.remove(t)
            G[nm].i = 0
            A.free(*tl)
        C["hb"] = k.rings("hb", 4, [128, 1024], BF16)
        xr = k.rings("xr", 2, [128, 1024], F32)
        for nm, n_ in (("e32", 2), ("c32", 4), ("l32", 2)):
            G[nm] = k.rings(nm, n_, [128, 512], F32)
        G["sq"] = k.rings("sq", 2, [128, 512], BF16)
        k.cp("pool", ext[:, :, 0:3], halo[:, :, :], [halo], [ext])
    k.dma("sp", IO["rec_p"].rearrange("h d e -> d h e"), S[:, :, :], R=[S])
    A.free(hT, ext, qkT, vT, G["halo"])

    xt = xr.next()
    k.dma("sp", xt[0:NS, :], IO["xs"][:, :], W=[xt])
    hTs = A.alloc("hTs", [128, 8, NS], BF16)
    norm_transpose(k, xt, NS, nmb, hTs, 0, C)
    exts = A.alloc("exts", [128, 12, 4, 11], BF16)
    sct = A.alloc("sct", [12, 1536], F32)
    k.dma("sp", sct[0:12, :], IO["sconv"].rearrange("s i c -> (s i) c"), W=[sct])
    psh = k.ps.next()
    for ch in range(12):
        k.tr(psh[:, ch * 12:(ch + 1) * 12], sct[0:12, ch * 128:(ch + 1) * 128], C["identf"][0:12, 0:12], [sct, C["identf"]], [psh])
    k.cp("dve", exts[:, :, :, 0:3], psh[:, 0:144].rearrange("p (c s i) -> p c s i", c=12, s=4), [psh], [exts])
    qks = A.alloc("qks", [128, 8, NS], BF16)
    vs = A.alloc("vs", [128, 4, NS], BF16)

    def ext_dst_s(ch, psA):
        k.cp("act", exts[:, ch, :, 3:11], psA[:, 0:NS].rearrange("p (s t) -> p s t", s=4), [psA], [exts])

    feature_chunks(list(range(12)), list(range(12)), lambda kc: hTs[:, kc, :], [hTs], NS, ext_dst_s,
                   lambda ch, i: exts[:, ch, :, i:i + 8],
                   lambda ch: qks[:, ch, :], lambda j: vs[:, j, :], {"ext": [exts], "qk": qks, "v": vs},
                   conv_out=lambda psB: psB[:, 0:NS].rearrange("p (s t) -> p s t", s=4))
    pres = A.alloc("pres", [NS, 1536], F32)
    for j in range(3):
        psT3 = k.ps.next()
        for kc in range(8):
            k.mm(psT3[0:NS, :], hTs[:, kc, :], w_a[:, kc, j * 512:(j + 1) * 512], [hTs, C["wa_g"][j]], [psT3], start=(kc == 0), stop=(kc == 7))
        k.cp("dve", pres[0:NS, j * 512:(j + 1) * 512], psT3[0:NS, :], [psT3], [pres])
    for s in range(4):
        k.dma("sp", IO["conv_s"][s, :, :], pres[8 * s + 5:8 * s + 8, :], R=[pres])
    hpad = k.rings("hpad", 2, [128, 8, 128], BF16)
    qkpad = k.rings("qkpad", 2, [128, 8, 128], BF16)
    vpad = k.rings("vpad", 2, [128, 4, 128], BF16)
    Ss = k.rings("Ss", 2, [128, 4, 128], F32)
    Sbs = k.rings("Sbs", 2, [128, 4, 128], BF16)
    for r in (hpad, qkpad, vpad):
        for t in r.t:
            k.memset(k.alt(), t[:, :, :], 0.0, [t])
    sgens = []
    for s in range(4):
        hp, qp, vp, S_s, Sb_s = hpad.next(), qkpad.next(), vpad.next(), Ss.next(), Sbs.next()
        k.cp("pool", hp[:, :, 0:8], hTs[:, :, 8 * s:8 * s + 8], [hTs], [hp])
        k.cp("pool", qp[:, :, 0:8], qks[:, :, 8 * s:8 * s + 8], [qks], [qp])
        k.cp("pool", vp[:, :, 0:8], vs[:, :, 8 * s:8 * s + 8], [vs], [vp])
        k.dma("sp", S_s[:, :, :], IO["srec"][s].rearrange("h d e -> d h e"), W=[S_s])
        k.cp("act", Sb_s[:, :, :], S_s[:, :, :], [S_s], [Sb_s])

        def mix_out_s(og, s=s):
            psm = k.ps.next()
            pbm = bank_bf(psm)
            for h in range(4):
                k.tr(pbm[:, h * 8:(h + 1) * 8], og[0:8, h * 128:(h + 1) * 128], C["ident"][0:8, 0:8], [og, C["ident"]], [psm])
            k.cp("act", mixT_a[:, :, NMAIN + 8 * s:NMAIN + 8 * s + 8], pbm[:, 0:32].rearrange("p (h c) -> p h c", h=4), [psm], [mixT_a])

        def seq_gen(s=s, hp=hp, qp=qp, vp=vp, S_s=S_s, Sb_s=Sb_s, mix_out_s=mix_out_s):
            yield from gdn_tile(k, C, G, hp[:, :, :], qp[:, 0:4, :], qp[:, 4:8, :], vp[:, :, :], S_s, Sb_s, [hp, qp, vp], True, 3, mix_out_s, vm=C["vmask"])
            k.dma("sp", IO["rec_s"][s].rearrange("h d e -> d h e"), S_s[:, :, :], R=[S_s])

        sgens.append(seq_gen())
        if s % 2 == 1:
            run_interleaved(sgens)
            sgens = []

    merge_free(A, w_a, list(wa_units.values()))
    A.free(nmb, wconv, diag, dtb, negA, noa, lnq, S, Sb, hTs, exts, sct, qks, vs, pres)
    k.free_ring(C["sm"], C["hb"], xr, hpad, qkpad, vpad, Ss, Sbs)
    for nm, r in G.items():
        if isinstance(r, Ring):
            k.free_ring(r)
    G.clear()


def merge_free(A, parent, children):
    for ch in children:
        for e2, seq in ch.rde.items():
            if parent.rde.get(e2, 0) < seq:
                parent.rde[e2] = seq
        parent.rdd.extend(ch.rdd)
        if ch.lw is not None:
            if ch.lw[0] == "e":
                if parent.rde.get(ch.lw[1], 0) < ch.lw[2]:
                    parent.rde[ch.lw[1]] = ch.lw[2]
            else:
                parent.rdd.append(ch.lw[1])
    A.free(parent)


def setup_consts(k, C, IO):
    A = k.A
    cf = IO["cf32"]
    identf = A.alloc("identf", [128, 128], F32)
    mIU = A.alloc("mIU", [128, 128], F32)
    onesf = A.alloc("onesf", [128, 128], F32)
    mSU4 = A.alloc("mSU4", [128, 4, 128], F32)
    mIU4 = A.alloc("mIU4", [128, 4, 128], F32)
    id4f = A.alloc("id4f", [128, 4, 128], F32)
    k.dma("sp", identf[:, :], cf[:, 0:128], W=[identf])
    k.dma("sp", id4f[:, :, :], cf[:, 0:512].rearrange("p (h c) -> p h c", h=4), W=[id4f])
    k.dma("sp", mSU4[:, :, :], cf[:, 512:1024].rearrange("p (h c) -> p h c", h=4), W=[mSU4])
    k.dma("sp", mIU4[:, :, :], cf[:, 1024:1536].rearrange("p (h c) -> p h c", h=4), W=[mIU4])
    k.dma("sp", mIU[:, :], cf[:, 1024:1152], W=[mIU])
    k.dma("sp", onesf[:, :], cf[:, 1536:1664], W=[onesf])
    ident = A.alloc("ident", [128, 128], BF16)
    onesb = A.alloc("onesb", [128, 128], BF16)
    id4b = A.alloc("id4b", [128, 4, 128], BF16)
    k.cp("dve", ident[:, :], identf[:, :], [identf], [ident])
    k.cp("dve", onesb[:, :], onesf[:, :], [onesf], [onesb])
    k.cp("dve", id4b[:, :, :], id4f[:, :, :], [id4f], [id4b])
    vmask = A.alloc("vmask", [128, 1], F32)
    edge = A.alloc("edge", [128, 1], F32)
    k.dma("sp", vmask[:, :], IO["vmask"][:, :], W=[vmask])
    k.dma("sp", edge[:, :], IO["edge8"][:, :], W=[edge])
    C.update(identf=identf, mIU=mIU, onesf=onesf, mSU4=mSU4, mIU4=mIU4, ident=ident, onesb=onesb, id4b=id4b,
             vmask=vmask, edge=edge)
    A.free(id4f)


def stage_K(k, C, IO):
    A = k.A
    w_b = A.alloc("w_b", [128, 8, 1536], BF16)
    for kc in range(8):
        k.dma("pool", w_b[:, kc, :], IO["w_in"][kc * 128:(kc + 1) * 128, 2056:3592], W=[w_b])
    nmb = A.alloc("nmb2", [128, 1024], F32)
    k.dma("sp", nmb[:, :], IO["norm_mix"].partition_broadcast(128), W=[nmb])
    C["sm"] = k.rings("smk", 6, [128, 8], F32)
    C["hb"] = k.rings("hbk", 5, [128, 1024], BF16)
    xr = k.rings("xrk", 4, [128, 1024], F32)
    hTr = k.rings("hTk", 2, [128, 8, 512], BF16)
    kT_b = A.alloc("kT_b", [128, 4, NTOK], BF16)
    vT_b = A.alloc("vT_b", [128, 4, NTOK], BF16)
    qT_b = A.alloc("qT_b", [128, 4, NMAIN], BF16)
    kv = [T("kv%d" % st, None) for st in range(NTOK // 512)]
    for t in kv:
        for par in (kT_b, vT_b, qT_b):
            for e2, seq in par.rde.items():
                if t.rde.get(e2, 0) < seq:
                    t.rde[e2] = seq
            t.rdd.extend(par.rdd)
    C.update(kT_b=kT_b, vT_b=vT_b, qT_b=qT_b, kv=kv)
    ost = k.rings("ost", 2, [128, 512], F32)
    def k_stats(st):
        hbs = []
        for tt4 in range(4):
            tok0 = st * 512 + tt4 * 128
            xt = xr.next()
            k.dma("sp", xt[:, :], IO["xp"][tok0:tok0 + 128, :], W=[xt])
            hbs.append(norm_stats(k, xt, 128, nmb, C))
        return hbs

    def k_tr(hbs):
        hT = hTr.next()
        for tt4 in range(4):
            transpose_to(k, hbs[tt4], 128, hT, tt4 * 128, C)
        return hT

    hT_next = k_tr(k_stats(0))
    for st in range(NTOK // 512):
        main = st >= NPRE // 512
        hT = hT_next
        hbs_next = k_stats(st + 1) if st + 1 < NTOK // 512 else None
        for ch in (range(12) if main else range(4, 12)):
            psA = k.ps.next()
            for kc in range(8):
                k.mm(psA[:, :], w_b[:, kc, ch * 128:(ch + 1) * 128], hT[:, kc, :], [hT, w_b], [psA], start=(kc == 0), stop=(kc == 7))
            if ch < 4:
                dst = qT_b[:, ch, (st * 512 - NPRE):(st * 512 - NPRE) + 512]
            elif ch < 8:
                dst = kT_b[:, ch - 4, st * 512:(st + 1) * 512]
            else:
                dst = vT_b[:, ch - 8, st * 512:(st + 1) * 512]
            k.cp(k.alt(("act", "dve")), dst, psA[:, :], [psA], [kv[st]])
        if hbs_next is not None:
            hT_next = k_tr(hbs_next)
        if main and KSTOP >= 2:
            for tt4 in range(4):
                tok0 = st * 512 + tt4 * 128
                for src, dstname in ((kT_b, "wk_p"), (vT_b, "wv_p")):
                    pst = k.ps.next()
                    pbt = bank_bf(pst)
                    for c in range(4):
                        k.tr(pbt[:, c * 128:(c + 1) * 128], src[:, c, tok0:tok0 + 128], C["ident"][:, :], [kv[st], C["ident"]], [pst])
                    o = ost.next()
                    k.cp(k.alt(("act", "dve")), o[:, :], pbt[:, 0:512], [pst], [o])
                    k.dma("sp", IO[dstname][tok0 - NPRE:tok0 - NPRE + 128, :], o[:, :], R=[o])
    if KSTOP < 3:
        return
    xt = xr.next()
    k.dma("sp", xt[0:NS, :], IO["xs"][:, :], W=[xt])
    hTs = A.alloc("hTs2", [128, 8, NS], BF16)
    norm_transpose(k, xt, NS, nmb, hTs, 0, C)
    qTs = A.alloc("qTs", [128, 4, NS], BF16)
    kTn = A.alloc("kTn", [128, 4, NS], BF16)
    for ch in range(8):
        psA = k.ps.next()
        for kc in range(8):
            k.mm(psA[:, 0:NS], w_b[:, kc, ch * 128:(ch + 1) * 128], hTs[:, kc, :], [hTs, w_b], [psA], start=(kc == 0), stop=(kc == 7))
        dstT = qTs if ch < 4 else kTn
        k.cp("act", dstT[:, ch % 4, :], psA[:, 0:NS], [psA], [dstT])
    if KSTOP < 4:
        return
    vn_aug = A.alloc("vn_aug", [NS, 8, 66], BF16)
    k.memset("pool", vn_aug[:, :, :], 1.0, [vn_aug])
    for j, dstname in ((1, "wk_s"), (2, "wv_s")):
        psA = k.ps.next()
        for kc in range(8):
            k.mm(psA[0:NS, :], hTs[:, kc, :], w_b[:, kc, j * 512:(j + 1) * 512], [hTs, w_b], [psA], start=(kc == 0), stop=(kc == 7))
        o = ost.next()
        k.cp("dve", o[0:NS, :], psA[0:NS, :], [psA], [o])
        for s in range(4):
            if "D" not in KOFF:
                k.dma("sp", IO[dstname][s, 2040:2048, :], o[8 * s:8 * s + 8, :], R=[o])
        if j == 2 and "A" not in KOFF:
            k.cp("act", vn_aug[:, :, 0:64], psA[0:NS, :].rearrange("p (h e) -> p h e", h=8), [psA], [vn_aug])
    if KSTOP < 5:
        return
    Qbd = A.alloc("Qbd", [128, 4, 4, 48], BF16)
    k.memset("pool", Qbd[:, :, :, :], 0.0, [Qbd])
    for c in range(4):
        for br in range(3):
            k.cp(k.alt(), Qbd[0:64, c, :, br * 8:br * 8 + 8], qTs[0:64, c, :].rearrange("p (s t) -> p s t", s=4), [qTs], [Qbd])
            k.cp(k.alt(), Qbd[64:128, c, :, 24 + br * 8:32 + br * 8], qTs[64:128, c, :].rearrange("p (s t) -> p s t", s=4), [qTs], [Qbd])
    C.update(kTn=kTn, vn_aug=vn_aug, Qbd=Qbd)
    A.free(w_b, nmb, hTs, qTs)
    k.free_ring(C["sm"], C["hb"], xr, hTr, ost)


def finalize_attn(k, C, F, acc_ap, accT, n, dst):
    for c0 in range(0, n, 512):
        w = min(512, n - c0)
        sq = F["sq"].next()
        k.act(sq[0:65, 0:w], acc_ap[0:65, c0:c0 + w], AF.Square, [accT], [sq])
        psF = k.ps.next()
        k.mm(psF[0:64, 0:w], C["gmb"][0:65, 0:64], sq[0:65, 0:w], [C["gmb"], sq], [psF])
        l32 = F["l32"].next()
        k.act(l32[0:64, 0:w], psF[0:64, 0:w], AF.Ln, [psF], [l32])
        k.act(l32[0:64, 0:w], l32[0:64, 0:w], AF.Exp, [l32], [l32], scale=-0.5)
        dap, dT = dst(c0, w)
        k.stt("dve", dap, acc_ap[0:64, c0:c0 + w], C["nob"][0:64, 0:1], l32[0:64, 0:w], ALU.mult, ALU.mult, [accT, C["nob"], l32], [dT])


def stage_B(k, C, IO):
    A = k.A
    kT_b, vT_b, qT_b, kv = C["kT_b"], C["vT_b"], C["qT_b"], C["kv"]
    identb = C["ident"]
    pb = A.alloc("pbias", [128, 8, 3, 256], BF16)
    k.dma("sp", pb[:, :, :, :], IO["pbias"][:, :, :, :], W=[pb])
    pe_ = A.alloc("pedge", [128, 8, 3, 128], BF16)
    for h in range(8):
        k.ts(k.alt(), pe_[:, h, :, :], pb[:, h, :, 128:256], C["edge"][:, 0:1], ALU.add, [pb, C["edge"]], [pe_])
    gmb = A.alloc("gmb", [65, 64], BF16)
    k.dma("sp", gmb[:, :], IO["gmb"][:, :], W=[gmb])
    nob = A.alloc("nob", [64, 1], F32)
    k.dma("sp", nob[:, :], IO["nob"].rearrange("(p o) -> p o", o=1), W=[nob])
    C.update(gmb=gmb, nob=nob)
    mixT_b = C["mixT_b"] = A.alloc("mixT_b", [128, 4, NMAIN + NS], BF16)
    F = {"sq": k.rings("fsq", 2, [65, 512], BF16), "l32": k.rings("fl32", 2, [64, 512], F32)}
    Vblk = A.alloc("Vblk", [128, 69, 2, 66], BF16)
    k.memset("pool", Vblk[:, :, :, :], 1.0, [Vblk])
    accr = k.rings("acc", 1, [65, NMAIN], F32)
    PTr = k.rings("PT", 6, [128, 256], BF16)
    otmp = k.rings("otmp", 2, [64, 512], BF16)
    blocks = []
    for br, d in enumerate(DILS):
        for r in range(d):
            for n in range(16 // d - 1, 32 // d):
                blocks.append((br, r, n))
    bidx = {b: i for i, b in enumerate(blocks)}
    assert len(blocks) == 69
    for c in range(4):
        for g0 in range(0, 69, 4):
            grp = blocks[g0:g0 + 4]
            psv = k.ps.next()
            pbv = bank_bf(psv)
            for j, (br, r, n) in enumerate(grp):
                d = DILS[br]
                k.tr(pbv[:, j * 128:(j + 1) * 128], vT_b[:, c, sl(r + d * 128 * n, 128, d)], identb[:, :], kv + [identb], [psv])
            ng = len(grp)
            k.cp(k.alt(("act", "dve")), Vblk[:, g0:g0 + ng, :, 0:64],
                 pbv[:, 0:ng * 128].rearrange("p (g h e) -> p g h e", g=ng, h=2), [psv], [Vblk])
        for hh in range(2):
            h = 2 * c + hh
            po = 64 * hh
            acc = accr.next()
            hb_list = []
            for br, d in enumerate(DILS):
                nq0, nq1 = 16 // d, 32 // d
                for r in range(d):
                    for n in range(nq0 - 1, nq1):
                        hb_list.append((br, d, r, n, n == nq0 - 1, n == nq1 - 1, nq0))
            PTs = {}

            def emit_S(i, h=h, c=c, po=po):
                br, d, r, n, first, last, nq0 = hb_list[i]
                ks = sl(r + d * 128 * n, 128, d)
                if first:
                    q0, N, bias, bT = r + d * 128 * nq0 - NPRE, 128, pe_[:, h, br, :], pe_
                elif last:
                    q0, N, bias, bT = r + d * 128 * n - NPRE, 128, pb[:, h, br, 0:128], pb
                else:
                    q0, N, bias, bT = r + d * 128 * n - NPRE, 256, pb[:, h, br, :], pb
                psS = k.ps.next()
                k.mm(psS[:, 0:N], kT_b[po:po + 64, c, ks], qT_b[po:po + 64, c, sl(q0, N, d)], kv, [psS], start=True, stop=False)
                k.mm(psS[:, 0:N], identb[:, :], bias, [identb, bT], [psS], start=False, stop=True)
                PT = PTr.next()
                k.act(PT[:, 0:N], psS[:, 0:N], AF.Exp, [psS], [PT], scale=0.125)
                PTs[i] = PT

            def emit_PV(i, hh=hh, acc=acc):
                br, d, r, n, first, last, nq0 = hb_list[i]
                if first:
                    return
                PT, prevPT = PTs[i], PTs[i - 1]
                prev_first = hb_list[i - 1][4]
                psO = k.ps.next()
                pp = prevPT[:, 0:128] if prev_first else prevPT[:, 128:256]
                k.mm(psO[0:65, 0:128], Vblk[:, bidx[(br, r, n - 1)], hh, 0:65], pp, [Vblk, prevPT], [psO], start=True, stop=False)
                k.mm(psO[0:65, 0:128], Vblk[:, bidx[(br, r, n)], hh, 0:65], PT[:, 0:128], [Vblk, PT], [psO], start=False, stop=True)
                qc = r + d * 128 * n - NPRE
                qcols = sl(qc, 128, d)
                if br == 0:
                    k.cp("dve", acc[:, qcols], psO[0:65, 0:128], [psO], [acc])
                else:
                    k.tt("dve", acc[:, qcols], acc[:, qcols], psO[0:65, 0:128], ALU.add, [acc, psO], [acc])
                PTs.pop(i - 1, None)

            LOOK = 3
            for i in range(len(hb_list) + LOOK):
                if i < len(hb_list):
                    emit_S(i)
                if i - LOOK >= 0:
                    emit_PV(i - LOOK)
            if hh == 0:
                finalize_attn(k, C, F, acc, acc, NMAIN, lambda c0, w, c=c: (mixT_b[0:64, c, c0:c0 + w], mixT_b))
            else:
                def dst(c0, w, c=c):
                    o = otmp.next()
                    dst.last = (o, c0, w)
                    return o[0:64, 0:w], o
                for c0 in range(0, NMAIN, 512):
                    finalize_attn(k, C, F, acc[:, c0:c0 + 512], acc, 512, dst)
                    o, _, w = dst.last
                    k.dma("sp", mixT_b[64:128, c, c0:c0 + 512], o[0:64, 0:512], R=[o], W=[mixT_b])
    merge_free(A, kT_b, kv)
    A.free(vT_b, qT_b, pb, pe_, Vblk)
    k.free_ring(accr, PTr)

    k.set_ring(7)
    psOs = k.banks[7]
    sbc = A.alloc("sbc", [128, 16, 192], BF16)
    sbn = A.alloc("sbn", [NS, 4, 192], BF16)
    k.dma("sp", sbc[:, :, :], IO["sbias_c"][:, :, :], W=[sbc])
    k.dma("sp", sbn[:, :, :], IO["sbias_n"][:, :, :], W=[sbn])
    kTn, vn_aug, Qbd = C["kTn"], C["vn_aug"], C["Qbd"]
    kc32r = k.rings("kc32", 2, [128, 512], F32)
    vc32r = k.rings("vc32", 2, [128, 512], F32)
    kcbr = k.rings("kcb", 2, [128, 512], BF16)
    vaugr = k.rings("vaug", 2, [128, 8, 66], BF16)
    kTsr = k.rings("kTs", 2, [128, 4, 128], BF16)
    PTsr = k.rings("PTs", 2, [128, 192], BF16)
    tmpr = k.rings("ptmp", 2, [128, 8, 8], BF16)
    Pqr = [k.rings("Pq%d" % s, 2, [128, 8, NS], BF16) for s in range(4)]
    for t in vaugr.t:
        k.memset("pool", t[:, :, :], 1.0, [t])
    for s in range(4):
        for t in Pqr[s].t:
            k.memset(k.alt(), t[:, :, :], 0.0, [t])
    for s in range(4):
        for kt in range(17):
            if kt < 16:
                kc32, vc32, kcb, vaug, kTs = kc32r.next(), vc32r.next(), kcbr.next(), vaugr.next(), kTsr.next()
                k.dma("sp", kc32[:, :], IO["ck"][s, 128 * kt:128 * kt + 128, :], W=[kc32])
                k.dma("sp", vc32[:, :], IO["cv"][s, 128 * kt:128 * kt + 128, :], W=[vc32])
                for src, nm in ((kc32, "wk_s"), (vc32, "wv_s")):
                    if kt == 0:
                        k.dma("sp", IO[nm][s, 0:120, :], src[8:128, :], R=[src])
                    else:
                        k.dma("sp", IO[nm][s, 128 * kt - 8:128 * kt + 120, :], src[:, :], R=[src])
                k.cp("act", kcb[:, :], kc32[:, :], [kc32], [kcb])
                k.cp("dve", vaug[:, :, 0:64], vc32[:, :].rearrange("p (h e) -> p h e", h=8), [vc32], [vaug])
                pst = k.ps.next()
                pbt = bank_bf(pst)
                for c in range(4):
                    k.tr(pbt[:, c * 128:(c + 1) * 128], kcb[:, c * 128:(c + 1) * 128], identb[:, :], [kcb, identb], [pst])
                k.cp("act", kTs[:, :, :], pbt[:, 0:512].rearrange("p (c t) -> p c t", c=4), [pst], [kTs])
                np_ = 128
                lhs_k = lambda c, kTs=kTs: kTs[:, c, :]
                kdep = kTs
                bias = sbc[:, kt, :]
                bT = sbc
                vsrc = vaug
                idb = identb[:, :]
            else:
                np_ = NS
                lhs_k = lambda c: kTn[:, c, :]
                kdep = kTn
                bias = sbn[0:NS, s, :]
                bT = sbn
                vsrc = vn_aug
                idb = identb[0:NS, 0:NS]
            psS = k.ps.next()
            k.mm(psS[0:np_, 0:192], idb, bias, [identb, bT], [psS], start=True, stop=False)
            for c in range(4):
                k.mm(psS[0:np_, c * 48:(c + 1) * 48], lhs_k(c), Qbd[:, c, s, :], [kdep, Qbd], [psS], start=False, stop=(c == 3))
            PTs = PTsr.next()
            k.act(PTs[0:np_, :], psS[0:np_, 0:192], AF.Exp, [psS], [PTs], scale=0.125)
            Pq = Pqr[s].next()
            tmp = tmpr.next()
            P4 = PTs[0:np_, :].rearrange("p (h b t) -> p h b t", h=8, b=3)
            k.tt("dve", tmp[0:np_, :, :], P4[:, :, 0, :], P4[:, :, 1, :], ALU.add, [PTs], [tmp])
            k.tt("dve", Pq[0:np_, :, 8 * s:8 * s + 8], tmp[0:np_, :, :], P4[:, :, 2, :], ALU.add, [PTs, tmp], [Pq])
            for h in range(8):
                k.mm(psOs[0:65, h * NS:(h + 1) * NS], vsrc[0:np_, h, 0:65], Pq[0:np_, h, :], [vsrc, Pq], [psOs],
                     start=(s == 0 and kt == 0 and h == 0), stop=(s == 3 and kt == 16), skip=True)
    accs = A.alloc("accs", [65, 8, NS], F32)
    k.cp("dve", accs[:, :, :], psOs[0:65, 0:8 * NS].rearrange("p (h t) -> p h t", h=8), [psOs], [accs])
    for h in range(8):
        c, hh = h // 2, h % 2
        if hh == 0:
            finalize_attn(k, C, F, accs[:, h, :], accs, NS, lambda c0, w, c=c: (mixT_b[0:64, c, NMAIN:NMAIN + NS], mixT_b))
        else:
            o = otmp.next()
            finalize_attn(k, C, F, accs[:, h, :], accs, NS, lambda c0, w, o=o: (o[0:64, 0:NS], o))
            k.dma("sp", mixT_b[64:128, c, NMAIN:NMAIN + NS], o[0:64, 0:NS], R=[o], W=[mixT_b])
    k.set_ring(8)
    A.free(sbc, sbn, kTn, vn_aug, Qbd, accs, gmb, nob)
    k.free_ring(kc32r, vc32r, kcbr, vaugr, kTsr, PTsr, tmpr, otmp, F["sq"], F["l32"], *Pqr)


def stage_C(k, C, IO):
    A = k.A
    k.set_ring(4)
    accb = k.banks[4:8]
    mixT_a, mixT_b = C["mixT_a"], C["mixT_b"]
    wo = A.alloc("wo", [128, 8, 1024], BF16)
    for kc in range(8):
        k.dma("pool", wo[:, kc, :], IO["w_out"][kc * 128:(kc + 1) * 128, :], W=[wo])
    WB = 256
    wgb = [A.alloc("wg%d" % i, [128, 8, WB], BF16) for i in range(DFF // WB)]
    wub = [A.alloc("wu%d" % i, [128, 8, WB], BF16) for i in range(DFF // WB)]
    for i in range(DFF // WB):
        k.dma("pool", wgb[i][:, :, :], IO["w_gate"][:, i * WB:(i + 1) * WB].rearrange("(c p) n -> p c n", p=128), W=[wgb[i]])
        k.dma("pool", wub[i][:, :, :], IO["w_up"][:, i * WB:(i + 1) * WB].rearrange("(c p) n -> p c n", p=128), W=[wub[i]])
    nfb = A.alloc("nfb", [128, 1024], F32)
    nfin = A.alloc("nfin", [128, 1024], F32)
    k.dma("sp", nfb[:, :], IO["norm_ffn"].partition_broadcast(128), W=[nfb])
    k.dma("sp", nfin[:, :], IO["norm_final"].partition_broadcast(128), W=[nfin])
    C["sm"] = k.rings("smc", 4, [128, 8], F32)
    C["hb"] = k.rings("hbc", 2, [128, 1024], BF16)
    x1r = k.rings("x1", 4, [128, 1024], F32)
    hfr = k.rings("hfT", 2, [128, 8, 256], BF16)
    wdr = k.rings("wd", 4, [128, 1024], BF16)
    e32r = k.rings("ce32", 4, [128, 256], F32)
    c32r = k.rings("cc32", 4, [128, 256], F32)
    u32r = k.rings("cu32", 4, [128, 256], F32)
    aTr = k.rings("aT", 4, [128, 256], BF16)
    units = [(IO["xp"], NPRE + u * 256, u * 256, 2, 128, IO["yp"], u * 256) for u in range(NMAIN // 256)]
    units.append((IO["xs"], 0, NMAIN, 1, NS, IO["ys"], 0))
    NJ = DFF // 128

    def pre(unit, st):
        (xsrc, xrow0, mcol0, ntile, p, ydst, yrow0) = unit
        st["hfT"] = hfr.next()
        st["x1s"] = []
        for t in range(ntile):
            x1 = x1r.next()
            st["x1s"].append(x1)
            k.dma("sp", x1[0:p, :], xsrc[xrow0 + t * 128:xrow0 + t * 128 + p, :], W=[x1])
            cols = slice(mcol0 + t * 128, mcol0 + t * 128 + p)
            for half in range(2):
                psX = k.ps.next()
                for kc in range(8):
                    lhsT = mixT_a[:, kc, cols] if kc < 4 else mixT_b[:, kc - 4, cols]
                    k.mm(psX[0:p, :], lhsT, wo[:, kc, half * 512:(half + 1) * 512], [mixT_a, mixT_b, wo], [psX], start=(kc == 0), stop=(kc == 7))
                k.tt("dve", x1[0:p, half * 512:(half + 1) * 512], x1[0:p, half * 512:(half + 1) * 512], psX[0:p, :], ALU.add, [x1, psX], [x1])
                yield
            norm_transpose(k, x1, p, nfb, st["hfT"], t * 128, C)
            yield

    def ffn(unit, st):
        (xsrc, xrow0, mcol0, ntile, p, ydst, yrow0) = unit
        ntok = (ntile - 1) * 128 + p
        hfT = st["hfT"]

        def issue_gu(j):
            psG = k.ps.next()
            for kc in range(8):
                k.mm(psG[:, 0:ntok], wgb[j // 2][:, kc, (j % 2) * 128:(j % 2) * 128 + 128], hfT[:, kc, 0:ntok], [wgb[j // 2], hfT], [psG], start=(kc == 0), stop=(kc == 7))
            psU = k.ps.next()
            for kc in range(8):
                k.mm(psU[:, 0:ntok], wub[j // 2][:, kc, (j % 2) * 128:(j % 2) * 128 + 128], hfT[:, kc, 0:ntok], [wub[j // 2], hfT], [psU], start=(kc == 0), stop=(kc == 7))
            return psG, psU

        def issue_pair(jp):
            return [issue_gu(jp), issue_gu(jp + 1)]

        pend = issue_pair(0)
        for jp in range(0, NJ, 2):
            cur = pend
            wds, es, gs, us, aTs = [], [], [], [], []
            for q in range(2):
                wd = wdr.next()
                k.dma("pool", wd[:, :], IO["w_down"][(jp + q) * 128:(jp + q + 1) * 128, :], W=[wd])
                wds.append(wd)
                es.append(e32r.next()); gs.append(c32r.next()); us.append(u32r.next()); aTs.append(aTr.next())
            for q in range(2):
                k.act(es[q][:, 0:ntok], cur[q][0][:, 0:ntok], AF.Exp, [cur[q][0]], [es[q]], scale=-1.0)
            for q in range(2):
                k.cp("dve", gs[q][:, 0:ntok], cur[q][0][:, 0:ntok], [cur[q][0]], [gs[q]])
            for q in range(2):
                k.cp("act", us[q][:, 0:ntok], cur[q][1][:, 0:ntok], [cur[q][1]], [us[q]])
            yield
            for q in range(2):
                k.act(es[q][:, 0:ntok], es[q][:, 0:ntok], AF.Ln, [es[q]], [es[q]], bias=1.0)
            for q in range(2):
                k.act(es[q][:, 0:ntok], es[q][:, 0:ntok], AF.Exp, [es[q]], [es[q]], scale=-1.0)
            for q in range(2):
                k.tt("dve", gs[q][:, 0:ntok], gs[q][:, 0:ntok], es[q][:, 0:ntok], ALU.mult, [gs[q], es[q]], [gs[q]])
            for q in range(2):
                k.tt("dve", aTs[q][:, 0:ntok], gs[q][:, 0:ntok], us[q][:, 0:ntok], ALU.mult, [gs[q], us[q]], [aTs[q]])
            if jp + 2 < NJ:
                pend = issue_pair(jp + 2)
            for q in range(2):
                j = jp + q
                for t in range(ntile):
                    for half in range(2):
                        ab = accb[t * 2 + half]
                        k.mm(ab[0:p, :], aTs[q][:, t * 128:t * 128 + p], wds[q][:, half * 512:(half + 1) * 512], [aTs[q], wds[q]], [ab],
                             start=(j == 0), stop=(j == NJ - 1))

    def post(unit, st):
        (xsrc, xrow0, mcol0, ntile, p, ydst, yrow0) = unit
        for t in range(ntile):
            x1 = st["x1s"][t]
            for half in range(2):
                ab = accb[t * 2 + half]
                k.tt("dve", x1[0:p, half * 512:(half + 1) * 512], x1[0:p, half * 512:(half + 1) * 512], ab[0:p, :], ALU.add, [x1, ab], [x1])
            sm = C["sm"].next()
            jk = C["hb"].next()
            k.act(jk[0:p, :], x1[0:p, :], AF.Square, [x1], [jk, sm], accum=sm[0:p, 0:1])
            k.act(sm[0:p, 1:2], sm[0:p, 0:1], AF.Ln, [sm], [sm], scale=1.0 / D, bias=EPS)
            k.act(sm[0:p, 2:3], sm[0:p, 1:2], AF.Exp, [sm], [sm], scale=-0.5)
            k.stt("dve", x1[0:p, :], x1[0:p, :], sm[0:p, 2:3], nfin[0:p, :], ALU.mult, ALU.mult, [x1, sm, nfin], [x1])
            k.dma("sp", ydst[yrow0 + t * 128:yrow0 + t * 128 + p, :], x1[0:p, :], R=[x1])

    states = [dict() for _ in units]
    for _ in pre(units[0], states[0]):
        pass
    for u, unit in enumerate(units):
        gens = [ffn(unit, states[u])]
        if u + 1 < len(units):
            gens.append(pre(units[u + 1], states[u + 1]))
        run_interleaved(gens, offs=1, maxact=2)
        post(unit, states[u])
    k.set_ring(8)


IN_SPECS = [
    ("xp", [NTOK, D], F32), ("xs", [NS, D], F32), ("sconv", [4, 3, 1536], F32), ("srec", [4, 4, 128, 128], F32),
    ("ck", [4, 2048, 512], F32), ("cv", [4, 2048, 512], F32),
    ("norm_mix", [D], F32), ("w_in", [D, 3592], F32), ("w_conv", [4, 1536], F32), ("a_log", [4], F32), ("dt_bias", [4], F32),
    ("noa", [128], F32), ("nob", [64], F32), ("w_out", [D, D], F32), ("norm_ffn", [D], F32),
    ("w_gate", [D, DFF], F32), ("w_up", [D, DFF], F32), ("w_down", [DFF, D], F32), ("norm_final", [D], F32),
    ("cf32", [128, 1664], F32), ("vmask", [128, 1], F32), ("edge8", [128, 1], F32),
    ("pbias", [128, 8, 3, 256], BF16), ("sbias_c", [128, 16, 192], BF16), ("sbias_n", [NS, 4, 192], BF16), ("gmb", [65, 64], BF16),
]
OUT_SPECS = [
    ("yp", [NMAIN, D]), ("ys", [NS, D]), ("conv_p", [3, 1536]), ("rec_p", [4, 128, 128]), ("wk_p", [NMAIN, 512]), ("wv_p", [NMAIN, 512]),
    ("conv_s", [4, 3, 1536]), ("rec_s", [4, 4, 128, 128]), ("wk_s", [4, 2048, 512]), ("wv_s", [4, 2048, 512]),
]


def build_program(stages="GKBC", dbg=False):
    nc = bass.Bass("TRN2", target_bir_lowering=False)
    IO = {}
    if dbg:
        IO["dbg_ma"] = nc.dram_tensor("dbg_ma", [128, 4, NMAIN + NS], F32, kind="ExternalOutput").ap()
        IO["dbg_mb"] = nc.dram_tensor("dbg_mb", [128, 4, NMAIN + NS], F32, kind="ExternalOutput").ap()
    for name, shape, dt in IN_SPECS:
        IO[name] = nc.dram_tensor(name, shape, dt, kind="ExternalInput").ap()
    for name, shape in OUT_SPECS:
        IO[name] = nc.dram_tensor(name, shape, F32, kind="ExternalOutput").ap()
    k = KB(nc)
    C = {}
    setup_consts(k, C, IO)
    C["mixT_a"] = k.A.alloc("mixT_a", [128, 4, NMAIN + NS], BF16)
    if "G" in stages:
        stage_G(k, C, IO)
    if "K" in stages:
        stage_K(k, C, IO)
    if "B" in stages:
        stage_B(k, C, IO)
    if dbg:
        k.dma("pool", IO["dbg_ma"][:, :, :], C["mixT_a"][:, :, :], R=[C["mixT_a"]])
        k.dma("pool", IO["dbg_mb"][:, :, :], C["mixT_b"][:, :, :], R=[C["mixT_b"]])
    if "C" in stages:
        stage_C(k, C, IO)
    k.S.finish()
    k.S.emit()
    return nc, k


def host_tables():
    p = np.arange(128)
    ident = np.eye(128, dtype=np.float32)
    mSU = (p[:, None] < p[None, :]).astype(np.float32)
    mIU = (p[:, None] <= p[None, :]).astype(np.float32)
    ones = np.ones((128, 128), np.float32)
    cf = np.concatenate([np.tile(ident, (1, 4)), np.tile(mSU, (1, 4)), np.tile(mIU, (1, 4)), ones], axis=1)
    slopes = 2.0 ** (-np.arange(1, 9, dtype=np.float64))
    ki = p[:, None].astype(np.float64)
    qi = p[None, :].astype(np.float64)
    pbias = np.zeros((128, 8, 3, 256), np.float64)
    for h in range(8):
        for br, d in enumerate(DILS):
            j = qi - ki
            pbias[:, h, br, 0:128] = np.where(j >= 0, -slopes[h] * d * j * 8.0, NEGB)
            j = qi + 128 - ki
            pbias[:, h, br, 128:256] = np.where(j <= 128, -slopes[h] * d * j * 8.0, NEGB)
    sbc = np.full((128, 16, 192), NEGB, np.float64)
    sbn = np.full((NS, 4, 192), NEGB, np.float64)
    for h in range(8):
        for br, d in enumerate(DILS):
            for t in range(8):
                col = h * 24 + br * 8 + t
                kp = np.arange(2048)
                dist = 2048 + t - kp
                ok = (dist % d == 0) & (dist // d <= 128)
                vals = np.where(ok, -slopes[h] * dist * 8.0, NEGB)
                sbc[:, :, col] = vals.reshape(16, 128).T
                for s in range(4):
                    for t2 in range(8):
                        dist2 = t - t2
                        if dist2 >= 0 and dist2 % d == 0:
                            sbn[8 * s + t2, s, col] = -slopes[h] * dist2 * 8.0
    gm = np.full((65, 64), 1.0 / 64, np.float64)
    gm[64, :] = EPS
    vmask = (p < 8).astype(np.float32).reshape(128, 1)
    bf = ml_dtypes.bfloat16
    return dict(cf32=cf, vmask=vmask, pbias=pbias.astype(np.float32).astype(bf), sbias_c=sbc.astype(np.float32).astype(bf),
                sbias_n=sbn.astype(np.float32).astype(bf), gmb=gm.astype(np.float32).astype(bf))


_CACHE = {}


def make_in_maps(x_prompt, x_sample, state_conv, state_rec, cache_win_k, cache_win_v, norm_mix, w_in, w_conv, a_log, dt_bias,
                 norm_out_a, norm_out_b, w_out, norm_ffn, w_gate, w_up, w_down, norm_final):
    f = lambda a: np.ascontiguousarray(np.asarray(a, dtype=np.float32))
    tabs = host_tables()
    shared = dict(norm_mix=f(norm_mix[0]), w_in=f(w_in[0]), w_conv=f(w_conv[0]), a_log=f(a_log[0]), dt_bias=f(dt_bias[0]),
                  noa=f(norm_out_a[0]), nob=f(norm_out_b[0]), w_out=f(w_out[0]), norm_ffn=f(norm_ffn[0]),
                  w_gate=f(w_gate[0]), w_up=f(w_up[0]), w_down=f(w_down[0]), norm_final=f(norm_final), **tabs)
    in_maps = []
    for c in range(NCORES):
        b, half = c // 2, c % 2
        xp = np.zeros((NTOK, D), np.float32)
        if half == 1:
            xp[:] = x_prompt[b]
        else:
            xp[NPRE:] = x_prompt[b, 0:NMAIN]
        sl = slice(4 * c, 4 * c + 4)
        m = dict(shared)
        m.update(xp=xp, xs=f(x_sample[sl]).reshape(NS, D), sconv=f(state_conv[0, sl]), srec=f(state_rec[0, sl]),
                 ck=f(cache_win_k[0, sl]).reshape(4, 2048, 512), cv=f(cache_win_v[0, sl]).reshape(4, 2048, 512),
                 edge8=np.full((128, 1), 0.0 if half == 1 else NEGB, np.float32))
        in_maps.append(m)
    return in_maps


def assemble(res):
    y_prompt = np.zeros((4, 4096, D), np.float32)
    y_sample = np.zeros((32, 8, D), np.float32)
    conv_p = np.zeros((1, 4, 3, 1536), np.float32)
    rec_p = np.zeros((1, 4, 4, 128, 128), np.float32)
    wk_p = np.zeros((1, 4, 2048, 8, 64), np.float32)
    wv_p = np.zeros((1, 4, 2048, 8, 64), np.float32)
    conv_s = np.zeros((1, 32, 3, 1536), np.float32)
    rec_s = np.zeros((1, 32, 4, 128, 128), np.float32)
    wk_s = np.zeros((1, 32, 2048, 8, 64), np.float32)
    wv_s = np.zeros((1, 32, 2048, 8, 64), np.float32)
    for c in range(NCORES):
        b, half = c // 2, c % 2
        r = res[c]
        y_prompt[b, half * NMAIN:(half + 1) * NMAIN] = r["yp"]
        sl = slice(4 * c, 4 * c + 4)
        y_sample[sl] = r["ys"].reshape(4, 8, D)
        conv_s[0, sl] = r["conv_s"]
        rec_s[0, sl] = r["rec_s"]
        wk_s[0, sl] = r["wk_s"].reshape(4, 2048, 8, 64)
        wv_s[0, sl] = r["wv_s"].reshape(4, 2048, 8, 64)
        if half == 1:
            conv_p[0, b] = r["conv_p"]
            rec_p[0, b] = r["rec_p"]
            wk_p[0, b] = r["wk_p"].reshape(2048, 8, 64)
            wv_p[0, b] = r["wv_p"].reshape(2048, 8, 64)
    return (y_prompt, y_sample, conv_p, rec_p, wk_p, wv_p, conv_s, rec_s, wk_s, wv_s)


def kernel(**inputs):
    in_maps = make_in_maps(**inputs)
    if "nc" not in _CACHE:
        _CACHE["nc"] = build_program()[0]
    res = run_bass_kernel_spmd(_CACHE["nc"], in_maps, core_ids=list(range(NCORES)))
    return assemble(res.results)
```

```python
import math
import numpy as np
import ml_dtypes
import concourse.bass as bass
import concourse.mybir as mybir
from concourse.bass_utils import run_bass_kernel_spmd

F32 = mybir.dt.float32
BF16 = mybir.dt.bfloat16
ALU = mybir.AluOpType
AF = mybir.ActivationFunctionType

ENGS = ("pe", "act", "dve", "pool", "sp")
NCORES = 8
D = 1024
NPRE = 2048
NMAIN = 2048
NTOK = NPRE + NMAIN
NS = 32
DFF = 2816
EPS = 1e-6
NEGB = -240000.0
DILS = (1, 4, 16)
import os
KSTOP = int(os.environ.get("KSTOP", "9"))
KOFF = os.environ.get("KOFF", "")


class T:
    __slots__ = ("name", "ap", "lw", "rde", "rdd", "rng", "psum")

    def __init__(self, name, ap, psum=False):
        self.name = name
        self.ap = ap
        self.psum = psum
        self.lw = None
        self.rde = {}
        self.rdd = []
        self.rng = None

    def __getitem__(self, idx):
        return self.ap[idx]


class Sched:
    def __init__(self, nc, n_dma_slots=10):
        self.nc = nc
        self.ops = {e: [] for e in ENGS}
        self.cnt = {e: 0 for e in ENGS}
        self.waited = {e: {} for e in ENGS}
        self.nslots = n_dma_slots
        self.slot_total = {}
        self.slot_next = {"sp": 0, "pool": 0, "act": 0}
        self.dma_info = []

    def _need(self, eng, dep, waits):
        if dep[0] == "e":
            _, e2, seq = dep
            if e2 == eng and eng in ("pe", "sp"):
                return
            key = ("e", e2)
            val = seq
        else:
            key, val = self.dma_info[dep[1]]
        w = self.waited[eng]
        if w.get(key, 0) >= val:
            return
        w[key] = val
        waits.append((key, val))

    def _deps(self, eng, reads, writes):
        waits = []
        for t in reads:
            if t.lw is not None:
                self._need(eng, t.lw, waits)
            if t.psum:
                for e2, seq in t.rde.items():
                    if e2 != eng:
                        self._need(eng, ("e", e2, seq), waits)
        for t in writes:
            lw = t.lw
            if lw is not None:
                self._need(eng, lw, waits)
            for e2, seq in t.rde.items():
                if e2 != eng or eng != "pe":
                    self._need(eng, ("e", e2, seq), waits)
            for did in t.rdd:
                self._need(eng, ("d", did), waits)
        return waits

    def _mark(self, me, reads, writes):
        for t in reads:
            if me[0] == "e":
                if t.rde.get(me[1], 0) < me[2]:
                    t.rde[me[1]] = me[2]
            else:
                t.rdd.append(me[1])
        for t in writes:
            t.lw = me
            t.rde = {}
            t.rdd = []

    def op(self, eng, fn, reads=(), writes=()):
        waits = self._deps(eng, reads, writes)
        self.cnt[eng] += 1
        me = ("e", eng, self.cnt[eng])
        self._mark(me, reads, writes)
        self.ops[eng].append((fn, waits, "c", None))

    def dma(self, eng, out_ap, in_ap, reads=(), writes=(), **kw):
        waits = self._deps(eng, reads, writes)
        slot = self.slot_next[eng]
        self.slot_next[eng] = (slot + 1) % self.nslots
        key = ("d", eng, slot)
        prev = self.slot_total.get(key, 0)
        if prev:
            w = self.waited[eng]
            if w.get(key, 0) < prev:
                w[key] = prev
                waits.append((key, prev))
        val = prev + 16
        self.slot_total[key] = val
        did = len(self.dma_info)
        self.dma_info.append((key, val))
        self._mark(("d", did), reads, writes)
        self.ops[eng].append(((out_ap, in_ap, kw), waits, "d", key))

    def finish(self):
        for eng in ("sp", "pool", "act"):
            waits = []
            for key, val in self.slot_total.items():
                if key[1] != eng:
                    continue
                w = self.waited[eng]
                if w.get(key, 0) < val:
                    w[key] = val
                    waits.append((key, val))
            if waits:
                self.ops[eng].append((None, waits, "w", None))

    def emit(self):
        nc = self.nc
        from contextlib import ExitStack
        with ExitStack() as es:
            sems = {}
            for e in ENGS:
                sems[("e", e)] = es.enter_context(nc.semaphore("s_" + e))
            for key in self.slot_total:
                sems[key] = es.enter_context(nc.semaphore("d_%s_%d" % (key[1], key[2])))
            block = es.enter_context(nc.Block())
            refd = {e: set() for e in ENGS}
            for e in ENGS:
                for fn, waits, kind, extra in self.ops[e]:
                    for key, val in waits:
                        if key[0] == "e":
                            refd[key[1]].add(val)
            rank = {e: {s: i + 1 for i, s in enumerate(sorted(refd[e]))} for e in ENGS}

            def run(engname):
                def body(eng):
                    mysem = sems[("e", engname)]
                    myrank = rank[engname]
                    seq = 0
                    for fn, waits, kind, extra in self.ops[engname]:
                        for key, val in waits:
                            if key[0] == "e":
                                eng.wait_ge(sems[key], rank[key[1]][val])
                            else:
                                eng.wait_ge(sems[key], val)
                        if kind == "c":
                            seq += 1
                            ins = fn(eng)
                            if seq in myrank:
                                ins.then_inc(mysem, 1)
                        elif kind == "d":
                            out_ap, in_ap, kw = fn
                            eng.dma_start(out=out_ap, in_=in_ap, **kw).then_inc(sems[extra], 16)
                return body

            block.tensor(run("pe"))
            block.scalar(run("act"))
            block.vector(run("dve"))
            block.gpsimd(run("pool"))
            block.sync(run("sp"))


class Arena:
    def __init__(self, nc, nbytes):
        self.n = nbytes
        self.base = nc.alloc_sbuf_tensor("arena", [128, nbytes // 2], BF16).ap()
        self.live = []
        self.retired = []
        self.peak = 0

    def alloc(self, name, shape, dt):
        esz = 4 if dt == F32 else 2
        n = 1
        for s in shape[1:]:
            n *= s
        nb = (n * esz + 63) // 64 * 64
        pos = 0
        for a, b, _ in sorted(self.live, key=lambda x: x[0]):
            if a - pos >= nb:
                break
            pos = max(pos, b)
        if pos + nb > self.n:
            raise RuntimeError("arena full allocating %s (%d bytes) live=%d" % (name, nb, sum(b - a for a, b, _ in self.live)))
        a, b = pos, pos + nb
        v = self.base[:, a // 2:(a + n * esz) // 2]
        if dt == F32:
            v = v.bitcast(F32)
        if len(shape) == 3:
            v = v.rearrange("p (x y) -> p x y", x=shape[1])
        elif len(shape) == 4:
            v = v.rearrange("p (x y z) -> p x y z", x=shape[1], y=shape[2])
        if shape[0] < 128:
            v = v[0:shape[0]]
        t = T(name, v)
        t.rng = (a, b)
        keep = []
        for ra, rb, rt in self.retired:
            if ra < b and a < rb:
                for e2, seq in rt.rde.items():
                    if t.rde.get(e2, 0) < seq:
                        t.rde[e2] = seq
                t.rdd.extend(rt.rdd)
                if rt.lw is not None:
                    if rt.lw[0] == "e":
                        if t.rde.get(rt.lw[1], 0) < rt.lw[2]:
                            t.rde[rt.lw[1]] = rt.lw[2]
                    else:
                        t.rdd.append(rt.lw[1])
                if ra >= a and rb <= b:
                    continue
            keep.append((ra, rb, rt))
        self.retired = keep
        self.live.append((a, b, t))
        self.peak = max(self.peak, b)
        return t

    def free(self, *ts):
        for t in ts:
            for i, (a, b, tt) in enumerate(self.live):
                if tt is t:
                    self.live.pop(i)
                    self.retired.append((a, b, t))
                    break
            else:
                raise RuntimeError("free of unknown tile " + t.name)


class Ring:
    def __init__(self, tiles):
        self.t = tiles
        self.i = 0

    def next(self):
        t = self.t[self.i]
        self.i = (self.i + 1) % len(self.t)
        return t


class KB:
    def __init__(self, nc):
        self.nc = nc
        self.S = Sched(nc)
        self.A = Arena(nc, 207 * 1024)
        self.banks = [T("ps%d" % i, nc.alloc_psum_tensor("ps%d" % i, [128, 512], F32).ap(), psum=True) for i in range(8)]
        self.ps = Ring(self.banks)
        self._rr = 0

    def set_ring(self, n):
        self.ps = Ring(self.banks[0:n])

    def act(self, out, in_, func, R, W, scale=None, bias=None, accum=None):
        kw = {}
        if scale is not None:
            kw["scale"] = scale
        if bias is not None:
            kw["bias"] = bias
        if accum is not None:
            kw["accum_out"] = accum
        self.S.op("act", lambda e: e.activation(out=out, in_=in_, func=func, **kw), R, W)

    def mm(self, out, lhsT, rhs, R, W, start=True, stop=True, skip=False):
        self.S.op("pe", lambda e: e.matmul(out, lhsT=lhsT, rhs=rhs, start=start, stop=stop, skip_group_check=skip), R, W)

    def tr(self, out, in_, ident, R, W):
        self.S.op("pe", lambda e: e.transpose(out=out, in_=in_, identity=ident), R, W)

    def tt(self, eng, out, in0, in1, op, R, W):
        self.S.op(eng, lambda e: e.tensor_tensor(out=out, in0=in0, in1=in1, op=op), R, W)

    def ts(self, eng, out, in0, s1, op0, R, W, s2=None, op1=None):
        if op1 is None:
            self.S.op(eng, lambda e: e.tensor_scalar(out=out, in0=in0, scalar1=s1, scalar2=None, op0=op0), R, W)
        else:
            self.S.op(eng, lambda e: e.tensor_scalar(out=out, in0=in0, scalar1=s1, scalar2=s2, op0=op0, op1=op1), R, W)

    def stt(self, eng, out, in0, scalar, in1, op0, op1, R, W):
        self.S.op(eng, lambda e: e.scalar_tensor_tensor(out=out, in0=in0, scalar=scalar, in1=in1, op0=op0, op1=op1), R, W)

    def cp(self, eng, out, in_, R, W):
        if eng == "act":
            self.S.op("act", lambda e: e.activation(out=out, in_=in_, func=AF.Copy), R, W)
        else:
            self.S.op(eng, lambda e: e.tensor_copy(out=out, in_=in_), R, W)

    def recip(self, out, in_, R, W):
        self.S.op("dve", lambda e: e.reciprocal(out=out, in_=in_), R, W)

    def memset(self, eng, ap, val, W):
        self.S.op(eng, lambda e: e.memset(ap, val), (), W)

    def dma(self, eng, out, in_, R=(), W=(), **kw):
        self.S.dma(eng, out, in_, R, W, **kw)

    def alt(self, engs=("dve", "pool")):
        self._rr += 1
        return engs[self._rr % len(engs)]

    def rings(self, name, n, shape, dt):
        return Ring([self.A.alloc("%s%d" % (name, i), shape, dt) for i in range(n)])

    def free_ring(self, *rings):
        for r in rings:
            self.A.free(*r.t)


def sl(start, count, step):
    return slice(start, start + step * (count - 1) + 1, step)


def bank_bf(ps):
    return ps.ap.bitcast(BF16)


def norm_stats(k, xt, p, nb_tile, C, dimscale=1.0 / D):
    sm = C["sm"].next()
    hb = C["hb"].next()
    k.act(hb[0:p, :], xt[0:p, :], AF.Square, [xt], [hb, sm], accum=sm[0:p, 0:1])
    k.act(sm[0:p, 1:2], sm[0:p, 0:1], AF.Ln, [sm], [sm], scale=dimscale, bias=EPS)
    k.act(sm[0:p, 2:3], sm[0:p, 1:2], AF.Exp, [sm], [sm], scale=-0.5)
    k.stt("dve", hb[0:p, :], xt[0:p, :], sm[0:p, 2:3], nb_tile[0:p, :], ALU.mult, ALU.mult, [xt, sm, nb_tile], [hb])
    return hb


def transpose_to(k, hb, p, hT, c0, C):
    ps = k.ps.next()
    pb = bank_bf(ps)
    for c in range(8):
        k.tr(pb[:, c * 128:c * 128 + p], hb[0:p, c * 128:(c + 1) * 128], C["ident"][0:p, 0:p], [hb, C["ident"]], [ps])
    k.cp("act", hT[:, :, c0:c0 + p], pb[:, :].rearrange("q (c t) -> q c t", c=8)[:, :, 0:p], [ps], [hT])


def norm_transpose(k, xt, p, nb_tile, hT, c0, C, dimscale=1.0 / D):
    hb = norm_stats(k, xt, p, nb_tile, C, dimscale)
    transpose_to(k, hb, p, hT, c0, C)


def gdn_tile(k, C, G, hT_ap, qT_ap, kT_ap, vT_ap, S, Sb, deps, main, nf, mix_out, vm=None):
    w_a = C["w_a"]
    ident = C["ident"]
    sc = G["sc"].next()

    def bc(c0):
        return sc[:, c0:c0 + 4].unsqueeze(2).to_broadcast([128, 4, 128])

    def ps4(ps):
        return ps[:, :].rearrange("p (h c) -> p h c", h=4)

    psBA = k.ps.next()
    for kc in range(8):
        k.mm(psBA[:, 0:8], hT_ap[:, kc, :], w_a[:, kc, 2048:2056], deps + [C["wa_ba"]], [psBA], start=(kc == 0), stop=(kc == 7))
    k.act(sc[:, 0:4], psBA[:, 0:4], AF.Exp, [psBA], [sc], scale=-1.0)
    k.ts("dve", sc[:, 0:4], sc[:, 0:4], 1.0, ALU.add, [sc], [sc])
    k.recip(sc[:, 0:4], sc[:, 0:4], [sc], [sc])
    if vm is not None:
        k.ts("dve", sc[:, 0:4], sc[:, 0:4], vm[:, 0:1], ALU.mult, [sc, vm], [sc])
    k.ts("dve", sc[:, 4:8], sc[:, 0:4], -1.0, ALU.mult, [sc], [sc])
    yield
    k.tt("dve", sc[:, 8:12], psBA[:, 4:8], C["dtb"][:, :], ALU.add, [psBA, C["dtb"]], [sc])
    k.act(sc[:, 8:12], sc[:, 8:12], AF.Exp, [sc], [sc])
    k.act(sc[:, 8:12], sc[:, 8:12], AF.Ln, [sc], [sc], bias=1.0)
    k.tt("dve", sc[:, 8:12], sc[:, 8:12], C["negA"][:, :], ALU.mult, [sc, C["negA"]], [sc])
    if vm is not None:
        k.ts("dve", sc[:, 8:12], sc[:, 8:12], vm[:, 0:1], ALU.mult, [sc, vm], [sc])
    yield
    psG = k.ps.next()
    k.mm(psG[:, 0:4], C["mIU"][:, 0:128], sc[:, 8:12], [C["mIU"], sc], [psG])
    k.mm(psG[:, 4:8], C["onesf"][:, :], sc[:, 8:12], [C["onesf"], sc], [psG])
    k.cp("dve", sc[:, 12:20], psG[:, 0:8], [psG], [sc])
    k.act(sc[:, 20:28], sc[:, 12:20], AF.Exp, [sc], [sc])
    k.tt("dve", sc[:, 28:32], sc[:, 16:20], sc[:, 12:16], ALU.subtract, [sc], [sc])
    k.act(sc[:, 28:32], sc[:, 28:32], AF.Exp, [sc], [sc])
    yield
    tg = G["tg"].next()
    k.tt("dve", tg[:, :, :], C["mIU4"][:, :, :], bc(8), ALU.mult, [C["mIU4"], sc], [tg])
    psR = k.ps.next()
    k.mm(psR[:, :], C["onesf"][:, :], tg[:, :, :], [C["onesf"], tg], [psR])
    dec = G["dec"].next()
    k.tt("dve", dec[:, :, :], ps4(psR), bc(12), ALU.subtract, [psR, sc], [dec])
    k.act(dec[:, :, :], dec[:, :, :], AF.Exp, [dec], [dec])
    dS = G["dS"].next()
    k.stt("dve", dS[:, :, :], dec[:, :, :], 1.0, C["mSU4"][:, :, :], ALU.min, ALU.mult, [dec, C["mSU4"]], [dS])
    k.tt("dve", dS[:, :, :], dS[:, :, :], bc(4), ALU.mult, [dS, sc], [dS])
    if main:
        egr = G["egr"].next()
        k.act(egr[:, :, :], psR[:, :].rearrange("p (h c) -> p h c", h=4), AF.Exp, [psR], [egr])
        dI = G["dI"].next()
        k.stt("dve", dI[:, :, :], dec[:, :, :], 1.0, C["mIU4"][:, :, :], ALU.min, ALU.mult, [dec, C["mIU4"]], [dI])
    yield
    psk = k.ps.next()
    pbk = bank_bf(psk)
    for h in range(4):
        k.tr(pbk[:, h * 128:(h + 1) * 128], kT_ap[:, h, :], ident[:, :], deps + [ident], [psk])
    psv = k.ps.next()
    pbv = bank_bf(psv)
    for h in range(4):
        k.tr(pbv[:, h * 128:(h + 1) * 128], vT_ap[:, h, :], ident[:, :], deps + [ident], [psv])
    kg = G["kg"].next()
    kdec = G["kdec"].next()
    vtok = G["vtok"].next()
    ktok = G["e"].next()
    k.cp("act", ktok[:, :, :], pbk[:, 0:512].rearrange("p (h c) -> p h c", h=4), [psk], [ktok])
    k.cp("dve", vtok[:, :, :], pbv[:, 0:512].rearrange("p (h c) -> p h c", h=4), [psv], [vtok])
    k.tt("dve", kg[:, :, :], ktok[:, :, :], bc(20), ALU.mult, [ktok, sc], [kg])
    k.tt("dve", kdec[:, :, :], ktok[:, :, :], bc(28), ALU.mult, [ktok, sc], [kdec])
    yield
    psGm = k.ps.next()
    for h in range(4):
        k.mm(psGm[:, h * 128:(h + 1) * 128], kT_ap[:, h, :], kT_ap[:, h, :], deps, [psGm])
    R = G["R"].next()
    Rf = dec
    k.tt("dve", Rf[:, :, :], ps4(psGm), dS[:, :, :], ALU.mult, [psGm, dS], [Rf])
    k.cp("act", R[:, :, :], Rf[:, :, :], [Rf], [R])
    if main:
        psQK = k.ps.next()
        for h in range(4):
            k.mm(psQK[:, h * 128:(h + 1) * 128], kT_ap[:, h, :], qT_ap[:, h, :], deps, [psQK])
        QKT = G["QKT"].next()
        k.tt("dve", QKT[:, :, :], psQK[:, :].rearrange("p (h c) -> p h c", h=4), dI[:, :, :], ALU.mult, [psQK, dI], [QKT])
        qg = G["qg"].next()
        k.tt("dve", qg[:, :, :], qT_ap, egr[:, :, :], ALU.mult, deps + [egr], [qg])
    P = G["P"].next()
    k.tt("dve", P[:, :, :], R[:, :, :], C["id4b"][:, :, :], ALU.add, [R, C["id4b"]], [P])
    yield
    psr = k.ps.next()
    pbr = bank_bf(psr)
    for h in range(4):
        k.tr(pbr[:, h * 128:(h + 1) * 128], R[:, h, :], ident[:, :], [R, ident], [psr])
    RT = G["RT"].next()
    k.cp("act", RT[:, :, :], pbr[:, 0:512].rearrange("p (h c) -> p h c", h=4), [psr], [RT])
    for kk in range(1, nf + 1):
        yield
        psRk = psRTk = psP = None
        if kk <= nf - 2:
            psRk = k.ps.next()
            for h in range(4):
                k.mm(psRk[:, h * 128:(h + 1) * 128], RT[:, h, :], R[:, h, :], [RT, R], [psRk])
        if kk <= nf - 1:
            psRTk = k.ps.next()
            for h in range(4):
                k.mm(psRTk[:, h * 128:(h + 1) * 128], R[:, h, :], RT[:, h, :], [RT, R], [psRTk])
        if kk >= 2:
            psP = k.ps.next()
            for h in range(4):
                k.mm(psP[:, h * 128:(h + 1) * 128], RT[:, h, :], P[:, h, :], [RT, P], [psP])
        if psRk is not None:
            Rn = G["R"].next()
            k.cp("act", Rn[:, :, :], psRk[:, :].rearrange("p (h c) -> p h c", h=4), [psRk], [Rn])
        if psRTk is not None:
            RTn = G["RT"].next()
            k.cp("dve", RTn[:, :, :], psRTk[:, :].rearrange("p (h c) -> p h c", h=4), [psRTk], [RTn])
        if psP is not None:
            Pn = G["P"].next()
            k.tt("dve", Pn[:, :, :], psP[:, :].rearrange("p (h c) -> p h c", h=4), P[:, :, :], ALU.add, [psP, P], [Pn])
            P = Pn
        if psRk is not None:
            R = Rn
        if psRTk is not None:
            RT = RTn
    yield
    pst_ = k.ps.next()
    pbt_ = bank_bf(pst_)
    for h in range(4):
        k.tr(pbt_[:, h * 128:(h + 1) * 128], P[:, h, :], ident[:, :], [P, ident], [pst_])
    PTf = tg
    k.cp("act", PTf[:, :, :], pbt_[:, 0:512].rearrange("p (h c) -> p h c", h=4), [pst_], [PTf])
    psE = k.ps.next()
    for h in range(4):
        k.mm(psE[:, h * 128:(h + 1) * 128], PTf[:, h, :], Rf[:, h, :], [PTf, Rf], [psE])
    Et = G["Ec"].next()
    Ec = G["Ec"].next()
    k.tt("dve", Et[:, :, :], C["id4b"][:, :, :], P[:, :, :], ALU.subtract, [C["id4b"], P], [Et])
    k.tt("dve", Ec[:, :, :], psE[:, :].rearrange("p (h c) -> p h c", h=4), Et[:, :, :], ALU.add, [psE, Et], [Ec])
    yield
    psC1 = k.ps.next()
    for h in range(4):
        k.mm(psC1[:, h * 128:(h + 1) * 128], Ec[:, h, :], vtok[:, h, :], [Ec, vtok], [psC1])
    psC2 = k.ps.next()
    for h in range(4):
        k.mm(psC2[:, h * 128:(h + 1) * 128], Ec[:, h, :], kg[:, h, :], [Ec, kg], [psC2])
    vtok2 = G["vtok"].next()
    kg2 = G["kg"].next()
    k.tt("dve", vtok2[:, :, :], psC1[:, :].rearrange("p (h c) -> p h c", h=4), vtok[:, :, :], ALU.add, [psC1, vtok], [vtok2])
    k.tt("dve", kg2[:, :, :], psC2[:, :].rearrange("p (h c) -> p h c", h=4), kg[:, :, :], ALU.add, [psC2, kg], [kg2])
    vtok, kg = vtok2, kg2
    yield
    psU = k.ps.next()
    for h in range(4):
        k.mm(psU[:, h * 128:(h + 1) * 128], P[:, h, :], vtok[:, h, :], [P, vtok], [psU])
    psW = k.ps.next()
    for h in range(4):
        k.mm(psW[:, h * 128:(h + 1) * 128], kg[:, h, :], P[:, h, :], [P, kg], [psW])
    ub = G["ub"].next()
    k.cp("dve", ub[:, :, :], ps4(psU), [psU], [ub])
    wT = G["wT"].next()
    k.cp("act", wT[:, :, :], psW[:, :].rearrange("p (h c) -> p h c", h=4), [psW], [wT])
    yield
    psS1 = k.ps.next()
    for h in range(4):
        k.mm(psS1[:, h * 128:(h + 1) * 128], wT[:, h, :], Sb[:, h, :], [wT, Sb], [psS1])
    e = G["e"].next()
    k.tt("dve", ub[:, :, :], ps4(psS1), ub[:, :, :], ALU.subtract, [psS1, ub], [ub])
    k.tt("dve", e[:, :, :], ub[:, :, :], bc(4), ALU.mult, [ub, sc], [e])
    if main:
        psO = k.ps.next()
        for h in range(4):
            k.mm(psO[:, h * 128:(h + 1) * 128], qg[:, h, :], Sb[:, h, :], [qg, Sb], [psO], start=True, stop=False)
            k.mm(psO[:, h * 128:(h + 1) * 128], QKT[:, h, :], e[:, h, :], [QKT, e], [psO], start=False, stop=True)
    if main:
        o32 = ub
        k.cp("act", o32[:, :, :], psO[:, :].rearrange("p (h c) -> p h c", h=4), [psO], [o32])
    psSn = k.ps.next()
    for h in range(4):
        k.mm(psSn[:, h * 128:(h + 1) * 128], kdec[:, h, :], e[:, h, :], [kdec, e], [psSn])
    k.tt("dve", S[:, :, :], S[:, :, :], bc(24), ALU.mult, [S, sc], [S])
    k.tt("dve", S[:, :, :], S[:, :, :], ps4(psSn), ALU.add, [S, psSn], [S])
    k.cp("act", Sb[:, :, :], S[:, :, :], [S], [Sb])
    if not main:
        return
    yield
    jk = G["jk"].next()
    for h in range(4):
        k.act(jk[:, :], o32[:, h, :], AF.Square, [o32], [jk, sc], accum=sc[:, 32 + h:33 + h])
    k.act(sc[:, 32:36], sc[:, 32:36], AF.Ln, [sc], [sc], scale=1.0 / 128, bias=EPS)
    k.act(sc[:, 32:36], sc[:, 32:36], AF.Exp, [sc], [sc], scale=-0.5)
    yield
    psZ = k.ps.next()
    for kc in range(8):
        k.mm(psZ[:, :], hT_ap[:, kc, :], w_a[:, kc, 1536:2048], deps + [C["wa_z"]], [psZ], start=(kc == 0), stop=(kc == 7))
    ez = G["ez"].next()
    k.act(ez[:, :], psZ[:, :], AF.Exp, [psZ], [ez], scale=-1.0)
    k.act(ez[:, :], ez[:, :], AF.Ln, [ez], [ez], bias=1.0)
    k.act(ez[:, :], ez[:, :], AF.Exp, [ez], [ez], scale=-1.0)
    zn = G["zn"].next()
    k.tt("dve", zn[:, :], psZ[:, :], C["noa"][:, :], ALU.mult, [psZ, C["noa"]], [zn])
    k.tt("dve", zn[:, :], zn[:, :], ez[:, :], ALU.mult, [zn, ez], [zn])
    yield
    og = G["og"].next()
    k.tt("dve", o32[:, :, :], o32[:, :, :], bc(32), ALU.mult, [o32, sc], [o32])
    k.tt("dve", og[:, :].rearrange("p (h c) -> p h c", h=4), o32[:, :, :], zn[:, :].rearrange("p (h c) -> p h c", h=4), ALU.mult, [o32, zn], [og])
    mix_out(og)


def run_interleaved(gens, offs=3, maxact=2):
    active, pending, steps = [], list(gens), {}
    while active or pending:
        if pending and len(active) < maxact and (not active or steps[id(active[-1])] >= offs):
            gnew = pending.pop(0)
            active.append(gnew)
            steps[id(gnew)] = 0
        for gg in list(active):
            try:
                next(gg)
                steps[id(gg)] += 1
            except StopIteration:
                active.remove(gg)


EXTRA_RINGS = [("R", 2, [128, 4, 128], BF16), ("RT", 2, [128, 4, 128], BF16), ("P", 2, [128, 4, 128], BF16),
               ("kg", 2, [128, 4, 128], BF16), ("vtok", 2, [128, 4, 128], BF16), ("Ec", 2, [128, 4, 128], BF16),
               ("sc", 1, [128, 40], F32), ("tg", 1, [128, 4, 128], F32), ("dec", 1, [128, 4, 128], F32),
               ("egr", 1, [128, 4, 128], F32), ("ub", 1, [128, 4, 128], F32)] + \
              [(nm, 1, [128, 4, 128], BF16) for nm in ("dS", "dI", "kdec", "QKT", "qg", "wT", "e")]


def silu_from_psum(k, G, ps_ap, psT, n):
    e32 = G["e32"].next()
    c32 = G["c32"].next()
    k.act(e32[:, 0:n], ps_ap, AF.Exp, [psT], [e32], scale=-1.0)
    k.act(e32[:, 0:n], e32[:, 0:n], AF.Ln, [e32], [e32], bias=1.0)
    k.act(e32[:, 0:n], e32[:, 0:n], AF.Exp, [e32], [e32], scale=-1.0)
    k.tt("dve", c32[:, 0:n], ps_ap, e32[:, 0:n], ALU.mult, [psT, e32], [c32])
    return c32


def l2norm_chunk(k, C, G, c32, n, out_ap, outT, qscale):
    sq = G["sq"].next()
    k.tt("dve", sq[:, 0:n], c32[:, 0:n], c32[:, 0:n], ALU.mult, [c32], [sq])
    psC = k.ps.next()
    k.mm(psC[:, 0:n], C["onesb"][:, :], sq[:, 0:n], [C["onesb"], sq], [psC])
    l32 = G["l32"].next()
    k.act(l32[:, 0:n], psC[:, 0:n], AF.Ln, [psC], [l32], bias=EPS)
    if qscale:
        k.act(l32[:, 0:n], l32[:, 0:n], AF.Exp, [l32], [l32], scale=-0.5, bias=C["lnq"][:, 0:1])
        rd = [c32, l32, C["lnq"]]
    else:
        k.act(l32[:, 0:n], l32[:, 0:n], AF.Exp, [l32], [l32], scale=-0.5)
        rd = [c32, l32]
    k.tt("dve", out_ap, c32[:, 0:n], l32[:, 0:n], ALU.mult, rd, [outT])


def stage_G(k, C, IO):
    A = k.A
    w_a = A.alloc("w_a", [128, 8, 2056], BF16)
    C["w_a"] = w_a
    wa_units = {nm: T("wa_" + nm, None) for nm in ("q", "k", "v", "z", "ba")}
    for t in wa_units.values():
        for e2, seq in w_a.rde.items():
            t.rde[e2] = seq
        t.rdd.extend(w_a.rdd)
    C["wa_g"] = [wa_units["q"], wa_units["k"], wa_units["v"]]
    C["wa_z"], C["wa_ba"] = wa_units["z"], wa_units["ba"]
    for nm, c0, c1 in (("k", 512, 1024), ("v", 1024, 1536), ("ba", 2048, 2056), ("q", 0, 512), ("z", 1536, 2048)):
        k.dma("pool", w_a[:, :, c0:c1], IO["w_in"][:, c0:c1].rearrange("(c p) n -> p c n", p=128), W=[wa_units[nm]],
              allow_slow_non_contiguous=(nm == "ba"))
    nmb = A.alloc("nmb", [128, 1024], F32)
    k.dma("sp", nmb[:, :], IO["norm_mix"].partition_broadcast(128), W=[nmb])
    wconv = A.alloc("wconv", [128, 4, 12], F32)
    for i in range(4):
        k.dma("sp", wconv[:, i, :], IO["w_conv"][i].rearrange("(c p) -> p c", p=128), W=[wconv], allow_slow_non_contiguous=True)
    diag = A.alloc("diag", [128, 12, 4, 128], BF16)
    for ch in range(12):
        for i in range(4):
            k.ts(k.alt(), diag[:, ch, i, :], C["identf"][:, 0:128], wconv[:, i, ch:ch + 1], ALU.mult, [C["identf"], wconv], [diag])
    dtb = A.alloc("dtb", [128, 4], F32)
    negA = A.alloc("negA", [128, 4], F32)
    C["dtb"], C["negA"] = dtb, negA
    k.dma("sp", dtb[:, :], IO["dt_bias"].partition_broadcast(128), W=[dtb])
    k.dma("sp", negA[:, :], IO["a_log"].partition_broadcast(128), W=[negA])
    k.act(negA[:, :], negA[:, :], AF.Exp, [negA], [negA])
    k.ts("dve", negA[:, :], negA[:, :], -1.0, ALU.mult, [negA], [negA])
    noa = A.alloc("noa", [128, 512], F32)
    C["noa"] = noa
    for h in range(4):
        k.dma("sp", noa[:, h * 128:(h + 1) * 128], IO["noa"].partition_broadcast(128), W=[noa])
    lnq = A.alloc("lnq", [128, 1], F32)
    C["lnq"] = lnq
    k.memset("pool", lnq[:, :], math.log(128.0 ** -0.5), [lnq])

    G = {}
    C["sm"] = k.rings("sm", 4, [128, 8], F32)
    C["hb"] = k.rings("hb", 4, [128, 1024], BF16)
    xr = k.rings("xr", 2, [128, 1024], F32)
    hT = A.alloc("hT", [128, 8, 512], BF16)
    ext = A.alloc("ext", [128, 12, 515], BF16)
    qkT = A.alloc("qkT", [128, 8, 512], BF16)
    vT = A.alloc("vT", [128, 4, 512], BF16)
    S = A.alloc("S", [128, 4, 128], F32)
    Sb = A.alloc("Sb", [128, 4, 128], BF16)
    for nm, n in (("e32", 2), ("c32", 4), ("l32", 2), ("ez", 1), ("zn", 1)):
        G[nm] = k.rings(nm, n, [128, 512], F32)
    G["sq"] = k.rings("sq", 2, [128, 512], BF16)
    G["og"] = k.rings("og", 2, [128, 512], BF16)
    G["sc"] = k.rings("sc", 3, [128, 40], F32)
    G["jk"] = k.rings("jk", 1, [128, 128], F32)
    for nm, n in (("tg", 2), ("dec", 2), ("egr", 2), ("ub", 2)):
        G[nm] = k.rings(nm, n, [128, 4, 128], F32)
    for nm, n in (("dS", 2), ("dI", 2), ("kg", 4), ("kdec", 2), ("vtok", 4), ("R", 4), ("RT", 4), ("P", 4), ("Ec", 4),
                  ("QKT", 2), ("qg", 2), ("wT", 2), ("e", 2)):
        G[nm] = k.rings(nm, n, [128, 4, 128], BF16)

    mixT_a = C["mixT_a"]
    k.memset("pool", ext[:, :, :], 0.0, [ext])
    k.memset("pool", S[:, :, :], 0.0, [S])
    k.memset("dve", Sb[:, :, :], 0.0, [Sb])

    def feature_chunks(pchs, chs, hT_ap, hdeps, n, ext_dst, conv_rhs, qk_out, v_out, outTs, conv_out=None):
        for ch in pchs:
            psA = k.ps.next()
            for kc in range(8):
                k.mm(psA[:, 0:n], w_a[:, kc, ch * 128:(ch + 1) * 128], hT_ap(kc), hdeps + [C["wa_g"][ch // 4]], [psA], start=(kc == 0), stop=(kc == 7))
            ext_dst(ch, psA)
        def stage1(pair):
            st1 = []
            for ch in pair:
                psB = k.ps.next()
                for i in range(4):
                    k.mm(psB[:, 0:n] if conv_out is None else conv_out(psB), diag[:, ch, i, :], conv_rhs(ch, i), [diag] + outTs["ext"], [psB], start=(i == 0), stop=(i == 3))
                st1.append((ch, psB, G["e32"].next(), G["c32"].next()))
            for ch, psB, e32, c32 in st1:
                k.act(e32[:, 0:n], psB[:, 0:n], AF.Exp, [psB], [e32], scale=-1.0)
            for ch, psB, e32, c32 in st1:
                k.act(e32[:, 0:n], e32[:, 0:n], AF.Ln, [e32], [e32], bias=1.0)
            for ch, psB, e32, c32 in st1:
                k.act(e32[:, 0:n], e32[:, 0:n], AF.Exp, [e32], [e32], scale=-1.0)
            for ch, psB, e32, c32 in st1:
                k.tt("dve", c32[:, 0:n], psB[:, 0:n], e32[:, 0:n], ALU.mult, [psB, e32], [c32])
            return [(ch, c32) for ch, psB, e32, c32 in st1]

        def stage2(items):
            qk = [(ch, c32) for ch, c32 in items if ch < 8]
            for ch, c32 in items:
                if ch >= 8:
                    k.cp("dve", v_out(ch - 8), c32[:, 0:n], [c32], [outTs["v"]])
            st2 = []
            for ch, c32 in qk:
                sq = G["sq"].next()
                k.tt("dve", sq[:, 0:n], c32[:, 0:n], c32[:, 0:n], ALU.mult, [c32], [sq])
                psC = k.ps.next()
                k.mm(psC[:, 0:n], C["onesb"][:, :], sq[:, 0:n], [C["onesb"], sq], [psC])
                st2.append((ch, c32, psC, G["l32"].next()))
            for ch, c32, psC, l32 in st2:
                k.act(l32[:, 0:n], psC[:, 0:n], AF.Ln, [psC], [l32], bias=EPS)
            for ch, c32, psC, l32 in st2:
                if ch < 4:
                    k.act(l32[:, 0:n], l32[:, 0:n], AF.Exp, [l32, C["lnq"]], [l32], scale=-0.5, bias=C["lnq"][:, 0:1])
                else:
                    k.act(l32[:, 0:n], l32[:, 0:n], AF.Exp, [l32], [l32], scale=-0.5)
            for ch, c32, psC, l32 in st2:
                k.tt("dve", qk_out(ch), c32[:, 0:n], l32[:, 0:n], ALU.mult, [c32, l32], [outTs["qk"]])

        pairs = [chs[i:i + 2] for i in range(0, len(chs), 2)]
        pend = None
        for pair in pairs:
            cur = stage1(pair)
            if pend is not None:
                stage2(pend)
            pend = cur
        if pend is not None:
            stage2(pend)

    for st in range(NTOK // 512):
        main = st >= NPRE // 512
        hbs = []
        for tt4 in range(4):
            tok0 = st * 512 + tt4 * 128
            xt = xr.next()
            k.dma("sp", xt[:, :], IO["xp"][tok0:tok0 + 128, :], W=[xt])
            hbs.append(norm_stats(k, xt, 128, nmb, C))
        for tt4 in range(4):
            transpose_to(k, hbs[tt4], 128, hT, tt4 * 128, C)
        chs = list(range(12)) if main else list(range(4, 12))
        pchs = list(range(12)) if st >= NPRE // 512 - 1 else chs

        def ext_dst(ch, psA):
            k.cp("dve", ext[:, ch, 3:515], psA[:, :], [psA], [ext])

        feature_chunks(pchs, chs, lambda kc: hT[:, kc, :], [hT], 512, ext_dst,
                       lambda ch, i: ext[:, ch, i:i + 512],
                       lambda ch: qkT[:, ch, :], lambda j: vT[:, j, :], {"ext": [ext], "qk": qkT, "v": vT})
        if st == NTOK // 512 - 1:
            for j in range(3):
                psT3 = k.ps.next()
                for kc in range(8):
                    k.mm(psT3[0:3, :], hT[:, kc, 509:512], w_a[:, kc, j * 512:(j + 1) * 512], [hT, C["wa_g"][j]], [psT3], start=(kc == 0), stop=(kc == 7))
                pre3 = G["l32"].next()
                k.cp("dve", pre3[0:3, :], psT3[0:3, :], [psT3], [pre3])
                k.dma("sp", IO["conv_p"][:, j * 512:(j + 1) * 512], pre3[0:3, :], R=[pre3])
        halo = G.setdefault("halo", A.alloc("halo", [128, 12, 3], BF16))
        k.cp("pool", halo[:, :, :], ext[:, :, 512:515], [ext], [halo])
        gens = []
        for tt4 in range(4):
            cs = slice(tt4 * 128, (tt4 + 1) * 128)
            gcol = st * 512 + tt4 * 128 - NPRE

            def mix_out(og, gcol=gcol):
                psm = k.ps.next()
                pbm = bank_bf(psm)
                for h in range(4):
                    k.tr(pbm[:, h * 128:(h + 1) * 128], og[:, h * 128:(h + 1) * 128], C["ident"][:, :], [og, C["ident"]], [psm])
                k.cp("act", mixT_a[:, :, gcol:gcol + 128], pbm[:, 0:512].rearrange("p (h c) -> p h c", h=4), [psm], [mixT_a])

            gens.append(gdn_tile(k, C, G, hT[:, :, cs], qkT[:, 0:4, cs], qkT[:, 4:8, cs], vT[:, :, cs], S, Sb, [hT, qkT, vT], main, 7, mix_out))
        k.free_ring(C["hb"], xr, G["e32"], G["c32"], G["l32"], G["sq"])
        extra = {}
        for nm, n, shp, dt in EXTRA_RINGS:
            extra[nm] = [A.alloc("x_%s%d" % (nm, i), shp, dt) for i in range(n)]
            G[nm].t.extend(extra[nm])
        run_interleaved(gens, offs=3, maxact=3)
        for nm, tl in extra.items():
            for t in tl:
                G[nm].t.remove(t)
            G[nm].i = 0
            A.free(*tl)
        C["hb"] = k.rings("hb", 4, [128, 1024], BF16)
        xr = k.rings("xr", 2, [128, 1024], F32)
        for nm, n_ in (("e32", 2), ("c32", 4), ("l32", 2)):
            G[nm] = k.rings(nm, n_, [128, 512], F32)
        G["sq"] = k.rings("sq", 2, [128, 512], BF16)
        k.cp("pool", ext[:, :, 0:3], halo[:, :, :], [halo], [ext])
    k.dma("sp", IO["rec_p"].rearrange("h d e -> d h e"), S[:, :, :], R=[S])
    A.free(hT, ext, qkT, vT, G["halo"])

    xt = xr.next()
    k.dma("sp", xt[0:NS, :], IO["xs"][:, :], W=[xt])
    hTs = A.alloc("hTs", [128, 8, NS], BF16)
    norm_transpose(k, xt, NS, nmb, hTs, 0, C)
    exts = A.alloc("exts", [128, 12, 4, 11], BF16)
    sct = A.alloc("sct", [12, 1536], F32)
    k.dma("sp", sct[0:12, :], IO["sconv"].rearrange("s i c -> (s i) c"), W=[sct])
    psh = k.ps.next()
    for ch in range(12):
        k.tr(psh[:, ch * 12:(ch + 1) * 12], sct[0:12, ch * 128:(ch + 1) * 128], C["identf"][0:12, 0:12], [sct, C["identf"]], [psh])
    k.cp("dve", exts[:, :, :, 0:3], psh[:, 0:144].rearrange("p (c s i) -> p c s i", c=12, s=4), [psh], [exts])
    qks = A.alloc("qks", [128, 8, NS], BF16)
    vs = A.alloc("vs", [128, 4, NS], BF16)

    def ext_dst_s(ch, psA):
        k.cp("act", exts[:, ch, :, 3:11], psA[:, 0:NS].rearrange("p (s t) -> p s t", s=4), [psA], [exts])

    feature_chunks(list(range(12)), list(range(12)), lambda kc: hTs[:, kc, :], [hTs], NS, ext_dst_s,
                   lambda ch, i: exts[:, ch, :, i:i + 8],
                   lambda ch: qks[:, ch, :], lambda j: vs[:, j, :], {"ext": [exts], "qk": qks, "v": vs},
                   conv_out=lambda psB: psB[:, 0:NS].rearrange("p (s t) -> p s t", s=4))
    pres = A.alloc("pres", [NS, 1536], F32)
    for j in range(3):
        psT3 = k.ps.next()
        for kc in range(8):
            k.mm(psT3[0:NS, :], hTs[:, kc, :], w_a[:, kc, j * 512:(j + 1) * 512], [hTs, C["wa_g"][j]], [psT3], start=(kc == 0), stop=(kc == 7))
        k.cp("dve", pres[0:NS, j * 512:(j + 1) * 512], psT3[0:NS, :], [psT3], [pres])
    for s in range(4):
        k.dma("sp", IO["conv_s"][s, :, :], pres[8 * s + 5:8 * s + 8, :], R=[pres])
    hpad = k.rings("hpad", 2, [128, 8, 128], BF16)
    qkpad = k.rings("qkpad", 2, [128, 8, 128], BF16)
    vpad = k.rings("vpad", 2, [128, 4, 128], BF16)
    Ss = k.rings("Ss", 2, [128, 4, 128], F32)
    Sbs = k.rings("Sbs", 2, [128, 4, 128], BF16)
    for r in (hpad, qkpad, vpad):
        for t in r.t:
            k.memset(k.alt(), t[:, :, :], 0.0, [t])
    sgens = []
    for s in range(4):
        hp, qp, vp, S_s, Sb_s = hpad.next(), qkpad.next(), vpad.next(), Ss.next(), Sbs.next()
        k.cp("pool", hp[:, :, 0:8], hTs[:, :, 8 * s:8 * s + 8], [hTs], [hp])
        k.cp("pool", qp[:, :, 0:8], qks[:, :, 8 * s:8 * s + 8], [qks], [qp])
        k.cp("pool", vp[:, :, 0:8], vs[:, :, 8 * s:8 * s + 8], [vs], [vp])
        k.dma("sp", S_s[:, :, :], IO["srec"][s].rearrange("h d e -> d h e"), W=[S_s])
        k.cp("act", Sb_s[:, :, :], S_s[:, :, :], [S_s], [Sb_s])

        def mix_out_s(og, s=s):
            psm = k.ps.next()
            pbm = bank_bf(psm)
            for h in range(4):
                k.tr(pbm[:, h * 8:(h + 1) * 8], og[0:8, h * 128:(h + 1) * 128], C["ident"][0:8, 0:8], [og, C["ident"]], [psm])
            k.cp("act", mixT_a[:, :, NMAIN + 8 * s:NMAIN + 8 * s + 8], pbm[:, 0:32].rearrange("p (h c) -> p h c", h=4), [psm], [mixT_a])

        def seq_gen(s=s, hp=hp, qp=qp, vp=vp, S_s=S_s, Sb_s=Sb_s, mix_out_s=mix_out_s):
            yield from gdn_tile(k, C, G, hp[:, :, :], qp[:, 0:4, :], qp[:, 4:8, :], vp[:, :, :], S_s, Sb_s, [hp, qp, vp], True, 3, mix_out_s, vm=C["vmask"])
            k.dma("sp", IO["rec_s"][s].rearrange("h d e -> d h e"), S_s[:, :, :], R=[S_s])

        sgens.append(seq_gen())
        if s % 2 == 1:
            run_interleaved(sgens)
            sgens = []

    merge_free(A, w_a, list(wa_units.values()))
    A.free(nmb, wconv, diag, dtb, negA, noa, lnq, S, Sb, hTs, exts, sct, qks, vs, pres)
    k.free_ring(C["sm"], C["hb"], xr, hpad, qkpad, vpad, Ss, Sbs)
    for nm, r in G.items():
        if isinstance(r, Ring):
            k.free_ring(r)
    G.clear()


def merge_free(A, parent, children):
    for ch in children:
        for e2, seq in ch.rde.items():
            if parent.rde.get(e2, 0) < seq:
                parent.rde[e2] = seq
        parent.rdd.extend(ch.rdd)
        if ch.lw is not None:
            if ch.lw[0] == "e":
                if parent.rde.get(ch.lw[1], 0) < ch.lw[2]:
                    parent.rde[ch.lw[1]] = ch.lw[2]
            else:
                parent.rdd.append(ch.lw[1])
    A.free(parent)


def setup_consts(k, C, IO):
    A = k.A
    cf = IO["cf32"]
    identf = A.alloc("identf", [128, 128], F32)
    mIU = A.alloc("mIU", [128, 128], F32)
    onesf = A.alloc("onesf", [128, 128], F32)
    mSU4 = A.alloc("mSU4", [128, 4, 128], F32)
    mIU4 = A.alloc("mIU4", [128, 4, 128], F32)
    id4f = A.alloc("id4f", [128, 4, 128], F32)
    k.dma("sp", identf[:, :], cf[:, 0:128], W=[identf])
    k.dma("sp", id4f[:, :, :], cf[:, 0:512].rearrange("p (h c) -> p h c", h=4), W=[id4f])
    k.dma("sp", mSU4[:, :, :], cf[:, 512:1024].rearrange("p (h c) -> p h c", h=4), W=[mSU4])
    k.dma("sp", mIU4[:, :, :], cf[:, 1024:1536].rearrange("p (h c) -> p h c", h=4), W=[mIU4])
    k.dma("sp", mIU[:, :], cf[:, 1024:1152], W=[mIU])
    k.dma("sp", onesf[:, :], cf[:, 1536:1664], W=[onesf])
    ident = A.alloc("ident", [128, 128], BF16)
    onesb = A.alloc("onesb", [128, 128], BF16)
    id4b = A.alloc("id4b", [128, 4, 128], BF16)
    k.cp("dve", ident[:, :], identf[:, :], [identf], [ident])
    k.cp("dve", onesb[:, :], onesf[:, :], [onesf], [onesb])
    k.cp("dve", id4b[:, :, :], id4f[:, :, :], [id4f], [id4b])
    vmask = A.alloc("vmask", [128, 1], F32)
    edge = A.alloc("edge", [128, 1], F32)
    k.dma("sp", vmask[:, :], IO["vmask"][:, :], W=[vmask])
    k.dma("sp", edge[:, :], IO["edge8"][:, :], W=[edge])
    C.update(identf=identf, mIU=mIU, onesf=onesf, mSU4=mSU4, mIU4=mIU4, ident=ident, onesb=onesb, id4b=id4b,
             vmask=vmask, edge=edge)
    A.free(id4f)


def stage_K(k, C, IO):
    A = k.A
    w_b = A.alloc("w_b", [128, 8, 1536], BF16)
    for kc in range(8):
        k.dma("pool", w_b[:, kc, :], IO["w_in"][kc * 128:(kc + 1) * 128, 2056:3592], W=[w_b])
    nmb = A.alloc("nmb2", [128, 1024], F32)
    k.dma("sp", nmb[:, :], IO["norm_mix"].partition_broadcast(128), W=[nmb])
    C["sm"] = k.rings("smk", 6, [128, 8], F32)
    C["hb"] = k.rings("hbk", 5, [128, 1024], BF16)
    xr = k.rings("xrk", 4, [128, 1024], F32)
    hTr = k.rings("hTk", 2, [128, 8, 512], BF16)
    kT_b = A.alloc("kT_b", [128, 4, NTOK], BF16)
    vT_b = A.alloc("vT_b", [128, 4, NTOK], BF16)
    qT_b = A.alloc("qT_b", [128, 4, NMAIN], BF16)
    kv = [T("kv%d" % st, None) for st in range(NTOK // 512)]
    for t in kv:
        for par in (kT_b, vT_b, qT_b):
            for e2, seq in par.rde.items():
                if t.rde.get(e2, 0) < seq:
                    t.rde[e2] = seq
            t.rdd.extend(par.rdd)
    C.update(kT_b=kT_b, vT_b=vT_b, qT_b=qT_b, kv=kv)
    ost = k.rings("ost", 2, [128, 512], F32)
    def k_stats(st):
        hbs = []
        for tt4 in range(4):
            tok0 = st * 512 + tt4 * 128
            xt = xr.next()
            k.dma("sp", xt[:, :], IO["xp"][tok0:tok0 + 128, :], W=[xt])
            hbs.append(norm_stats(k, xt, 128, nmb, C))
        return hbs

    def k_tr(hbs):
        hT = hTr.next()
        for tt4 in range(4):
            transpose_to(k, hbs[tt4], 128, hT, tt4 * 128, C)
        return hT

    hT_next = k_tr(k_stats(0))
    for st in range(NTOK // 512):
        main = st >= NPRE // 512
        hT = hT_next
        hbs_next = k_stats(st + 1) if st + 1 < NTOK // 512 else None
        for ch in (range(12) if main else range(4, 12)):
            psA = k.ps.next()
            for kc in range(8):
                k.mm(psA[:, :], w_b[:, kc, ch * 128:(ch + 1) * 128], hT[:, kc, :], [hT, w_b], [psA], start=(kc == 0), stop=(kc == 7))
            if ch < 4:
                dst = qT_b[:, ch, (st * 512 - NPRE):(st * 512 - NPRE) + 512]
            elif ch < 8:
                dst = kT_b[:, ch - 4, st * 512:(st + 1) * 512]
            else:
                dst = vT_b[:, ch - 8, st * 512:(st + 1) * 512]
            k.cp(k.alt(("act", "dve")), dst, psA[:, :], [psA], [kv[st]])
        if hbs_next is not None:
            hT_next = k_tr(hbs_next)
        if main and KSTOP >= 2:
            for tt4 in range(4):
                tok0 = st * 512 + tt4 * 128
                for src, dstname in ((kT_b, "wk_p"), (vT_b, "wv_p")):
                    pst = k.ps.next()
                    pbt = bank_bf(pst)
                    for c in range(4):
                        k.tr(pbt[:, c * 128:(c + 1) * 128], src[:, c, tok0:tok0 + 128], C["ident"][:, :], [kv[st], C["ident"]], [pst])
                    o = ost.next()
                    k.cp(k.alt(("act", "dve")), o[:, :], pbt[:, 0:512], [pst], [o])
                    k.dma("sp", IO[dstname][tok0 - NPRE:tok0 - NPRE + 128, :], o[:, :], R=[o])
    if KSTOP < 3:
        return
    xt = xr.next()
    k.dma("sp", xt[0:NS, :], IO["xs"][:, :], W=[xt])
    hTs = A.alloc("hTs2", [128, 8, NS], BF16)
    norm_transpose(k, xt, NS, nmb, hTs, 0, C)
    qTs = A.alloc("qTs", [128, 4, NS], BF16)
    kTn = A.alloc("kTn", [128, 4, NS], BF16)
    for ch in range(8):
        psA = k.ps.next()
        for kc in range(8):
            k.mm(psA[:, 0:NS], w_b[:, kc, ch * 128:(ch + 1) * 128], hTs[:, kc, :], [hTs, w_b], [psA], start=(kc == 0), stop=(kc == 7))
        dstT = qTs if ch < 4 else kTn
        k.cp("act", dstT[:, ch % 4, :], psA[:, 0:NS], [psA], [dstT])
    if KSTOP < 4:
        return
    vn_aug = A.alloc("vn_aug", [NS, 8, 66], BF16)
    k.memset("pool", vn_aug[:, :, :], 1.0, [vn_aug])
    for j, dstname in ((1, "wk_s"), (2, "wv_s")):
        psA = k.ps.next()
        for kc in range(8):
            k.mm(psA[0:NS, :], hTs[:, kc, :], w_b[:, kc, j * 512:(j + 1) * 512], [hTs, w_b], [psA], start=(kc == 0), stop=(kc == 7))
        o = ost.next()
        k.cp("dve", o[0:NS, :], psA[0:NS, :], [psA], [o])
        for s in range(4):
            if "D" not in KOFF:
                k.dma("sp", IO[dstname][s, 2040:2048, :], o[8 * s:8 * s + 8, :], R=[o])
        if j == 2 and "A" not in KOFF:
            k.cp("act", vn_aug[:, :, 0:64], psA[0:NS, :].rearrange("p (h e) -> p h e", h=8), [psA], [vn_aug])
    if KSTOP < 5:
        return
    Qbd = A.alloc("Qbd", [128, 4, 4, 48], BF16)
    k.memset("pool", Qbd[:, :, :, :], 0.0, [Qbd])
    for c in range(4):
        for br in range(3):
            k.cp(k.alt(), Qbd[0:64, c, :, br * 8:br * 8 + 8], qTs[0:64, c, :].rearrange("p (s t) -> p s t", s=4), [qTs], [Qbd])
            k.cp(k.alt(), Qbd[64:128, c, :, 24 + br * 8:32 + br * 8], qTs[64:128, c, :].rearrange("p (s t) -> p s t", s=4), [qTs], [Qbd])
    C.update(kTn=kTn, vn_aug=vn_aug, Qbd=Qbd)
    A.free(w_b, nmb, hTs, qTs)
    k.free_ring(C["sm"], C["hb"], xr, hTr, ost)


def finalize_attn(k, C, F, acc_ap, accT, n, dst):
    for c0 in range(0, n, 512):
        w = min(512, n - c0)
        sq = F["sq"].next()
        k.act(sq[0:65, 0:w], acc_ap[0:65, c0:c0 + w], AF.Square, [accT], [sq])
        psF = k.ps.next()
        k.mm(psF[0:64, 0:w], C["gmb"][0:65, 0:64], sq[0:65, 0:w], [C["gmb"], sq], [psF])
        l32 = F["l32"].next()
        k.act(l32[0:64, 0:w], psF[0:64, 0:w], AF.Ln, [psF], [l32])
        k.act(l32[0:64, 0:w], l32[0:64, 0:w], AF.Exp, [l32], [l32], scale=-0.5)
        dap, dT = dst(c0, w)
        k.stt("dve", dap, acc_ap[0:64, c0:c0 + w], C["nob"][0:64, 0:1], l32[0:64, 0:w], ALU.mult, ALU.mult, [accT, C["nob"], l32], [dT])


def stage_B(k, C, IO):
    A = k.A
    kT_b, vT_b, qT_b, kv = C["kT_b"], C["vT_b"], C["qT_b"], C["kv"]
    identb = C["ident"]
    pb = A.alloc("pbias", [128, 8, 3, 256], BF16)
    k.dma("sp", pb[:, :, :, :], IO["pbias"][:, :, :, :], W=[pb])
    pe_ = A.alloc("pedge", [128, 8, 3, 128], BF16)
    for h in range(8):
        k.ts(k.alt(), pe_[:, h, :, :], pb[:, h, :, 128:256], C["edge"][:, 0:1], ALU.add, [pb, C["edge"]], [pe_])
    gmb = A.alloc("gmb", [65, 64], BF16)
    k.dma("sp", gmb[:, :], IO["gmb"][:, :], W=[gmb])
    nob = A.alloc("nob", [64, 1], F32)
    k.dma("sp", nob[:, :], IO["nob"].rearrange("(p o) -> p o", o=1), W=[nob])
    C.update(gmb=gmb, nob=nob)
    mixT_b = C["mixT_b"] = A.alloc("mixT_b", [128, 4, NMAIN + NS], BF16)
    F = {"sq": k.rings("fsq", 2, [65, 512], BF16), "l32": k.rings("fl32", 2, [64, 512], F32)}
    Vblks = [A.alloc("Vblk%d" % i, [128, 69, 2, 66], BF16) for i in range(2)]
    for Vb in Vblks:
        k.memset("pool", Vb[:, :, :, :], 1.0, [Vb])
    accr = k.rings("acc", 1, [65, NMAIN], F32)
    PTr = k.rings("PT", 6, [128, 256], BF16)
    otmp = k.rings("otmp", 2, [64, 512], BF16)
    blocks = []
    for br, d in enumerate(DILS):
        for r in range(d):
            for n in range(16 // d - 1, 32 // d):
                blocks.append((br, r, n))
    bidx = {b: i for i, b in enumerate(blocks)}
    assert len(blocks) == 69
    def build_vblocks(c):
        Vb = Vblks[c % 2]
        for g0 in range(0, 69, 4):
            grp = blocks[g0:g0 + 4]
            psv = k.ps.next()
            pbv = bank_bf(psv)
            for j, (br, r, n) in enumerate(grp):
                d = DILS[br]
                k.tr(pbv[:, j * 128:(j + 1) * 128], vT_b[:, c, sl(r + d * 128 * n, 128, d)], identb[:, :], kv + [identb], [psv])
            ng = len(grp)
            k.cp(k.alt(("act", "dve")), Vb[:, g0:g0 + ng, :, 0:64],
                 pbv[:, 0:ng * 128].rearrange("p (g h e) -> p g h e", g=ng, h=2), [psv], [Vb])
            yield

    for _ in build_vblocks(0):
        pass
    for c in range(4):
        Vblk = Vblks[c % 2]
        vgen = build_vblocks(c + 1) if c + 1 < 4 else None
        for hh in range(2):
            h = 2 * c + hh
            po = 64 * hh
            acc = accr.next()
            hb_list = []
            for br, d in enumerate(DILS):
                nq0, nq1 = 16 // d, 32 // d
                for r in range(d):
                    for n in range(nq0 - 1, nq1):
                        hb_list.append((br, d, r, n, n == nq0 - 1, n == nq1 - 1, nq0))
            PTs = {}

            def emit_S(i, h=h, c=c, po=po):
                br, d, r, n, first, last, nq0 = hb_list[i]
                ks = sl(r + d * 128 * n, 128, d)
                if first:
                    q0, N, bias, bT = r + d * 128 * nq0 - NPRE, 128, pe_[:, h, br, :], pe_
                elif last:
                    q0, N, bias, bT = r + d * 128 * n - NPRE, 128, pb[:, h, br, 0:128], pb
                else:
                    q0, N, bias, bT = r + d * 128 * n - NPRE, 256, pb[:, h, br, :], pb
                psS = k.ps.next()
                k.mm(psS[:, 0:N], kT_b[po:po + 64, c, ks], qT_b[po:po + 64, c, sl(q0, N, d)], kv, [psS], start=True, stop=False)
                k.mm(psS[:, 0:N], identb[:, :], bias, [identb, bT], [psS], start=False, stop=True)
                PT = PTr.next()
                k.act(PT[:, 0:N], psS[:, 0:N], AF.Exp, [psS], [PT], scale=0.125)
                PTs[i] = PT

            def emit_PV(i, hh=hh, acc=acc):
                br, d, r, n, first, last, nq0 = hb_list[i]
                if first:
                    return
                PT, prevPT = PTs[i], PTs[i - 1]
                prev_first = hb_list[i - 1][4]
                psO = k.ps.next()
                pp = prevPT[:, 0:128] if prev_first else prevPT[:, 128:256]
                k.mm(psO[0:65, 0:128], Vblk[:, bidx[(br, r, n - 1)], hh, 0:65], pp, [Vblk, prevPT], [psO], start=True, stop=False)
                k.mm(psO[0:65, 0:128], Vblk[:, bidx[(br, r, n)], hh, 0:65], PT[:, 0:128], [Vblk, PT], [psO], start=False, stop=True)
                qc = r + d * 128 * n - NPRE
                qcols = sl(qc, 128, d)
                if br == 0:
                    k.cp("dve", acc[:, qcols], psO[0:65, 0:128], [psO], [acc])
                else:
                    k.tt("dve", acc[:, qcols], acc[:, qcols], psO[0:65, 0:128], ALU.add, [acc, psO], [acc])
                PTs.pop(i - 1, None)

            LOOK = 3
            for i in range(len(hb_list) + LOOK):
                if i < len(hb_list):
                    emit_S(i)
                if i - LOOK >= 0:
                    emit_PV(i - LOOK)
                if vgen is not None and hh == 0 and i % 3 == 2:
                    next(vgen, None)
            if vgen is not None and hh == 1:
                for _ in vgen:
                    pass
            if hh == 0:
                finalize_attn(k, C, F, acc, acc, NMAIN, lambda c0, w, c=c: (mixT_b[0:64, c, c0:c0 + w], mixT_b))
            else:
                def dst(c0, w, c=c):
                    o = otmp.next()
                    dst.last = (o, c0, w)
                    return o[0:64, 0:w], o
                for c0 in range(0, NMAIN, 512):
                    finalize_attn(k, C, F, acc[:, c0:c0 + 512], acc, 512, dst)
                    o, _, w = dst.last
                    k.dma("sp", mixT_b[64:128, c, c0:c0 + 512], o[0:64, 0:512], R=[o], W=[mixT_b])
    merge_free(A, kT_b, kv)
    A.free(vT_b, qT_b, pb, pe_, *Vblks)
    k.free_ring(accr, PTr)

    k.set_ring(7)
    psOs = k.banks[7]
    sbc = A.alloc("sbc", [128, 16, 192], BF16)
    sbn = A.alloc("sbn", [NS, 4, 192], BF16)
    k.dma("sp", sbc[:, :, :], IO["sbias_c"][:, :, :], W=[sbc])
    k.dma("sp", sbn[:, :, :], IO["sbias_n"][:, :, :], W=[sbn])
    kTn, vn_aug, Qbd = C["kTn"], C["vn_aug"], C["Qbd"]
    kc32r = k.rings("kc32", 2, [128, 512], F32)
    vc32r = k.rings("vc32", 2, [128, 512], F32)
    kcbr = k.rings("kcb", 2, [128, 512], BF16)
    vaugr = k.rings("vaug", 2, [128, 8, 66], BF16)
    kTsr = k.rings("kTs", 2, [128, 4, 128], BF16)
    PTsr = k.rings("PTs", 2, [128, 192], BF16)
    tmpr = k.rings("ptmp", 2, [128, 8, 8], BF16)
    Pqr = [k.rings("Pq%d" % s, 2, [128, 8, NS], BF16) for s in range(4)]
    for t in vaugr.t:
        k.memset("pool", t[:, :, :], 1.0, [t])
    for s in range(4):
        for t in Pqr[s].t:
            k.memset(k.alt(), t[:, :, :], 0.0, [t])
    for s in range(4):
        for kt in range(17):
            if kt < 16:
                kc32, vc32, kcb, vaug, kTs = kc32r.next(), vc32r.next(), kcbr.next(), vaugr.next(), kTsr.next()
                k.dma("sp", kc32[:, :], IO["ck"][s, 128 * kt:128 * kt + 128, :], W=[kc32])
                k.dma("sp", vc32[:, :], IO["cv"][s, 128 * kt:128 * kt + 128, :], W=[vc32])
                for src, nm in ((kc32, "wk_s"), (vc32, "wv_s")):
                    if kt == 0:
                        k.dma("sp", IO[nm][s, 0:120, :], src[8:128, :], R=[src])
                    else:
                        k.dma("sp", IO[nm][s, 128 * kt - 8:128 * kt + 120, :], src[:, :], R=[src])
                k.cp("act", kcb[:, :], kc32[:, :], [kc32], [kcb])
                k.cp("dve", vaug[:, :, 0:64], vc32[:, :].rearrange("p (h e) -> p h e", h=8), [vc32], [vaug])
                pst = k.ps.next()
                pbt = bank_bf(pst)
                for c in range(4):
                    k.tr(pbt[:, c * 128:(c + 1) * 128], kcb[:, c * 128:(c + 1) * 128], identb[:, :], [kcb, identb], [pst])
                k.cp("act", kTs[:, :, :], pbt[:, 0:512].rearrange("p (c t) -> p c t", c=4), [pst], [kTs])
                np_ = 128
                lhs_k = lambda c, kTs=kTs: kTs[:, c, :]
                kdep = kTs
                bias = sbc[:, kt, :]
                bT = sbc
                vsrc = vaug
                idb = identb[:, :]
            else:
                np_ = NS
                lhs_k = lambda c: kTn[:, c, :]
                kdep = kTn
                bias = sbn[0:NS, s, :]
                bT = sbn
                vsrc = vn_aug
                idb = identb[0:NS, 0:NS]
            psS = k.ps.next()
            k.mm(psS[0:np_, 0:192], idb, bias, [identb, bT], [psS], start=True, stop=False)
            for c in range(4):
                k.mm(psS[0:np_, c * 48:(c + 1) * 48], lhs_k(c), Qbd[:, c, s, :], [kdep, Qbd], [psS], start=False, stop=(c == 3))
            PTs = PTsr.next()
            k.act(PTs[0:np_, :], psS[0:np_, 0:192], AF.Exp, [psS], [PTs], scale=0.125)
            Pq = Pqr[s].next()
            tmp = tmpr.next()
            P4 = PTs[0:np_, :].rearrange("p (h b t) -> p h b t", h=8, b=3)
            k.tt("dve", tmp[0:np_, :, :], P4[:, :, 0, :], P4[:, :, 1, :], ALU.add, [PTs], [tmp])
            k.tt("dve", Pq[0:np_, :, 8 * s:8 * s + 8], tmp[0:np_, :, :], P4[:, :, 2, :], ALU.add, [PTs, tmp], [Pq])
            for h in range(8):
                k.mm(psOs[0:65, h * NS:(h + 1) * NS], vsrc[0:np_, h, 0:65], Pq[0:np_, h, :], [vsrc, Pq], [psOs],
                     start=(s == 0 and kt == 0 and h == 0), stop=(s == 3 and kt == 16), skip=True)
    accs = A.alloc("accs", [65, 8, NS], F32)
    k.cp("dve", accs[:, :, :], psOs[0:65, 0:8 * NS].rearrange("p (h t) -> p h t", h=8), [psOs], [accs])
    for h in range(8):
        c, hh = h // 2, h % 2
        if hh == 0:
            finalize_attn(k, C, F, accs[:, h, :], accs, NS, lambda c0, w, c=c: (mixT_b[0:64, c, NMAIN:NMAIN + NS], mixT_b))
        else:
            o = otmp.next()
            finalize_attn(k, C, F, accs[:, h, :], accs, NS, lambda c0, w, o=o: (o[0:64, 0:NS], o))
            k.dma("sp", mixT_b[64:128, c, NMAIN:NMAIN + NS], o[0:64, 0:NS], R=[o], W=[mixT_b])
    k.set_ring(8)
    A.free(sbc, sbn, kTn, vn_aug, Qbd, accs, gmb, nob)
    k.free_ring(kc32r, vc32r, kcbr, vaugr, kTsr, PTsr, tmpr, otmp, F["sq"], F["l32"], *Pqr)


def stage_C(k, C, IO):
    A = k.A
    k.set_ring(4)
    accb = k.banks[4:8]
    mixT_a, mixT_b = C["mixT_a"], C["mixT_b"]
    wo = A.alloc("wo", [128, 8, 1024], BF16)
    for kc in range(8):
        k.dma("pool", wo[:, kc, :], IO["w_out"][kc * 128:(kc + 1) * 128, :], W=[wo])
    WB = 256
    wgb = [A.alloc("wg%d" % i, [128, 8, WB], BF16) for i in range(DFF // WB)]
    wub = [A.alloc("wu%d" % i, [128, 8, WB], BF16) for i in range(DFF // WB)]
    for i in range(DFF // WB):
        k.dma("pool", wgb[i][:, :, :], IO["w_gate"][:, i * WB:(i + 1) * WB].rearrange("(c p) n -> p c n", p=128), W=[wgb[i]])
        k.dma("pool", wub[i][:, :, :], IO["w_up"][:, i * WB:(i + 1) * WB].rearrange("(c p) n -> p c n", p=128), W=[wub[i]])
    nfb = A.alloc("nfb", [128, 1024], F32)
    nfin = A.alloc("nfin", [128, 1024], F32)
    k.dma("sp", nfb[:, :], IO["norm_ffn"].partition_broadcast(128), W=[nfb])
    k.dma("sp", nfin[:, :], IO["norm_final"].partition_broadcast(128), W=[nfin])
    C["sm"] = k.rings("smc", 4, [128, 8], F32)
    C["hb"] = k.rings("hbc", 2, [128, 1024], BF16)
    x1r = k.rings("x1", 4, [128, 1024], F32)
    hfr = k.rings("hfT", 2, [128, 8, 256], BF16)
    wdr = k.rings("wd", 4, [128, 1024], BF16)
    e32r = k.rings("ce32", 4, [128, 256], F32)
    c32r = k.rings("cc32", 4, [128, 256], F32)
    u32r = k.rings("cu32", 4, [128, 256], F32)
    aTr = k.rings("aT", 4, [128, 256], BF16)
    units = [(IO["xp"], NPRE + u * 256, u * 256, 2, 128, IO["yp"], u * 256) for u in range(NMAIN // 256)]
    units.append((IO["xs"], 0, NMAIN, 1, NS, IO["ys"], 0))
    NJ = DFF // 128

    def pre(unit, st):
        (xsrc, xrow0, mcol0, ntile, p, ydst, yrow0) = unit
        st["hfT"] = hfr.next()
        st["x1s"] = []
        for t in range(ntile):
            x1 = x1r.next()
            st["x1s"].append(x1)
            k.dma("sp", x1[0:p, :], xsrc[xrow0 + t * 128:xrow0 + t * 128 + p, :], W=[x1])
            cols = slice(mcol0 + t * 128, mcol0 + t * 128 + p)
            for half in range(2):
                psX = k.ps.next()
                for kc in range(8):
                    lhsT = mixT_a[:, kc, cols] if kc < 4 else mixT_b[:, kc - 4, cols]
                    k.mm(psX[0:p, :], lhsT, wo[:, kc, half * 512:(half + 1) * 512], [mixT_a, mixT_b, wo], [psX], start=(kc == 0), stop=(kc == 7))
                k.tt("dve", x1[0:p, half * 512:(half + 1) * 512], x1[0:p, half * 512:(half + 1) * 512], psX[0:p, :], ALU.add, [x1, psX], [x1])
                yield
            norm_transpose(k, x1, p, nfb, st["hfT"], t * 128, C)
            yield

    def ffn(unit, st):
        (xsrc, xrow0, mcol0, ntile, p, ydst, yrow0) = unit
        ntok = (ntile - 1) * 128 + p
        hfT = st["hfT"]

        def issue_gu(j):
            psG = k.ps.next()
            for kc in range(8):
                k.mm(psG[:, 0:ntok], wgb[j // 2][:, kc, (j % 2) * 128:(j % 2) * 128 + 128], hfT[:, kc, 0:ntok], [wgb[j // 2], hfT], [psG], start=(kc == 0), stop=(kc == 7))
            psU = k.ps.next()
            for kc in range(8):
                k.mm(psU[:, 0:ntok], wub[j // 2][:, kc, (j % 2) * 128:(j % 2) * 128 + 128], hfT[:, kc, 0:ntok], [wub[j // 2], hfT], [psU], start=(kc == 0), stop=(kc == 7))
            return psG, psU

        def issue_pair(jp):
            return [issue_gu(jp), issue_gu(jp + 1)]

        pend = issue_pair(0)
        for jp in range(0, NJ, 2):
            cur = pend
            wds, es, gs, us, aTs = [], [], [], [], []
            for q in range(2):
                wd = wdr.next()
                k.dma("pool", wd[:, :], IO["w_down"][(jp + q) * 128:(jp + q + 1) * 128, :], W=[wd])
                wds.append(wd)
                es.append(e32r.next()); gs.append(c32r.next()); us.append(u32r.next()); aTs.append(aTr.next())
            for q in range(2):
                k.act(es[q][:, 0:ntok], cur[q][0][:, 0:ntok], AF.Exp, [cur[q][0]], [es[q]], scale=-1.0)
            for q in range(2):
                k.cp("dve", gs[q][:, 0:ntok], cur[q][0][:, 0:ntok], [cur[q][0]], [gs[q]])
            for q in range(2):
                k.cp("act", us[q][:, 0:ntok], cur[q][1][:, 0:ntok], [cur[q][1]], [us[q]])
            yield
            for q in range(2):
                k.act(es[q][:, 0:ntok], es[q][:, 0:ntok], AF.Ln, [es[q]], [es[q]], bias=1.0)
            for q in range(2):
                k.act(es[q][:, 0:ntok], es[q][:, 0:ntok], AF.Exp, [es[q]], [es[q]], scale=-1.0)
            for q in range(2):
                k.tt("dve", gs[q][:, 0:ntok], gs[q][:, 0:ntok], es[q][:, 0:ntok], ALU.mult, [gs[q], es[q]], [gs[q]])
            for q in range(2):
                k.tt("dve", aTs[q][:, 0:ntok], gs[q][:, 0:ntok], us[q][:, 0:ntok], ALU.mult, [gs[q], us[q]], [aTs[q]])
            if jp + 2 < NJ:
                pend = issue_pair(jp + 2)
            for q in range(2):
                j = jp + q
                for t in range(ntile):
                    for half in range(2):
                        ab = accb[t * 2 + half]
                        k.mm(ab[0:p, :], aTs[q][:, t * 128:t * 128 + p], wds[q][:, half * 512:(half + 1) * 512], [aTs[q], wds[q]], [ab],
                             start=(j == 0), stop=(j == NJ - 1))

    def post(unit, st):
        (xsrc, xrow0, mcol0, ntile, p, ydst, yrow0) = unit
        for t in range(ntile):
            x1 = st["x1s"][t]
            for half in range(2):
                ab = accb[t * 2 + half]
                k.tt("dve", x1[0:p, half * 512:(half + 1) * 512], x1[0:p, half * 512:(half + 1) * 512], ab[0:p, :], ALU.add, [x1, ab], [x1])
            sm = C["sm"].next()
            jk = C["hb"].next()
            k.act(jk[0:p, :], x1[0:p, :], AF.Square, [x1], [jk, sm], accum=sm[0:p, 0:1])
            k.act(sm[0:p, 1:2], sm[0:p, 0:1], AF.Ln, [sm], [sm], scale=1.0 / D, bias=EPS)
            k.act(sm[0:p, 2:3], sm[0:p, 1:2], AF.Exp, [sm], [sm], scale=-0.5)
            k.stt("dve", x1[0:p, :], x1[0:p, :], sm[0:p, 2:3], nfin[0:p, :], ALU.mult, ALU.mult, [x1, sm, nfin], [x1])
            k.dma("sp", ydst[yrow0 + t * 128:yrow0 + t * 128 + p, :], x1[0:p, :], R=[x1])

    states = [dict() for _ in units]
    for _ in pre(units[0], states[0]):
        pass
    for u, unit in enumerate(units):
        gens = [ffn(unit, states[u])]
        if u + 1 < len(units):
            gens.append(pre(units[u + 1], states[u + 1]))
        run_interleaved(gens, offs=1, maxact=2)
        post(unit, states[u])
    k.set_ring(8)


IN_SPECS = [
    ("xp", [NTOK, D], F32), ("xs", [NS, D], F32), ("sconv", [4, 3, 1536], F32), ("srec", [4, 4, 128, 128], F32),
    ("ck", [4, 2048, 512], F32), ("cv", [4, 2048, 512], F32),
    ("norm_mix", [D], F32), ("w_in", [D, 3592], F32), ("w_conv", [4, 1536], F32), ("a_log", [4], F32), ("dt_bias", [4], F32),
    ("noa", [128], F32), ("nob", [64], F32), ("w_out", [D, D], F32), ("norm_ffn", [D], F32),
    ("w_gate", [D, DFF], F32), ("w_up", [D, DFF], F32), ("w_down", [DFF, D], F32), ("norm_final", [D], F32),
    ("cf32", [128, 1664], F32), ("vmask", [128, 1], F32), ("edge8", [128, 1], F32),
    ("pbias", [128, 8, 3, 256], BF16), ("sbias_c", [128, 16, 192], BF16), ("sbias_n", [NS, 4, 192], BF16), ("gmb", [65, 64], BF16),
]
OUT_SPECS = [
    ("yp", [NMAIN, D]), ("ys", [NS, D]), ("conv_p", [3, 1536]), ("rec_p", [4, 128, 128]), ("wk_p", [NMAIN, 512]), ("wv_p", [NMAIN, 512]),
    ("conv_s", [4, 3, 1536]), ("rec_s", [4, 4, 128, 128]), ("wk_s", [4, 2048, 512]), ("wv_s", [4, 2048, 512]),
]


def build_program(stages="GKBC", dbg=False):
    nc = bass.Bass("TRN2", target_bir_lowering=False)
    IO = {}
    if dbg:
        IO["dbg_ma"] = nc.dram_tensor("dbg_ma", [128, 4, NMAIN + NS], F32, kind="ExternalOutput").ap()
        IO["dbg_mb"] = nc.dram_tensor("dbg_mb", [128, 4, NMAIN + NS], F32, kind="ExternalOutput").ap()
    for name, shape, dt in IN_SPECS:
        IO[name] = nc.dram_tensor(name, shape, dt, kind="ExternalInput").ap()
    for name, shape in OUT_SPECS:
        IO[name] = nc.dram_tensor(name, shape, F32, kind="ExternalOutput").ap()
    k = KB(nc)
    C = {}
    setup_consts(k, C, IO)
    C["mixT_a"] = k.A.alloc("mixT_a", [128, 4, NMAIN + NS], BF16)
    if "G" in stages:
        stage_G(k, C, IO)
    if "K" in stages:
        stage_K(k, C, IO)
    if "B" in stages:
        stage_B(k, C, IO)
    if dbg:
        k.dma("pool", IO["dbg_ma"][:, :, :], C["mixT_a"][:, :, :], R=[C["mixT_a"]])
        k.dma("pool", IO["dbg_mb"][:, :, :], C["mixT_b"][:, :, :], R=[C["mixT_b"]])
    if "C" in stages:
        stage_C(k, C, IO)
    k.S.finish()
    k.S.emit()
    return nc, k


def host_tables():
    p = np.arange(128)
    ident = np.eye(128, dtype=np.float32)
    mSU = (p[:, None] < p[None, :]).astype(np.float32)
    mIU = (p[:, None] <= p[None, :]).astype(np.float32)
    ones = np.ones((128, 128), np.float32)
    cf = np.concatenate([np.tile(ident, (1, 4)), np.tile(mSU, (1, 4)), np.tile(mIU, (1, 4)), ones], axis=1)
    slopes = 2.0 ** (-np.arange(1, 9, dtype=np.float64))
    ki = p[:, None].astype(np.float64)
    qi = p[None, :].astype(np.float64)
    pbias = np.zeros((128, 8, 3, 256), np.float64)
    for h in range(8):
        for br, d in enumerate(DILS):
            j = qi - ki
            pbias[:, h, br, 0:128] = np.where(j >= 0, -slopes[h] * d * j * 8.0, NEGB)
            j = qi + 128 - ki
            pbias[:, h, br, 128:256] = np.where(j <= 128, -slopes[h] * d * j * 8.0, NEGB)
    sbc = np.full((128, 16, 192), NEGB, np.float64)
    sbn = np.full((NS, 4, 192), NEGB, np.float64)
    for h in range(8):
        for br, d in enumerate(DILS):
            for t in range(8):
                col = h * 24 + br * 8 + t
                kp = np.arange(2048)
                dist = 2048 + t - kp
                ok = (dist % d == 0) & (dist // d <= 128)
                vals = np.where(ok, -slopes[h] * dist * 8.0, NEGB)
                sbc[:, :, col] = vals.reshape(16, 128).T
                for s in range(4):
                    for t2 in range(8):
                        dist2 = t - t2
                        if dist2 >= 0 and dist2 % d == 0:
                            sbn[8 * s + t2, s, col] = -slopes[h] * dist2 * 8.0
    gm = np.full((65, 64), 1.0 / 64, np.float64)
    gm[64, :] = EPS
    vmask = (p < 8).astype(np.float32).reshape(128, 1)
    bf = ml_dtypes.bfloat16
    return dict(cf32=cf, vmask=vmask, pbias=pbias.astype(np.float32).astype(bf), sbias_c=sbc.astype(np.float32).astype(bf),
                sbias_n=sbn.astype(np.float32).astype(bf), gmb=gm.astype(np.float32).astype(bf))


_CACHE = {}


def make_in_maps(x_prompt, x_sample, state_conv, state_rec, cache_win_k, cache_win_v, norm_mix, w_in, w_conv, a_log, dt_bias,
                 norm_out_a, norm_out_b, w_out, norm_ffn, w_gate, w_up, w_down, norm_final):
    f = lambda a: np.ascontiguousarray(np.asarray(a, dtype=np.float32))
    tabs = host_tables()
    shared = dict(norm_mix=f(norm_mix[0]), w_in=f(w_in[0]), w_conv=f(w_conv[0]), a_log=f(a_log[0]), dt_bias=f(dt_bias[0]),
                  noa=f(norm_out_a[0]), nob=f(norm_out_b[0]), w_out=f(w_out[0]), norm_ffn=f(norm_ffn[0]),
                  w_gate=f(w_gate[0]), w_up=f(w_up[0]), w_down=f(w_down[0]), norm_final=f(norm_final), **tabs)
    in_maps = []
    for c in range(NCORES):
        b, half = c // 2, c % 2
        xp = np.zeros((NTOK, D), np.float32)
        if half == 1:
            xp[:] = x_prompt[b]
        else:
            xp[NPRE:] = x_prompt[b, 0:NMAIN]
        sl = slice(4 * c, 4 * c + 4)
        m = dict(shared)
        m.update(xp=xp, xs=f(x_sample[sl]).reshape(NS, D), sconv=f(state_conv[0, sl]), srec=f(state_rec[0, sl]),
                 ck=f(cache_win_k[0, sl]).reshape(4, 2048, 512), cv=f(cache_win_v[0, sl]).reshape(4, 2048, 512),
                 edge8=np.full((128, 1), 0.0 if half == 1 else NEGB, np.float32))
        in_maps.append(m)
    return in_maps


def assemble(res):
    y_prompt = np.zeros((4, 4096, D), np.float32)
    y_sample = np.zeros((32, 8, D), np.float32)
    conv_p = np.zeros((1, 4, 3, 1536), np.float32)
    rec_p = np.zeros((1, 4, 4, 128, 128), np.float32)
    wk_p = np.zeros((1, 4, 2048, 8, 64), np.float32)
    wv_p = np.zeros((1, 4, 2048, 8, 64), np.float32)
    conv_s = np.zeros((1, 32, 3, 1536), np.float32)
    rec_s = np.zeros((1, 32, 4, 128, 128), np.float32)
    wk_s = np.zeros((1, 32, 2048, 8, 64), np.float32)
    wv_s = np.zeros((1, 32, 2048, 8, 64), np.float32)
    for c in range(NCORES):
        b, half = c // 2, c % 2
        r = res[c]
        y_prompt[b, half * NMAIN:(half + 1) * NMAIN] = r["yp"]
        sl = slice(4 * c, 4 * c + 4)
        y_sample[sl] = r["ys"].reshape(4, 8, D)
        conv_s[0, sl] = r["conv_s"]
        rec_s[0, sl] = r["rec_s"]
        wk_s[0, sl] = r["wk_s"].reshape(4, 2048, 8, 64)
        wv_s[0, sl] = r["wv_s"].reshape(4, 2048, 8, 64)
        if half == 1:
            conv_p[0, b] = r["conv_p"]
            rec_p[0, b] = r["rec_p"]
            wk_p[0, b] = r["wk_p"].reshape(2048, 8, 64)
            wv_p[0, b] = r["wv_p"].reshape(2048, 8, 64)
    return (y_prompt, y_sample, conv_p, rec_p, wk_p, wv_p, conv_s, rec_s, wk_s, wv_s)


def kernel(**inputs):
    in_maps = make_in_maps(**inputs)
    if "nc" not in _CACHE:
        _CACHE["nc"] = build_program()[0]
    res = run_bass_kernel_spmd(_CACHE["nc"], in_maps, core_ids=list(range(NCORES)))
    return assemble(res.results)
```
